# Optimizing a Trainium2 kernel written in Bass

```python
import math
import jax, jax.numpy as jnp
from jax import lax
import numpy as np

D_MODEL = 1024
BATCH = 8
SEQ = 2048
DEPTH = 1
DEC_BATCH = 128
DEC_SEQ = 8
PAST_LEN = 2048
PAGE_SIZE = 128

N_DIFF_HEADS = 4
DIFF_HEAD_DIM = 64
DIFF_V_DIM = 2 * DIFF_HEAD_DIM
ATTN_WIDTH = N_DIFF_HEADS * DIFF_V_DIM
POOL_WINDOWS = (2, 4, 8, 16)
N_POOL_GROUPS = len(POOL_WINDOWS)
POOL_GROUP_DIM = 64
POOL_WIDTH = N_POOL_GROUPS * POOL_GROUP_DIM
POOL_STATE = max(POOL_WINDOWS) - 1
N_MEM_HEADS = 4
MEM_HEAD_DIM = 64
MEM_WIDTH = N_MEM_HEADS * MEM_HEAD_DIM
N_MEM = 256
N_BRANCHES = 3
D_FF = 2816
CONV_WIDTH = 3
N_BUCKETS = 32
MAX_DISTANCE = 128
Q_BLOCK = 128
EPS = 1e-6
NEG_INF = -1e30
IN_SIZES = (N_DIFF_HEADS * 2 * DIFF_HEAD_DIM, N_DIFF_HEADS * 2 * DIFF_HEAD_DIM, ATTN_WIDTH,
            POOL_WIDTH, MEM_WIDTH, N_BRANCHES * D_MODEL)
D_IN = sum(IN_SIZES)
IN_SPLITS = tuple(int(s) for s in np.cumsum(IN_SIZES)[:-1])

kernel_name = 'hybrid_diffattn_pool_memxattn_convffn_step'


def rmsnorm(x, g):
    xf = x.astype(jnp.float32)
    y = xf * lax.rsqrt(jnp.mean(xf * xf, axis=-1, keepdims=True) + EPS)
    return (y * g.astype(jnp.float32)).astype(x.dtype)


def lambda_init(layer_idx):
    return 0.8 - 0.6 * math.exp(-0.3 * layer_idx)


def rel_bucket(rel):
    n = jnp.maximum(rel, 0)
    max_exact = N_BUCKETS // 2
    nf = jnp.maximum(n, 1).astype(jnp.float32)
    large = max_exact + (jnp.log(nf / max_exact) / math.log(MAX_DISTANCE / max_exact)
                         * (N_BUCKETS - max_exact)).astype(jnp.int32)
    large = jnp.minimum(large, N_BUCKETS - 1)
    return jnp.where(n < max_exact, n, large)


def diff_attn_block(qb, qpos_b, k, v, k_pos, rel_bias, lam):
    s = jnp.einsum('bqhcd,bkhcd->bchqk', qb.astype(jnp.float32), k.astype(jnp.float32)) * (DIFF_HEAD_DIM ** -0.5)
    rel = qpos_b[:, None] - k_pos[None, :]
    bias = jnp.transpose(rel_bias.astype(jnp.float32)[rel_bucket(rel)], (2, 0, 1))
    s = jnp.where(rel[None, None, None] >= 0, s + bias[None, None], NEG_INF)
    p = jax.nn.softmax(s, axis=-1)
    a = p[:, 0] - lam * p[:, 1]
    return jnp.einsum('bhqk,bkhe->bqhe', a, v.astype(jnp.float32))


def diff_attention(q, k, v, q_pos, k_pos, rel_bias, lam):
    B, Q = q.shape[0], q.shape[1]
    blk = Q_BLOCK if Q % Q_BLOCK == 0 else Q
    nb = Q // blk
    qb = jnp.moveaxis(q.reshape(B, nb, blk, N_DIFF_HEADS, 2, DIFF_HEAD_DIM), 1, 0)
    pb = q_pos.reshape(nb, blk)
    out = lax.map(lambda a: diff_attn_block(a[0], a[1], k, v, k_pos, rel_bias, lam), (qb, pb))
    return jnp.moveaxis(out, 0, 1).reshape(B, Q, N_DIFF_HEADS, DIFF_V_DIM)


def pool_mixer(u, prefix, pos, w_grp, scale):
    B, L, _ = u.shape
    P = prefix.shape[1]
    ext = jnp.concatenate([prefix, u], axis=1)
    c = jnp.pad(jnp.cumsum(ext.astype(jnp.float32), axis=1), ((0, 0), (1, 0), (0, 0)))
    means = []
    for gi, w in enumerate(POOL_WINDOWS):
        sl = slice(gi * POOL_GROUP_DIM, (gi + 1) * POOL_GROUP_DIM)
        win_sum = c[:, P + 1:P + 1 + L, sl] - c[:, P + 1 - w:P + 1 - w + L, sl]
        cnt = jnp.minimum(pos + 1, w).astype(jnp.float32)[None, :, None]
        means.append(win_sum / cnt)
    d = (jnp.concatenate(means, axis=-1) - u.astype(jnp.float32)).reshape(B, L, N_POOL_GROUPS, POOL_GROUP_DIM)
    y = jnp.einsum('blgc,gcd->blgd', d, w_grp.astype(jnp.float32)).reshape(B, L, POOL_WIDTH)
    y = y * scale.astype(jnp.float32)
    return y.astype(u.dtype), ext[:, ext.shape[1] - P:]


def mem_kv(mem, g, w):
    Bm, M, _ = mem.shape
    k, v = jnp.split(rmsnorm(mem, g) @ w, 2, axis=-1)
    return (k.reshape(Bm, M, N_MEM_HEADS, MEM_HEAD_DIM), v.reshape(Bm, M, N_MEM_HEADS, MEM_HEAD_DIM))


def conv_ffn(h, prefix, w_gate, w_up, conv_w, conv_b, w_down):
    L = h.shape[1]
    g = h @ w_gate
    u = h @ w_up
    ext = jnp.concatenate([prefix, g], axis=1)
    gc = conv_b
    for j in range(CONV_WIDTH):
        gc = gc + conv_w[j] * ext[:, j:j + L]
    f = (jax.nn.gelu(gc) * u) @ w_down
    return f, ext[:, ext.shape[1] - (CONV_WIDTH - 1):]


def layer(x, pos, k_past, v_past, k_pos, pool_prefix, conv_prefix, mem_k, mem_v, prm, rel_bias, lam_init):
    (norm1_g, w_in, lam_q1, lam_k1, lam_q2, lam_k2, subln_g, w_pool_grp, pool_scale,
     w_br_attn, w_br_pool, w_br_mem, w_out, norm2_g, w_ffn_gate, w_ffn_up,
     ffn_conv_w, ffn_conv_b, w_ffn_down) = prm
    B, L, _ = x.shape
    h = rmsnorm(x, norm1_g)
    q, k, v, u, qm, gl = jnp.split(h @ w_in, IN_SPLITS, axis=-1)
    q = q.reshape(B, L, N_DIFF_HEADS, 2, DIFF_HEAD_DIM)
    k = k.reshape(B, L, N_DIFF_HEADS, 2 * DIFF_HEAD_DIM)
    v = v.reshape(B, L, N_DIFF_HEADS, DIFF_V_DIM)
    k_all = k if k_past is None else jnp.concatenate([k_past.astype(k.dtype), k], axis=1)
    v_all = v if v_past is None else jnp.concatenate([v_past.astype(v.dtype), v], axis=1)
    K = k_all.shape[1]
    f32 = jnp.float32
    lam = (jnp.exp(jnp.sum(lam_q1.astype(f32) * lam_k1.astype(f32)))
           - jnp.exp(jnp.sum(lam_q2.astype(f32) * lam_k2.astype(f32))) + lam_init)
    o = diff_attention(q, k_all.reshape(B, K, N_DIFF_HEADS, 2, DIFF_HEAD_DIM), v_all, pos, k_pos, rel_bias, lam)
    o = (rmsnorm(o, subln_g) * (1.0 - lam_init)).astype(x.dtype).reshape(B, L, ATTN_WIDTH)
    pool_out, pool_state = pool_mixer(u, pool_prefix.astype(u.dtype), pos, w_pool_grp, pool_scale)
    qm = qm.reshape(B, L, N_MEM_HEADS, MEM_HEAD_DIM)
    sm = jnp.einsum('bqhd,bmhd->bhqm', qm.astype(f32), mem_k.astype(f32)) * (MEM_HEAD_DIM ** -0.5)
    pm = jax.nn.softmax(sm, axis=-1)
    om = jnp.einsum('bhqm,bmhd->bqhd', pm, mem_v.astype(f32)).astype(x.dtype).reshape(B, L, MEM_WIDTH)
    ga, gb, gm = jnp.split(jax.nn.sigmoid(gl), N_BRANCHES, axis=-1)
    merged = ga * (o @ w_br_attn) + gb * (pool_out @ w_br_pool) + gm * (om @ w_br_mem)
    x = x + merged @ w_out
    f, conv_state = conv_ffn(rmsnorm(x, norm2_g), conv_prefix.astype(x.dtype), w_ffn_gate, w_ffn_up,
                             ffn_conv_w, ffn_conv_b, w_ffn_down)
    x = x + f
    return x, k, v, pool_state, conv_state


def setup_inputs(seed: int = 0) -> dict:
    key = jax.random.key(seed)
    ks = jax.random.split(key, 40)
    n_pages = PAST_LEN // PAGE_SIZE
    n_phys = (DEC_BATCH * n_pages * 5) // 4
    nrm = lambda i, shape, s=1.0: jax.random.normal(ks[i], shape, jnp.float32) * s
    gain = lambda i, shape: 1.0 + 0.02 * jax.random.normal(ks[i], shape, jnp.float32)
    kv_shape = (DEPTH, n_phys, PAGE_SIZE, N_DIFF_HEADS, 2 * DIFF_HEAD_DIM)
    page_table = jax.random.permutation(ks[5], n_phys)[:DEC_BATCH * n_pages].reshape(DEC_BATCH, n_pages).astype(jnp.int32)
    return {
        'x_prompt': nrm(0, (BATCH, SEQ, D_MODEL)),
        'x_sample': nrm(1, (DEC_BATCH, DEC_SEQ, D_MODEL)),
        'mem_prompt': nrm(2, (BATCH, N_MEM, D_MODEL)),
        'cache_k': nrm(3, kv_shape),
        'cache_v': nrm(4, kv_shape),
        'page_table': page_table,
        'state_pool': nrm(6, (DEPTH, DEC_BATCH, POOL_STATE, POOL_WIDTH)),
        'state_ffn_conv': nrm(7, (DEPTH, DEC_BATCH, CONV_WIDTH - 1, D_FF)),
        'cache_mem_k': nrm(8, (DEPTH, DEC_BATCH, N_MEM, N_MEM_HEADS, MEM_HEAD_DIM)),
        'cache_mem_v': nrm(9, (DEPTH, DEC_BATCH, N_MEM, N_MEM_HEADS, MEM_HEAD_DIM)),
        'norm1_g': gain(10, (DEPTH, D_MODEL)),
        'w_in': nrm(11, (DEPTH, D_MODEL, D_IN), D_MODEL ** -0.5),
        'lam_q1': nrm(12, (DEPTH, DIFF_HEAD_DIM), 0.1),
        'lam_k1': nrm(13, (DEPTH, DIFF_HEAD_DIM), 0.1),
        'lam_q2': nrm(14, (DEPTH, DIFF_HEAD_DIM), 0.1),
        'lam_k2': nrm(15, (DEPTH, DIFF_HEAD_DIM), 0.1),
        'subln_g': gain(16, (DEPTH, DIFF_V_DIM)),
        'w_pool_grp': nrm(17, (DEPTH, N_POOL_GROUPS, POOL_GROUP_DIM, POOL_GROUP_DIM), POOL_GROUP_DIM ** -0.5),
        'pool_scale': 1.0 + 0.1 * nrm(18, (DEPTH, POOL_WIDTH)),
        'w_br_attn': nrm(19, (DEPTH, ATTN_WIDTH, D_MODEL), ATTN_WIDTH ** -0.5),
        'w_br_pool': nrm(20, (DEPTH, POOL_WIDTH, D_MODEL), POOL_WIDTH ** -0.5),
        'w_br_mem': nrm(21, (DEPTH, MEM_WIDTH, D_MODEL), MEM_WIDTH ** -0.5),
        'mem_norm_g': gain(22, (DEPTH, D_MODEL)),
        'w_mem_kv': nrm(23, (DEPTH, D_MODEL, 2 * MEM_WIDTH), D_MODEL ** -0.5),
        'w_out': nrm(24, (DEPTH, D_MODEL, D_MODEL), D_MODEL ** -0.5),
        'norm2_g': gain(25, (DEPTH, D_MODEL)),
        'w_ffn_gate': nrm(26, (DEPTH, D_MODEL, D_FF), D_MODEL ** -0.5),
        'w_ffn_up': nrm(27, (DEPTH, D_MODEL, D_FF), D_MODEL ** -0.5),
        'ffn_conv_w': nrm(28, (DEPTH, CONV_WIDTH, D_FF), CONV_WIDTH ** -0.5),
        'ffn_conv_b': nrm(29, (DEPTH, D_FF), 0.01),
        'w_ffn_down': nrm(30, (DEPTH, D_FF, D_MODEL), D_FF ** -0.5),
        'rel_bias': nrm(31, (N_BUCKETS, N_DIFF_HEADS), 0.5),
        'final_norm_g': gain(32, (D_MODEL,)),
    }


def reference(x_prompt, x_sample, mem_prompt, cache_k, cache_v, page_table, state_pool, state_ffn_conv,
              cache_mem_k, cache_mem_v, norm1_g, w_in, lam_q1, lam_k1, lam_q2, lam_k2, subln_g,
              w_pool_grp, pool_scale, w_br_attn, w_br_pool, w_br_mem, mem_norm_g, w_mem_kv, w_out,
              norm2_g, w_ffn_gate, w_ffn_up, ffn_conv_w, ffn_conv_b, w_ffn_down, rel_bias, final_norm_g):
    B, S, _ = x_prompt.shape
    DB, DS, _ = x_sample.shape
    n_pages = page_table.shape[1]
    page = cache_k.shape[2]
    past = n_pages * page
    pos_p = jnp.arange(S, dtype=jnp.int32)
    pos_s = past + jnp.arange(DS, dtype=jnp.int32)
    kpos_s = jnp.arange(past + DS, dtype=jnp.int32)
    xp, xs = x_prompt, x_sample
    kp_l, vp_l, ks_l, vs_l, poolp_l, pools_l, convp_l, convs_l, mkp_l, mvp_l = ([] for _ in range(10))
    for l in range(DEPTH):
        lam_init = lambda_init(l)
        prm = (norm1_g[l], w_in[l], lam_q1[l], lam_k1[l], lam_q2[l], lam_k2[l], subln_g[l], w_pool_grp[l],
               pool_scale[l], w_br_attn[l], w_br_pool[l], w_br_mem[l], w_out[l], norm2_g[l], w_ffn_gate[l],
               w_ffn_up[l], ffn_conv_w[l], ffn_conv_b[l], w_ffn_down[l])
        mk_p, mv_p = mem_kv(mem_prompt, mem_norm_g[l], w_mem_kv[l])
        pool0 = jnp.zeros((B, POOL_STATE, POOL_WIDTH), xp.dtype)
        conv0 = jnp.zeros((B, CONV_WIDTH - 1, D_FF), xp.dtype)
        xp, kp, vp, poolp, convp = layer(xp, pos_p, None, None, pos_p, pool0, conv0, mk_p, mv_p,
                                         prm, rel_bias, lam_init)
        k_past = cache_k[l][page_table].reshape(DB, past, N_DIFF_HEADS, 2 * DIFF_HEAD_DIM)
        v_past = cache_v[l][page_table].reshape(DB, past, N_DIFF_HEADS, DIFF_V_DIM)
        xs, ks_, vs_, pools, convs = layer(xs, pos_s, k_past, v_past, kpos_s, state_pool[l], state_ffn_conv[l],
                                           cache_mem_k[l], cache_mem_v[l], prm, rel_bias, lam_init)
        kp_l.append(kp); vp_l.append(vp); ks_l.append(ks_); vs_l.append(vs_)
        poolp_l.append(poolp); pools_l.append(pools); convp_l.append(convp); convs_l.append(convs)
        mkp_l.append(mk_p); mvp_l.append(mv_p)
    y_prompt = rmsnorm(xp, final_norm_g)
    y_sample = rmsnorm(xs, final_norm_g)
    return (y_prompt, y_sample, jnp.stack(kp_l), jnp.stack(vp_l), jnp.stack(ks_l), jnp.stack(vs_l),
            jnp.stack(poolp_l), jnp.stack(pools_l), jnp.stack(convp_l), jnp.stack(convs_l),
            jnp.stack(mkp_l), jnp.stack(mvp_l))
```

```python
import numpy as np
from contextlib import ExitStack

import concourse.bass as bass
import concourse.mybir as mybir
from concourse.bass_utils import run_bass_kernel_spmd

F32 = mybir.dt.float32
BF16 = mybir.dt.bfloat16
I32 = mybir.dt.int32
AF = mybir.ActivationFunctionType
ALU = mybir.AluOpType
AX = mybir.AxisListType

D = 1024
NTOK = 2176
NT = 17
D_IN = 5120
D_FF = 2816
NFF = 22
EPS = 1e-6
LAM_INIT = 0.8 - 0.6
NSEQ = 16
NPAGE = 16
WZ = 384

TCH = [(0, 512), (512, 512), (1024, 512), (1536, 512), (2048, 128)]
GROUPS = [(0, 4), (4, 8), (8, 12), (12, 16), (16, 17)]


class Buf:
    __slots__ = ("name", "w", "r")

    def __init__(self, name):
        self.name = name
        self.w = None
        self.r = []


class Op:
    __slots__ = ("eng", "fn", "waits", "signal", "idx", "count", "dma", "dsem", "dval", "pre")

    def __init__(self, eng, fn, dma):
        self.eng = eng
        self.fn = fn
        self.waits = []
        self.signal = False
        self.idx = -1
        self.count = 0
        self.dma = dma
        self.dsem = None
        self.dval = 0
        self.pre = None


ENGS = ("pe", "act", "dve", "pool", "sp")
NDSEM = 24


class K:
    def __init__(self, nc, es):
        self.nc = nc
        self.ops = {e: [] for e in ENGS}
        self.waited = {e: {p: -1 for p in ENGS} for e in ENGS}
        self.waited_dma = {e: set() for e in ENGS}
        self.sem = {e: es.enter_context(nc.semaphore("s_" + e)) for e in ENGS}
        self.dsems = {q: [es.enter_context(nc.semaphore("d_%s%d" % (q, i))) for i in range(NDSEM)]
                      for q in ("sp", "pool")}
        self.ndma = {"sp": 0, "pool": 0}
        self.dma_ops = {"sp": [], "pool": []}

    def _dep(self, op, d, force=False):
        e = op.eng
        if d is None or d is op:
            return
        if d.dma:
            if id(d) in self.waited_dma[e]:
                return
            self.waited_dma[e].add(id(d))
            op.waits.append(d)
            return
        p = d.eng
        if p == "pe" and e == "pe" and not force:
            return
        if self.waited[e][p] >= d.idx:
            return
        self.waited[e][p] = d.idx
        d.signal = True
        op.waits.append(d)

    def op(self, eng, fn, reads=(), writes=(), dma=False):
        o = Op(eng, fn, dma)
        o.idx = len(self.ops[eng])
        for b in reads:
            self._dep(o, b.w)
        for b in writes:
            self._dep(o, b.w)
            for r in b.r:
                self._dep(o, r)
        if dma:
            n = self.ndma[eng]
            self.ndma[eng] += 1
            o.dsem = self.dsems[eng][n % NDSEM]
            o.dval = 16 * (n // NDSEM + 1)
            if n >= NDSEM:
                prev = self.dma_ops[eng][n - NDSEM]
                o.pre = prev
            self.dma_ops[eng].append(o)
        self.ops[eng].append(o)
        for b in reads:
            b.r.append(o)
        for b in writes:
            b.w = o
            b.r = []
        return o

    def pe(self, fn, reads=(), writes=()):
        return self.op("pe", fn, reads, writes)

    def act(self, fn, reads=(), writes=()):
        return self.op("act", fn, reads, writes)

    def dve(self, fn, reads=(), writes=()):
        return self.op("dve", fn, reads, writes)

    def pool(self, fn, reads=(), writes=()):
        return self.op("pool", fn, reads, writes)

    def dma(self, q, out, in_, reads=(), writes=(), **kw):
        return self.op(q, lambda e: e.dma_start(out=out, in_=in_, **kw), reads, writes, dma=True)

    def barrier(self):
        lasts = []
        for e in ENGS:
            real = [o for o in self.ops[e] if o.fn is not None and not o.dma]
            if real:
                lasts.append(real[-1])
        dmas = self.dma_ops["sp"][-NDSEM:] + self.dma_ops["pool"][-NDSEM:]
        for e in ENGS:
            o = Op(e, None, False)
            o.idx = len(self.ops[e])
            for d in lasts + dmas:
                self._dep(o, d, force=True)
            self.ops[e].append(o)

    def emit(self, block):
        for e in ENGS:
            c = 0
            for o in self.ops[e]:
                if o.signal:
                    c += 1
                    o.count = c
        def run(e, eng):
            for o in self.ops[e]:
                if o.pre is not None:
                    eng.wait_ge(o.pre.dsem, o.pre.dval)
                for d in o.waits:
                    if d.dma:
                        eng.wait_ge(d.dsem, d.dval)
                    else:
                        eng.wait_ge(self.sem[d.eng], d.count)
                if o.fn is None:
                    continue
                ins = o.fn(eng)
                if o.dma:
                    ins.then_inc(o.dsem, 16)
                elif o.signal:
                    ins.then_inc(self.sem[e], 1)

        @block.tensor
        def _(eng):
            run("pe", eng)

        @block.scalar
        def _(eng):
            run("act", eng)

        @block.vector
        def _(eng):
            run("dve", eng)

        @block.gpsimd
        def _(eng):
            run("pool", eng)

        @block.sync
        def _(eng):
            run("sp", eng)


class Arena:
    def __init__(self, nc, base, cap):
        self.nc = nc
        self.base = base
        self.cap = cap
        self.top = base
        self.n = 0

    def alloc(self, shape, dtype, name=None):
        nbytes = int(np.prod(shape[1:])) * mybir.dt.size(dtype)
        off = (self.top + 31) // 32 * 32
        assert off + nbytes <= self.cap, ("SBUF arena overflow", name, off, nbytes, self.cap)
        self.top = off + nbytes
        self.n += 1
        nm = "%s_%d_%d" % (name or "t", off, self.n)
        return self.nc.alloc_sbuf_tensor_at(nm, list(shape), dtype, offset=off)

    def mark(self):
        return self.top

    def reset(self, m):
        self.top = m


def rel_bucket_np(rel):
    n = np.maximum(rel, 0)
    max_exact = 16
    nf = np.maximum(n, 1).astype(np.float32)
    large = max_exact + (np.log(nf / max_exact) / np.log(128 / max_exact) * (32 - max_exact)).astype(np.int32)
    large = np.minimum(large, 31)
    return np.where(n < max_exact, n, large)


def host_constants():
    c = {}
    c["ident"] = np.eye(128, dtype=np.float32)
    rel = np.arange(WZ) - 128
    b = rel_bucket_np(rel)
    oh = np.zeros((32, WZ), np.float32)
    oh[b, np.arange(WZ)] = 1.0
    oh[:, rel < 0] = 0.0
    c["bucket_oh"] = oh
    c["relmask"] = np.repeat((rel >= 0).astype(np.float32)[None, :], 128, axis=0)
    pc = np.zeros((128, 2, 16), np.float32)
    for ch in range(2):
        for p in range(128):
            w = 2 ** (2 * ch + p // 64 + 1)
            pc[p, ch, :] = 1.0 / np.minimum(np.arange(16) + 1, w)
    c["poolc"] = pc
    bd = np.zeros((128, 16), np.float32)
    bd[np.arange(128), np.arange(128) // 8] = 1.0
    c["blockdiag"] = bd
    sel = np.zeros((64, 2, 32), np.float32)
    for h in range(4):
        for cc in range(2):
            for q in range(8):
                sel[h * 16 + cc * 8 + q, cc, h * 8 + q] = 1.0
    c["sel"] = sel
    c["iota_f"] = np.arange(128, dtype=np.float32).reshape(128, 1)
    return c


CONST_SHAPES = {
    "ident": ([128, 128], F32), "bucket_oh": ([32, WZ], F32), "relmask": ([128, WZ], F32),
    "poolc": ([128, 2, 16], F32), "blockdiag": ([128, 16], F32), "sel": ([64, 2, 32], F32),
    "iota_f": ([128, 1], F32),
}

IN_SHAPES = {
    "x_all": ([NTOK, D], F32), "mem": ([256, D], F32),
    "cache_k": ([2560 * 128, 512], F32), "cache_v": ([2560 * 128, 512], F32),
    "page_table": ([1, NSEQ * NPAGE], I32),
    "state_pool": ([NSEQ * 15, 256], F32), "state_conv": ([NSEQ * 2, D_FF], F32),
    "cmem_k": ([NSEQ, 256, 256], F32), "cmem_v": ([NSEQ, 256, 256], F32),
    "norm1_g": ([D], F32), "w_in": ([D, D_IN], F32),
    "lam_q1": ([1, 64], F32), "lam_k1": ([1, 64], F32), "lam_q2": ([1, 64], F32), "lam_k2": ([1, 64], F32),
    "subln_g": ([128], F32), "w_pool_grp": ([4, 64, 64], F32), "pool_scale": ([256], F32),
    "w_br_attn": ([512, D], F32), "w_br_pool": ([256, D], F32), "w_br_mem": ([256, D], F32),
    "mem_norm_g": ([D], F32), "w_mem_kv": ([D, 512], F32), "w_out": ([D, D], F32),
    "norm2_g": ([D], F32), "w_ffn_gate": ([D, D_FF], F32), "w_ffn_up": ([D, D_FF], F32),
    "ffn_conv_w": ([3, D_FF], F32), "ffn_conv_b": ([D_FF], F32), "w_ffn_down": ([D_FF, D], F32),
    "rel_bias": ([32, 4], F32), "final_norm_g": ([D], F32),
}

OUT_SHAPES = {
    "y_all": [NTOK, D], "newk": [NTOK, 512], "newv": [NTOK, 512],
    "pool_p": [15, 256], "pool_s": [NSEQ, 15, 256],
    "conv_all": [2 + 2 * NSEQ, D_FF],
    "memk": [256, 256], "memv": [256, 256],
}


def build_program(phases=("all",), debug=None, nphys=2560):
    nc = bass.Bass("TRN2", target_bir_lowering=False)
    I = {}
    for k, (shp, dt) in {**IN_SHAPES, **CONST_SHAPES}.items():
        if k in ("cache_k", "cache_v"):
            shp = [nphys * 128, 512]
        I[k] = nc.dram_tensor(k, shp, dt, kind="ExternalInput").ap()
    O = {}
    for k, shp in OUT_SHAPES.items():
        O[k] = nc.dram_tensor(k, shp, F32, kind="ExternalOutput").ap()
    zscr = nc.dram_tensor("zscr", [128, 4 * WZ], F32, kind="Internal").ap()
    dbg_out = {}
    if debug:
        for k, shp in debug.items():
            dbg_out[k] = nc.dram_tensor("dbg_" + k, shp, F32, kind="ExternalOutput").ap()

    with ExitStack() as es:
        kk = K(nc, es)
        banks = [es.enter_context(nc.psum_tensor("bank%d" % i, [128, 512], F32)) for i in range(8)]
        pb = [Buf("psum%d" % i) for i in range(8)]
        block = es.enter_context(nc.Block())
        _build(nc, kk, I, O, zscr, banks, pb, dbg_out, phases)
        kk.emit(block)
    return nc


def _build(nc, kk, I, O, zscr, banks, pb, dbg_out, phases):
    ALL = "all" in phases
    STOP = [p for p in phases if p.startswith("p")]
    STOP = STOP[0] if STOP else None
    ar = Arena(nc, (nc.sbuf_base + 63) // 64 * 64, nc.sbuf_top)

    def mm(out, lhsT, rhs, start, stop):
        return lambda e: e.matmul(out, lhsT, rhs, start=start, stop=stop)

    def f_tt(out, in0, in1, op):
        return lambda e: e.tensor_tensor(out=out, in0=in0, in1=in1, op=op)

    def f_ts(out, in0, s1, s2, op0, op1=None):
        if op1 is None:
            return lambda e: e.tensor_scalar(out=out, in0=in0, scalar1=s1, scalar2=None, op0=op0)
        return lambda e: e.tensor_scalar(out=out, in0=in0, scalar1=s1, scalar2=s2, op0=op0, op1=op1)

    def f_stt(out, in0, scalar, in1, op0, op1):
        return lambda e: e.scalar_tensor_tensor(out=out, in0=in0, scalar=scalar, in1=in1, op0=op0, op1=op1)

    def f_copy(out, in_):
        return lambda e: e.tensor_copy(out=out, in_=in_)

    def f_act(out, in_, func, **kw):
        return lambda e: e.activation(out=out, in_=in_, func=func, **kw)

    def f_recip(out, in_):
        return lambda e: e.reciprocal(out=out, in_=in_)

    def f_memset(ap, v):
        return lambda e: e.memset(ap, v)

    def f_tr(out, in_, ident):
        return lambda e: e.transpose(out, in_, ident)

    cst = {}
    ident_f = ar.alloc([128, 128], F32, "identf")
    ident_b = ar.alloc([128, 128], BF16, "identb")
    ones_f = ar.alloc([128, 128], F32, "onesf")
    ones_b = ar.alloc([128, 128], BF16, "onesb")
    g1T = ar.alloc([128, 8], F32, "g1T")
    g2T = ar.alloc([128, 8], F32, "g2T")
    gmT = ar.alloc([128, 8], F32, "gmT")
    gfin = ar.alloc([128, D], F32, "gfin")
    sublnT = ar.alloc([128, 1], F32, "subln")
    pscaleT = ar.alloc([128, 2], F32, "pscale")
    convw = ar.alloc([128, 3, NFF], F32, "convw")
    convb = ar.alloc([128, NFF], F32, "convb")
    lamv = ar.alloc([128, 4, 64], F32, "lamv")
    lamt = ar.alloc([128, 8], F32, "lamt")
    neg_lam = ar.alloc([128, 1], F32, "neglam")
    eps_t = ar.alloc([128, 1], F32, "eps")
    poolc = ar.alloc([128, 2, 16], F32, "poolc")
    bdiag = ar.alloc([128, 16], F32, "bdiag")
    selc = ar.alloc([64, 2, 32], F32, "sel")
    comb = ar.alloc([64, 32], F32, "comb")
    Bc = Buf("consts")

    kk.dma("sp", ident_f[:], I["ident"][:, :], writes=[Bc])
    kk.dma("pool", ident_b[:], I["ident"][:, :], writes=[Bc])
    kk.dve(lambda e: e.memset(ones_f[:], 1.0), writes=[Bc])
    kk.dve(lambda e: e.memset(ones_b[:], 1.0), writes=[Bc])
    kk.dve(lambda e: e.memset(eps_t[:], EPS), writes=[Bc])
    for t, src in ((g1T, "norm1_g"), (g2T, "norm2_g"), (gmT, "mem_norm_g")):
        kk.dma("sp", t[:], I[src].rearrange("(k p) -> p k", p=128), writes=[Bc], allow_slow_non_contiguous=True)
    kk.dma("sp", gfin[:], I["final_norm_g"].partition_broadcast(128), writes=[Bc])
    kk.dma("sp", sublnT[:], I["subln_g"].rearrange("(p o) -> p o", o=1), writes=[Bc], allow_slow_non_contiguous=True)
    kk.dma("sp", pscaleT[:], I["pool_scale"].rearrange("(k p) -> p k", p=128), writes=[Bc], allow_slow_non_contiguous=True)
    kk.dma("sp", convw[:], I["ffn_conv_w"].rearrange("j (c p) -> p j c", p=128), writes=[Bc], allow_slow_non_contiguous=True)
    kk.dma("sp", convb[:], I["ffn_conv_b"].rearrange("(c p) -> p c", p=128), writes=[Bc], allow_slow_non_contiguous=True)
    for i, nm in enumerate(("lam_q1", "lam_k1", "lam_q2", "lam_k2")):
        kk.dma("sp", lamv[:, i, :], I[nm][0, :].partition_broadcast(128), writes=[Bc])
    kk.dma("sp", poolc[:], I["poolc"][:, :, :], writes=[Bc])
    kk.dma("sp", bdiag[:], I["blockdiag"][:, :], writes=[Bc])
    kk.dma("sp", selc[:], I["sel"][:, :, :], writes=[Bc])
    Bl = Buf("lam")
    kk.dve(lambda e: e.tensor_tensor(out=lamv[:, 0, :], in0=lamv[:, 0, :], in1=lamv[:, 1, :], op=ALU.mult), reads=[Bc], writes=[Bl])
    kk.dve(lambda e: e.tensor_tensor(out=lamv[:, 2, :], in0=lamv[:, 2, :], in1=lamv[:, 3, :], op=ALU.mult), reads=[Bl], writes=[Bl])
    kk.dve(lambda e: e.tensor_reduce(out=lamt[:, 0:1], in_=lamv[:, 0, :], axis=AX.X, op=ALU.add), reads=[Bl], writes=[Bl])
    kk.dve(lambda e: e.tensor_reduce(out=lamt[:, 1:2], in_=lamv[:, 2, :], axis=AX.X, op=ALU.add), reads=[Bl], writes=[Bl])
    kk.act(lambda e: e.activation(out=lamt[:, 2:4], in_=lamt[:, 0:2], func=AF.Exp), reads=[Bl], writes=[Bl])
    kk.dve(lambda e: e.tensor_tensor(out=lamt[:, 4:5], in0=lamt[:, 3:4], in1=lamt[:, 2:3], op=ALU.subtract), reads=[Bl], writes=[Bl])
    kk.dve(lambda e: e.tensor_scalar(out=neg_lam[:], in0=lamt[:, 4:5], scalar1=-LAM_INIT, scalar2=None, op0=ALU.add), reads=[Bl], writes=[Bl])
    kk.dve(lambda e: e.scalar_tensor_tensor(out=comb[:], in0=selc[:, 1, :], scalar=neg_lam[0:64, 0:1], in1=selc[:, 0, :],
                                            op0=ALU.mult, op1=ALU.add), reads=[Bl, Bc], writes=[Bl])
    kk.dve(lambda e: e.tensor_scalar(out=sublnT[:], in0=sublnT[:], scalar1=1.0 - LAM_INIT, scalar2=None, op0=ALU.mult), reads=[Bc], writes=[Bc])

    T0 = ar.alloc([128, 4, 128], F32, "T0")
    T1 = ar.alloc([128, 4, 128], F32, "T1")
    cmark = ar.mark()
    rb = ar.alloc([32, 4], F32, "rb")
    rbrep = ar.alloc([32, 4, 128], F32, "rbrep")
    oh = ar.alloc([32, WZ], F32, "oh")
    relmask = ar.alloc([128, WZ], F32, "relmask")
    erow = ar.alloc([128, 4, WZ], F32, "erow")
    Bt = Buf("T")
    kk.dma("sp", rb[:], I["rel_bias"][:, :], writes=[Bt])
    kk.dma("sp", oh[:], I["bucket_oh"][:, :], writes=[Bt])
    kk.dma("sp", relmask[:], I["relmask"][:, :], writes=[Bt])
    for h in range(4):
        kk.dve(lambda e, h=h: e.tensor_copy(out=rbrep[:, h, :], in_=rb[:, h:h + 1].to_broadcast([32, 128])), reads=[Bt], writes=[Bt])
    for h in range(4):
        kk.pe(mm(banks[0][:, 0:WZ], rbrep[:, h, :], oh[:, :], True, True), reads=[Bt], writes=[pb[0]])
        kk.dve(lambda e, h=h: e.tensor_scalar(out=erow[:, h, 0:1], in0=banks[0][:, WZ - 1:WZ], scalar1=-1.0, scalar2=None, op0=ALU.mult),
               reads=[pb[0]], writes=[Bt])
        kk.act(lambda e, h=h: e.activation(out=erow[:, h, 1:WZ], in_=banks[0][:, 1:WZ], func=AF.Exp, bias=erow[:, h, 0:1]),
               reads=[pb[0], Bt], writes=[Bt])
        kk.dve(lambda e, h=h: e.tensor_tensor(out=erow[:, h, :], in0=erow[:, h, :], in1=relmask[:, :], op=ALU.mult), reads=[Bt], writes=[Bt])
    Bz = Buf("zscr")
    kk.dma("sp", zscr[:, :], erow[:].rearrange("p h w -> p (h w)"), reads=[Bt], writes=[Bz])
    for h in range(4):
        s0 = bass.AP(zscr.tensor, h * WZ + 128, [[4 * WZ - 1, 128], [1, 128]])
        s1 = bass.AP(zscr.tensor, h * WZ + 256, [[4 * WZ - 1, 128], [1, 128]])
        kk.dma("sp", T0[:, h, :], s0, reads=[Bz], writes=[Bt])
        kk.dma("sp", T1[:, h, :], s1, reads=[Bz], writes=[Bt])

    def dbg(name, ap, rd):
        if name in dbg_out:
            kk.dma("sp", dbg_out[name], ap, reads=rd)

    dbg("T0", T0[:].rearrange("p h w -> p (h w)"), [Bt])
    dbg("T1", T1[:].rearrange("p h w -> p (h w)"), [Bt])
    dbg("neglam", neg_lam[:], [Bl])
    kk.barrier()
    if STOP == "p0":
        return
    ar.reset(cmark)


    amark = ar.mark()
    hT = ar.alloc([128, 8, NTOK], BF16, "hT")
    oT = ar.alloc([128, 4, NTOK], BF16, "oT")
    poolT = ar.alloc([128, 2, NTOK], BF16, "poolT")
    omT = ar.alloc([128, 2, NTOK], BF16, "omT")
    omark = ar.mark()
    qT = ar.alloc([128, 4, NTOK], BF16, "qT")
    kT = ar.alloc([128, 4, NTOK], BF16, "kT")
    v_bf = ar.alloc([128, NT, 512], BF16, "vbf")
    qmT = ar.alloc([128, 2, NTOK], BF16, "qmT")
    B_hT = [Buf("hT%d" % t) for t in range(NT)]
    B_qT = [Buf("qT%d" % i) for i in range(len(TCH))]
    B_kT = [Buf("kT%d" % i) for i in range(len(TCH))]
    B_qmT = [Buf("qmT%d" % i) for i in range(len(TCH))]
    B_v = [Buf("v%d" % t) for t in range(NT)]
    B_oT = [Buf("oT%d" % i) for i in range(len(TCH))]
    B_poolT = [Buf("poolT%d" % i) for i in range(len(TCH))]
    B_omT = [Buf("omT%d" % i) for i in range(len(TCH))]
    wmark = ar.mark()

    def tiles_of(tc):
        o, n = TCH[tc]
        return list(range(o // 128, (o + n) // 128))

    def norm_transpose(src_rows, gT, dst, dst_cols, xin, Bx, xn, Bxn, ss, Bss, bank, Bbank, Bdst, junk, i):
        if src_rows is not None:
            kk.dma("sp", xin[:], src_rows, writes=[Bx])
        kk.act(lambda e: e.activation(out=junk[:], in_=xin[:], func=AF.Square, accum_out=ss[:, 0:1]), reads=[Bx], writes=[Bss, Bjunk])
        kk.act(lambda e: e.activation(out=ss[:, 1:2], in_=ss[:, 0:1], func=AF.Sqrt, scale=1.0 / D, bias=eps_t[:, 0:1]), reads=[Bss, Bc], writes=[Bss])
        kk.dve(lambda e: e.reciprocal(out=ss[:, 2:3], in_=ss[:, 1:2]), reads=[Bss], writes=[Bss])
        kk.dve(lambda e: e.tensor_scalar(out=xn[:], in0=xin[:], scalar1=ss[:, 2:3], scalar2=None, op0=ALU.mult), reads=[Bx, Bss], writes=[Bxn])
        pbf = bank[:].bitcast(BF16)
        for kc in range(8):
            kk.pe(lambda e, kc=kc: e.transpose(pbf[:, kc * 128:(kc + 1) * 128], xn[:, kc * 128:(kc + 1) * 128], ident_b[:]),
                  reads=[Bxn, Bc], writes=[Bbank])
        kk.dve(f_tt(dst[:, :, dst_cols], pbf[:, 0:1024].rearrange("p (k t) -> p k t", k=8),
                    gT[:, :].unsqueeze(2).to_broadcast([128, 8, 128]), ALU.mult),
               reads=[Bbank, Bc], writes=[Bdst])

    xins = [ar.alloc([128, D], F32, "xin%d" % i) for i in range(3)]
    Bxins = [Buf("xin%d" % i) for i in range(3)]
    xns = [ar.alloc([128, D], BF16, "xn%d" % i) for i in range(2)]
    Bxns = [Buf("xn%d" % i) for i in range(2)]
    sss = [ar.alloc([128, 4], F32, "ss%d" % i) for i in range(3)]
    Bsss = [Buf("ss%d" % i) for i in range(3)]
    junk = ar.alloc([128, D], BF16, "junk")
    Bjunk = Buf("junk")
    for t in range(NT):
        norm_transpose(I["x_all"][t * 128:(t + 1) * 128, :], g1T, hT, slice(t * 128, (t + 1) * 128),
                       xins[t % 3], Bxins[t % 3], xns[t % 2], Bxns[t % 2], sss[t % 3], Bsss[t % 3],
                       banks[t % 2], pb[t % 2], B_hT[t], junk, t)
    if "hT" in dbg_out:
        hdbg = ar.alloc([128, 8 * 128], F32, "hdbg")
        Bh = Buf("hdbg")
        kk.dve(lambda e: e.tensor_copy(out=hdbg[:].rearrange("p (k t) -> p k t", k=8), in_=hT[:, :, 2048:2176]), reads=B_hT, writes=[Bh])
        dbg("hT", hdbg[:], [Bh])
    kk.barrier()
    if STOP == "p1":
        return
    ar.reset(wmark)

    wps = [ar.alloc([128, 8, 512], BF16, "wp%d" % i) for i in range(2)]
    Bwps = [Buf("wp%d" % i) for i in range(2)]
    stg = [ar.alloc([128, 512], F32, "stg%d" % i) for i in range(3)]
    Bstg = [Buf("stg%d" % i) for i in range(3)]
    nstg = [0]
    Eb = ar.alloc([128, 15 + 2048], F32, "Eb")
    Es = ar.alloc([128, NSEQ, 23], F32, "Es")
    W1 = ar.alloc([128, 15 + 2048], F32, "W1")
    W1s = ar.alloc([128, NSEQ, 23], F32, "W1s")
    W2 = ar.alloc([128, 15 + 2048], F32, "W2")
    W2s = ar.alloc([128, NSEQ, 23], F32, "W2s")
    dTb = ar.alloc([128, NTOK], BF16, "dTb")
    tmp16 = ar.alloc([128, 16], F32, "tmp16")
    bdw = ar.alloc([128, 128], BF16, "bdw")
    stp = ar.alloc([120, 2, 256], F32, "stp")
    BE, BW1, BW2, BdT, Bbdw, Bstp, Bt16 = Buf("E"), Buf("W1"), Buf("W2"), Buf("dT"), Buf("bdw"), Buf("stp"), Buf("t16")

    def load_wpiece(i, c0):
        kk.dma("pool", wps[i][:], I["w_in"][:, c0:c0 + 512].rearrange("(k p) c -> p k c", p=128), writes=[Bwps[i]])

    def fm_group(wp, Bwp, col0, tc, bank, Bbank):
        o, n = TCH[tc]
        for kc in range(8):
            kk.pe(mm(bank[:, 0:n], wp[:, kc, col0:col0 + 128], hT[:, kc, o:o + n], kc == 0, kc == 7),
                  reads=[Bwp] + [B_hT[t] for t in tiles_of(tc)], writes=[Bbank])

    def tm_group(wp, Bwp, t, bank, Bbank, ncols=512, c0=0):
        for kc in range(8):
            kk.pe(mm(bank[:, 0:ncols], hT[:, kc, t * 128:(t + 1) * 128], wp[:, kc, c0:c0 + ncols], kc == 0, kc == 7),
                  reads=[Bwp, B_hT[t]], writes=[Bbank])

    nb = [0]

    def next_bank():
        b = nb[0] % 4
        nb[0] += 1
        return banks[b], pb[b]

    ev = [0]

    def evac_copy(out_ap, in_ap, reads, writes):
        ev[0] += 1
        if ev[0] % 2 == 0:
            kk.dve(lambda e: e.tensor_copy(out=out_ap, in_=in_ap), reads=reads, writes=writes)
        else:
            kk.act(lambda e: e.activation(out=out_ap, in_=in_ap, func=AF.Copy), reads=reads, writes=writes)

    load_wpiece(0, 0)
    load_wpiece(1, 512)
    for h in range(4):
        for tc in range(len(TCH)):
            o, n = TCH[tc]
            bk, Bb = next_bank()
            fm_group(wps[0], Bwps[0], h * 128, tc, bk, Bb)
            evac_copy(qT[:, h, o:o + n], bk[:, 0:n], [Bb], [B_qT[tc]])
    if STOP == "p2q":
        kk.barrier()
        return
    for h in range(4):
        for tc in range(len(TCH)):
            o, n = TCH[tc]
            bk, Bb = next_bank()
            fm_group(wps[1], Bwps[1], h * 128, tc, bk, Bb)
            evac_copy(kT[:, h, o:o + n], bk[:, 0:n], [Bb], [B_kT[tc]])
    if STOP == "p2k":
        kk.barrier()
        return
    load_wpiece(0, 1024)
    for t in range(NT):
        bk, Bb = next_bank()
        tm_group(wps[1], Bwps[1], t, bk, Bb)
        si = nstg[0] % 3
        nstg[0] += 1
        evac_copy(stg[si][:], bk[:, :], [Bb], [Bstg[si]])
        kk.dma("sp", O["newk"][t * 128:(t + 1) * 128, :], stg[si][:], reads=[Bstg[si]])
    if STOP == "p2kt":
        kk.barrier()
        return
    load_wpiece(1, 1536)
    for t in range(NT):
        bk, Bb = next_bank()
        tm_group(wps[0], Bwps[0], t, bk, Bb)
        si = nstg[0] % 3
        nstg[0] += 1
        kk.act(lambda e, si=si, bk=bk: e.activation(out=stg[si][:], in_=bk[:, :], func=AF.Copy), reads=[Bb], writes=[Bstg[si]])
        kk.dve(f_copy(v_bf[:, t, :], stg[si][:]), reads=[Bstg[si]], writes=[B_v[t]])
        kk.dma("sp", O["newv"][t * 128:(t + 1) * 128, :], stg[si][:], reads=[Bstg[si]])
    for hp in range(2):
        for tc in range(len(TCH)):
            o, n = TCH[tc]
            bk, Bb = next_bank()
            fm_group(wps[1], Bwps[1], 256 + hp * 128, tc, bk, Bb)
            evac_copy(qmT[:, hp, o:o + n], bk[:, 0:n], [Bb], [B_qmT[tc]])
    if STOP == "p2a":
        kk.barrier()
        return
    for t in (15, 16):
        bk, Bb = next_bank()
        tm_group(wps[1], Bwps[1], t, bk, Bb, ncols=256, c0=0)
        si = nstg[0] % 3
        nstg[0] += 1
        evac_copy(stg[si][:, 0:256], bk[:, 0:256], [Bb], [Bstg[si]])
        if t == 15:
            kk.dma("sp", O["pool_p"][:, :], stg[si][113:128, 0:256], reads=[Bstg[si]])
        else:
            for s_ in range(NSEQ):
                kk.dma("sp", O["pool_s"][s_, 7:15, :], stg[si][s_ * 8:(s_ + 1) * 8, 0:256], reads=[Bstg[si]])
    kk.dma("sp", O["pool_s"][:, 0:7, :], I["state_pool"].rearrange("(s r) c -> s r c", r=15)[:, 8:15, :])

    if STOP == "p2b":
        kk.barrier()
        return
    kk.dve(lambda e: e.memset(Eb[:, 0:15], 0.0), writes=[BE])
    kk.dve(lambda e: e.memset(bdw[:], 0.0), writes=[Bbdw])
    kk.dma("sp", stp[:, 0, :], I["state_pool"][0:120, :], writes=[Bstp])
    kk.dma("sp", stp[:, 1, :], I["state_pool"][120:240, :], writes=[Bstp])
    for ch in range(2):
        for tc in range(len(TCH)):
            o, n = TCH[tc]
            bk, Bb = next_bank()
            fm_group(wps[1], Bwps[1], ch * 128, tc, bk, Bb)
            if tc < 4:
                evac_copy(Eb[:, 15 + o:15 + o + n], bk[:, 0:n], [Bb], [BE])
            else:
                evac_copy(Es[:, :, 15:23], bk[:, 0:128].rearrange("p (s i) -> p s i", i=8), [Bb], [BE])
        for j in range(2):
            bk, Bb = next_bank()
            kk.pe(mm(bk[:, 0:120], stp[:, j, ch * 128:(ch + 1) * 128], ident_f[0:120, 0:120], True, True), reads=[Bstp, Bc], writes=[Bb])
            evac_copy(Es[:, j * 8:(j + 1) * 8, 0:15], bk[:, 0:120].rearrange("p (s r) -> p s r", r=15), [Bb], [BE])
        def dbl(dst, dsts, src, srcs, sh, first):
            lo = 2 * sh - 1
            kk.dve(lambda e: e.tensor_tensor(out=dst[:, lo:], in0=src[:, lo:], in1=src[:, lo - sh:15 + 2048 - sh], op=ALU.add),
                   reads=[first], writes=[BW1 if dst is W1 else BW2])
            kk.dve(lambda e: e.tensor_tensor(out=dsts[:, :, lo:], in0=srcs[:, :, lo:], in1=srcs[:, :, lo - sh:23 - sh], op=ALU.add),
                   reads=[first], writes=[BW1 if dst is W1 else BW2])
        dbl(W1, W1s, Eb, Es, 1, BE)
        dbl(W2, W2s, W1, W1s, 2, BW1)
        if ch == 1:
            dbl(W1, W1s, W2, W2s, 4, BW2)
            dbl(W2, W2s, W1, W1s, 8, BW1)
        for half, (Wb, Wbs, BWb) in enumerate(((W1, W1s, BW1), (W2, W2s, BW2))):
            ps = slice(half * 64, (half + 1) * 64)
            kk.dve(f_stt(dTb[ps, 0:2048], Wb[ps, 15:15 + 2048], poolc[ps, ch, 15:16], Eb[ps, 15:15 + 2048], ALU.mult, ALU.subtract),
                   reads=[BWb, BE, Bc], writes=[BdT])
            kk.dve(f_tt(tmp16[ps, :], Wb[ps, 15:31], poolc[ps, ch, :], ALU.mult), reads=[BWb, Bc], writes=[Bt16])
            kk.dve(f_tt(dTb[ps, 0:16], tmp16[ps, :], Eb[ps, 15:31], ALU.subtract), reads=[Bt16, BE, BdT], writes=[BdT])
            kk.dve(f_stt(dTb[ps, 2048:2176].rearrange("p (s i) -> p s i", i=8), Wbs[ps, :, 15:23], poolc[ps, ch, 15:16],
                         Es[ps, :, 15:23], ALU.mult, ALU.subtract),
                   reads=[BWb, BE, Bc, BdT], writes=[BdT])
        for half in range(2):
            ps = slice(half * 64, (half + 1) * 64)
            kk.dma("pool", bdw[ps, half * 64:(half + 1) * 64], I["w_pool_grp"][2 * ch + half, :, :], reads=[Bbdw], writes=[Bbdw])
        for tc in range(len(TCH)):
            o, n = TCH[tc]
            bk, Bb = next_bank()
            kk.pe(mm(bk[:, 0:n], bdw[:, :], dTb[:, o:o + n], True, True), reads=[Bbdw, BdT], writes=[Bb])
            kk.dve(f_ts(poolT[:, ch, o:o + n], bk[:, 0:n], pscaleT[:, ch:ch + 1], None, ALU.mult), reads=[Bb, Bc], writes=[B_poolT[tc]])
    if "poolT" in dbg_out:
        pdbg = ar.alloc([128, 2 * 256], F32, "pdbg")
        Bp = Buf("pdbg")
        kk.dve(lambda e: e.tensor_copy(out=pdbg[:, 0:128], in_=poolT[:, 0, 0:128]), reads=B_poolT, writes=[Bp])
        kk.dve(lambda e: e.tensor_copy(out=pdbg[:, 128:256], in_=poolT[:, 1, 0:128]), reads=B_poolT, writes=[Bp])
        kk.dve(lambda e: e.tensor_copy(out=pdbg[:, 256:384], in_=poolT[:, 0, 2048:2176]), reads=B_poolT, writes=[Bp])
        kk.dve(lambda e: e.tensor_copy(out=pdbg[:, 384:512], in_=poolT[:, 1, 2048:2176]), reads=B_poolT, writes=[Bp])
        dbg("poolT", pdbg[:], [Bp])
    kk.barrier()
    if STOP == "p2":
        return
    ar.reset(wmark)


    P3 = ALL or "p3" in phases
    xin0 = ar.alloc([128, D], F32, "mxin0")
    xin1 = ar.alloc([128, D], F32, "mxin1")
    mxn = ar.alloc([128, D], BF16, "mxn")
    mjunk = ar.alloc([128, D], BF16, "mjunk")
    mss = [ar.alloc([128, 4], F32, "mss%d" % i) for i in range(2)]
    mhT = ar.alloc([128, 8, 256], BF16, "mhT")
    wmkv = ar.alloc([128, 8, 512], BF16, "wmkv")
    memkT = ar.alloc([128, 2, 256], BF16, "memkT")
    memv_pad = ar.alloc([128, 2, 4, 128], BF16, "memvpad")
    onesE = ar.alloc([128, 128], BF16, "onesE")
    onesO = ar.alloc([128, 128], BF16, "onesO")
    mstg = [ar.alloc([128, 512], F32, "mstg%d" % i) for i in range(2)]
    Bmx = [Buf("mx0"), Buf("mx1")]
    Bmxn, Bmhs, Bwmkv, BmkT, Bmvp, Bones2 = Buf("mxn"), [Buf("mh0"), Buf("mh1")], Buf("wmkv"), Buf("memkT"), Buf("memvpad"), Buf("ones2")
    Bmss = [Buf("mss0"), Buf("mss1")]
    Bmstg = [Buf("mstg0"), Buf("mstg1")]
    kk.dma("pool", wmkv[:], I["w_mem_kv"].rearrange("(k p) c -> p k c", p=128), writes=[Bwmkv])
    kk.dve(f_memset(memv_pad[:], 0.0), writes=[Bmvp])
    kk.dve(f_memset(onesE[:], 0.0), writes=[Bones2])
    kk.dve(f_memset(onesO[:], 0.0), writes=[Bones2])
    kk.dve(f_memset(onesE[:, 0:64], 1.0), writes=[Bones2])
    kk.dve(f_memset(onesO[:, 64:128], 1.0), writes=[Bones2])
    for mt in range(2):
        norm_transpose(I["mem"][mt * 128:(mt + 1) * 128, :], gmT, mhT, slice(mt * 128, (mt + 1) * 128),
                       (xin0, xin1)[mt], Bmx[mt], mxn, Bmxn, mss[mt], Bmss[mt], banks[mt], pb[mt], Bmhs[mt], mjunk, mt)
    for mt in range(2):
        bk, Bb = banks[2 + mt], pb[2 + mt]
        for kc in range(8):
            kk.pe(mm(bk[:, :], mhT[:, kc, mt * 128:(mt + 1) * 128], wmkv[:, kc, :], kc == 0, kc == 7), reads=[Bmhs[mt], Bwmkv], writes=[Bb])
        kk.act(f_act(mstg[mt][:], bk[:, :], AF.Copy), reads=[Bb], writes=[Bmstg[mt]])
        for h in range(4):
            kk.dve(f_copy(memv_pad[:, mt, h, (h % 2) * 64:(h % 2) * 64 + 64], mstg[mt][:, 256 + h * 64:256 + (h + 1) * 64]), reads=[Bmstg[mt]], writes=[Bmvp])
        kk.dma("sp", O["memk"][mt * 128:(mt + 1) * 128, :], mstg[mt][:, 0:256], reads=[Bmstg[mt]])
        kk.dma("sp", O["memv"][mt * 128:(mt + 1) * 128, :], mstg[mt][:, 256:512], reads=[Bmstg[mt]])
    for hp in range(2):
        bk, Bb = banks[4 + hp], pb[4 + hp]
        for kc in range(8):
            kk.pe(mm(bk[:, 0:256], wmkv[:, kc, hp * 128:(hp + 1) * 128], mhT[:, kc, :], kc == 0, kc == 7), reads=Bmhs + [Bwmkv], writes=[Bb])
        kk.dve(f_copy(memkT[:, hp, :], bk[:, 0:256]), reads=[Bb], writes=[BmkT])

    mpT = [ar.alloc([128, 2, 512], BF16, "mpT%d" % i) for i in range(2)]
    BmpT = [Buf("mpT0"), Buf("mpT1")]
    mrs = [ar.alloc([128, 512], F32, "mrs%d" % i) for i in range(2)]
    Bmrs = [Buf("mrs0"), Buf("mrs1")]
    it = 0
    for tc in range(4):
        o, n = TCH[tc]
        for hp in range(2):
            oc, Boc = banks[4 + 2 * (it % 2)], pb[4 + 2 * (it % 2)]
            oz, Boz = banks[5 + 2 * (it % 2)], pb[5 + 2 * (it % 2)]
            for mt in range(2):
                sa, Bsa = banks[2 * mt], pb[2 * mt]
                sb, Bsb = banks[2 * mt + 1], pb[2 * mt + 1]
                p_, Bp_ = mpT[mt], BmpT[mt]
                kk.pe(mm(sa[:, :], memkT[0:64, hp, mt * 128:(mt + 1) * 128], qmT[0:64, hp, o:o + n], True, True), reads=[BmkT, B_qmT[tc]], writes=[Bsa])
                kk.pe(mm(sb[:, :], memkT[64:128, hp, mt * 128:(mt + 1) * 128], qmT[64:128, hp, o:o + n], True, True), reads=[BmkT, B_qmT[tc]], writes=[Bsb])
                kk.act(f_act(p_[:, 0, :], sa[:, :], AF.Exp, scale=0.125), reads=[Bsa], writes=[Bp_])
                kk.act(f_act(p_[:, 1, :], sb[:, :], AF.Exp, scale=0.125), reads=[Bsb], writes=[Bp_])
                kk.pe(mm(oc[:, :], memv_pad[:, mt, 2 * hp, :], p_[:, 0, :], mt == 0, False), reads=[Bmvp, Bp_], writes=[Boc])
                kk.pe(mm(oc[:, :], memv_pad[:, mt, 2 * hp + 1, :], p_[:, 1, :], False, mt == 1), reads=[Bmvp, Bp_], writes=[Boc])
                kk.pe(mm(oz[:, :], onesE[:, :], p_[:, 0, :], mt == 0, False), reads=[Bones2, Bp_], writes=[Boz])
                kk.pe(mm(oz[:, :], onesO[:, :], p_[:, 1, :], False, mt == 1), reads=[Bones2, Bp_], writes=[Boz])
            r_, Br_ = mrs[it % 2], Bmrs[it % 2]
            kk.dve(f_recip(r_[:], oz[:, :]), reads=[Boz], writes=[Br_])
            kk.dve(f_tt(omT[:, hp, o:o + n], oc[:, :], r_[:], ALU.mult), reads=[Boc, Br_], writes=[B_omT[tc]])
            it += 1
    cmk = [ar.alloc([128, 2, 256], BF16, "cmk%d" % i) for i in range(2)]
    Bcmk = [Buf("cmk0"), Buf("cmk1")]
    cmvp = [ar.alloc([128, 2, 4, 128], BF16, "cmvp%d" % i) for i in range(2)]
    Bcmvp = [Buf("cmvp0"), Buf("cmvp1")]
    kTs = [ar.alloc([128, 2, 256], BF16, "kTs%d" % i) for i in range(2)]
    BkTs = [Buf("kTs0"), Buf("kTs1")]
    pTs = [ar.alloc([128, 2, 32], BF16, "pTs%d" % i) for i in range(2)]
    BpTs = [Buf("pTs0"), Buf("pTs1")]
    for i in range(2):
        kk.dve(f_memset(cmvp[i][:], 0.0), writes=[Bcmvp[i]])
    ocs, Bocs = banks[6], pb[6]
    ozs, Bozs = banks[7], pb[7]
    for s_ in range(NSEQ):
        b = s_ % 2
        kk.dma("pool", cmk[b][:], I["cmem_k"][s_].rearrange("(t p) c -> p t c", p=128), writes=[Bcmk[b]])
        for h in range(4):
            kk.dma("pool", cmvp[b][:, :, h, (h % 2) * 64:(h % 2) * 64 + 64],
                   I["cmem_v"][s_][:, h * 64:(h + 1) * 64].rearrange("(t p) c -> p t c", p=128), reads=[Bcmvp[b]], writes=[Bcmvp[b]])
        tb, Btb = banks[b], pb[b]
        tbf = tb[:].bitcast(BF16)
        for hp in range(2):
            for mt in range(2):
                kk.pe(f_tr(tbf[:, (hp * 2 + mt) * 128:(hp * 2 + mt + 1) * 128], cmk[b][:, mt, hp * 128:(hp + 1) * 128], ident_b[:]),
                      reads=[Bcmk[b], Bc], writes=[Btb])
        kk.dve(f_copy(kTs[b][:].rearrange("p h m -> p (h m)"), tbf[:, 0:512]), reads=[Btb], writes=[BkTs[b]])
        sa, Bsa = banks[2 + 2 * b], pb[2 + 2 * b]
        sb, Bsb = banks[3 + 2 * b], pb[3 + 2 * b]
        qs = slice(2048 + 8 * s_, 2048 + 8 * s_ + 8)
        for hp in range(2):
            for mt in range(2):
                c0 = (hp * 2 + mt) * 8
                kk.pe(mm(sa[:, c0:c0 + 8], kTs[b][0:64, hp, mt * 128:(mt + 1) * 128], qmT[0:64, hp, qs], True, True), reads=[BkTs[b], B_qmT[4]], writes=[Bsa])
                kk.pe(mm(sb[:, c0:c0 + 8], kTs[b][64:128, hp, mt * 128:(mt + 1) * 128], qmT[64:128, hp, qs], True, True), reads=[BkTs[b], B_qmT[4]], writes=[Bsb])
        kk.act(f_act(pTs[b][:, 0, :], sa[:, 0:32], AF.Exp, scale=0.125), reads=[Bsa], writes=[BpTs[b]])
        kk.act(f_act(pTs[b][:, 1, :], sb[:, 0:32], AF.Exp, scale=0.125), reads=[Bsb], writes=[BpTs[b]])
        for hp in range(2):
            oc0 = (s_ * 2 + hp) * 8
            for mt in range(2):
                c0 = (hp * 2 + mt) * 8
                kk.pe(mm(ocs[:, oc0:oc0 + 8], cmvp[b][:, mt, 2 * hp, :], pTs[b][:, 0, c0:c0 + 8], mt == 0, False), reads=[Bcmvp[b], BpTs[b]], writes=[Bocs])
                kk.pe(mm(ocs[:, oc0:oc0 + 8], cmvp[b][:, mt, 2 * hp + 1, :], pTs[b][:, 1, c0:c0 + 8], False, mt == 1), reads=[Bcmvp[b], BpTs[b]], writes=[Bocs])
            for mt in range(2):
                c0 = (hp * 2 + mt) * 8
                kk.pe(mm(ozs[:, oc0:oc0 + 8], onesE[:, :], pTs[b][:, 0, c0:c0 + 8], mt == 0, False), reads=[Bones2, BpTs[b]], writes=[Bozs])
                kk.pe(mm(ozs[:, oc0:oc0 + 8], onesO[:, :], pTs[b][:, 1, c0:c0 + 8], False, mt == 1), reads=[Bones2, BpTs[b]], writes=[Bozs])
    kk.dve(f_recip(mrs[0][:, 0:256], ozs[:, 0:256]), reads=[Bozs], writes=[Bmrs[0]])
    kk.dve(f_tt(omT[:, :, 2048:2176].rearrange("p h (s q) -> p h s q", q=8),
                ocs[:, 0:256].rearrange("p (s h q) -> p h s q", h=2, q=8),
                mrs[0][:, 0:256].rearrange("p (s h q) -> p h s q", h=2, q=8), ALU.mult),
           reads=[Bocs, Bmrs[0]], writes=[B_omT[4]])
    if "omT" in dbg_out:
        odbg = ar.alloc([128, 512], F32, "odbg")
        Bo = Buf("odbg")
        kk.dve(f_copy(odbg[:, 0:128], omT[:, 0, 0:128]), reads=B_omT, writes=[Bo])
        kk.dve(f_copy(odbg[:, 128:256], omT[:, 1, 1920:2048]), reads=B_omT, writes=[Bo])
        kk.dve(f_copy(odbg[:, 256:384], omT[:, 0, 2048:2176]), reads=B_omT, writes=[Bo])
        kk.dve(f_copy(odbg[:, 384:512], omT[:, 1, 2048:2176]), reads=B_omT, writes=[Bo])
        dbg("omT", odbg[:], [Bo])
    kk.barrier()
    if STOP == "p3b":
        return
    ar.reset(wmark)


    apT = [ar.alloc([128, 2, 512], BF16, "apT%d" % i) for i in range(3)]
    BapT = [Buf("apT%d" % i) for i in range(3)]
    tA = ar.alloc([128, 512], F32, "tA")
    tB = ar.alloc([128, 512], F32, "tB")
    tC = ar.alloc([128, 512], F32, "tC")
    tD = ar.alloc([128, 512], F32, "tD")
    tE = ar.alloc([128, 512], F32, "tE")
    BtA, BtB, BtC, BtD, BtE = Buf("tA"), Buf("tB"), Buf("tC"), Buf("tD"), Buf("tE")
    S0, S1, O0, O1, Z0, Z1, SSb = banks[0], banks[1], banks[2], banks[3], banks[4], banks[5], banks[6]
    BS0, BS1, BO0, BO1, BZ0, BZ1, BSS = pb[0], pb[1], pb[2], pb[3], pb[4], pb[5], pb[6]
    npt = 0
    for h in range(4):
        for c in range(4):
            q0 = c * 512
            nj = 4 * c + 4
            for j in range(nj):
                jj = j - 4 * c
                lo = max(jj, 0) * 128
                pT_, BpT_ = apT[npt % 3], BapT[npt % 3]
                npt += 1
                ks = slice(j * 128, (j + 1) * 128)
                kk.pe(mm(S0[:, lo:512], kT[0:64, h, ks], qT[0:64, h, q0 + lo:q0 + 512], True, True), reads=[B_kT[j // 4], B_qT[c]], writes=[BS0])
                kk.pe(mm(S1[:, lo:512], kT[64:128, h, ks], qT[64:128, h, q0 + lo:q0 + 512], True, True), reads=[B_kT[j // 4], B_qT[c]], writes=[BS1])
                kk.act(f_act(pT_[:, 0, lo:512], S0[:, lo:512], AF.Exp, scale=0.125), reads=[BS0], writes=[BpT_])
                kk.act(f_act(pT_[:, 1, lo:512], S1[:, lo:512], AF.Exp, scale=0.125), reads=[BS1], writes=[BpT_])
                for m in range(2):
                    if jj >= 0:
                        kk.dve(f_tt(pT_[:, m, lo:lo + 128], pT_[:, m, lo:lo + 128], T0[:, h, :], ALU.mult), reads=[BpT_, Bt], writes=[BpT_])
                        if jj < 3:
                            kk.dve(f_tt(pT_[:, m, lo + 128:lo + 256], pT_[:, m, lo + 128:lo + 256], T1[:, h, :], ALU.mult), reads=[BpT_, Bt], writes=[BpT_])
                    elif jj == -1:
                        kk.dve(f_tt(pT_[:, m, 0:128], pT_[:, m, 0:128], T1[:, h, :], ALU.mult), reads=[BpT_, Bt], writes=[BpT_])
                vv = v_bf[:, j, h * 128:(h + 1) * 128]
                for m, (Ob, BOb, Zb, BZb) in enumerate(((O0, BO0, Z0, BZ0), (O1, BO1, Z1, BZ1))):
                    kk.pe(mm(Ob[:, lo:512], vv, pT_[:, m, lo:512], j == 0, j == nj - 1), reads=[B_v[j], BpT_], writes=[BOb])
                    kk.pe(mm(Zb[:, lo:512], ones_b[:, :], pT_[:, m, lo:512], j == 0, j == nj - 1), reads=[Bc, BpT_], writes=[BZb])
            kk.dve(f_recip(tA[:], Z0[:, :]), reads=[BZ0], writes=[BtA])
            kk.dve(f_recip(tB[:], Z1[:, :]), reads=[BZ1], writes=[BtB])
            kk.dve(f_tt(tA[:], O0[:, :], tA[:], ALU.mult), reads=[BO0, BtA], writes=[BtA])
            kk.dve(f_tt(tB[:], O1[:, :], tB[:], ALU.mult), reads=[BO1, BtB], writes=[BtB])
            kk.dve(f_stt(tC[:], tB[:], neg_lam[:, 0:1], tA[:], ALU.mult, ALU.add), reads=[BtA, BtB, Bl], writes=[BtC])
            kk.act(f_act(tD[:], tC[:], AF.Square), reads=[BtC], writes=[BtD])
            kk.pe(mm(SSb[:, :], ones_f[:, :], tD[:], True, True), reads=[Bc, BtD], writes=[BSS])
            kk.act(f_act(tE[:], SSb[:, :], AF.Sqrt, scale=1.0 / 128, bias=eps_t[:, 0:1]), reads=[BSS, Bc], writes=[BtE])
            kk.dve(f_recip(tE[:], tE[:]), reads=[BtE], writes=[BtE])
            kk.dve(f_stt(oT[:, h, q0:q0 + 512], tC[:], sublnT[:, 0:1], tE[:], ALU.mult, ALU.mult), reads=[BtC, BtE, Bc], writes=[B_oT[c]])
    if "oTp" in dbg_out:
        odbg2 = ar.alloc([128, 512], F32, "odbg2")
        Bo2 = Buf("odbg2")
        kk.dve(f_copy(odbg2[:, 0:128], oT[:, 0, 0:128]), reads=B_oT, writes=[Bo2])
        kk.dve(f_copy(odbg2[:, 128:256], oT[:, 1, 640:768]), reads=B_oT, writes=[Bo2])
        kk.dve(f_copy(odbg2[:, 256:384], oT[:, 2, 1920:2048]), reads=B_oT, writes=[Bo2])
        kk.dve(f_copy(odbg2[:, 384:512], oT[:, 3, 1024:1152]), reads=B_oT, writes=[Bo2])
        dbg("oTp", odbg2[:], [Bo2])
    kk.barrier()
    if STOP == "p3c":
        return
    ar.reset(wmark)


    ptb = ar.alloc([128, NSEQ * NPAGE], I32, "ptb")
    idx = ar.alloc([128, NSEQ * NPAGE], I32, "idx")
    iotaf = ar.alloc([128, 1], F32, "iotaf")
    qpad = ar.alloc([128, 4, NSEQ, 16], BF16, "qpad")
    M15 = ar.alloc([128, 4, 2, 8], F32, "M15")
    MN = ar.alloc([128, NSEQ, 4, 2, 8], F32, "MN")
    gbc = ar.alloc([128, 128], F32, "gbc")
    NK, NV = 6, 12
    kpg = [ar.alloc([128, 512], BF16, "kpg%d" % i) for i in range(NK)]
    vpg = [ar.alloc([128, 512], BF16, "vpg%d" % i) for i in range(NV)]
    Bkpg = [Buf("kpg%d" % i) for i in range(NK)]
    Bvpg = [Buf("vpg%d" % i) for i in range(NV)]
    KTs = [ar.alloc([128, 4, 128], BF16, "KTs%d" % i) for i in range(2)]
    BKTs = [Buf("KTs0"), Buf("KTs1")]
    spT = [ar.alloc([128, 8, 64], BF16, "spT%d" % i) for i in range(2)]
    BspT = [Buf("spT0"), Buf("spT1")]
    pn = ar.alloc([128, 64], BF16, "pn")
    Bpn = Buf("pn")
    rz = ar.alloc([64, 1], F32, "rz")
    onr = ar.alloc([64, 512], F32, "onr")
    c2 = ar.alloc([32, 4, 128], F32, "c2")
    sq2 = ar.alloc([32, 4, 128], F32, "sq2")
    ss2 = ar.alloc([32, 8], F32, "ss2")
    on3 = ar.alloc([32, 4, 128], BF16, "on3")
    Brz, Bonr, Bc2, Bsq2, Bss2, Bon3 = Buf("rz"), Buf("onr"), Buf("c2"), Buf("sq2"), Buf("ss2"), Buf("on3")
    Bsetup = Buf("p3dsetup")
    kk.dma("sp", ptb[:], I["page_table"][0, :].partition_broadcast(128), writes=[Bsetup])
    kk.dma("sp", iotaf[:], I["iota_f"][:, :], writes=[Bsetup])
    kk.dve(f_ts(idx[:], ptb[:], 128.0, iotaf[:, 0:1], ALU.mult, ALU.add), reads=[Bsetup], writes=[Bsetup])
    kk.dve(f_memset(qpad[:], 0.0), writes=[Bsetup])
    kk.dve(f_copy(qpad[0:64, :, :, 0:8], qT[0:64, :, 2048:2176].rearrange("p h (s q) -> p h s q", q=8)), reads=[B_qT[4], Bsetup], writes=[Bsetup])
    kk.dve(f_copy(qpad[64:128, :, :, 8:16], qT[64:128, :, 2048:2176].rearrange("p h (s q) -> p h s q", q=8)), reads=[B_qT[4], Bsetup], writes=[Bsetup])
    for c in range(2):
        kk.dve(f_copy(M15[:, :, c, :], T1[:, :, 0:8]), reads=[Bt], writes=[Bsetup])
    for h in range(4):
        for c in range(2):
            kk.dve(f_tt(MN[:, :, h, c, :], T0[:, h, :].rearrange("p (s q) -> p s q", q=8), bdiag[:].unsqueeze(2).to_broadcast([128, NSEQ, 8]), ALU.mult),
                   reads=[Bt, Bc], writes=[Bsetup])
    kk.dma("sp", gbc[:], I["subln_g"].partition_broadcast(128), writes=[Bsetup])
    kk.dve(f_ts(gbc[:], gbc[:], 1.0 - LAM_INIT, None, ALU.mult), reads=[Bsetup], writes=[Bsetup])
    OS, BOS = banks[2], pb[2]
    ZS, BZS = banks[3], pb[3]
    C2b, BC2 = banks[4], pb[4]
    TTb, BTT = banks[5], pb[5]
    nk = nv = 0
    npg = 0
    for s_ in range(NSEQ):
        vq = []
        for half in range(2):
            Sb, BSb = banks[6 + half], pb[6 + half]
            sp_, Bsp_ = spT[half], BspT[half]
            for jj in range(8):
                j = half * 8 + jj
                col = s_ * NPAGE + j
                kb, Bkb = kpg[nk % NK], Bkpg[nk % NK]
                nk += 1
                vb, Bvb = vpg[nv % NV], Bvpg[nv % NV]
                nv += 1
                kk.op("pool", (lambda e, kb=kb, col=col: e.indirect_dma_start(
                    out=kb[:, :], out_offset=None, in_=I["cache_k"][:, :],
                    in_offset=bass.IndirectOffsetOnAxis(ap=idx[:, col:col + 1], axis=0))), reads=[Bsetup], writes=[Bkb], dma=True)
                kk.op("pool", (lambda e, vb=vb, col=col: e.indirect_dma_start(
                    out=vb[:, :], out_offset=None, in_=I["cache_v"][:, :],
                    in_offset=bass.IndirectOffsetOnAxis(ap=idx[:, col:col + 1], axis=0))), reads=[Bsetup], writes=[Bvb], dma=True)
                vq.append((vb, Bvb))
                tb, Btb = banks[npg % 2], pb[npg % 2]
                kt_, Bkt_ = KTs[npg % 2], BKTs[npg % 2]
                npg += 1
                tbf = tb[:].bitcast(BF16)
                for h in range(4):
                    kk.pe(f_tr(tbf[:, h * 128:(h + 1) * 128], kb[:, h * 128:(h + 1) * 128], ident_b[:]), reads=[Bkb, Bc], writes=[Btb])
                if npg % 2 == 0:
                    kk.dve(f_copy(kt_[:].rearrange("p h k -> p (h k)"), tbf[:, 0:512]), reads=[Btb], writes=[Bkt_])
                else:
                    kk.act(f_act(kt_[:].rearrange("p h k -> p (h k)"), tbf[:, 0:512], AF.Copy), reads=[Btb], writes=[Bkt_])
                for h in range(4):
                    c0 = jj * 64 + h * 16
                    kk.pe(mm(Sb[:, c0:c0 + 16], kt_[:, h, :], qpad[:, h, s_, :], True, True), reads=[Bkt_, Bsetup], writes=[BSb])
            kk.act(f_act(sp_[:].rearrange("p j c -> p (j c)"), Sb[:, :], AF.Exp, scale=0.125), reads=[BSb], writes=[Bsp_])
            if half == 1:
                kk.dve(f_tt(sp_[:, 7, :], sp_[:, 7, :], M15[:].rearrange("p h c q -> p (h c q)"), ALU.mult), reads=[Bsp_, Bsetup], writes=[Bsp_])
            for jj in range(8):
                j = half * 8 + jj
                vb, Bvb = vq[j]
                kk.pe(mm(OS[0:64, :], sp_[:, jj, :], vb[:, :], j == 0, False), reads=[Bsp_, Bvb], writes=[BOS])
                kk.pe(mm(ZS[0:64, 0:1], sp_[:, jj, :], ones_b[:, 0:1], j == 0, False), reads=[Bsp_, Bc], writes=[BZS])
        Sb, BSb = banks[6], pb[6]
        for h in range(4):
            kk.pe(mm(Sb[:, h * 16:(h + 1) * 16], kT[:, h, 2048:2176], qpad[:, h, s_, :], True, True), reads=[B_kT[4], Bsetup], writes=[BSb])
        kk.act(f_act(pn[:], Sb[:, 0:64], AF.Exp, scale=0.125), reads=[BSb], writes=[Bpn])
        kk.dve(f_tt(pn[:], pn[:], MN[:, s_].rearrange("p h c q -> p (h c q)"), ALU.mult), reads=[Bpn, Bsetup], writes=[Bpn])
        kk.pe(mm(OS[0:64, :], pn[:, :], v_bf[:, 16, :], False, True), reads=[Bpn, B_v[16]], writes=[BOS])
        kk.pe(mm(ZS[0:64, 0:1], pn[:, :], ones_b[:, 0:1], False, True), reads=[Bpn, Bc], writes=[BZS])
        kk.dve(f_recip(rz[:], ZS[0:64, 0:1]), reads=[BZS], writes=[Brz])
        kk.dve(f_ts(onr[:], OS[0:64, :], rz[:, 0:1], None, ALU.mult), reads=[BOS, Brz], writes=[Bonr])
        kk.pe(mm(C2b[0:32, :], comb[:, :], onr[:, :], True, True), reads=[Bl, Bonr], writes=[BC2])
        kk.dve(f_copy(c2[:].rearrange("p h e -> p (h e)"), C2b[0:32, :]), reads=[BC2], writes=[Bc2])
        kk.act(f_act(sq2[:], c2[:], AF.Square), reads=[Bc2], writes=[Bsq2])
        kk.dve(lambda e: e.tensor_reduce(out=ss2[:, 0:4], in_=sq2[:], axis=AX.X, op=ALU.add), reads=[Bsq2], writes=[Bss2])
        kk.act(f_act(ss2[:, 4:8], ss2[:, 0:4], AF.Sqrt, scale=1.0 / 128, bias=eps_t[0:32, 0:1]), reads=[Bss2, Bc], writes=[Bss2])
        kk.dve(f_recip(ss2[:, 4:8], ss2[:, 4:8]), reads=[Bss2], writes=[Bss2])
        kk.dve(f_tt(c2[:], c2[:], ss2[:, 4:8].unsqueeze(2).to_broadcast([32, 4, 128]), ALU.mult), reads=[Bc2, Bss2], writes=[Bc2])
        kk.dve(f_tt(on3[:], c2[:], gbc[0:32, :].unsqueeze(1).to_broadcast([32, 4, 128]), ALU.mult), reads=[Bc2, Bsetup], writes=[Bon3])
        ttf = TTb[:].bitcast(BF16)
        for h in range(4):
            kk.pe(f_tr(ttf[:, h * 32:(h + 1) * 32], on3[:, h, :], ident_b[0:32, 0:32]), reads=[Bon3, Bc], writes=[BTT])
        for h in range(4):
            kk.dve(f_copy(oT[:, h, 2048 + 8 * s_:2048 + 8 * s_ + 8], ttf[:, h * 32 + h * 8:h * 32 + h * 8 + 8]), reads=[BTT], writes=[B_oT[4]])
    if "oTs" in dbg_out:
        odbg3 = ar.alloc([128, 512], F32, "odbg3")
        Bo3 = Buf("odbg3")
        kk.dve(f_copy(odbg3[:].rearrange("p (h t) -> p h t", h=4), oT[:, :, 2048:2176]), reads=B_oT, writes=[Bo3])
        dbg("oTs", odbg3[:], [Bo3])
    kk.barrier()
    if STOP == "p3d":
        return
    ar.reset(wmark)


    ar.reset(omark)
    mergedT = ar.alloc([128, 8, NTOK], BF16, "mergedT")
    B_mg = [Buf("mg%d" % i) for i in range(len(TCH))]
    p4mark = ar.mark()
    wg = [ar.alloc([128, 8, 3, 128], BF16, "wg%d" % i) for i in range(2)]
    Bwg = [Buf("wg0"), Buf("wg1")]
    wbr = [ar.alloc([128, 8, 128], BF16, "wbr%d" % i) for i in range(2)]
    Bwbr = [Buf("wbr0"), Buf("wbr1")]
    sg = [ar.alloc([128, 512], F32, "sg%d" % i) for i in range(6)]
    Bsg = [Buf("sg%d" % i) for i in range(6)]
    mt_ = [ar.alloc([128, 512], F32, "mtmp%d" % i) for i in range(4)]
    Bmt = [Buf("mtmp%d" % i) for i in range(4)]
    bankctr = [0]

    def rbank():
        b = bankctr[0] % 8
        bankctr[0] += 1
        return banks[b], pb[b]

    def load_p4(fc):
        b = fc % 2
        for g in range(3):
            c0 = 2048 + g * 1024 + fc * 128
            kk.dma("pool", wg[b][:, :, g, :], I["w_in"][:, c0:c0 + 128].rearrange("(k p) c -> p k c", p=128), writes=[Bwg[b]])
        kk.dma("pool", wbr[b][:, 0:4, :], I["w_br_attn"][:, fc * 128:(fc + 1) * 128].rearrange("(k p) c -> p k c", p=128), writes=[Bwbr[b]])
        kk.dma("pool", wbr[b][:, 4:6, :], I["w_br_pool"][:, fc * 128:(fc + 1) * 128].rearrange("(k p) c -> p k c", p=128), writes=[Bwbr[b]])
        kk.dma("pool", wbr[b][:, 6:8, :], I["w_br_mem"][:, fc * 128:(fc + 1) * 128].rearrange("(k p) c -> p k c", p=128), writes=[Bwbr[b]])

    load_p4(0)
    un = 0
    for fc in range(8):
        if fc + 1 < 8:
            load_p4(fc + 1)
        b = fc % 2
        for tc in range(len(TCH)):
            o, n = TCH[tc]
            hreads = [B_hT[t] for t in tiles_of(tc)]
            prods = []
            for g in range(3):
                gb, Bgb = rbank()
                for kc in range(8):
                    kk.pe(mm(gb[:, 0:n], wg[b][:, kc, g, :], hT[:, kc, o:o + n], kc == 0, kc == 7), reads=[Bwg[b]] + hreads, writes=[Bgb])
                bb, Bbb = rbank()
                if g == 0:
                    for h in range(4):
                        kk.pe(mm(bb[:, 0:n], wbr[b][:, h, :], oT[:, h, o:o + n], h == 0, h == 3), reads=[Bwbr[b], B_oT[tc]], writes=[Bbb])
                elif g == 1:
                    for ch in range(2):
                        kk.pe(mm(bb[:, 0:n], wbr[b][:, 4 + ch, :], poolT[:, ch, o:o + n], ch == 0, ch == 1), reads=[Bwbr[b], B_poolT[tc]], writes=[Bbb])
                else:
                    for hp in range(2):
                        kk.pe(mm(bb[:, 0:n], wbr[b][:, 6 + hp, :], omT[:, hp, o:o + n], hp == 0, hp == 1), reads=[Bwbr[b], B_omT[tc]], writes=[Bbb])
                si = (un * 3 + g) % 6
                kk.act(f_act(sg[si][:, 0:n], gb[:, 0:n], AF.Sigmoid), reads=[Bgb], writes=[Bsg[si]])
                kk.dve(f_tt(sg[si][:, 0:n], sg[si][:, 0:n], bb[:, 0:n], ALU.mult), reads=[Bsg[si], Bbb], writes=[Bsg[si]])
                prods.append(si)
            mi = un % 4
            kk.dve(f_tt(mt_[mi][:, 0:n], sg[prods[0]][:, 0:n], sg[prods[1]][:, 0:n], ALU.add), reads=[Bsg[prods[0]], Bsg[prods[1]]], writes=[Bmt[mi]])
            kk.dve(f_tt(mergedT[:, fc, o:o + n], mt_[mi][:, 0:n], sg[prods[2]][:, 0:n], ALU.add), reads=[Bmt[mi], Bsg[prods[2]]], writes=[B_mg[tc]])
            un += 1
    if "mergedT" in dbg_out:
        mdbg = ar.alloc([128, 512], F32, "mdbg")
        Bm_ = Buf("mdbg")
        kk.dve(f_copy(mdbg[:, 0:128], mergedT[:, 0, 0:128]), reads=B_mg, writes=[Bm_])
        kk.dve(f_copy(mdbg[:, 128:256], mergedT[:, 7, 1024:1152]), reads=B_mg, writes=[Bm_])
        kk.dve(f_copy(mdbg[:, 256:384], mergedT[:, 3, 2048:2176]), reads=B_mg, writes=[Bm_])
        kk.dve(f_copy(mdbg[:, 384:512], mergedT[:, 5, 2048:2176]), reads=B_mg, writes=[Bm_])
        dbg("mergedT", mdbg[:], [Bm_])
    kk.barrier()
    if STOP == "p4":
        return
    ar.reset(p4mark)

    arA = Arena(nc, amark, omark)
    wout = arA.alloc([128, 8, D], BF16, "wout")
    Bwout = Buf("wout")
    wd = arA.alloc([128, NFF, D], BF16, "wd")
    Bwd = [Buf("wd%d" % i) for i in range(NFF)]
    wgu = [arA.alloc([128, 8, 2, 128], BF16, "wgu%d" % i) for i in range(2)]
    Bwgu = [Buf("wgu%d" % i) for i in range(2)]
    stc = [ar.alloc([32, 512], F32, "stc%d" % i) for i in range(2)]
    stT = ar.alloc([128, NFF, NSEQ, 2], F32, "stT")
    cs = ar.alloc([128, NFF, 34], F32, "cs")
    halo = [ar.alloc([128, NFF, 2], F32, "halo%d" % i) for i in range(2)]
    Bstc, BstT, Bcs, Bhalo = [Buf("stc0"), Buf("stc1")], Buf("stT"), Buf("cs"), [Buf("halo0"), Buf("halo1")]
    for k2 in range(2):
        kk.dma("pool", wout[:, k2 * 4:(k2 + 1) * 4, :], I["w_out"][k2 * 512:(k2 + 1) * 512, :].rearrange("(k p) c -> p k c", p=128), writes=[Bwout])
    kk.dve(f_memset(halo[0][:], 0.0), writes=[Bhalo[0]])
    for q4 in range(6):
        nf = min(4, NFF - q4 * 4)
        kk.dma("sp", stc[q4 % 2][:, 0:nf * 128], I["state_conv"][:, q4 * 512:q4 * 512 + nf * 128], writes=[Bstc[q4 % 2]])
        for i in range(nf):
            fcx = q4 * 4 + i
            bk, Bb = rbank()
            kk.pe(mm(bk[:, 0:32], stc[q4 % 2][:, i * 128:(i + 1) * 128], ident_f[0:32, 0:32], True, True), reads=[Bstc[q4 % 2], Bc], writes=[Bb])
            kk.dve(f_copy(stT[:, fcx].rearrange("p s r -> p (s r)"), bk[:, 0:32]), reads=[Bb], writes=[BstT])
    x2 = ar.alloc([128, 4, D], F32, "x2")
    Bx2 = [Buf("x2_%d" % i) for i in range(4)]
    h2T = ar.alloc([128, 8, 512], BF16, "h2T")
    Bh2 = [Buf("h2_%d" % i) for i in range(4)]
    actT = ar.alloc([128, NFF, 512], BF16, "actT")
    Bact = [Buf("act%d" % i) for i in range(NFF)]
    gS = [ar.alloc([128, 2 + 512], F32, "gS%d" % i) for i in range(2)]
    BgS = [Buf("gS0"), Buf("gS1")]
    gSs = [ar.alloc([128, NSEQ, 10], F32, "gSs%d" % i) for i in range(2)]
    BgSs = [Buf("gSs0"), Buf("gSs1")]
    c1 = [ar.alloc([128, 512], F32, "c1_%d" % i) for i in range(2)]
    Bc1 = [Buf("c1_0"), Buf("c1_1")]
    ge = [ar.alloc([128, 512], F32, "ge%d" % i) for i in range(2)]
    Bge = [Buf("ge0"), Buf("ge1")]
    xin5 = [ar.alloc([128, D], F32, "xin5_%d" % i) for i in range(2)]
    Bxin5 = [Buf("xin5_0"), Buf("xin5_1")]
    xn5 = ar.alloc([128, D], BF16, "xn5")
    Bxn5 = Buf("xn5")
    junk5 = ar.alloc([128, D], BF16, "junk5")
    Bjunk5 = Buf("junk5")
    ss5 = [ar.alloc([128, 4], F32, "ss5_%d" % i) for i in range(2)]
    Bss5 = [Buf("ss5_0"), Buf("ss5_1")]
    yt = [ar.alloc([128, D], F32, "yt0")] * 2
    Byt = [Buf("yt0")] * 2
    csr = [ar.alloc([34, 512], F32, "csr%d" % i) for i in range(2)]
    Bcsr = [Buf("csr0"), Buf("csr1")]
    for fcx in range(NFF):
        kk.dma("pool", wd[:, fcx, :], I["w_ffn_down"][fcx * 128:(fcx + 1) * 128, :], writes=[Bwd[fcx]])
    nwl = [0]
    nt5 = 0
    for gi, (t0, t1) in enumerate(GROUPS):
        ntile = t1 - t0
        smp = gi == len(GROUPS) - 1
        lastp = gi == len(GROUPS) - 2
        ntk = ntile * 128
        for li in range(ntile):
            t = t0 + li
            xb_, Bxb_ = xin5[nt5 % 2], Bxin5[nt5 % 2]
            kk.dma("sp", xb_[:], I["x_all"][t * 128:(t + 1) * 128, :], writes=[Bxb_])
            for half in range(2):
                bk, Bb = rbank()
                for kc in range(8):
                    kk.pe(mm(bk[:, :], mergedT[:, kc, t * 128:(t + 1) * 128], wout[:, kc, half * 512:(half + 1) * 512], kc == 0, kc == 7),
                          reads=[B_mg[t // 4], Bwout], writes=[Bb])
                kk.dve(f_tt(x2[:, li, half * 512:(half + 1) * 512], bk[:, :], xb_[:, half * 512:(half + 1) * 512], ALU.add), reads=[Bb, Bxb_], writes=[Bx2[li]])
            bk, Bb = rbank()
            norm_transpose(None, g2T, h2T, slice(li * 128, (li + 1) * 128), x2[:, li, :], Bx2[li], xn5, Bxn5,
                           ss5[nt5 % 2], Bss5[nt5 % 2], bk, Bb, Bh2[li], junk5, t)
            nt5 += 1
        n = ntk
        hin, hout = halo[gi % 2], halo[(gi + 1) % 2]
        Bhin, Bhout = Bhalo[gi % 2], Bhalo[(gi + 1) % 2]
        for fcx in range(NFF):
            wi = nwl[0] % 2
            nwl[0] += 1
            kk.dma("pool", wgu[wi][:, :, 0, :], I["w_ffn_gate"][:, fcx * 128:(fcx + 1) * 128].rearrange("(k p) c -> p k c", p=128), writes=[Bwgu[wi]])
            kk.dma("pool", wgu[wi][:, :, 1, :], I["w_ffn_up"][:, fcx * 128:(fcx + 1) * 128].rearrange("(k p) c -> p k c", p=128), writes=[Bwgu[wi]])
            bi = fcx % 2
            g_, Bg_ = gS[bi], BgS[bi]
            gs_, Bgs_ = gSs[bi], BgSs[bi]
            c_, Bc_ = c1[bi], Bc1[bi]
            e_, Be_ = ge[bi], Bge[bi]
            hreads = [Bh2[i] for i in range(ntile)]
            gb, Bgb = rbank()
            for kc in range(8):
                kk.pe(mm(gb[:, 0:n], wgu[wi][:, kc, 0, :], h2T[:, kc, 0:n], kc == 0, kc == 7), reads=[Bwgu[wi]] + hreads, writes=[Bgb])
            ub, Bub = rbank()
            for kc in range(8):
                kk.pe(mm(ub[:, 0:n], wgu[wi][:, kc, 1, :], h2T[:, kc, 0:n], kc == 0, kc == 7), reads=[Bwgu[wi]] + hreads, writes=[Bub])
            w0, w1, w2, bb_ = convw[:, 0, fcx:fcx + 1], convw[:, 1, fcx:fcx + 1], convw[:, 2, fcx:fcx + 1], convb[:, fcx:fcx + 1]
            if not smp:
                kk.act(f_act(g_[:, 2:2 + 512], gb[:, 0:512], AF.Copy), reads=[Bgb], writes=[Bg_])
                kk.dve(f_copy(g_[:, 0:2], hin[:, fcx, :]), reads=[Bhin], writes=[Bg_])
                kk.dve(f_copy(hout[:, fcx, :], g_[:, 512:514]), reads=[Bg_], writes=[Bhout])
                kk.dve(f_ts(c_[:, 0:512], g_[:, 0:512], w0, bb_, ALU.mult, ALU.add), reads=[Bg_, Bc], writes=[Bc_])
                kk.dve(f_stt(c_[:, 0:512], g_[:, 1:513], w1, c_[:, 0:512], ALU.mult, ALU.add), reads=[Bg_, Bc_, Bc], writes=[Bc_])
                kk.dve(f_stt(c_[:, 0:512], g_[:, 2:514], w2, c_[:, 0:512], ALU.mult, ALU.add), reads=[Bg_, Bc_, Bc], writes=[Bc_])
                if lastp:
                    kk.dve(f_copy(cs[:, fcx, 0:2], g_[:, 512:514]), reads=[Bg_], writes=[Bcs])
            else:
                kk.act(f_act(gs_[:, :, 2:10], gb[:, 0:128].rearrange("p (s i) -> p s i", i=8), AF.Copy), reads=[Bgb], writes=[Bgs_])
                kk.dve(f_copy(gs_[:, :, 0:2], stT[:, fcx]), reads=[BstT], writes=[Bgs_])
                cv = c_[:, 0:128].rearrange("p (s i) -> p s i", i=8)
                kk.dve(f_ts(cv, gs_[:, :, 0:8], w0, bb_, ALU.mult, ALU.add), reads=[Bgs_, Bc], writes=[Bc_])
                kk.dve(f_stt(cv, gs_[:, :, 1:9], w1, cv, ALU.mult, ALU.add), reads=[Bgs_, Bc_, Bc], writes=[Bc_])
                kk.dve(f_stt(cv, gs_[:, :, 2:10], w2, cv, ALU.mult, ALU.add), reads=[Bgs_, Bc_, Bc], writes=[Bc_])
                kk.dve(f_copy(cs[:, fcx, 2:34].rearrange("p (s r) -> p s r", r=2), gs_[:, :, 8:10]), reads=[Bgs_], writes=[Bcs])
            kk.act(f_act(e_[:, 0:n], c_[:, 0:n], AF.Gelu_apprx_tanh), reads=[Bc_], writes=[Be_])
            kk.dve(f_tt(actT[:, fcx, 0:n], e_[:, 0:n], ub[:, 0:n], ALU.mult), reads=[Be_, Bub], writes=[Bact[fcx]])
        for li in range(ntile):
            t = t0 + li
            for half in range(2):
                bk, Bb = rbank()
                for fcx in range(NFF):
                    kk.pe(mm(bk[:, :], actT[:, fcx, li * 128:(li + 1) * 128], wd[:, fcx, half * 512:(half + 1) * 512], fcx == 0, fcx == NFF - 1),
                          reads=[Bact[fcx], Bwd[fcx]], writes=[Bb])
                kk.dve(f_tt(x2[:, li, half * 512:(half + 1) * 512], bk[:, :], x2[:, li, half * 512:(half + 1) * 512], ALU.add), reads=[Bb, Bx2[li]], writes=[Bx2[li]])
            si = nt5 % 2
            nt5 += 1
            kk.act(f_act(junk5[:], x2[:, li, :], AF.Square, accum_out=ss5[si][:, 0:1]), reads=[Bx2[li]], writes=[Bss5[si], Bjunk5])
            kk.act(f_act(ss5[si][:, 1:2], ss5[si][:, 0:1], AF.Sqrt, scale=1.0 / D, bias=eps_t[:, 0:1]), reads=[Bss5[si], Bc], writes=[Bss5[si]])
            kk.dve(f_recip(ss5[si][:, 2:3], ss5[si][:, 1:2]), reads=[Bss5[si]], writes=[Bss5[si]])
            kk.dve(f_stt(yt[si][:], x2[:, li, :], ss5[si][:, 2:3], gfin[:], ALU.mult, ALU.mult), reads=[Bx2[li], Bss5[si], Bc], writes=[Byt[si]])
            kk.dma("sp", O["y_all"][t * 128:(t + 1) * 128, :], yt[si][:], reads=[Byt[si]])
    for q4 in range(6):
        bk, Bb = rbank()
        nf = min(4, NFF - q4 * 4)
        for i in range(nf):
            fcx = q4 * 4 + i
            kk.pe(mm(bk[0:34, i * 128:(i + 1) * 128], cs[:, fcx, :], ident_f[:, :], True, True), reads=[Bcs, Bc], writes=[Bb])
        kk.dve(f_copy(csr[q4 % 2][:, 0:nf * 128], bk[0:34, 0:nf * 128]), reads=[Bb], writes=[Bcsr[q4 % 2]])
        kk.dma("sp", O["conv_all"][:, q4 * 512:q4 * 512 + nf * 128], csr[q4 % 2][:, 0:nf * 128], reads=[Bcsr[q4 % 2]])

    kk.barrier()


_NC_CACHE = {}


def kernel(**inputs):
    f32 = lambda a: np.ascontiguousarray(np.asarray(a, dtype=np.float32))
    if "nc" not in _NC_CACHE:
        _NC_CACHE["nc"] = build_program()
    nc = _NC_CACHE["nc"]
    consts = host_constants()
    x_prompt = f32(inputs["x_prompt"])
    x_sample = f32(inputs["x_sample"])
    mem_prompt = f32(inputs["mem_prompt"])
    cache_k = f32(inputs["cache_k"]).reshape(-1, 512)
    cache_v = f32(inputs["cache_v"]).reshape(-1, 512)
    page_table = np.ascontiguousarray(np.asarray(inputs["page_table"], dtype=np.int32))
    state_pool = f32(inputs["state_pool"])[0]
    state_conv = f32(inputs["state_ffn_conv"])[0]
    cmk = f32(inputs["cache_mem_k"])[0]
    cmv = f32(inputs["cache_mem_v"])[0]
    shared = {
        "cache_k": cache_k, "cache_v": cache_v,
        "norm1_g": f32(inputs["norm1_g"])[0], "w_in": f32(inputs["w_in"])[0],
        "lam_q1": f32(inputs["lam_q1"]), "lam_k1": f32(inputs["lam_k1"]),
        "lam_q2": f32(inputs["lam_q2"]), "lam_k2": f32(inputs["lam_k2"]),
        "subln_g": f32(inputs["subln_g"])[0], "w_pool_grp": f32(inputs["w_pool_grp"])[0],
        "pool_scale": f32(inputs["pool_scale"])[0],
        "w_br_attn": f32(inputs["w_br_attn"])[0], "w_br_pool": f32(inputs["w_br_pool"])[0],
        "w_br_mem": f32(inputs["w_br_mem"])[0], "mem_norm_g": f32(inputs["mem_norm_g"])[0],
        "w_mem_kv": f32(inputs["w_mem_kv"])[0], "w_out": f32(inputs["w_out"])[0],
        "norm2_g": f32(inputs["norm2_g"])[0], "w_ffn_gate": f32(inputs["w_ffn_gate"])[0],
        "w_ffn_up": f32(inputs["w_ffn_up"])[0], "ffn_conv_w": f32(inputs["ffn_conv_w"])[0],
        "ffn_conv_b": f32(inputs["ffn_conv_b"])[0], "w_ffn_down": f32(inputs["w_ffn_down"])[0],
        "rel_bias": f32(inputs["rel_bias"]), "final_norm_g": f32(inputs["final_norm_g"]),
    }
    shared.update(consts)
    in_maps = []
    for c in range(8):
        sl = slice(NSEQ * c, NSEQ * (c + 1))
        m = dict(shared)
        m["x_all"] = np.ascontiguousarray(np.concatenate([x_prompt[c], x_sample[sl].reshape(128, D)], axis=0))
        m["mem"] = mem_prompt[c]
        m["page_table"] = np.ascontiguousarray(page_table[sl].reshape(1, NSEQ * NPAGE))
        m["state_pool"] = np.ascontiguousarray(state_pool[sl].reshape(NSEQ * 15, 256))
        m["state_conv"] = np.ascontiguousarray(state_conv[sl].reshape(NSEQ * 2, D_FF))
        m["cmem_k"] = np.ascontiguousarray(cmk[sl].reshape(NSEQ, 256, 256))
        m["cmem_v"] = np.ascontiguousarray(cmv[sl].reshape(NSEQ, 256, 256))
        in_maps.append(m)
    res = run_bass_kernel_spmd(nc, in_maps, core_ids=list(range(8)))
    R = res.results
    g = lambda k: [np.asarray(R[c][k], dtype=np.float32) for c in range(8)]
    y = g("y_all"); nk = g("newk"); nv = g("newv")
    y_prompt = np.stack([a[:2048] for a in y], 0)
    y_sample = np.concatenate([a[2048:].reshape(NSEQ, 8, D) for a in y], 0)
    nkp = np.stack([a[:2048].reshape(2048, 4, 128) for a in nk], 0)[None]
    nvp = np.stack([a[:2048].reshape(2048, 4, 128) for a in nv], 0)[None]
    nks = np.concatenate([a[2048:].reshape(NSEQ, 8, 4, 128) for a in nk], 0)[None]
    nvs = np.concatenate([a[2048:].reshape(NSEQ, 8, 4, 128) for a in nv], 0)[None]
    pp = np.stack(g("pool_p"), 0)[None]
    ps = np.concatenate(g("pool_s"), 0)[None]
    cv = g("conv_all")
    cp = np.stack([a[:2] for a in cv], 0)[None]
    cs = np.concatenate([a[2:].reshape(NSEQ, 2, D_FF) for a in cv], 0)[None]
    mk = np.stack([a.reshape(256, 4, 64) for a in g("memk")], 0)[None]
    mv = np.stack([a.reshape(256, 4, 64) for a in g("memv")], 0)[None]
    return (y_prompt, y_sample, nkp, nvp, nks, nvs, pp, ps, cp, cs, mk, mv)
```

```python
import numpy as np
from contextlib import ExitStack

import concourse.bass as bass
import concourse.mybir as mybir
from concourse.bass_utils import run_bass_kernel_spmd

F32 = mybir.dt.float32
BF16 = mybir.dt.bfloat16
I32 = mybir.dt.int32
AF = mybir.ActivationFunctionType
ALU = mybir.AluOpType
AX = mybir.AxisListType

D = 1024
NTOK = 2176
NT = 17
D_IN = 5120
D_FF = 2816
NFF = 22
EPS = 1e-6
LAM_INIT = 0.8 - 0.6
NSEQ = 16
NPAGE = 16
WZ = 384

TCH = [(0, 512), (512, 512), (1024, 512), (1536, 512), (2048, 128)]
GROUPS = [(0, 4), (4, 8), (8, 12), (12, 16), (16, 17)]


class Buf:
    __slots__ = ("name", "w", "r")

    def __init__(self, name):
        self.name = name
        self.w = None
        self.r = []


class Op:
    __slots__ = ("eng", "fn", "waits", "signal", "idx", "count", "dma", "dsem", "dval", "pre")

    def __init__(self, eng, fn, dma):
        self.eng = eng
        self.fn = fn
        self.waits = []
        self.signal = False
        self.idx = -1
        self.count = 0
        self.dma = dma
        self.dsem = None
        self.dval = 0
        self.pre = None


ENGS = ("pe", "act", "dve", "pool", "sp")
NDSEM = 24


class K:
    def __init__(self, nc, es):
        self.nc = nc
        self.ops = {e: [] for e in ENGS}
        self.waited = {e: {p: -1 for p in ENGS} for e in ENGS}
        self.waited_dma = {e: set() for e in ENGS}
        self.sem = {e: es.enter_context(nc.semaphore("s_" + e)) for e in ENGS}
        self.dsems = {q: [es.enter_context(nc.semaphore("d_%s%d" % (q, i))) for i in range(NDSEM)]
                      for q in ("sp", "pool")}
        self.ndma = {"sp": 0, "pool": 0}
        self.dma_ops = {"sp": [], "pool": []}

    def _dep(self, op, d, force=False):
        e = op.eng
        if d is None or d is op:
            return
        if d.dma:
            if id(d) in self.waited_dma[e]:
                return
            self.waited_dma[e].add(id(d))
            op.waits.append(d)
            return
        p = d.eng
        if p == "pe" and e == "pe" and not force:
            return
        if self.waited[e][p] >= d.idx:
            return
        self.waited[e][p] = d.idx
        d.signal = True
        op.waits.append(d)

    def op(self, eng, fn, reads=(), writes=(), dma=False):
        o = Op(eng, fn, dma)
        o.idx = len(self.ops[eng])
        deps = []
        for b in reads:
            if b.w is not None:
                deps.append(b.w)
        for b in writes:
            if b.w is not None:
                deps.append(b.w)
            deps.extend(b.r)
        latest = {}
        for d in deps:
            if d.dma:
                self._dep(o, d)
            elif d.eng not in latest or latest[d.eng].idx < d.idx:
                latest[d.eng] = d
        for d in latest.values():
            self._dep(o, d)
        if dma:
            n = self.ndma[eng]
            self.ndma[eng] += 1
            o.dsem = self.dsems[eng][n % NDSEM]
            o.dval = 16 * (n // NDSEM + 1)
            if n >= NDSEM:
                prev = self.dma_ops[eng][n - NDSEM]
                o.pre = prev
            self.dma_ops[eng].append(o)
        self.ops[eng].append(o)
        for b in reads:
            b.r.append(o)
        for b in writes:
            b.w = o
            b.r = []
        return o

    def pe(self, fn, reads=(), writes=()):
        return self.op("pe", fn, reads, writes)

    def act(self, fn, reads=(), writes=()):
        return self.op("act", fn, reads, writes)

    def dve(self, fn, reads=(), writes=()):
        return self.op("dve", fn, reads, writes)

    def pool(self, fn, reads=(), writes=()):
        return self.op("pool", fn, reads, writes)

    def dma(self, q, out, in_, reads=(), writes=(), **kw):
        return self.op(q, lambda e: e.dma_start(out=out, in_=in_, **kw), reads, writes, dma=True)

    def barrier(self):
        lasts = []
        for e in ENGS:
            real = [o for o in self.ops[e] if o.fn is not None and not o.dma]
            if real:
                lasts.append(real[-1])
        dmas = self.dma_ops["sp"][-NDSEM:] + self.dma_ops["pool"][-NDSEM:]
        for e in ENGS:
            o = Op(e, None, False)
            o.idx = len(self.ops[e])
            for d in lasts + dmas:
                self._dep(o, d, force=True)
            self.ops[e].append(o)

    def emit(self, block):
        for e in ENGS:
            c = 0
            for o in self.ops[e]:
                if o.signal:
                    c += 1
                    o.count = c
        def run(e, eng):
            for o in self.ops[e]:
                if o.pre is not None:
                    eng.wait_ge(o.pre.dsem, o.pre.dval)
                for d in o.waits:
                    if d.dma:
                        eng.wait_ge(d.dsem, d.dval)
                    else:
                        eng.wait_ge(self.sem[d.eng], d.count)
                if o.fn is None:
                    continue
                ins = o.fn(eng)
                if o.dma:
                    ins.then_inc(o.dsem, 16)
                elif o.signal:
                    ins.then_inc(self.sem[e], 1)

        @block.tensor
        def _(eng):
            run("pe", eng)

        @block.scalar
        def _(eng):
            run("act", eng)

        @block.vector
        def _(eng):
            run("dve", eng)

        @block.gpsimd
        def _(eng):
            run("pool", eng)

        @block.sync
        def _(eng):
            run("sp", eng)


class Arena:
    def __init__(self, nc, base, cap):
        self.nc = nc
        self.base = base
        self.cap = cap
        self.top = base
        self.n = 0

    def alloc(self, shape, dtype, name=None):
        nbytes = int(np.prod(shape[1:])) * mybir.dt.size(dtype)
        off = (self.top + 31) // 32 * 32
        assert off + nbytes <= self.cap, ("SBUF arena overflow", name, off, nbytes, self.cap)
        self.top = off + nbytes
        self.n += 1
        nm = "%s_%d_%d" % (name or "t", off, self.n)
        return self.nc.alloc_sbuf_tensor_at(nm, list(shape), dtype, offset=off)

    def mark(self):
        return self.top

    def reset(self, m):
        self.top = m


def rel_bucket_np(rel):
    n = np.maximum(rel, 0)
    max_exact = 16
    nf = np.maximum(n, 1).astype(np.float32)
    large = max_exact + (np.log(nf / max_exact) / np.log(128 / max_exact) * (32 - max_exact)).astype(np.int32)
    large = np.minimum(large, 31)
    return np.where(n < max_exact, n, large)


def host_constants():
    c = {}
    c["ident"] = np.eye(128, dtype=np.float32)
    rel = np.arange(WZ) - 128
    b = rel_bucket_np(rel)
    oh = np.zeros((32, WZ), np.float32)
    oh[b, np.arange(WZ)] = 1.0
    oh[:, rel < 0] = 0.0
    c["bucket_oh"] = oh
    c["relmask"] = np.repeat((rel >= 0).astype(np.float32)[None, :], 128, axis=0)
    pc = np.zeros((128, 2, 16), np.float32)
    for ch in range(2):
        for p in range(128):
            w = 2 ** (2 * ch + p // 64 + 1)
            pc[p, ch, :] = 1.0 / np.minimum(np.arange(16) + 1, w)
    c["poolc"] = pc
    bd = np.zeros((128, 16), np.float32)
    bd[np.arange(128), np.arange(128) // 8] = 1.0
    c["blockdiag"] = bd
    sel = np.zeros((64, 2, 32), np.float32)
    for h in range(4):
        for cc in range(2):
            for q in range(8):
                sel[h * 16 + cc * 8 + q, cc, h * 8 + q] = 1.0
    c["sel"] = sel
    c["iota_f"] = np.arange(128, dtype=np.float32).reshape(128, 1)
    return c


CONST_SHAPES = {
    "ident": ([128, 128], F32), "bucket_oh": ([32, WZ], F32), "relmask": ([128, WZ], F32),
    "poolc": ([128, 2, 16], F32), "blockdiag": ([128, 16], F32), "sel": ([64, 2, 32], F32),
    "iota_f": ([128, 1], F32),
}

IN_SHAPES = {
    "x_all": ([NTOK, D], F32), "mem": ([256, D], F32),
    "cache_kv": ([2560 * 128, 1024], F32),
    "page_table": ([1, NSEQ * NPAGE], I32),
    "state_pool": ([NSEQ * 15, 256], F32), "state_conv": ([NSEQ * 2, D_FF], F32),
    "cmem_k": ([NSEQ, 256, 256], F32), "cmem_v": ([NSEQ, 256, 256], F32),
    "norm1_g": ([D], F32), "w_in": ([D, D_IN], F32),
    "lam_q1": ([1, 64], F32), "lam_k1": ([1, 64], F32), "lam_q2": ([1, 64], F32), "lam_k2": ([1, 64], F32),
    "subln_g": ([128], F32), "w_pool_grp": ([4, 64, 64], F32), "pool_scale": ([256], F32),
    "w_br_attn": ([512, D], F32), "w_br_pool": ([256, D], F32), "w_br_mem": ([256, D], F32),
    "mem_norm_g": ([D], F32), "w_mem_kv": ([D, 512], F32), "w_out": ([D, D], F32),
    "norm2_g": ([D], F32), "w_ffn_gate": ([D, D_FF], F32), "w_ffn_up": ([D, D_FF], F32),
    "ffn_conv_w": ([3, D_FF], F32), "ffn_conv_b": ([D_FF], F32), "w_ffn_down": ([D_FF, D], F32),
    "rel_bias": ([32, 4], F32), "final_norm_g": ([D], F32),
}

OUT_SHAPES = {
    "y_all": [NTOK, D], "newk": [NTOK, 512], "newv": [NTOK, 512],
    "pool_p": [15, 256], "pool_s": [NSEQ, 15, 256],
    "conv_all": [2 + 2 * NSEQ, D_FF],
    "memk": [256, 256], "memv": [256, 256],
}


def build_program(phases=("all",), debug=None, nphys=2560):
    nc = bass.Bass("TRN2", target_bir_lowering=False)
    I = {}
    for k, (shp, dt) in {**IN_SHAPES, **CONST_SHAPES}.items():
        if k == "cache_kv":
            shp = [nphys * 128, 1024]
        I[k] = nc.dram_tensor(k, shp, dt, kind="ExternalInput").ap()
    O = {}
    for k, shp in OUT_SHAPES.items():
        O[k] = nc.dram_tensor(k, shp, F32, kind="ExternalOutput").ap()
    zscr = nc.dram_tensor("zscr", [128, 4 * WZ], F32, kind="Internal").ap()
    dbg_out = {}
    if debug:
        for k, shp in debug.items():
            dbg_out[k] = nc.dram_tensor("dbg_" + k, shp, F32, kind="ExternalOutput").ap()

    with ExitStack() as es:
        kk = K(nc, es)
        banks = [es.enter_context(nc.psum_tensor("bank%d" % i, [128, 512], F32)) for i in range(8)]
        pb = [Buf("psum%d" % i) for i in range(8)]
        block = es.enter_context(nc.Block())
        _build(nc, kk, I, O, zscr, banks, pb, dbg_out, phases)
        kk.emit(block)
    return nc


def _build(nc, kk, I, O, zscr, banks, pb, dbg_out, phases):
    ALL = "all" in phases
    STOP = [p for p in phases if p.startswith("p")]
    STOP = STOP[0] if STOP else None
    ar = Arena(nc, (nc.sbuf_base + 63) // 64 * 64, nc.sbuf_top)

    def mm(out, lhsT, rhs, start, stop):
        return lambda e: e.matmul(out, lhsT, rhs, start=start, stop=stop)

    def f_tt(out, in0, in1, op):
        return lambda e: e.tensor_tensor(out=out, in0=in0, in1=in1, op=op)

    def f_ts(out, in0, s1, s2, op0, op1=None):
        if op1 is None:
            return lambda e: e.tensor_scalar(out=out, in0=in0, scalar1=s1, scalar2=None, op0=op0)
        return lambda e: e.tensor_scalar(out=out, in0=in0, scalar1=s1, scalar2=s2, op0=op0, op1=op1)

    def f_stt(out, in0, scalar, in1, op0, op1):
        return lambda e: e.scalar_tensor_tensor(out=out, in0=in0, scalar=scalar, in1=in1, op0=op0, op1=op1)

    def f_copy(out, in_):
        return lambda e: e.tensor_copy(out=out, in_=in_)

    def f_act(out, in_, func, **kw):
        return lambda e: e.activation(out=out, in_=in_, func=func, **kw)

    def f_recip(out, in_):
        return lambda e: e.reciprocal(out=out, in_=in_)

    def f_memset(ap, v):
        return lambda e: e.memset(ap, v)

    def f_tr(out, in_, ident):
        return lambda e: e.transpose(out, in_, ident)

    cst = {}
    ident_f = ar.alloc([128, 128], F32, "identf")
    ident_b = ar.alloc([128, 128], BF16, "identb")
    ones_f = ar.alloc([128, 128], F32, "onesf")
    ones_b = ar.alloc([128, 128], BF16, "onesb")
    g1T = ar.alloc([128, 8], F32, "g1T")
    g2T = ar.alloc([128, 8], F32, "g2T")
    gmT = ar.alloc([128, 8], F32, "gmT")
    gfin = ar.alloc([128, D], F32, "gfin")
    sublnT = ar.alloc([128, 1], F32, "subln")
    pscaleT = ar.alloc([128, 2], F32, "pscale")
    convw = ar.alloc([128, 3, NFF], F32, "convw")
    convb = ar.alloc([128, NFF], F32, "convb")
    lamv = ar.alloc([128, 4, 64], F32, "lamv")
    lamt = ar.alloc([128, 8], F32, "lamt")
    neg_lam = ar.alloc([128, 1], F32, "neglam")
    eps_t = ar.alloc([128, 1], F32, "eps")
    poolc = ar.alloc([128, 2, 16], F32, "poolc")
    bdiag = ar.alloc([128, 16], F32, "bdiag")
    selc = ar.alloc([64, 2, 32], F32, "sel")
    comb = ar.alloc([64, 32], F32, "comb")
    Bc = Buf("consts")

    kk.dma("sp", ident_f[:], I["ident"][:, :], writes=[Bc])
    kk.dma("pool", ident_b[:], I["ident"][:, :], writes=[Bc])
    kk.dve(lambda e: e.memset(ones_f[:], 1.0), writes=[Bc])
    kk.dve(lambda e: e.memset(ones_b[:], 1.0), writes=[Bc])
    kk.dve(lambda e: e.memset(eps_t[:], EPS), writes=[Bc])
    for t, src in ((g1T, "norm1_g"), (g2T, "norm2_g"), (gmT, "mem_norm_g")):
        kk.dma("sp", t[:], I[src].rearrange("(k p) -> p k", p=128), writes=[Bc], allow_slow_non_contiguous=True)
    kk.dma("sp", gfin[:], I["final_norm_g"].partition_broadcast(128), writes=[Bc])
    kk.dma("sp", sublnT[:], I["subln_g"].rearrange("(p o) -> p o", o=1), writes=[Bc], allow_slow_non_contiguous=True)
    kk.dma("sp", pscaleT[:], I["pool_scale"].rearrange("(k p) -> p k", p=128), writes=[Bc], allow_slow_non_contiguous=True)
    kk.dma("sp", convw[:], I["ffn_conv_w"].rearrange("j (c p) -> p j c", p=128), writes=[Bc], allow_slow_non_contiguous=True)
    kk.dma("sp", convb[:], I["ffn_conv_b"].rearrange("(c p) -> p c", p=128), writes=[Bc], allow_slow_non_contiguous=True)
    for i, nm in enumerate(("lam_q1", "lam_k1", "lam_q2", "lam_k2")):
        kk.dma("sp", lamv[:, i, :], I[nm][0, :].partition_broadcast(128), writes=[Bc])
    kk.dma("sp", poolc[:], I["poolc"][:, :, :], writes=[Bc])
    kk.dma("sp", bdiag[:], I["blockdiag"][:, :], writes=[Bc])
    kk.dma("sp", selc[:], I["sel"][:, :, :], writes=[Bc])
    Bl = Buf("lam")
    kk.dve(lambda e: e.tensor_tensor(out=lamv[:, 0, :], in0=lamv[:, 0, :], in1=lamv[:, 1, :], op=ALU.mult), reads=[Bc], writes=[Bl])
    kk.dve(lambda e: e.tensor_tensor(out=lamv[:, 2, :], in0=lamv[:, 2, :], in1=lamv[:, 3, :], op=ALU.mult), reads=[Bl], writes=[Bl])
    kk.dve(lambda e: e.tensor_reduce(out=lamt[:, 0:1], in_=lamv[:, 0, :], axis=AX.X, op=ALU.add), reads=[Bl], writes=[Bl])
    kk.dve(lambda e: e.tensor_reduce(out=lamt[:, 1:2], in_=lamv[:, 2, :], axis=AX.X, op=ALU.add), reads=[Bl], writes=[Bl])
    kk.act(lambda e: e.activation(out=lamt[:, 2:4], in_=lamt[:, 0:2], func=AF.Exp), reads=[Bl], writes=[Bl])
    kk.dve(lambda e: e.tensor_tensor(out=lamt[:, 4:5], in0=lamt[:, 3:4], in1=lamt[:, 2:3], op=ALU.subtract), reads=[Bl], writes=[Bl])
    kk.dve(lambda e: e.tensor_scalar(out=neg_lam[:], in0=lamt[:, 4:5], scalar1=-LAM_INIT, scalar2=None, op0=ALU.add), reads=[Bl], writes=[Bl])
    kk.dve(lambda e: e.scalar_tensor_tensor(out=comb[:], in0=selc[:, 1, :], scalar=neg_lam[0:64, 0:1], in1=selc[:, 0, :],
                                            op0=ALU.mult, op1=ALU.add), reads=[Bl, Bc], writes=[Bl])
    kk.dve(lambda e: e.tensor_scalar(out=sublnT[:], in0=sublnT[:], scalar1=1.0 - LAM_INIT, scalar2=None, op0=ALU.mult), reads=[Bc], writes=[Bc])

    T0 = ar.alloc([128, 4, 128], F32, "T0")
    T1 = ar.alloc([128, 4, 128], F32, "T1")
    cmark = ar.mark()
    rb = ar.alloc([32, 4], F32, "rb")
    rbrep = ar.alloc([32, 4, 128], F32, "rbrep")
    oh = ar.alloc([32, WZ], F32, "oh")
    relmask = ar.alloc([128, WZ], F32, "relmask")
    erow = ar.alloc([128, 4, WZ], F32, "erow")
    Bt = Buf("T")
    kk.dma("sp", rb[:], I["rel_bias"][:, :], writes=[Bt])
    kk.dma("sp", oh[:], I["bucket_oh"][:, :], writes=[Bt])
    kk.dma("sp", relmask[:], I["relmask"][:, :], writes=[Bt])
    for h in range(4):
        kk.dve(lambda e, h=h: e.tensor_copy(out=rbrep[:, h, :], in_=rb[:, h:h + 1].to_broadcast([32, 128])), reads=[Bt], writes=[Bt])
    for h in range(4):
        kk.pe(mm(banks[0][:, 0:WZ], rbrep[:, h, :], oh[:, :], True, True), reads=[Bt], writes=[pb[0]])
        kk.dve(lambda e, h=h: e.tensor_scalar(out=erow[:, h, 0:1], in0=banks[0][:, WZ - 1:WZ], scalar1=-1.0, scalar2=None, op0=ALU.mult),
               reads=[pb[0]], writes=[Bt])
        kk.act(lambda e, h=h: e.activation(out=erow[:, h, 1:WZ], in_=banks[0][:, 1:WZ], func=AF.Exp, bias=erow[:, h, 0:1]),
               reads=[pb[0], Bt], writes=[Bt])
        kk.dve(lambda e, h=h: e.tensor_tensor(out=erow[:, h, :], in0=erow[:, h, :], in1=relmask[:, :], op=ALU.mult), reads=[Bt], writes=[Bt])
    Bz = Buf("zscr")
    kk.dma("sp", zscr[:, :], erow[:].rearrange("p h w -> p (h w)"), reads=[Bt], writes=[Bz])
    for h in range(4):
        s0 = bass.AP(zscr.tensor, h * WZ + 128, [[4 * WZ - 1, 128], [1, 128]])
        s1 = bass.AP(zscr.tensor, h * WZ + 256, [[4 * WZ - 1, 128], [1, 128]])
        kk.dma("sp", T0[:, h, :], s0, reads=[Bz], writes=[Bt])
        kk.dma("sp", T1[:, h, :], s1, reads=[Bz], writes=[Bt])

    def dbg(name, ap, rd):
        if name in dbg_out:
            kk.dma("sp", dbg_out[name], ap, reads=rd)

    dbg("T0", T0[:].rearrange("p h w -> p (h w)"), [Bt])
    dbg("T1", T1[:].rearrange("p h w -> p (h w)"), [Bt])
    dbg("neglam", neg_lam[:], [Bl])
    kk.barrier()
    if STOP == "p0":
        return
    ar.reset(cmark)


    amark = ar.mark()
    hT = ar.alloc([128, 8, NTOK], BF16, "hT")
    oT = ar.alloc([128, 4, NTOK], BF16, "oT")
    poolT = ar.alloc([128, 2, NTOK], BF16, "poolT")
    omT = ar.alloc([128, 2, NTOK], BF16, "omT")
    omark = ar.mark()
    qT = ar.alloc([128, 4, NTOK], BF16, "qT")
    kT = ar.alloc([128, 4, NTOK], BF16, "kT")
    v_bf = ar.alloc([128, NT, 512], BF16, "vbf")
    qmT = ar.alloc([128, 2, NTOK], BF16, "qmT")
    B_hT = [Buf("hT%d" % t) for t in range(NT)]
    B_qT = [Buf("qT%d" % i) for i in range(len(TCH))]
    B_kT = [Buf("kT%d" % i) for i in range(len(TCH))]
    B_qmT = [Buf("qmT%d" % i) for i in range(len(TCH))]
    B_v = [Buf("v%d" % t) for t in range(NT)]
    B_oT = [Buf("oT%d" % i) for i in range(len(TCH))]
    B_poolT = [Buf("poolT%d" % i) for i in range(len(TCH))]
    B_omT = [Buf("omT%d" % i) for i in range(len(TCH))]
    wmark = ar.mark()

    def tiles_of(tc):
        o, n = TCH[tc]
        return list(range(o // 128, (o + n) // 128))

    def norm_transpose(src_rows, gT, dst, dst_cols, xin, Bx, xn, Bxn, ss, Bss, bank, Bbank, Bdst, junk, i):
        if src_rows is not None:
            kk.dma("sp", xin[:], src_rows, writes=[Bx])
        kk.act(lambda e: e.activation(out=junk[:], in_=xin[:], func=AF.Square, accum_out=ss[:, 0:1]), reads=[Bx], writes=[Bss, Bjunk])
        kk.act(lambda e: e.activation(out=ss[:, 1:2], in_=ss[:, 0:1], func=AF.Sqrt, scale=1.0 / D, bias=eps_t[:, 0:1]), reads=[Bss, Bc], writes=[Bss])
        kk.dve(lambda e: e.reciprocal(out=ss[:, 2:3], in_=ss[:, 1:2]), reads=[Bss], writes=[Bss])
        kk.dve(lambda e: e.tensor_scalar(out=xn[:], in0=xin[:], scalar1=ss[:, 2:3], scalar2=None, op0=ALU.mult), reads=[Bx, Bss], writes=[Bxn])
        pbf = bank[:].bitcast(BF16)
        for kc in range(8):
            kk.pe(lambda e, kc=kc: e.transpose(pbf[:, kc * 128:(kc + 1) * 128], xn[:, kc * 128:(kc + 1) * 128], ident_b[:]),
                  reads=[Bxn, Bc], writes=[Bbank])
        kk.dve(f_tt(dst[:, :, dst_cols], pbf[:, 0:1024].rearrange("p (k t) -> p k t", k=8),
                    gT[:, :].unsqueeze(2).to_broadcast([128, 8, 128]), ALU.mult),
               reads=[Bbank, Bc], writes=[Bdst])

    xins = [ar.alloc([128, D], F32, "xin%d" % i) for i in range(3)]
    Bxins = [Buf("xin%d" % i) for i in range(3)]
    xns = [ar.alloc([128, D], BF16, "xn%d" % i) for i in range(2)]
    Bxns = [Buf("xn%d" % i) for i in range(2)]
    sss = [ar.alloc([128, 4], F32, "ss%d" % i) for i in range(3)]
    Bsss = [Buf("ss%d" % i) for i in range(3)]
    junk = ar.alloc([128, D], BF16, "junk")
    Bjunk = Buf("junk")
    for t in range(NT):
        norm_transpose(I["x_all"][t * 128:(t + 1) * 128, :], g1T, hT, slice(t * 128, (t + 1) * 128),
                       xins[t % 3], Bxins[t % 3], xns[t % 2], Bxns[t % 2], sss[t % 3], Bsss[t % 3],
                       banks[t % 2], pb[t % 2], B_hT[t], junk, t)
    if "hT" in dbg_out:
        hdbg = ar.alloc([128, 8 * 128], F32, "hdbg")
        Bh = Buf("hdbg")
        kk.dve(lambda e: e.tensor_copy(out=hdbg[:].rearrange("p (k t) -> p k t", k=8), in_=hT[:, :, 2048:2176]), reads=B_hT, writes=[Bh])
        dbg("hT", hdbg[:], [Bh])
    kk.barrier()
    if STOP == "p1":
        return
    ar.reset(wmark)

    wps = [ar.alloc([128, 8, 512], BF16, "wp%d" % i) for i in range(2)]
    Bwps = [Buf("wp%d" % i) for i in range(2)]
    stg = [ar.alloc([128, 512], F32, "stg%d" % i) for i in range(3)]
    Bstg = [Buf("stg%d" % i) for i in range(3)]
    nstg = [0]
    Eb = ar.alloc([128, 15 + 2048], F32, "Eb")
    Es = ar.alloc([128, NSEQ, 23], F32, "Es")
    W1 = ar.alloc([128, 15 + 2048], F32, "W1")
    W1s = ar.alloc([128, NSEQ, 23], F32, "W1s")
    W2 = ar.alloc([128, 15 + 2048], F32, "W2")
    W2s = ar.alloc([128, NSEQ, 23], F32, "W2s")
    dTb = ar.alloc([128, NTOK], BF16, "dTb")
    tmp16 = ar.alloc([128, 16], F32, "tmp16")
    bdw = ar.alloc([128, 128], BF16, "bdw")
    stp = ar.alloc([120, 2, 256], F32, "stp")
    BE, BW1, BW2, BdT, Bbdw, Bstp, Bt16 = Buf("E"), Buf("W1"), Buf("W2"), Buf("dT"), Buf("bdw"), Buf("stp"), Buf("t16")

    def load_wpiece(i, c0):
        kk.dma("pool", wps[i][:], I["w_in"][:, c0:c0 + 512].rearrange("(k p) c -> p k c", p=128), writes=[Bwps[i]])

    def fm_group(wp, Bwp, col0, tc, bank, Bbank):
        o, n = TCH[tc]
        for kc in range(8):
            kk.pe(mm(bank[:, 0:n], wp[:, kc, col0:col0 + 128], hT[:, kc, o:o + n], kc == 0, kc == 7),
                  reads=[Bwp] + [B_hT[t] for t in tiles_of(tc)], writes=[Bbank])

    def tm_group(wp, Bwp, t, bank, Bbank, ncols=512, c0=0):
        for kc in range(8):
            kk.pe(mm(bank[:, 0:ncols], hT[:, kc, t * 128:(t + 1) * 128], wp[:, kc, c0:c0 + ncols], kc == 0, kc == 7),
                  reads=[Bwp, B_hT[t]], writes=[Bbank])

    nb = [0]

    def next_bank():
        b = nb[0] % 4
        nb[0] += 1
        return banks[b], pb[b]

    ev = [0]

    def evac_copy(out_ap, in_ap, reads, writes):
        ev[0] += 1
        if ev[0] % 2 == 0:
            kk.dve(lambda e: e.tensor_copy(out=out_ap, in_=in_ap), reads=reads, writes=writes)
        else:
            kk.act(lambda e: e.activation(out=out_ap, in_=in_ap, func=AF.Copy), reads=reads, writes=writes)

    load_wpiece(0, 0)
    load_wpiece(1, 512)
    for h in range(4):
        for tc in range(len(TCH)):
            o, n = TCH[tc]
            bk, Bb = next_bank()
            fm_group(wps[0], Bwps[0], h * 128, tc, bk, Bb)
            evac_copy(qT[:, h, o:o + n], bk[:, 0:n], [Bb], [B_qT[tc]])
    if STOP == "p2q":
        kk.barrier()
        return
    for h in range(4):
        for tc in range(len(TCH)):
            o, n = TCH[tc]
            bk, Bb = next_bank()
            fm_group(wps[1], Bwps[1], h * 128, tc, bk, Bb)
            evac_copy(kT[:, h, o:o + n], bk[:, 0:n], [Bb], [B_kT[tc]])
    if STOP == "p2k":
        kk.barrier()
        return
    load_wpiece(0, 1024)
    for t in range(NT):
        bk, Bb = next_bank()
        tm_group(wps[1], Bwps[1], t, bk, Bb)
        si = nstg[0] % 3
        nstg[0] += 1
        evac_copy(stg[si][:], bk[:, :], [Bb], [Bstg[si]])
        kk.dma("sp", O["newk"][t * 128:(t + 1) * 128, :], stg[si][:], reads=[Bstg[si]])
    if STOP == "p2kt":
        kk.barrier()
        return
    load_wpiece(1, 1536)
    for t in range(NT):
        bk, Bb = next_bank()
        tm_group(wps[0], Bwps[0], t, bk, Bb)
        si = nstg[0] % 3
        nstg[0] += 1
        kk.act(lambda e, si=si, bk=bk: e.activation(out=stg[si][:], in_=bk[:, :], func=AF.Copy), reads=[Bb], writes=[Bstg[si]])
        kk.dve(f_copy(v_bf[:, t, :], stg[si][:]), reads=[Bstg[si]], writes=[B_v[t]])
        kk.dma("sp", O["newv"][t * 128:(t + 1) * 128, :], stg[si][:], reads=[Bstg[si]])
    for hp in range(2):
        for tc in range(len(TCH)):
            o, n = TCH[tc]
            bk, Bb = next_bank()
            fm_group(wps[1], Bwps[1], 256 + hp * 128, tc, bk, Bb)
            evac_copy(qmT[:, hp, o:o + n], bk[:, 0:n], [Bb], [B_qmT[tc]])
    if STOP == "p2a":
        kk.barrier()
        return
    for t in (15, 16):
        bk, Bb = next_bank()
        tm_group(wps[1], Bwps[1], t, bk, Bb, ncols=256, c0=0)
        si = nstg[0] % 3
        nstg[0] += 1
        evac_copy(stg[si][:, 0:256], bk[:, 0:256], [Bb], [Bstg[si]])
        if t == 15:
            kk.dma("sp", O["pool_p"][:, :], stg[si][113:128, 0:256], reads=[Bstg[si]])
        else:
            for s_ in range(NSEQ):
                kk.dma("sp", O["pool_s"][s_, 7:15, :], stg[si][s_ * 8:(s_ + 1) * 8, 0:256], reads=[Bstg[si]])
    kk.dma("sp", O["pool_s"][:, 0:7, :], I["state_pool"].rearrange("(s r) c -> s r c", r=15)[:, 8:15, :])

    if STOP == "p2b":
        kk.barrier()
        return
    kk.dve(lambda e: e.memset(Eb[:, 0:15], 0.0), writes=[BE])
    kk.dve(lambda e: e.memset(bdw[:], 0.0), writes=[Bbdw])
    kk.dma("sp", stp[:, 0, :], I["state_pool"][0:120, :], writes=[Bstp])
    kk.dma("sp", stp[:, 1, :], I["state_pool"][120:240, :], writes=[Bstp])
    for ch in range(2):
        for tc in range(len(TCH)):
            o, n = TCH[tc]
            bk, Bb = next_bank()
            fm_group(wps[1], Bwps[1], ch * 128, tc, bk, Bb)
            if tc < 4:
                evac_copy(Eb[:, 15 + o:15 + o + n], bk[:, 0:n], [Bb], [BE])
            else:
                evac_copy(Es[:, :, 15:23], bk[:, 0:128].rearrange("p (s i) -> p s i", i=8), [Bb], [BE])
        for j in range(2):
            bk, Bb = next_bank()
            kk.pe(mm(bk[:, 0:120], stp[:, j, ch * 128:(ch + 1) * 128], ident_f[0:120, 0:120], True, True), reads=[Bstp, Bc], writes=[Bb])
            evac_copy(Es[:, j * 8:(j + 1) * 8, 0:15], bk[:, 0:120].rearrange("p (s r) -> p s r", r=15), [Bb], [BE])
        def dbl(dst, dsts, src, srcs, sh, first):
            lo = 2 * sh - 1
            kk.dve(lambda e: e.tensor_tensor(out=dst[:, lo:], in0=src[:, lo:], in1=src[:, lo - sh:15 + 2048 - sh], op=ALU.add),
                   reads=[first], writes=[BW1 if dst is W1 else BW2])
            kk.dve(lambda e: e.tensor_tensor(out=dsts[:, :, lo:], in0=srcs[:, :, lo:], in1=srcs[:, :, lo - sh:23 - sh], op=ALU.add),
                   reads=[first], writes=[BW1 if dst is W1 else BW2])
        dbl(W1, W1s, Eb, Es, 1, BE)
        dbl(W2, W2s, W1, W1s, 2, BW1)
        if ch == 1:
            dbl(W1, W1s, W2, W2s, 4, BW2)
            dbl(W2, W2s, W1, W1s, 8, BW1)
        for half, (Wb, Wbs, BWb) in enumerate(((W1, W1s, BW1), (W2, W2s, BW2))):
            ps = slice(half * 64, (half + 1) * 64)
            kk.dve(f_stt(dTb[ps, 0:2048], Wb[ps, 15:15 + 2048], poolc[ps, ch, 15:16], Eb[ps, 15:15 + 2048], ALU.mult, ALU.subtract),
                   reads=[BWb, BE, Bc], writes=[BdT])
            kk.dve(f_tt(tmp16[ps, :], Wb[ps, 15:31], poolc[ps, ch, :], ALU.mult), reads=[BWb, Bc], writes=[Bt16])
            kk.dve(f_tt(dTb[ps, 0:16], tmp16[ps, :], Eb[ps, 15:31], ALU.subtract), reads=[Bt16, BE, BdT], writes=[BdT])
            kk.dve(f_stt(dTb[ps, 2048:2176].rearrange("p (s i) -> p s i", i=8), Wbs[ps, :, 15:23], poolc[ps, ch, 15:16],
                         Es[ps, :, 15:23], ALU.mult, ALU.subtract),
                   reads=[BWb, BE, Bc, BdT], writes=[BdT])
        for half in range(2):
            ps = slice(half * 64, (half + 1) * 64)
            kk.dma("pool", bdw[ps, half * 64:(half + 1) * 64], I["w_pool_grp"][2 * ch + half, :, :], reads=[Bbdw], writes=[Bbdw])
        for tc in range(len(TCH)):
            o, n = TCH[tc]
            bk, Bb = next_bank()
            kk.pe(mm(bk[:, 0:n], bdw[:, :], dTb[:, o:o + n], True, True), reads=[Bbdw, BdT], writes=[Bb])
            kk.dve(f_ts(poolT[:, ch, o:o + n], bk[:, 0:n], pscaleT[:, ch:ch + 1], None, ALU.mult), reads=[Bb, Bc], writes=[B_poolT[tc]])
    if "poolT" in dbg_out:
        pdbg = ar.alloc([128, 2 * 256], F32, "pdbg")
        Bp = Buf("pdbg")
        kk.dve(lambda e: e.tensor_copy(out=pdbg[:, 0:128], in_=poolT[:, 0, 0:128]), reads=B_poolT, writes=[Bp])
        kk.dve(lambda e: e.tensor_copy(out=pdbg[:, 128:256], in_=poolT[:, 1, 0:128]), reads=B_poolT, writes=[Bp])
        kk.dve(lambda e: e.tensor_copy(out=pdbg[:, 256:384], in_=poolT[:, 0, 2048:2176]), reads=B_poolT, writes=[Bp])
        kk.dve(lambda e: e.tensor_copy(out=pdbg[:, 384:512], in_=poolT[:, 1, 2048:2176]), reads=B_poolT, writes=[Bp])
        dbg("poolT", pdbg[:], [Bp])
    kk.barrier()
    if STOP == "p2":
        return
    ar.reset(wmark)


    P3 = ALL or "p3" in phases
    xin0 = ar.alloc([128, D], F32, "mxin0")
    xin1 = ar.alloc([128, D], F32, "mxin1")
    mxn = ar.alloc([128, D], BF16, "mxn")
    mjunk = ar.alloc([128, D], BF16, "mjunk")
    mss = [ar.alloc([128, 4], F32, "mss%d" % i) for i in range(2)]
    mhT = ar.alloc([128, 8, 256], BF16, "mhT")
    wmkv = ar.alloc([128, 8, 512], BF16, "wmkv")
    memkT = ar.alloc([128, 2, 256], BF16, "memkT")
    memv_pad = ar.alloc([128, 2, 4, 128], BF16, "memvpad")
    onesE = ar.alloc([128, 128], BF16, "onesE")
    onesO = ar.alloc([128, 128], BF16, "onesO")
    mstg = [ar.alloc([128, 512], F32, "mstg%d" % i) for i in range(2)]
    Bmx = [Buf("mx0"), Buf("mx1")]
    Bmxn, Bmhs, Bwmkv, BmkT, Bmvp, Bones2 = Buf("mxn"), [Buf("mh0"), Buf("mh1")], Buf("wmkv"), Buf("memkT"), Buf("memvpad"), Buf("ones2")
    Bmss = [Buf("mss0"), Buf("mss1")]
    Bmstg = [Buf("mstg0"), Buf("mstg1")]
    kk.dma("pool", wmkv[:], I["w_mem_kv"].rearrange("(k p) c -> p k c", p=128), writes=[Bwmkv])
    kk.dve(f_memset(memv_pad[:], 0.0), writes=[Bmvp])
    kk.dve(f_memset(onesE[:], 0.0), writes=[Bones2])
    kk.dve(f_memset(onesO[:], 0.0), writes=[Bones2])
    kk.dve(f_memset(onesE[:, 0:64], 1.0), writes=[Bones2])
    kk.dve(f_memset(onesO[:, 64:128], 1.0), writes=[Bones2])
    for mt in range(2):
        norm_transpose(I["mem"][mt * 128:(mt + 1) * 128, :], gmT, mhT, slice(mt * 128, (mt + 1) * 128),
                       (xin0, xin1)[mt], Bmx[mt], mxn, Bmxn, mss[mt], Bmss[mt], banks[mt], pb[mt], Bmhs[mt], mjunk, mt)
    for mt in range(2):
        bk, Bb = banks[2 + mt], pb[2 + mt]
        for kc in range(8):
            kk.pe(mm(bk[:, :], mhT[:, kc, mt * 128:(mt + 1) * 128], wmkv[:, kc, :], kc == 0, kc == 7), reads=[Bmhs[mt], Bwmkv], writes=[Bb])
        kk.act(f_act(mstg[mt][:], bk[:, :], AF.Copy), reads=[Bb], writes=[Bmstg[mt]])
        for h in range(4):
            kk.dve(f_copy(memv_pad[:, mt, h, (h % 2) * 64:(h % 2) * 64 + 64], mstg[mt][:, 256 + h * 64:256 + (h + 1) * 64]), reads=[Bmstg[mt]], writes=[Bmvp])
        kk.dma("sp", O["memk"][mt * 128:(mt + 1) * 128, :], mstg[mt][:, 0:256], reads=[Bmstg[mt]])
        kk.dma("sp", O["memv"][mt * 128:(mt + 1) * 128, :], mstg[mt][:, 256:512], reads=[Bmstg[mt]])
    for hp in range(2):
        bk, Bb = banks[4 + hp], pb[4 + hp]
        for kc in range(8):
            kk.pe(mm(bk[:, 0:256], wmkv[:, kc, hp * 128:(hp + 1) * 128], mhT[:, kc, :], kc == 0, kc == 7), reads=Bmhs + [Bwmkv], writes=[Bb])
        kk.dve(f_copy(memkT[:, hp, :], bk[:, 0:256]), reads=[Bb], writes=[BmkT])

    mpT = [ar.alloc([128, 2, 512], BF16, "mpT%d" % i) for i in range(2)]
    BmpT = [Buf("mpT0"), Buf("mpT1")]
    mrs = [ar.alloc([128, 512], F32, "mrs%d" % i) for i in range(2)]
    Bmrs = [Buf("mrs0"), Buf("mrs1")]
    it = 0
    for tc in range(4):
        o, n = TCH[tc]
        for hp in range(2):
            oc, Boc = banks[4 + 2 * (it % 2)], pb[4 + 2 * (it % 2)]
            oz, Boz = banks[5 + 2 * (it % 2)], pb[5 + 2 * (it % 2)]
            for mt in range(2):
                sa, Bsa = banks[2 * mt], pb[2 * mt]
                sb, Bsb = banks[2 * mt + 1], pb[2 * mt + 1]
                p_, Bp_ = mpT[mt], BmpT[mt]
                kk.pe(mm(sa[:, :], memkT[0:64, hp, mt * 128:(mt + 1) * 128], qmT[0:64, hp, o:o + n], True, True), reads=[BmkT, B_qmT[tc]], writes=[Bsa])
                kk.pe(mm(sb[:, :], memkT[64:128, hp, mt * 128:(mt + 1) * 128], qmT[64:128, hp, o:o + n], True, True), reads=[BmkT, B_qmT[tc]], writes=[Bsb])
                kk.act(f_act(p_[:, 0, :], sa[:, :], AF.Exp, scale=0.125), reads=[Bsa], writes=[Bp_])
                kk.act(f_act(p_[:, 1, :], sb[:, :], AF.Exp, scale=0.125), reads=[Bsb], writes=[Bp_])
                kk.pe(mm(oc[:, :], memv_pad[:, mt, 2 * hp, :], p_[:, 0, :], mt == 0, False), reads=[Bmvp, Bp_], writes=[Boc])
                kk.pe(mm(oc[:, :], memv_pad[:, mt, 2 * hp + 1, :], p_[:, 1, :], False, mt == 1), reads=[Bmvp, Bp_], writes=[Boc])
                kk.pe(mm(oz[:, :], onesE[:, :], p_[:, 0, :], mt == 0, False), reads=[Bones2, Bp_], writes=[Boz])
                kk.pe(mm(oz[:, :], onesO[:, :], p_[:, 1, :], False, mt == 1), reads=[Bones2, Bp_], writes=[Boz])
            r_, Br_ = mrs[it % 2], Bmrs[it % 2]
            kk.act(f_act(r_[:], oz[:, :], AF.Ln), reads=[Boz], writes=[Br_])
            kk.act(f_act(r_[:], r_[:], AF.Exp, scale=-1.0), reads=[Br_], writes=[Br_])
            kk.dve(f_tt(omT[:, hp, o:o + n], oc[:, :], r_[:], ALU.mult), reads=[Boc, Br_], writes=[B_omT[tc]])
            it += 1
    cmk = [ar.alloc([128, 2, 256], BF16, "cmk%d" % i) for i in range(2)]
    Bcmk = [Buf("cmk0"), Buf("cmk1")]
    cmvp = [ar.alloc([128, 2, 4, 128], BF16, "cmvp%d" % i) for i in range(2)]
    Bcmvp = [Buf("cmvp0"), Buf("cmvp1")]
    kTs = [ar.alloc([128, 2, 256], BF16, "kTs%d" % i) for i in range(2)]
    BkTs = [Buf("kTs0"), Buf("kTs1")]
    pTs = [ar.alloc([128, 2, 32], BF16, "pTs%d" % i) for i in range(2)]
    BpTs = [Buf("pTs0"), Buf("pTs1")]
    for i in range(2):
        kk.dve(f_memset(cmvp[i][:], 0.0), writes=[Bcmvp[i]])
    ocs, Bocs = banks[6], pb[6]
    ozs, Bozs = banks[7], pb[7]
    for s_ in range(NSEQ):
        b = s_ % 2
        kk.dma("pool", cmk[b][:], I["cmem_k"][s_].rearrange("(t p) c -> p t c", p=128), writes=[Bcmk[b]])
        for h in range(4):
            kk.dma("pool", cmvp[b][:, :, h, (h % 2) * 64:(h % 2) * 64 + 64],
                   I["cmem_v"][s_][:, h * 64:(h + 1) * 64].rearrange("(t p) c -> p t c", p=128), reads=[Bcmvp[b]], writes=[Bcmvp[b]])
        tb, Btb = banks[b], pb[b]
        tbf = tb[:].bitcast(BF16)
        for hp in range(2):
            for mt in range(2):
                kk.pe(f_tr(tbf[:, (hp * 2 + mt) * 128:(hp * 2 + mt + 1) * 128], cmk[b][:, mt, hp * 128:(hp + 1) * 128], ident_b[:]),
                      reads=[Bcmk[b], Bc], writes=[Btb])
        kk.dve(f_copy(kTs[b][:].rearrange("p h m -> p (h m)"), tbf[:, 0:512]), reads=[Btb], writes=[BkTs[b]])
        sa, Bsa = banks[2 + 2 * b], pb[2 + 2 * b]
        sb, Bsb = banks[3 + 2 * b], pb[3 + 2 * b]
        qs = slice(2048 + 8 * s_, 2048 + 8 * s_ + 8)
        for hp in range(2):
            for mt in range(2):
                c0 = (hp * 2 + mt) * 8
                kk.pe(mm(sa[:, c0:c0 + 8], kTs[b][0:64, hp, mt * 128:(mt + 1) * 128], qmT[0:64, hp, qs], True, True), reads=[BkTs[b], B_qmT[4]], writes=[Bsa])
                kk.pe(mm(sb[:, c0:c0 + 8], kTs[b][64:128, hp, mt * 128:(mt + 1) * 128], qmT[64:128, hp, qs], True, True), reads=[BkTs[b], B_qmT[4]], writes=[Bsb])
        kk.act(f_act(pTs[b][:, 0, :], sa[:, 0:32], AF.Exp, scale=0.125), reads=[Bsa], writes=[BpTs[b]])
        kk.act(f_act(pTs[b][:, 1, :], sb[:, 0:32], AF.Exp, scale=0.125), reads=[Bsb], writes=[BpTs[b]])
        for hp in range(2):
            oc0 = (s_ * 2 + hp) * 8
            for mt in range(2):
                c0 = (hp * 2 + mt) * 8
                kk.pe(mm(ocs[:, oc0:oc0 + 8], cmvp[b][:, mt, 2 * hp, :], pTs[b][:, 0, c0:c0 + 8], mt == 0, False), reads=[Bcmvp[b], BpTs[b]], writes=[Bocs])
                kk.pe(mm(ocs[:, oc0:oc0 + 8], cmvp[b][:, mt, 2 * hp + 1, :], pTs[b][:, 1, c0:c0 + 8], False, mt == 1), reads=[Bcmvp[b], BpTs[b]], writes=[Bocs])
            for mt in range(2):
                c0 = (hp * 2 + mt) * 8
                kk.pe(mm(ozs[:, oc0:oc0 + 8], onesE[:, :], pTs[b][:, 0, c0:c0 + 8], mt == 0, False), reads=[Bones2, BpTs[b]], writes=[Bozs])
                kk.pe(mm(ozs[:, oc0:oc0 + 8], onesO[:, :], pTs[b][:, 1, c0:c0 + 8], False, mt == 1), reads=[Bones2, BpTs[b]], writes=[Bozs])
    kk.act(f_act(mrs[0][:, 0:256], ozs[:, 0:256], AF.Ln), reads=[Bozs], writes=[Bmrs[0]])
    kk.act(f_act(mrs[0][:, 0:256], mrs[0][:, 0:256], AF.Exp, scale=-1.0), reads=[Bmrs[0]], writes=[Bmrs[0]])
    kk.dve(f_tt(omT[:, :, 2048:2176].rearrange("p h (s q) -> p h s q", q=8),
                ocs[:, 0:256].rearrange("p (s h q) -> p h s q", h=2, q=8),
                mrs[0][:, 0:256].rearrange("p (s h q) -> p h s q", h=2, q=8), ALU.mult),
           reads=[Bocs, Bmrs[0]], writes=[B_omT[4]])
    if "omT" in dbg_out:
        odbg = ar.alloc([128, 512], F32, "odbg")
        Bo = Buf("odbg")
        kk.dve(f_copy(odbg[:, 0:128], omT[:, 0, 0:128]), reads=B_omT, writes=[Bo])
        kk.dve(f_copy(odbg[:, 128:256], omT[:, 1, 1920:2048]), reads=B_omT, writes=[Bo])
        kk.dve(f_copy(odbg[:, 256:384], omT[:, 0, 2048:2176]), reads=B_omT, writes=[Bo])
        kk.dve(f_copy(odbg[:, 384:512], omT[:, 1, 2048:2176]), reads=B_omT, writes=[Bo])
        dbg("omT", odbg[:], [Bo])
    kk.barrier()
    if STOP == "p3b":
        return
    ar.reset(wmark)


    apT = [ar.alloc([128, 2, 512], BF16, "apT%d" % i) for i in range(3)]
    BapT = [Buf("apT%d" % i) for i in range(3)]
    tA = ar.alloc([128, 512], F32, "tA")
    tB = ar.alloc([128, 512], F32, "tB")
    tC = ar.alloc([128, 512], F32, "tC")
    tD = ar.alloc([128, 512], F32, "tD")
    tE = ar.alloc([128, 512], F32, "tE")
    BtA, BtB, BtC, BtD, BtE = Buf("tA"), Buf("tB"), Buf("tC"), Buf("tD"), Buf("tE")
    Sset = [((banks[0], pb[0]), (banks[1], pb[1])), ((banks[6], pb[6]), (banks[7], pb[7]))]
    O0, O1, Z0, Z1 = banks[2], banks[3], banks[4], banks[5]
    BO0, BO1, BZ0, BZ1 = pb[2], pb[3], pb[4], pb[5]
    units = []
    for h in range(4):
        for c in range(4):
            for j in range(4 * c + 4):
                units.append((h, c, j))

    def u_lo(c, j):
        return max(j - 4 * c, 0) * 128

    def emit_qk(i):
        h, c, j = units[i]
        (S0, BS0), (S1, BS1) = Sset[i % 2]
        lo = u_lo(c, j)
        q0 = c * 512
        ks = slice(j * 128, (j + 1) * 128)
        kk.pe(mm(S0[:, lo:512], kT[0:64, h, ks], qT[0:64, h, q0 + lo:q0 + 512], True, True), reads=[B_kT[j // 4], B_qT[c]], writes=[BS0])
        kk.pe(mm(S1[:, lo:512], kT[64:128, h, ks], qT[64:128, h, q0 + lo:q0 + 512], True, True), reads=[B_kT[j // 4], B_qT[c]], writes=[BS1])

    def emit_softmax(i):
        h, c, j = units[i]
        (S0, BS0), (S1, BS1) = Sset[i % 2]
        lo = u_lo(c, j)
        jj = j - 4 * c
        pT_, BpT_ = apT[i % 3], BapT[i % 3]
        kk.act(f_act(pT_[:, 0, lo:512], S0[:, lo:512], AF.Exp, scale=0.125), reads=[BS0], writes=[BpT_])
        kk.act(f_act(pT_[:, 1, lo:512], S1[:, lo:512], AF.Exp, scale=0.125), reads=[BS1], writes=[BpT_])
        for m in range(2):
            if jj >= 0:
                kk.dve(f_tt(pT_[:, m, lo:lo + 128], pT_[:, m, lo:lo + 128], T0[:, h, :], ALU.mult), reads=[BpT_, Bt], writes=[BpT_])
                if jj < 3:
                    kk.dve(f_tt(pT_[:, m, lo + 128:lo + 256], pT_[:, m, lo + 128:lo + 256], T1[:, h, :], ALU.mult), reads=[BpT_, Bt], writes=[BpT_])
            elif jj == -1:
                kk.dve(f_tt(pT_[:, m, 0:128], pT_[:, m, 0:128], T1[:, h, :], ALU.mult), reads=[BpT_, Bt], writes=[BpT_])

    def emit_pv(i):
        h, c, j = units[i]
        lo = u_lo(c, j)
        nj = 4 * c + 4
        pT_, BpT_ = apT[i % 3], BapT[i % 3]
        vv = v_bf[:, j, h * 128:(h + 1) * 128]
        for m, (Ob, BOb, Zb, BZb) in enumerate(((O0, BO0, Z0, BZ0), (O1, BO1, Z1, BZ1))):
            kk.pe(mm(Ob[:, lo:512], vv, pT_[:, m, lo:512], j == 0, j == nj - 1), reads=[B_v[j], BpT_], writes=[BOb])
            kk.pe(mm(Zb[:, lo:512], ones_b[:, :], pT_[:, m, lo:512], j == 0, j == nj - 1), reads=[Bc, BpT_], writes=[BZb])

    def emit_tail(h, c, SSb, BSS):
        q0 = c * 512
        kk.act(f_act(tA[:], Z0[:, :], AF.Ln), reads=[BZ0], writes=[BtA])
        kk.act(f_act(tB[:], Z1[:, :], AF.Ln), reads=[BZ1], writes=[BtB])
        kk.act(f_act(tA[:], tA[:], AF.Exp, scale=-1.0), reads=[BtA], writes=[BtA])
        kk.act(f_act(tB[:], tB[:], AF.Exp, scale=-1.0), reads=[BtB], writes=[BtB])
        kk.dve(f_tt(tA[:], O0[:, :], tA[:], ALU.mult), reads=[BO0, BtA], writes=[BtA])
        kk.dve(f_tt(tB[:], O1[:, :], tB[:], ALU.mult), reads=[BO1, BtB], writes=[BtB])
        kk.dve(f_stt(tC[:], tB[:], neg_lam[:, 0:1], tA[:], ALU.mult, ALU.add), reads=[BtA, BtB, Bl], writes=[BtC])
        kk.act(f_act(tD[:], tC[:], AF.Square), reads=[BtC], writes=[BtD])
        kk.pe(mm(SSb[:, :], ones_f[:, :], tD[:], True, True), reads=[Bc, BtD], writes=[BSS])
        kk.act(f_act(tE[:], SSb[:, :], AF.Ln, scale=1.0 / 128, bias=eps_t[:, 0:1]), reads=[BSS, Bc], writes=[BtE])
        kk.act(f_act(tE[:], tE[:], AF.Exp, scale=-0.5), reads=[BtE], writes=[BtE])
        kk.dve(f_stt(oT[:, h, q0:q0 + 512], tC[:], sublnT[:, 0:1], tE[:], ALU.mult, ALU.mult), reads=[BtC, BtE, Bc], writes=[B_oT[c]])

    emit_qk(0)
    for i, (h, c, j) in enumerate(units):
        emit_softmax(i)
        if i + 1 < len(units):
            emit_qk(i + 1)
        emit_pv(i)
        if j == 4 * c + 3:
            (SSb, BSS), _ = Sset[i % 2]
            emit_tail(h, c, SSb, BSS)
    if "oTp" in dbg_out:
        odbg2 = ar.alloc([128, 512], F32, "odbg2")
        Bo2 = Buf("odbg2")
        kk.dve(f_copy(odbg2[:, 0:128], oT[:, 0, 0:128]), reads=B_oT, writes=[Bo2])
        kk.dve(f_copy(odbg2[:, 128:256], oT[:, 1, 640:768]), reads=B_oT, writes=[Bo2])
        kk.dve(f_copy(odbg2[:, 256:384], oT[:, 2, 1920:2048]), reads=B_oT, writes=[Bo2])
        kk.dve(f_copy(odbg2[:, 384:512], oT[:, 3, 1024:1152]), reads=B_oT, writes=[Bo2])
        dbg("oTp", odbg2[:], [Bo2])
    kk.barrier()
    if STOP == "p3c":
        return
    ar.reset(wmark)


    ptb = ar.alloc([128, NSEQ * NPAGE], I32, "ptb")
    idx = ar.alloc([128, NSEQ * NPAGE], I32, "idx")
    iotaf = ar.alloc([128, 1], F32, "iotaf")
    qpad = ar.alloc([128, 4, NSEQ, 16], BF16, "qpad")
    M15 = ar.alloc([128, 4, 2, 8], F32, "M15")
    MN = ar.alloc([128, NSEQ, 4, 2, 8], F32, "MN")
    gbc = ar.alloc([128, 128], F32, "gbc")
    NV = 12
    kvpg = [ar.alloc([128, 1024], BF16, "kvpg%d" % i) for i in range(NV)]
    Bkvpg = [Buf("kvpg%d" % i) for i in range(NV)]
    KTs = [ar.alloc([128, 4, 128], BF16, "KTs%d" % i) for i in range(2)]
    BKTs = [Buf("KTs0"), Buf("KTs1")]
    spT = [ar.alloc([128, 8, 64], BF16, "spT%d" % i) for i in range(2)]
    BspT = [Buf("spT0"), Buf("spT1")]
    pn = ar.alloc([128, 64], BF16, "pn")
    Bpn = Buf("pn")
    rz = ar.alloc([64, 1], F32, "rz")
    onr = ar.alloc([64, 512], F32, "onr")
    c2 = ar.alloc([32, 4, 128], F32, "c2")
    sq2 = ar.alloc([32, 4, 128], F32, "sq2")
    ss2 = ar.alloc([32, 8], F32, "ss2")
    on3 = ar.alloc([32, 4, 128], BF16, "on3")
    Brz, Bonr, Bc2, Bsq2, Bss2, Bon3 = Buf("rz"), Buf("onr"), Buf("c2"), Buf("sq2"), Buf("ss2"), Buf("on3")
    Bsetup = Buf("p3dsetup")
    kk.dma("sp", ptb[:], I["page_table"][0, :].partition_broadcast(128), writes=[Bsetup])
    kk.dma("sp", iotaf[:], I["iota_f"][:, :], writes=[Bsetup])
    kk.dve(f_ts(idx[:], ptb[:], 128.0, iotaf[:, 0:1], ALU.mult, ALU.add), reads=[Bsetup], writes=[Bsetup])
    kk.dve(f_memset(qpad[:], 0.0), writes=[Bsetup])
    kk.dve(f_copy(qpad[0:64, :, :, 0:8], qT[0:64, :, 2048:2176].rearrange("p h (s q) -> p h s q", q=8)), reads=[B_qT[4], Bsetup], writes=[Bsetup])
    kk.dve(f_copy(qpad[64:128, :, :, 8:16], qT[64:128, :, 2048:2176].rearrange("p h (s q) -> p h s q", q=8)), reads=[B_qT[4], Bsetup], writes=[Bsetup])
    for c in range(2):
        kk.dve(f_copy(M15[:, :, c, :], T1[:, :, 0:8]), reads=[Bt], writes=[Bsetup])
    for h in range(4):
        for c in range(2):
            kk.dve(f_tt(MN[:, :, h, c, :], T0[:, h, :].rearrange("p (s q) -> p s q", q=8), bdiag[:].unsqueeze(2).to_broadcast([128, NSEQ, 8]), ALU.mult),
                   reads=[Bt, Bc], writes=[Bsetup])
    kk.dma("sp", gbc[:], I["subln_g"].partition_broadcast(128), writes=[Bsetup])
    kk.dve(f_ts(gbc[:], gbc[:], 1.0 - LAM_INIT, None, ALU.mult), reads=[Bsetup], writes=[Bsetup])
    OS, BOS = banks[2], pb[2]
    ZS, BZS = banks[3], pb[3]
    C2b, BC2 = banks[4], pb[4]
    TTb, BTT = banks[5], pb[5]
    nk = nv = 0
    npg = 0
    for s_ in range(NSEQ):
        vq = []
        for half in range(2):
            Sb, BSb = banks[6 + half], pb[6 + half]
            sp_, Bsp_ = spT[half], BspT[half]
            for jj in range(8):
                j = half * 8 + jj
                col = s_ * NPAGE + j
                kvb, Bkvb = kvpg[nv % NV], Bkvpg[nv % NV]
                nv += 1
                kb, Bkb = kvb[:, 0:512], Bkvb
                vb, Bvb = kvb[:, 512:1024], Bkvb
                kk.op("pool", (lambda e, kvb=kvb, col=col: e.indirect_dma_start(
                    out=kvb[:, :], out_offset=None, in_=I["cache_kv"][:, :],
                    in_offset=bass.IndirectOffsetOnAxis(ap=idx[:, col:col + 1], axis=0))), reads=[Bsetup], writes=[Bkvb], dma=True)
                vq.append((vb, Bvb))
                tb, Btb = banks[npg % 2], pb[npg % 2]
                kt_, Bkt_ = KTs[npg % 2], BKTs[npg % 2]
                npg += 1
                tbf = tb[:].bitcast(BF16)
                for h in range(4):
                    kk.pe(f_tr(tbf[:, h * 128:(h + 1) * 128], kvb[:, h * 128:(h + 1) * 128], ident_b[:]), reads=[Bkb, Bc], writes=[Btb])
                if npg % 2 == 0:
                    kk.dve(f_copy(kt_[:].rearrange("p h k -> p (h k)"), tbf[:, 0:512]), reads=[Btb], writes=[Bkt_])
                else:
                    kk.act(f_act(kt_[:].rearrange("p h k -> p (h k)"), tbf[:, 0:512], AF.Copy), reads=[Btb], writes=[Bkt_])
                for h in range(4):
                    c0 = jj * 64 + h * 16
                    kk.pe(mm(Sb[:, c0:c0 + 16], kt_[:, h, :], qpad[:, h, s_, :], True, True), reads=[Bkt_, Bsetup], writes=[BSb])
            kk.act(f_act(sp_[:].rearrange("p j c -> p (j c)"), Sb[:, :], AF.Exp, scale=0.125), reads=[BSb], writes=[Bsp_])
            if half == 1:
                kk.dve(f_tt(sp_[:, 7, :], sp_[:, 7, :], M15[:].rearrange("p h c q -> p (h c q)"), ALU.mult), reads=[Bsp_, Bsetup], writes=[Bsp_])
            for jj in range(8):
                j = half * 8 + jj
                vb, Bvb = vq[j]
                kk.pe(mm(OS[0:64, :], sp_[:, jj, :], vb, j == 0, False), reads=[Bsp_, Bvb], writes=[BOS])
                kk.pe(mm(ZS[0:64, 0:1], sp_[:, jj, :], ones_b[:, 0:1], j == 0, False), reads=[Bsp_, Bc], writes=[BZS])
        Sb, BSb = banks[6], pb[6]
        for h in range(4):
            kk.pe(mm(Sb[:, h * 16:(h + 1) * 16], kT[:, h, 2048:2176], qpad[:, h, s_, :], True, True), reads=[B_kT[4], Bsetup], writes=[BSb])
        kk.act(f_act(pn[:], Sb[:, 0:64], AF.Exp, scale=0.125), reads=[BSb], writes=[Bpn])
        kk.dve(f_tt(pn[:], pn[:], MN[:, s_].rearrange("p h c q -> p (h c q)"), ALU.mult), reads=[Bpn, Bsetup], writes=[Bpn])
        kk.pe(mm(OS[0:64, :], pn[:, :], v_bf[:, 16, :], False, True), reads=[Bpn, B_v[16]], writes=[BOS])
        kk.pe(mm(ZS[0:64, 0:1], pn[:, :], ones_b[:, 0:1], False, True), reads=[Bpn, Bc], writes=[BZS])
        kk.dve(f_recip(rz[:], ZS[0:64, 0:1]), reads=[BZS], writes=[Brz])
        kk.dve(f_ts(onr[:], OS[0:64, :], rz[:, 0:1], None, ALU.mult), reads=[BOS, Brz], writes=[Bonr])
        kk.pe(mm(C2b[0:32, :], comb[:, :], onr[:, :], True, True), reads=[Bl, Bonr], writes=[BC2])
        kk.dve(f_copy(c2[:].rearrange("p h e -> p (h e)"), C2b[0:32, :]), reads=[BC2], writes=[Bc2])
        kk.act(f_act(sq2[:], c2[:], AF.Square), reads=[Bc2], writes=[Bsq2])
        kk.dve(lambda e: e.tensor_reduce(out=ss2[:, 0:4], in_=sq2[:], axis=AX.X, op=ALU.add), reads=[Bsq2], writes=[Bss2])
        kk.act(f_act(ss2[:, 4:8], ss2[:, 0:4], AF.Sqrt, scale=1.0 / 128, bias=eps_t[0:32, 0:1]), reads=[Bss2, Bc], writes=[Bss2])
        kk.dve(f_recip(ss2[:, 4:8], ss2[:, 4:8]), reads=[Bss2], writes=[Bss2])
        kk.dve(f_tt(c2[:], c2[:], ss2[:, 4:8].unsqueeze(2).to_broadcast([32, 4, 128]), ALU.mult), reads=[Bc2, Bss2], writes=[Bc2])
        kk.dve(f_tt(on3[:], c2[:], gbc[0:32, :].unsqueeze(1).to_broadcast([32, 4, 128]), ALU.mult), reads=[Bc2, Bsetup], writes=[Bon3])
        ttf = TTb[:].bitcast(BF16)
        for h in range(4):
            kk.pe(f_tr(ttf[:, h * 32:(h + 1) * 32], on3[:, h, :], ident_b[0:32, 0:32]), reads=[Bon3, Bc], writes=[BTT])
        for h in range(4):
            kk.dve(f_copy(oT[:, h, 2048 + 8 * s_:2048 + 8 * s_ + 8], ttf[:, h * 32 + h * 8:h * 32 + h * 8 + 8]), reads=[BTT], writes=[B_oT[4]])
    if "oTs" in dbg_out:
        odbg3 = ar.alloc([128, 512], F32, "odbg3")
        Bo3 = Buf("odbg3")
        kk.dve(f_copy(odbg3[:].rearrange("p (h t) -> p h t", h=4), oT[:, :, 2048:2176]), reads=B_oT, writes=[Bo3])
        dbg("oTs", odbg3[:], [Bo3])
    kk.barrier()
    if STOP == "p3d":
        return
    ar.reset(wmark)


    ar.reset(omark)
    mergedT = ar.alloc([128, 8, NTOK], BF16, "mergedT")
    B_mg = [Buf("mg%d" % i) for i in range(len(TCH))]
    p4mark = ar.mark()
    wg = [ar.alloc([128, 8, 3, 128], BF16, "wg%d" % i) for i in range(2)]
    Bwg = [Buf("wg0"), Buf("wg1")]
    wbr = [ar.alloc([128, 8, 128], BF16, "wbr%d" % i) for i in range(2)]
    Bwbr = [Buf("wbr0"), Buf("wbr1")]
    sg = [ar.alloc([128, 512], F32, "sg%d" % i) for i in range(6)]
    Bsg = [Buf("sg%d" % i) for i in range(6)]
    mt_ = [ar.alloc([128, 512], F32, "mtmp%d" % i) for i in range(4)]
    Bmt = [Buf("mtmp%d" % i) for i in range(4)]
    bankctr = [0]

    def rbank():
        b = bankctr[0] % 8
        bankctr[0] += 1
        return banks[b], pb[b]

    def load_p4(fc):
        b = fc % 2
        for g in range(3):
            c0 = 2048 + g * 1024 + fc * 128
            kk.dma("pool", wg[b][:, :, g, :], I["w_in"][:, c0:c0 + 128].rearrange("(k p) c -> p k c", p=128), writes=[Bwg[b]])
        kk.dma("pool", wbr[b][:, 0:4, :], I["w_br_attn"][:, fc * 128:(fc + 1) * 128].rearrange("(k p) c -> p k c", p=128), writes=[Bwbr[b]])
        kk.dma("pool", wbr[b][:, 4:6, :], I["w_br_pool"][:, fc * 128:(fc + 1) * 128].rearrange("(k p) c -> p k c", p=128), writes=[Bwbr[b]])
        kk.dma("pool", wbr[b][:, 6:8, :], I["w_br_mem"][:, fc * 128:(fc + 1) * 128].rearrange("(k p) c -> p k c", p=128), writes=[Bwbr[b]])

    load_p4(0)
    un = 0
    for fc in range(8):
        if fc + 1 < 8:
            load_p4(fc + 1)
        b = fc % 2
        for tc in range(len(TCH)):
            o, n = TCH[tc]
            hreads = [B_hT[t] for t in tiles_of(tc)]
            prods = []
            for g in range(3):
                gb, Bgb = rbank()
                for kc in range(8):
                    kk.pe(mm(gb[:, 0:n], wg[b][:, kc, g, :], hT[:, kc, o:o + n], kc == 0, kc == 7), reads=[Bwg[b]] + hreads, writes=[Bgb])
                bb, Bbb = rbank()
                if g == 0:
                    for h in range(4):
                        kk.pe(mm(bb[:, 0:n], wbr[b][:, h, :], oT[:, h, o:o + n], h == 0, h == 3), reads=[Bwbr[b], B_oT[tc]], writes=[Bbb])
                elif g == 1:
                    for ch in range(2):
                        kk.pe(mm(bb[:, 0:n], wbr[b][:, 4 + ch, :], poolT[:, ch, o:o + n], ch == 0, ch == 1), reads=[Bwbr[b], B_poolT[tc]], writes=[Bbb])
                else:
                    for hp in range(2):
                        kk.pe(mm(bb[:, 0:n], wbr[b][:, 6 + hp, :], omT[:, hp, o:o + n], hp == 0, hp == 1), reads=[Bwbr[b], B_omT[tc]], writes=[Bbb])
                si = (un * 3 + g) % 6
                kk.act(f_act(sg[si][:, 0:n], gb[:, 0:n], AF.Sigmoid), reads=[Bgb], writes=[Bsg[si]])
                kk.dve(f_tt(sg[si][:, 0:n], sg[si][:, 0:n], bb[:, 0:n], ALU.mult), reads=[Bsg[si], Bbb], writes=[Bsg[si]])
                prods.append(si)
            mi = un % 4
            kk.dve(f_tt(mt_[mi][:, 0:n], sg[prods[0]][:, 0:n], sg[prods[1]][:, 0:n], ALU.add), reads=[Bsg[prods[0]], Bsg[prods[1]]], writes=[Bmt[mi]])
            kk.dve(f_tt(mergedT[:, fc, o:o + n], mt_[mi][:, 0:n], sg[prods[2]][:, 0:n], ALU.add), reads=[Bmt[mi], Bsg[prods[2]]], writes=[B_mg[tc]])
            un += 1
    if "mergedT" in dbg_out:
        mdbg = ar.alloc([128, 512], F32, "mdbg")
        Bm_ = Buf("mdbg")
        kk.dve(f_copy(mdbg[:, 0:128], mergedT[:, 0, 0:128]), reads=B_mg, writes=[Bm_])
        kk.dve(f_copy(mdbg[:, 128:256], mergedT[:, 7, 1024:1152]), reads=B_mg, writes=[Bm_])
        kk.dve(f_copy(mdbg[:, 256:384], mergedT[:, 3, 2048:2176]), reads=B_mg, writes=[Bm_])
        kk.dve(f_copy(mdbg[:, 384:512], mergedT[:, 5, 2048:2176]), reads=B_mg, writes=[Bm_])
        dbg("mergedT", mdbg[:], [Bm_])
    kk.barrier()
    if STOP == "p4":
        return
    ar.reset(p4mark)

    arA = Arena(nc, amark, omark)
    wout = arA.alloc([128, 8, D], BF16, "wout")
    Bwout = Buf("wout")
    wd = arA.alloc([128, NFF, D], BF16, "wd")
    Bwd = [Buf("wd%d" % i) for i in range(NFF)]
    wgu = [arA.alloc([128, 8, 2, 128], BF16, "wgu%d" % i) for i in range(2)]
    Bwgu = [Buf("wgu%d" % i) for i in range(3)]
    stc = [ar.alloc([32, 512], F32, "stc%d" % i) for i in range(2)]
    stT = ar.alloc([128, NFF, NSEQ, 2], F32, "stT")
    cs = ar.alloc([128, NFF, 34], F32, "cs")
    halo = [ar.alloc([128, NFF, 2], F32, "halo%d" % i) for i in range(2)]
    Bstc, BstT, Bcs, Bhalo = [Buf("stc0"), Buf("stc1")], Buf("stT"), Buf("cs"), [Buf("halo0"), Buf("halo1")]
    for k2 in range(2):
        kk.dma("pool", wout[:, k2 * 4:(k2 + 1) * 4, :], I["w_out"][k2 * 512:(k2 + 1) * 512, :].rearrange("(k p) c -> p k c", p=128), writes=[Bwout])
    kk.dve(f_memset(halo[0][:], 0.0), writes=[Bhalo[0]])
    for q4 in range(6):
        nf = min(4, NFF - q4 * 4)
        kk.dma("sp", stc[q4 % 2][:, 0:nf * 128], I["state_conv"][:, q4 * 512:q4 * 512 + nf * 128], writes=[Bstc[q4 % 2]])
        for i in range(nf):
            fcx = q4 * 4 + i
            bk, Bb = rbank()
            kk.pe(mm(bk[:, 0:32], stc[q4 % 2][:, i * 128:(i + 1) * 128], ident_f[0:32, 0:32], True, True), reads=[Bstc[q4 % 2], Bc], writes=[Bb])
            kk.dve(f_copy(stT[:, fcx].rearrange("p s r -> p (s r)"), bk[:, 0:32]), reads=[Bb], writes=[BstT])
    x2 = ar.alloc([128, 4, D], F32, "x2")
    Bx2 = [Buf("x2_%d" % i) for i in range(4)]
    h2T = ar.alloc([128, 8, 512], BF16, "h2T")
    Bh2 = [Buf("h2_%d" % i) for i in range(4)]
    actT = ar.alloc([128, NFF, 512], BF16, "actT")
    Bact = [Buf("act%d" % i) for i in range(NFF)]
    gS = [ar.alloc([128, 2 + 512], F32, "gS%d" % i) for i in range(2)]
    BgS = [Buf("gS0"), Buf("gS1")]
    gSs = [ar.alloc([128, NSEQ, 10], F32, "gSs%d" % i) for i in range(2)]
    BgSs = [Buf("gSs0"), Buf("gSs1")]
    c1 = [ar.alloc([128, 512], F32, "c1_%d" % i) for i in range(2)]
    Bc1 = [Buf("c1_0"), Buf("c1_1")]
    ge = [ar.alloc([128, 512], F32, "ge%d" % i) for i in range(2)]
    Bge = [Buf("ge0"), Buf("ge1")]
    xin5 = [ar.alloc([128, D], F32, "xin5_%d" % i) for i in range(2)]
    Bxin5 = [Buf("xin5_0"), Buf("xin5_1")]
    xn5 = ar.alloc([128, D], BF16, "xn5")
    Bxn5 = Buf("xn5")
    junk5 = ar.alloc([128, D], BF16, "junk5")
    Bjunk5 = Buf("junk5")
    ss5 = [ar.alloc([128, 4], F32, "ss5_%d" % i) for i in range(2)]
    Bss5 = [Buf("ss5_0"), Buf("ss5_1")]
    wgu.append(ar.alloc([128, 8, 2, 128], BF16, "wgu2"))
    yt = xin5
    Byt = Bxin5
    csr = [ar.alloc([34, 512], F32, "csr%d" % i) for i in range(2)]
    Bcsr = [Buf("csr0"), Buf("csr1")]
    for fcx in range(NFF):
        kk.dma("pool", wd[:, fcx, :], I["w_ffn_down"][fcx * 128:(fcx + 1) * 128, :], writes=[Bwd[fcx]])
    nwl = [0]
    nt5 = 0
    for gi, (t0, t1) in enumerate(GROUPS):
        ntile = t1 - t0
        smp = gi == len(GROUPS) - 1
        lastp = gi == len(GROUPS) - 2
        ntk = ntile * 128
        for li in range(ntile):
            t = t0 + li
            xb_, Bxb_ = xin5[nt5 % 2], Bxin5[nt5 % 2]
            kk.dma("sp", xb_[:], I["x_all"][t * 128:(t + 1) * 128, :], writes=[Bxb_])
            for half in range(2):
                bk, Bb = rbank()
                for kc in range(8):
                    kk.pe(mm(bk[:, :], mergedT[:, kc, t * 128:(t + 1) * 128], wout[:, kc, half * 512:(half + 1) * 512], kc == 0, kc == 7),
                          reads=[B_mg[t // 4], Bwout], writes=[Bb])
                kk.dve(f_tt(x2[:, li, half * 512:(half + 1) * 512], bk[:, :], xb_[:, half * 512:(half + 1) * 512], ALU.add), reads=[Bb, Bxb_], writes=[Bx2[li]])
            bk, Bb = rbank()
            norm_transpose(None, g2T, h2T, slice(li * 128, (li + 1) * 128), x2[:, li, :], Bx2[li], xn5, Bxn5,
                           ss5[nt5 % 2], Bss5[nt5 % 2], bk, Bb, Bh2[li], junk5, t)
            nt5 += 1
        n = ntk
        hin, hout = halo[gi % 2], halo[(gi + 1) % 2]
        Bhin, Bhout = Bhalo[gi % 2], Bhalo[(gi + 1) % 2]
        for fcx in range(NFF):
            wi = nwl[0] % 3
            nwl[0] += 1
            kk.dma("pool", wgu[wi][:, :, 0, :], I["w_ffn_gate"][:, fcx * 128:(fcx + 1) * 128].rearrange("(k p) c -> p k c", p=128), writes=[Bwgu[wi]])
            kk.dma("pool", wgu[wi][:, :, 1, :], I["w_ffn_up"][:, fcx * 128:(fcx + 1) * 128].rearrange("(k p) c -> p k c", p=128), writes=[Bwgu[wi]])
            bi = fcx % 2
            g_, Bg_ = gS[bi], BgS[bi]
            gs_, Bgs_ = gSs[bi], BgSs[bi]
            c_, Bc_ = c1[bi], Bc1[bi]
            e_, Be_ = ge[bi], Bge[bi]
            hreads = [Bh2[i] for i in range(ntile)]
            gb, Bgb = rbank()
            for kc in range(8):
                kk.pe(mm(gb[:, 0:n], wgu[wi][:, kc, 0, :], h2T[:, kc, 0:n], kc == 0, kc == 7), reads=[Bwgu[wi]] + hreads, writes=[Bgb])
            ub, Bub = rbank()
            for kc in range(8):
                kk.pe(mm(ub[:, 0:n], wgu[wi][:, kc, 1, :], h2T[:, kc, 0:n], kc == 0, kc == 7), reads=[Bwgu[wi]] + hreads, writes=[Bub])
            w0, w1, w2, bb_ = convw[:, 0, fcx:fcx + 1], convw[:, 1, fcx:fcx + 1], convw[:, 2, fcx:fcx + 1], convb[:, fcx:fcx + 1]
            if not smp:
                kk.act(f_act(g_[:, 2:2 + 512], gb[:, 0:512], AF.Copy), reads=[Bgb], writes=[Bg_])
                kk.dve(f_copy(g_[:, 0:2], hin[:, fcx, :]), reads=[Bhin], writes=[Bg_])
                kk.dve(f_copy(hout[:, fcx, :], g_[:, 512:514]), reads=[Bg_], writes=[Bhout])
                kk.dve(f_ts(c_[:, 0:512], g_[:, 0:512], w0, bb_, ALU.mult, ALU.add), reads=[Bg_, Bc], writes=[Bc_])
                kk.dve(f_stt(c_[:, 0:512], g_[:, 1:513], w1, c_[:, 0:512], ALU.mult, ALU.add), reads=[Bg_, Bc_, Bc], writes=[Bc_])
                kk.dve(f_stt(c_[:, 0:512], g_[:, 2:514], w2, c_[:, 0:512], ALU.mult, ALU.add), reads=[Bg_, Bc_, Bc], writes=[Bc_])
                if lastp:
                    kk.dve(f_copy(cs[:, fcx, 0:2], g_[:, 512:514]), reads=[Bg_], writes=[Bcs])
            else:
                kk.act(f_act(gs_[:, :, 2:10], gb[:, 0:128].rearrange("p (s i) -> p s i", i=8), AF.Copy), reads=[Bgb], writes=[Bgs_])
                kk.dve(f_copy(gs_[:, :, 0:2], stT[:, fcx]), reads=[BstT], writes=[Bgs_])
                cv = c_[:, 0:128].rearrange("p (s i) -> p s i", i=8)
                kk.dve(f_ts(cv, gs_[:, :, 0:8], w0, bb_, ALU.mult, ALU.add), reads=[Bgs_, Bc], writes=[Bc_])
                kk.dve(f_stt(cv, gs_[:, :, 1:9], w1, cv, ALU.mult, ALU.add), reads=[Bgs_, Bc_, Bc], writes=[Bc_])
                kk.dve(f_stt(cv, gs_[:, :, 2:10], w2, cv, ALU.mult, ALU.add), reads=[Bgs_, Bc_, Bc], writes=[Bc_])
                kk.dve(f_copy(cs[:, fcx, 2:34].rearrange("p (s r) -> p s r", r=2), gs_[:, :, 8:10]), reads=[Bgs_], writes=[Bcs])
            kk.act(f_act(e_[:, 0:n], c_[:, 0:n], AF.Gelu_apprx_tanh), reads=[Bc_], writes=[Be_])
            kk.dve(f_tt(actT[:, fcx, 0:n], e_[:, 0:n], ub[:, 0:n], ALU.mult), reads=[Be_, Bub], writes=[Bact[fcx]])
        for li in range(ntile):
            t = t0 + li
            for half in range(2):
                bk, Bb = rbank()
                for fcx in range(NFF):
                    kk.pe(mm(bk[:, :], actT[:, fcx, li * 128:(li + 1) * 128], wd[:, fcx, half * 512:(half + 1) * 512], fcx == 0, fcx == NFF - 1),
                          reads=[Bact[fcx], Bwd[fcx]], writes=[Bb])
                kk.dve(f_tt(x2[:, li, half * 512:(half + 1) * 512], bk[:, :], x2[:, li, half * 512:(half + 1) * 512], ALU.add), reads=[Bb, Bx2[li]], writes=[Bx2[li]])
            si = nt5 % 2
            nt5 += 1
            kk.act(f_act(junk5[:], x2[:, li, :], AF.Square, accum_out=ss5[si][:, 0:1]), reads=[Bx2[li]], writes=[Bss5[si], Bjunk5])
            kk.act(f_act(ss5[si][:, 1:2], ss5[si][:, 0:1], AF.Sqrt, scale=1.0 / D, bias=eps_t[:, 0:1]), reads=[Bss5[si], Bc], writes=[Bss5[si]])
            kk.dve(f_recip(ss5[si][:, 2:3], ss5[si][:, 1:2]), reads=[Bss5[si]], writes=[Bss5[si]])
            kk.dve(f_stt(yt[si][:], x2[:, li, :], ss5[si][:, 2:3], gfin[:], ALU.mult, ALU.mult), reads=[Bx2[li], Bss5[si], Bc], writes=[Byt[si]])
            kk.dma("sp", O["y_all"][t * 128:(t + 1) * 128, :], yt[si][:], reads=[Byt[si]])
    for q4 in range(6):
        bk, Bb = rbank()
        nf = min(4, NFF - q4 * 4)
        for i in range(nf):
            fcx = q4 * 4 + i
            kk.pe(mm(bk[0:34, i * 128:(i + 1) * 128], cs[:, fcx, :], ident_f[:, :], True, True), reads=[Bcs, Bc], writes=[Bb])
        kk.dve(f_copy(csr[q4 % 2][:, 0:nf * 128], bk[0:34, 0:nf * 128]), reads=[Bb], writes=[Bcsr[q4 % 2]])
        kk.dma("sp", O["conv_all"][:, q4 * 512:q4 * 512 + nf * 128], csr[q4 % 2][:, 0:nf * 128], reads=[Bcsr[q4 % 2]])

    kk.barrier()


_NC_CACHE = {}


def kernel(**inputs):
    f32 = lambda a: np.ascontiguousarray(np.asarray(a, dtype=np.float32))
    if "nc" not in _NC_CACHE:
        _NC_CACHE["nc"] = build_program()
    nc = _NC_CACHE["nc"]
    consts = host_constants()
    x_prompt = f32(inputs["x_prompt"])
    x_sample = f32(inputs["x_sample"])
    mem_prompt = f32(inputs["mem_prompt"])
    cache_kv = np.concatenate([f32(inputs["cache_k"]).reshape(-1, 512), f32(inputs["cache_v"]).reshape(-1, 512)], axis=1)
    page_table = np.ascontiguousarray(np.asarray(inputs["page_table"], dtype=np.int32))
    state_pool = f32(inputs["state_pool"])[0]
    state_conv = f32(inputs["state_ffn_conv"])[0]
    cmk = f32(inputs["cache_mem_k"])[0]
    cmv = f32(inputs["cache_mem_v"])[0]
    shared = {
        "cache_kv": cache_kv,
        "norm1_g": f32(inputs["norm1_g"])[0], "w_in": f32(inputs["w_in"])[0],
        "lam_q1": f32(inputs["lam_q1"]), "lam_k1": f32(inputs["lam_k1"]),
        "lam_q2": f32(inputs["lam_q2"]), "lam_k2": f32(inputs["lam_k2"]),
        "subln_g": f32(inputs["subln_g"])[0], "w_pool_grp": f32(inputs["w_pool_grp"])[0],
        "pool_scale": f32(inputs["pool_scale"])[0],
        "w_br_attn": f32(inputs["w_br_attn"])[0], "w_br_pool": f32(inputs["w_br_pool"])[0],
        "w_br_mem": f32(inputs["w_br_mem"])[0], "mem_norm_g": f32(inputs["mem_norm_g"])[0],
        "w_mem_kv": f32(inputs["w_mem_kv"])[0], "w_out": f32(inputs["w_out"])[0],
        "norm2_g": f32(inputs["norm2_g"])[0], "w_ffn_gate": f32(inputs["w_ffn_gate"])[0],
        "w_ffn_up": f32(inputs["w_ffn_up"])[0], "ffn_conv_w": f32(inputs["ffn_conv_w"])[0],
        "ffn_conv_b": f32(inputs["ffn_conv_b"])[0], "w_ffn_down": f32(inputs["w_ffn_down"])[0],
        "rel_bias": f32(inputs["rel_bias"]), "final_norm_g": f32(inputs["final_norm_g"]),
    }
    shared.update(consts)
    in_maps = []
    for c in range(8):
        sl = slice(NSEQ * c, NSEQ * (c + 1))
        m = dict(shared)
        m["x_all"] = np.ascontiguousarray(np.concatenate([x_prompt[c], x_sample[sl].reshape(128, D)], axis=0))
        m["mem"] = mem_prompt[c]
        m["page_table"] = np.ascontiguousarray(page_table[sl].reshape(1, NSEQ * NPAGE))
        m["state_pool"] = np.ascontiguousarray(state_pool[sl].reshape(NSEQ * 15, 256))
        m["state_conv"] = np.ascontiguousarray(state_conv[sl].reshape(NSEQ * 2, D_FF))
        m["cmem_k"] = np.ascontiguousarray(cmk[sl].reshape(NSEQ, 256, 256))
        m["cmem_v"] = np.ascontiguousarray(cmv[sl].reshape(NSEQ, 256, 256))
        in_maps.append(m)
    res = run_bass_kernel_spmd(nc, in_maps, core_ids=list(range(8)))
    R = res.results
    g = lambda k: [np.asarray(R[c][k], dtype=np.float32) for c in range(8)]
    y = g("y_all"); nk = g("newk"); nv = g("newv")
    y_prompt = np.stack([a[:2048] for a in y], 0)
    y_sample = np.concatenate([a[2048:].reshape(NSEQ, 8, D) for a in y], 0)
    nkp = np.stack([a[:2048].reshape(2048, 4, 128) for a in nk], 0)[None]
    nvp = np.stack([a[:2048].reshape(2048, 4, 128) for a in nv], 0)[None]
    nks = np.concatenate([a[2048:].reshape(NSEQ, 8, 4, 128) for a in nk], 0)[None]
    nvs = np.concatenate([a[2048:].reshape(NSEQ, 8, 4, 128) for a in nv], 0)[None]
    pp = np.stack(g("pool_p"), 0)[None]
    ps = np.concatenate(g("pool_s"), 0)[None]
    cv = g("conv_all")
    cp = np.stack([a[:2] for a in cv], 0)[None]
    cs = np.concatenate([a[2:].reshape(NSEQ, 2, D_FF) for a in cv], 0)[None]
    mk = np.stack([a.reshape(256, 4, 64) for a in g("memk")], 0)[None]
    mv = np.stack([a.reshape(256, 4, 64) for a in g("memv")], 0)[None]
    return (y_prompt, y_sample, nkp, nvp, nks, nvs, pp, ps, cp, cs, mk, mv)
```

```python
import numpy as np
from contextlib import ExitStack

import concourse.bass as bass
import concourse.mybir as mybir
from concourse.bass_utils import run_bass_kernel_spmd

F32 = mybir.dt.float32
BF16 = mybir.dt.bfloat16
I32 = mybir.dt.int32
AF = mybir.ActivationFunctionType
ALU = mybir.AluOpType
AX = mybir.AxisListType

D = 1024
NTOK = 2176
NT = 17
D_IN = 5120
D_FF = 2816
NFF = 22
EPS = 1e-6
LAM_INIT = 0.8 - 0.6
NSEQ = 16
NPAGE = 16
WZ = 384

TCH = [(0, 512), (512, 512), (1024, 512), (1536, 512), (2048, 128)]
GROUPS = [(0, 4), (4, 8), (8, 12), (12, 16), (16, 17)]


class Buf:
    __slots__ = ("name", "w", "r")

    def __init__(self, name):
        self.name = name
        self.w = None
        self.r = []


class Op:
    __slots__ = ("eng", "fn", "waits", "signal", "idx", "count", "dma", "dsem", "dval", "pre")

    def __init__(self, eng, fn, dma):
        self.eng = eng
        self.fn = fn
        self.waits = []
        self.signal = False
        self.idx = -1
        self.count = 0
        self.dma = dma
        self.dsem = None
        self.dval = 0
        self.pre = None


ENGS = ("pe", "act", "dve", "pool", "sp")
NDSEM = 24


class K:
    def __init__(self, nc, es):
        self.nc = nc
        self.ops = {e: [] for e in ENGS}
        self.waited = {e: {p: -1 for p in ENGS} for e in ENGS}
        self.waited_dma = {e: set() for e in ENGS}
        self.sem = {e: es.enter_context(nc.semaphore("s_" + e)) for e in ENGS}
        self.dsems = {q: [es.enter_context(nc.semaphore("d_%s%d" % (q, i))) for i in range(NDSEM)]
                      for q in ("sp", "pool")}
        self.ndma = {"sp": 0, "pool": 0}
        self.dma_ops = {"sp": [], "pool": []}

    def _dep(self, op, d, force=False):
        e = op.eng
        if d is None or d is op:
            return
        if d.dma:
            if id(d) in self.waited_dma[e]:
                return
            self.waited_dma[e].add(id(d))
            op.waits.append(d)
            return
        p = d.eng
        if p == "pe" and e == "pe" and not force:
            return
        if self.waited[e][p] >= d.idx:
            return
        self.waited[e][p] = d.idx
        d.signal = True
        op.waits.append(d)

    def op(self, eng, fn, reads=(), writes=(), dma=False):
        o = Op(eng, fn, dma)
        o.idx = len(self.ops[eng])
        deps = []
        for b in reads:
            if b.w is not None:
                deps.append(b.w)
        for b in writes:
            if b.w is not None:
                deps.append(b.w)
            deps.extend(b.r)
        latest = {}
        for d in deps:
            if d.dma:
                self._dep(o, d)
            elif d.eng not in latest or latest[d.eng].idx < d.idx:
                latest[d.eng] = d
        for d in latest.values():
            self._dep(o, d)
        if dma:
            n = self.ndma[eng]
            self.ndma[eng] += 1
            o.dsem = self.dsems[eng][n % NDSEM]
            o.dval = 16 * (n // NDSEM + 1)
            if n >= NDSEM:
                prev = self.dma_ops[eng][n - NDSEM]
                o.pre = prev
            self.dma_ops[eng].append(o)
        self.ops[eng].append(o)
        for b in reads:
            b.r.append(o)
        for b in writes:
            b.w = o
            b.r = []
        return o

    def pe(self, fn, reads=(), writes=()):
        return self.op("pe", fn, reads, writes)

    def act(self, fn, reads=(), writes=()):
        return self.op("act", fn, reads, writes)

    def dve(self, fn, reads=(), writes=()):
        return self.op("dve", fn, reads, writes)

    def pool(self, fn, reads=(), writes=()):
        return self.op("pool", fn, reads, writes)

    def dma(self, q, out, in_, reads=(), writes=(), **kw):
        return self.op(q, lambda e: e.dma_start(out=out, in_=in_, **kw), reads, writes, dma=True)

    def barrier(self):
        lasts = []
        for e in ENGS:
            real = [o for o in self.ops[e] if o.fn is not None and not o.dma]
            if real:
                lasts.append(real[-1])
        dmas = self.dma_ops["sp"][-NDSEM:] + self.dma_ops["pool"][-NDSEM:]
        for e in ENGS:
            o = Op(e, None, False)
            o.idx = len(self.ops[e])
            for d in lasts + dmas:
                self._dep(o, d, force=True)
            self.ops[e].append(o)

    def emit(self, block):
        for e in ENGS:
            c = 0
            for o in self.ops[e]:
                if o.signal:
                    c += 1
                    o.count = c
        def run(e, eng):
            for o in self.ops[e]:
                if o.pre is not None:
                    eng.wait_ge(o.pre.dsem, o.pre.dval)
                for d in o.waits:
                    if d.dma:
                        eng.wait_ge(d.dsem, d.dval)
                    else:
                        eng.wait_ge(self.sem[d.eng], d.count)
                if o.fn is None:
                    continue
                ins = o.fn(eng)
                if o.dma:
                    ins.then_inc(o.dsem, 16)
                elif o.signal:
                    ins.then_inc(self.sem[e], 1)

        @block.tensor
        def _(eng):
            run("pe", eng)

        @block.scalar
        def _(eng):
            run("act", eng)

        @block.vector
        def _(eng):
            run("dve", eng)

        @block.gpsimd
        def _(eng):
            run("pool", eng)

        @block.sync
        def _(eng):
            run("sp", eng)


class Arena:
    def __init__(self, nc, base, cap):
        self.nc = nc
        self.base = base
        self.cap = cap
        self.top = base
        self.n = 0

    def alloc(self, shape, dtype, name=None):
        nbytes = int(np.prod(shape[1:])) * mybir.dt.size(dtype)
        off = (self.top + 31) // 32 * 32
        assert off + nbytes <= self.cap, ("SBUF arena overflow", name, off, nbytes, self.cap)
        self.top = off + nbytes
        self.n += 1
        nm = "%s_%d_%d" % (name or "t", off, self.n)
        return self.nc.alloc_sbuf_tensor_at(nm, list(shape), dtype, offset=off)

    def mark(self):
        return self.top

    def reset(self, m):
        self.top = m


def rel_bucket_np(rel):
    n = np.maximum(rel, 0)
    max_exact = 16
    nf = np.maximum(n, 1).astype(np.float32)
    large = max_exact + (np.log(nf / max_exact) / np.log(128 / max_exact) * (32 - max_exact)).astype(np.int32)
    large = np.minimum(large, 31)
    return np.where(n < max_exact, n, large)


def host_constants():
    c = {}
    c["ident"] = np.eye(128, dtype=np.float32)
    rel = np.arange(WZ) - 128
    b = rel_bucket_np(rel)
    oh = np.zeros((32, WZ), np.float32)
    oh[b, np.arange(WZ)] = 1.0
    oh[:, rel < 0] = 0.0
    c["bucket_oh"] = oh
    c["relmask"] = np.repeat((rel >= 0).astype(np.float32)[None, :], 128, axis=0)
    pc = np.zeros((128, 2, 16), np.float32)
    for ch in range(2):
        for p in range(128):
            w = 2 ** (2 * ch + p // 64 + 1)
            pc[p, ch, :] = 1.0 / np.minimum(np.arange(16) + 1, w)
    c["poolc"] = pc
    bd = np.zeros((128, 16), np.float32)
    bd[np.arange(128), np.arange(128) // 8] = 1.0
    c["blockdiag"] = bd
    sel = np.zeros((64, 2, 32), np.float32)
    for h in range(4):
        for cc in range(2):
            for q in range(8):
                sel[h * 16 + cc * 8 + q, cc, h * 8 + q] = 1.0
    c["sel"] = sel
    c["iota_f"] = np.arange(128, dtype=np.float32).reshape(128, 1)
    return c


CONST_SHAPES = {
    "ident": ([128, 128], F32), "bucket_oh": ([32, WZ], F32), "relmask": ([128, WZ], F32),
    "poolc": ([128, 2, 16], F32), "blockdiag": ([128, 16], F32), "sel": ([64, 2, 32], F32),
    "iota_f": ([128, 1], F32),
}

IN_SHAPES = {
    "x_all": ([NTOK, D], F32), "mem": ([256, D], F32),
    "cache_kv": ([2560 * 128, 1024], F32),
    "page_table": ([1, NSEQ * NPAGE], I32),
    "state_pool": ([NSEQ * 15, 256], F32), "state_conv": ([NSEQ * 2, D_FF], F32),
    "cmem_k": ([NSEQ, 256, 256], F32), "cmem_v": ([NSEQ, 256, 256], F32),
    "norm1_g": ([D], F32), "w_in": ([D, D_IN], F32),
    "lam_q1": ([1, 64], F32), "lam_k1": ([1, 64], F32), "lam_q2": ([1, 64], F32), "lam_k2": ([1, 64], F32),
    "subln_g": ([128], F32), "w_pool_grp": ([4, 64, 64], F32), "pool_scale": ([256], F32),
    "w_br_attn": ([512, D], F32), "w_br_pool": ([256, D], F32), "w_br_mem": ([256, D], F32),
    "mem_norm_g": ([D], F32), "w_mem_kv": ([D, 512], F32), "w_out": ([D, D], F32),
    "norm2_g": ([D], F32), "w_ffn_gate": ([D, D_FF], F32), "w_ffn_up": ([D, D_FF], F32),
    "ffn_conv_w": ([3, D_FF], F32), "ffn_conv_b": ([D_FF], F32), "w_ffn_down": ([D_FF, D], F32),
    "rel_bias": ([32, 4], F32), "final_norm_g": ([D], F32),
}

OUT_SHAPES = {
    "y_all": [NTOK, D], "newk": [NTOK, 512], "newv": [NTOK, 512],
    "pool_p": [15, 256], "pool_s": [NSEQ, 15, 256],
    "conv_all": [2 + 2 * NSEQ, D_FF],
    "memk": [256, 256], "memv": [256, 256],
}


def build_program(phases=("all",), debug=None, nphys=2560):
    nc = bass.Bass("TRN2", target_bir_lowering=False)
    I = {}
    for k, (shp, dt) in {**IN_SHAPES, **CONST_SHAPES}.items():
        if k == "cache_kv":
            shp = [nphys * 128, 1024]
        I[k] = nc.dram_tensor(k, shp, dt, kind="ExternalInput").ap()
    O = {}
    for k, shp in OUT_SHAPES.items():
        O[k] = nc.dram_tensor(k, shp, F32, kind="ExternalOutput").ap()
    zscr = nc.dram_tensor("zscr", [128, 4 * WZ], F32, kind="Internal").ap()
    wscr_t = nc.dram_tensor("wscr", [NFF, 128, 2048], BF16, kind="Internal").ap()
    dbg_out = {}
    if debug:
        for k, shp in debug.items():
            dbg_out[k] = nc.dram_tensor("dbg_" + k, shp, F32, kind="ExternalOutput").ap()

    with ExitStack() as es:
        kk = K(nc, es)
        banks = [es.enter_context(nc.psum_tensor("bank%d" % i, [128, 512], F32)) for i in range(8)]
        pb = [Buf("psum%d" % i) for i in range(8)]
        block = es.enter_context(nc.Block())
        _build(nc, kk, I, O, zscr, banks, pb, dbg_out, phases, wscr_t)
        kk.emit(block)
    return nc


def _build(nc, kk, I, O, zscr, banks, pb, dbg_out, phases, wscr):
    Bwscr = [Buf("wscr%d" % i) for i in range(NFF)]
    ALL = "all" in phases
    STOP = [p for p in phases if p.startswith("p")]
    STOP = STOP[0] if STOP else None
    ar = Arena(nc, (nc.sbuf_base + 63) // 64 * 64, nc.sbuf_top)

    def mm(out, lhsT, rhs, start, stop):
        return lambda e: e.matmul(out, lhsT, rhs, start=start, stop=stop)

    def f_tt(out, in0, in1, op):
        return lambda e: e.tensor_tensor(out=out, in0=in0, in1=in1, op=op)

    def f_ts(out, in0, s1, s2, op0, op1=None):
        if op1 is None:
            return lambda e: e.tensor_scalar(out=out, in0=in0, scalar1=s1, scalar2=None, op0=op0)
        return lambda e: e.tensor_scalar(out=out, in0=in0, scalar1=s1, scalar2=s2, op0=op0, op1=op1)

    def f_stt(out, in0, scalar, in1, op0, op1):
        return lambda e: e.scalar_tensor_tensor(out=out, in0=in0, scalar=scalar, in1=in1, op0=op0, op1=op1)

    def f_copy(out, in_):
        return lambda e: e.tensor_copy(out=out, in_=in_)

    def f_act(out, in_, func, **kw):
        return lambda e: e.activation(out=out, in_=in_, func=func, **kw)

    def f_recip(out, in_):
        return lambda e: e.reciprocal(out=out, in_=in_)

    def f_memset(ap, v):
        return lambda e: e.memset(ap, v)

    def f_tr(out, in_, ident):
        return lambda e: e.transpose(out, in_, ident)

    cst = {}
    ident_f = ar.alloc([128, 128], F32, "identf")
    ident_b = ar.alloc([128, 128], BF16, "identb")
    ones_f = ar.alloc([128, 128], F32, "onesf")
    ones_b = ar.alloc([128, 128], BF16, "onesb")
    g1T = ar.alloc([128, 8], F32, "g1T")
    g2T = ar.alloc([128, 8], F32, "g2T")
    gmT = ar.alloc([128, 8], F32, "gmT")
    gfin = ar.alloc([128, D], F32, "gfin")
    sublnT = ar.alloc([128, 1], F32, "subln")
    pscaleT = ar.alloc([128, 2], F32, "pscale")
    convw = ar.alloc([128, 3, NFF], F32, "convw")
    convb = ar.alloc([128, NFF], F32, "convb")
    lamv = ar.alloc([128, 4, 64], F32, "lamv")
    lamt = ar.alloc([128, 8], F32, "lamt")
    neg_lam = ar.alloc([128, 1], F32, "neglam")
    eps_t = ar.alloc([128, 1], F32, "eps")
    poolc = ar.alloc([128, 2, 16], F32, "poolc")
    bdiag = ar.alloc([128, 16], F32, "bdiag")
    selc = ar.alloc([64, 2, 32], F32, "sel")
    comb = ar.alloc([64, 32], F32, "comb")
    Bc = Buf("consts")

    kk.dma("sp", ident_f[:], I["ident"][:, :], writes=[Bc])
    kk.dma("pool", ident_b[:], I["ident"][:, :], writes=[Bc])
    kk.dve(lambda e: e.memset(ones_f[:], 1.0), writes=[Bc])
    kk.dve(lambda e: e.memset(ones_b[:], 1.0), writes=[Bc])
    kk.dve(lambda e: e.memset(eps_t[:], EPS), writes=[Bc])
    for t, src in ((g1T, "norm1_g"), (g2T, "norm2_g"), (gmT, "mem_norm_g")):
        kk.dma("sp", t[:], I[src].rearrange("(k p) -> p k", p=128), writes=[Bc], allow_slow_non_contiguous=True)
    kk.dma("sp", gfin[:], I["final_norm_g"].partition_broadcast(128), writes=[Bc])
    kk.dma("sp", sublnT[:], I["subln_g"].rearrange("(p o) -> p o", o=1), writes=[Bc], allow_slow_non_contiguous=True)
    kk.dma("sp", pscaleT[:], I["pool_scale"].rearrange("(k p) -> p k", p=128), writes=[Bc], allow_slow_non_contiguous=True)
    kk.dma("sp", convw[:], I["ffn_conv_w"].rearrange("j (c p) -> p j c", p=128), writes=[Bc], allow_slow_non_contiguous=True)
    kk.dma("sp", convb[:], I["ffn_conv_b"].rearrange("(c p) -> p c", p=128), writes=[Bc], allow_slow_non_contiguous=True)
    for i, nm in enumerate(("lam_q1", "lam_k1", "lam_q2", "lam_k2")):
        kk.dma("sp", lamv[:, i, :], I[nm][0, :].partition_broadcast(128), writes=[Bc])
    kk.dma("sp", poolc[:], I["poolc"][:, :, :], writes=[Bc])
    kk.dma("sp", bdiag[:], I["blockdiag"][:, :], writes=[Bc])
    kk.dma("sp", selc[:], I["sel"][:, :, :], writes=[Bc])
    Bl = Buf("lam")
    kk.dve(lambda e: e.tensor_tensor(out=lamv[:, 0, :], in0=lamv[:, 0, :], in1=lamv[:, 1, :], op=ALU.mult), reads=[Bc], writes=[Bl])
    kk.dve(lambda e: e.tensor_tensor(out=lamv[:, 2, :], in0=lamv[:, 2, :], in1=lamv[:, 3, :], op=ALU.mult), reads=[Bl], writes=[Bl])
    kk.dve(lambda e: e.tensor_reduce(out=lamt[:, 0:1], in_=lamv[:, 0, :], axis=AX.X, op=ALU.add), reads=[Bl], writes=[Bl])
    kk.dve(lambda e: e.tensor_reduce(out=lamt[:, 1:2], in_=lamv[:, 2, :], axis=AX.X, op=ALU.add), reads=[Bl], writes=[Bl])
    kk.act(lambda e: e.activation(out=lamt[:, 2:4], in_=lamt[:, 0:2], func=AF.Exp), reads=[Bl], writes=[Bl])
    kk.dve(lambda e: e.tensor_tensor(out=lamt[:, 4:5], in0=lamt[:, 3:4], in1=lamt[:, 2:3], op=ALU.subtract), reads=[Bl], writes=[Bl])
    kk.dve(lambda e: e.tensor_scalar(out=neg_lam[:], in0=lamt[:, 4:5], scalar1=-LAM_INIT, scalar2=None, op0=ALU.add), reads=[Bl], writes=[Bl])
    kk.dve(lambda e: e.scalar_tensor_tensor(out=comb[:], in0=selc[:, 1, :], scalar=neg_lam[0:64, 0:1], in1=selc[:, 0, :],
                                            op0=ALU.mult, op1=ALU.add), reads=[Bl, Bc], writes=[Bl])
    kk.dve(lambda e: e.tensor_scalar(out=sublnT[:], in0=sublnT[:], scalar1=1.0 - LAM_INIT, scalar2=None, op0=ALU.mult), reads=[Bc], writes=[Bc])

    T0 = ar.alloc([128, 4, 128], F32, "T0")
    T1 = ar.alloc([128, 4, 128], F32, "T1")
    cmark = ar.mark()
    art = Arena(nc, nc.sbuf_top - 12 * 1024, nc.sbuf_top)
    rb = art.alloc([32, 4], F32, "rb")
    rbrep = art.alloc([32, 4, 128], F32, "rbrep")
    oh = art.alloc([32, WZ], F32, "oh")
    relmask = art.alloc([128, WZ], F32, "relmask")
    erow = art.alloc([128, 4, WZ], F32, "erow")
    Bt = Buf("T")
    kk.dma("sp", rb[:], I["rel_bias"][:, :], writes=[Bt])
    kk.dma("sp", oh[:], I["bucket_oh"][:, :], writes=[Bt])
    kk.dma("sp", relmask[:], I["relmask"][:, :], writes=[Bt])
    for h in range(4):
        kk.dve(lambda e, h=h: e.tensor_copy(out=rbrep[:, h, :], in_=rb[:, h:h + 1].to_broadcast([32, 128])), reads=[Bt], writes=[Bt])
    for h in range(4):
        kk.pe(mm(banks[0][:, 0:WZ], rbrep[:, h, :], oh[:, :], True, True), reads=[Bt], writes=[pb[0]])
        kk.dve(lambda e, h=h: e.tensor_scalar(out=erow[:, h, 0:1], in0=banks[0][:, WZ - 1:WZ], scalar1=-1.0, scalar2=None, op0=ALU.mult),
               reads=[pb[0]], writes=[Bt])
        kk.act(lambda e, h=h: e.activation(out=erow[:, h, 1:WZ], in_=banks[0][:, 1:WZ], func=AF.Exp, bias=erow[:, h, 0:1]),
               reads=[pb[0], Bt], writes=[Bt])
        kk.dve(lambda e, h=h: e.tensor_tensor(out=erow[:, h, :], in0=erow[:, h, :], in1=relmask[:, :], op=ALU.mult), reads=[Bt], writes=[Bt])
    Bz = Buf("zscr")
    kk.dma("sp", zscr[:, :], erow[:].rearrange("p h w -> p (h w)"), reads=[Bt], writes=[Bz])
    for h in range(4):
        s0 = bass.AP(zscr.tensor, h * WZ + 128, [[4 * WZ - 1, 128], [1, 128]])
        s1 = bass.AP(zscr.tensor, h * WZ + 256, [[4 * WZ - 1, 128], [1, 128]])
        kk.dma("sp", T0[:, h, :], s0, reads=[Bz], writes=[Bt])
        kk.dma("sp", T1[:, h, :], s1, reads=[Bz], writes=[Bt])

    def dbg(name, ap, rd):
        if name in dbg_out:
            kk.dma("sp", dbg_out[name], ap, reads=rd)

    dbg("T0", T0[:].rearrange("p h w -> p (h w)"), [Bt])
    dbg("T1", T1[:].rearrange("p h w -> p (h w)"), [Bt])
    dbg("neglam", neg_lam[:], [Bl])
    if STOP == "p0":
        kk.barrier()
        return
    ar.reset(cmark)


    amark = ar.mark()
    hT = ar.alloc([128, 8, NTOK], BF16, "hT")
    oT = ar.alloc([128, 4, NTOK], BF16, "oT")
    poolT = ar.alloc([128, 2, NTOK], BF16, "poolT")
    omT = ar.alloc([128, 2, NTOK], BF16, "omT")
    omark = ar.mark()
    qT = ar.alloc([128, 4, NTOK], BF16, "qT")
    kT = ar.alloc([128, 4, NTOK], BF16, "kT")
    v_bf = ar.alloc([128, NT, 512], BF16, "vbf")
    qmT = ar.alloc([128, 2, NTOK], BF16, "qmT")
    B_hT = [Buf("hT%d" % t) for t in range(NT)]
    B_qT = [Buf("qT%d" % i) for i in range(len(TCH))]
    B_kT = [Buf("kT%d" % i) for i in range(len(TCH))]
    B_qmT = [Buf("qmT%d" % i) for i in range(len(TCH))]
    B_v = [Buf("v%d" % t) for t in range(NT)]
    B_oT = [Buf("oT%d" % i) for i in range(len(TCH))]
    B_poolT = [Buf("poolT%d" % i) for i in range(len(TCH))]
    B_omT = [Buf("omT%d" % i) for i in range(len(TCH))]
    wmark = ar.mark()

    def tiles_of(tc):
        o, n = TCH[tc]
        return list(range(o // 128, (o + n) // 128))

    def norm_transpose(src_rows, gT, dst, dst_cols, xin, Bx, xn, Bxn, ss, Bss, bank, Bbank, Bdst, junk, i):
        if src_rows is not None:
            kk.dma("sp", xin[:], src_rows, writes=[Bx])
        kk.act(lambda e: e.activation(out=junk[:], in_=xin[:], func=AF.Square, accum_out=ss[:, 0:1]), reads=[Bx], writes=[Bss, Bjunk])
        kk.act(lambda e: e.activation(out=ss[:, 1:2], in_=ss[:, 0:1], func=AF.Sqrt, scale=1.0 / D, bias=eps_t[:, 0:1]), reads=[Bss, Bc], writes=[Bss])
        kk.dve(lambda e: e.reciprocal(out=ss[:, 2:3], in_=ss[:, 1:2]), reads=[Bss], writes=[Bss])
        kk.dve(lambda e: e.tensor_scalar(out=xn[:], in0=xin[:], scalar1=ss[:, 2:3], scalar2=None, op0=ALU.mult), reads=[Bx, Bss], writes=[Bxn])
        pbf = bank[:].bitcast(BF16)
        for kc in range(8):
            kk.pe(lambda e, kc=kc: e.transpose(pbf[:, kc * 128:(kc + 1) * 128], xn[:, kc * 128:(kc + 1) * 128], ident_b[:]),
                  reads=[Bxn, Bc], writes=[Bbank])
        kk.dve(f_tt(dst[:, :, dst_cols], pbf[:, 0:1024].rearrange("p (k t) -> p k t", k=8),
                    gT[:, :].unsqueeze(2).to_broadcast([128, 8, 128]), ALU.mult),
               reads=[Bbank, Bc], writes=[Bdst])

    xins = [ar.alloc([128, D], F32, "xin%d" % i) for i in range(3)]
    Bxins = [Buf("xin%d" % i) for i in range(3)]
    xns = [ar.alloc([128, D], BF16, "xn%d" % i) for i in range(2)]
    Bxns = [Buf("xn%d" % i) for i in range(2)]
    sss = [ar.alloc([128, 4], F32, "ss%d" % i) for i in range(3)]
    Bsss = [Buf("ss%d" % i) for i in range(3)]
    junk = ar.alloc([128, D], BF16, "junk")
    Bjunk = Buf("junk")
    for t in range(NT):
        norm_transpose(I["x_all"][t * 128:(t + 1) * 128, :], g1T, hT, slice(t * 128, (t + 1) * 128),
                       xins[t % 3], Bxins[t % 3], xns[t % 2], Bxns[t % 2], sss[t % 3], Bsss[t % 3],
                       banks[t % 2], pb[t % 2], B_hT[t], junk, t)
    if "hT" in dbg_out:
        hdbg = ar.alloc([128, 8 * 128], F32, "hdbg")
        Bh = Buf("hdbg")
        kk.dve(lambda e: e.tensor_copy(out=hdbg[:].rearrange("p (k t) -> p k t", k=8), in_=hT[:, :, 2048:2176]), reads=B_hT, writes=[Bh])
        dbg("hT", hdbg[:], [Bh])
    kk.barrier()
    if STOP == "p1":
        return
    ar.reset(wmark)

    wps = [ar.alloc([128, 8, 512], BF16, "wp%d" % i) for i in range(2)]
    Bwps = [Buf("wp%d" % i) for i in range(2)]
    stg = [ar.alloc([128, 512], F32, "stg%d" % i) for i in range(3)]
    Bstg = [Buf("stg%d" % i) for i in range(3)]
    nstg = [0]
    Eb = ar.alloc([128, 15 + 2048], F32, "Eb")
    Es = ar.alloc([128, NSEQ, 23], F32, "Es")
    W1 = ar.alloc([128, 15 + 2048], F32, "W1")
    W1s = ar.alloc([128, NSEQ, 23], F32, "W1s")
    W2 = ar.alloc([128, 15 + 2048], F32, "W2")
    W2s = ar.alloc([128, NSEQ, 23], F32, "W2s")
    dTb = ar.alloc([128, NTOK], BF16, "dTb")
    tmp16 = ar.alloc([128, 16], F32, "tmp16")
    bdw = ar.alloc([128, 128], BF16, "bdw")
    stp = ar.alloc([120, 2, 256], F32, "stp")
    BE, BW1, BW2, BdT, Bbdw, Bstp, Bt16 = Buf("E"), Buf("W1"), Buf("W2"), Buf("dT"), Buf("bdw"), Buf("stp"), Buf("t16")

    def load_wpiece(i, c0):
        kk.dma("pool", wps[i][:], I["w_in"][:, c0:c0 + 512].rearrange("(k p) c -> p k c", p=128), writes=[Bwps[i]])

    def fm_group(wp, Bwp, col0, tc, bank, Bbank):
        o, n = TCH[tc]
        for kc in range(8):
            kk.pe(mm(bank[:, 0:n], wp[:, kc, col0:col0 + 128], hT[:, kc, o:o + n], kc == 0, kc == 7),
                  reads=[Bwp] + [B_hT[t] for t in tiles_of(tc)], writes=[Bbank])

    def tm_group(wp, Bwp, t, bank, Bbank, ncols=512, c0=0):
        for kc in range(8):
            kk.pe(mm(bank[:, 0:ncols], hT[:, kc, t * 128:(t + 1) * 128], wp[:, kc, c0:c0 + ncols], kc == 0, kc == 7),
                  reads=[Bwp, B_hT[t]], writes=[Bbank])

    nb = [0]

    def next_bank():
        b = nb[0] % 4
        nb[0] += 1
        return banks[b], pb[b]

    ev = [0]

    def evac_copy(out_ap, in_ap, reads, writes):
        ev[0] += 1
        if ev[0] % 2 == 0:
            kk.dve(lambda e: e.tensor_copy(out=out_ap, in_=in_ap), reads=reads, writes=writes)
        else:
            kk.act(lambda e: e.activation(out=out_ap, in_=in_ap, func=AF.Copy), reads=reads, writes=writes)

    wps.append(ar.alloc([128, 8, 512], BF16, "wp2"))
    Bwps.append(Buf("wp2"))
    WU, WQ, WK, WV = 0, 1, 2, 1
    load_wpiece(WU, 1536)
    load_wpiece(WQ, 0)
    load_wpiece(WK, 512)

    def pool_gen():
        kk.dve(f_memset(Eb[:, 0:15], 0.0), writes=[BE])
        kk.dve(f_memset(bdw[:], 0.0), writes=[Bbdw])
        kk.dma("sp", stp[:, 0, :], I["state_pool"][0:120, :], writes=[Bstp])
        kk.dma("sp", stp[:, 1, :], I["state_pool"][120:240, :], writes=[Bstp])
        yield
        for ch in range(2):
            for tc in range(len(TCH)):
                o, n = TCH[tc]
                bk, Bb = next_bank()
                fm_group(wps[WU], Bwps[WU], ch * 128, tc, bk, Bb)
                if tc < 4:
                    evac_copy(Eb[:, 15 + o:15 + o + n], bk[:, 0:n], [Bb], [BE])
                else:
                    evac_copy(Es[:, :, 15:23], bk[:, 0:128].rearrange("p (s i) -> p s i", i=8), [Bb], [BE])
                yield
            for j in range(2):
                bk, Bb = next_bank()
                kk.pe(mm(bk[:, 0:120], stp[:, j, ch * 128:(ch + 1) * 128], ident_f[0:120, 0:120], True, True), reads=[Bstp, Bc], writes=[Bb])
                evac_copy(Es[:, j * 8:(j + 1) * 8, 0:15], bk[:, 0:120].rearrange("p (s r) -> p s r", r=15), [Bb], [BE])
                yield

            def dbl(dst, dsts, src, srcs, sh, first):
                lo = 2 * sh - 1
                kk.dve(f_tt(dst[:, lo:], src[:, lo:], src[:, lo - sh:15 + 2048 - sh], ALU.add),
                       reads=[first], writes=[BW1 if dst is W1 else BW2])
                kk.dve(f_tt(dsts[:, :, lo:], srcs[:, :, lo:], srcs[:, :, lo - sh:23 - sh], ALU.add),
                       reads=[first], writes=[BW1 if dst is W1 else BW2])
            dbl(W1, W1s, Eb, Es, 1, BE)
            yield
            dbl(W2, W2s, W1, W1s, 2, BW1)
            yield
            if ch == 1:
                dbl(W1, W1s, W2, W2s, 4, BW2)
                yield
                dbl(W2, W2s, W1, W1s, 8, BW1)
                yield
            for half, (Wb, Wbs, BWb) in enumerate(((W1, W1s, BW1), (W2, W2s, BW2))):
                ps = slice(half * 64, (half + 1) * 64)
                kk.dve(f_stt(dTb[ps, 0:2048], Wb[ps, 15:15 + 2048], poolc[ps, ch, 15:16], Eb[ps, 15:15 + 2048], ALU.mult, ALU.subtract),
                       reads=[BWb, BE, Bc], writes=[BdT])
                yield
                kk.dve(f_tt(tmp16[ps, :], Wb[ps, 15:31], poolc[ps, ch, :], ALU.mult), reads=[BWb, Bc], writes=[Bt16])
                kk.dve(f_tt(dTb[ps, 0:16], tmp16[ps, :], Eb[ps, 15:31], ALU.subtract), reads=[Bt16, BE, BdT], writes=[BdT])
                kk.dve(f_stt(dTb[ps, 2048:2176].rearrange("p (s i) -> p s i", i=8), Wbs[ps, :, 15:23], poolc[ps, ch, 15:16],
                             Es[ps, :, 15:23], ALU.mult, ALU.subtract),
                       reads=[BWb, BE, Bc, BdT], writes=[BdT])
                yield
            for half in range(2):
                ps = slice(half * 64, (half + 1) * 64)
                kk.dma("pool", bdw[ps, half * 64:(half + 1) * 64], I["w_pool_grp"][2 * ch + half, :, :], reads=[Bbdw], writes=[Bbdw])
            for tc in range(len(TCH)):
                o, n = TCH[tc]
                bk, Bb = next_bank()
                kk.pe(mm(bk[:, 0:n], bdw[:, :], dTb[:, o:o + n], True, True), reads=[Bbdw, BdT], writes=[Bb])
                kk.dve(f_ts(poolT[:, ch, o:o + n], bk[:, 0:n], pscaleT[:, ch:ch + 1], None, ALU.mult), reads=[Bb, Bc], writes=[B_poolT[tc]])
                yield

    pgen = pool_gen()

    def pstep():
        next(pgen, None)

    for hp in range(2):
        for tc in range(len(TCH)):
            o, n = TCH[tc]
            bk, Bb = next_bank()
            fm_group(wps[WU], Bwps[WU], 256 + hp * 128, tc, bk, Bb)
            evac_copy(qmT[:, hp, o:o + n], bk[:, 0:n], [Bb], [B_qmT[tc]])
    for t in (15, 16):
        bk, Bb = next_bank()
        tm_group(wps[WU], Bwps[WU], t, bk, Bb, ncols=256, c0=0)
        si = nstg[0] % 3
        nstg[0] += 1
        evac_copy(stg[si][:, 0:256], bk[:, 0:256], [Bb], [Bstg[si]])
        if t == 15:
            kk.dma("sp", O["pool_p"][:, :], stg[si][113:128, 0:256], reads=[Bstg[si]])
        else:
            for s_ in range(NSEQ):
                kk.dma("sp", O["pool_s"][s_, 7:15, :], stg[si][s_ * 8:(s_ + 1) * 8, 0:256], reads=[Bstg[si]])
    kk.dma("sp", O["pool_s"][:, 0:7, :], I["state_pool"].rearrange("(s r) c -> s r c", r=15)[:, 8:15, :])
    for h in range(4):
        for tc in range(len(TCH)):
            o, n = TCH[tc]
            bk, Bb = next_bank()
            fm_group(wps[WQ], Bwps[WQ], h * 128, tc, bk, Bb)
            evac_copy(qT[:, h, o:o + n], bk[:, 0:n], [Bb], [B_qT[tc]])
            pstep()
    load_wpiece(WV, 1024)
    for h in range(4):
        for tc in range(len(TCH)):
            o, n = TCH[tc]
            bk, Bb = next_bank()
            fm_group(wps[WK], Bwps[WK], h * 128, tc, bk, Bb)
            evac_copy(kT[:, h, o:o + n], bk[:, 0:n], [Bb], [B_kT[tc]])
            pstep()
    for t in range(NT):
        bk, Bb = next_bank()
        tm_group(wps[WK], Bwps[WK], t, bk, Bb)
        si = nstg[0] % 3
        nstg[0] += 1
        evac_copy(stg[si][:], bk[:, :], [Bb], [Bstg[si]])
        kk.dma("sp", O["newk"][t * 128:(t + 1) * 128, :], stg[si][:], reads=[Bstg[si]])
        pstep()
    for t in range(NT):
        bk, Bb = next_bank()
        tm_group(wps[WV], Bwps[WV], t, bk, Bb)
        si = nstg[0] % 3
        nstg[0] += 1
        kk.act(f_act(stg[si][:], bk[:, :], AF.Copy), reads=[Bb], writes=[Bstg[si]])
        kk.dve(f_copy(v_bf[:, t, :], stg[si][:]), reads=[Bstg[si]], writes=[B_v[t]])
        kk.dma("sp", O["newv"][t * 128:(t + 1) * 128, :], stg[si][:], reads=[Bstg[si]])
        pstep()
    for _ in pgen:
        pass
    if "poolT" in dbg_out:
        pdbg = ar.alloc([128, 2 * 256], F32, "pdbg")
        Bp = Buf("pdbg")
        kk.dve(lambda e: e.tensor_copy(out=pdbg[:, 0:128], in_=poolT[:, 0, 0:128]), reads=B_poolT, writes=[Bp])
        kk.dve(lambda e: e.tensor_copy(out=pdbg[:, 128:256], in_=poolT[:, 1, 0:128]), reads=B_poolT, writes=[Bp])
        kk.dve(lambda e: e.tensor_copy(out=pdbg[:, 256:384], in_=poolT[:, 0, 2048:2176]), reads=B_poolT, writes=[Bp])
        kk.dve(lambda e: e.tensor_copy(out=pdbg[:, 384:512], in_=poolT[:, 1, 2048:2176]), reads=B_poolT, writes=[Bp])
        dbg("poolT", pdbg[:], [Bp])
    kk.barrier()
    if STOP == "p2":
        return
    ar.reset(wmark)


    P3 = ALL or "p3" in phases
    xin0 = ar.alloc([128, D], F32, "mxin0")
    xin1 = ar.alloc([128, D], F32, "mxin1")
    mxn = ar.alloc([128, D], BF16, "mxn")
    mjunk = ar.alloc([128, D], BF16, "mjunk")
    mss = [ar.alloc([128, 4], F32, "mss%d" % i) for i in range(2)]
    mhT = ar.alloc([128, 8, 256], BF16, "mhT")
    wmkv = ar.alloc([128, 8, 512], BF16, "wmkv")
    memkT = ar.alloc([128, 2, 256], BF16, "memkT")
    memv_pad = ar.alloc([128, 2, 4, 128], BF16, "memvpad")
    onesE = ar.alloc([128, 128], BF16, "onesE")
    onesO = ar.alloc([128, 128], BF16, "onesO")
    mstg = [ar.alloc([128, 512], F32, "mstg%d" % i) for i in range(2)]
    Bmx = [Buf("mx0"), Buf("mx1")]
    Bmxn, Bmhs, Bwmkv, BmkT, Bmvp, Bones2 = Buf("mxn"), [Buf("mh0"), Buf("mh1")], Buf("wmkv"), Buf("memkT"), Buf("memvpad"), Buf("ones2")
    Bmss = [Buf("mss0"), Buf("mss1")]
    Bmstg = [Buf("mstg0"), Buf("mstg1")]
    kk.dma("pool", wmkv[:], I["w_mem_kv"].rearrange("(k p) c -> p k c", p=128), writes=[Bwmkv])
    kk.dve(f_memset(memv_pad[:], 0.0), writes=[Bmvp])
    kk.dve(f_memset(onesE[:], 0.0), writes=[Bones2])
    kk.dve(f_memset(onesO[:], 0.0), writes=[Bones2])
    kk.dve(f_memset(onesE[:, 0:64], 1.0), writes=[Bones2])
    kk.dve(f_memset(onesO[:, 64:128], 1.0), writes=[Bones2])
    for mt in range(2):
        norm_transpose(I["mem"][mt * 128:(mt + 1) * 128, :], gmT, mhT, slice(mt * 128, (mt + 1) * 128),
                       (xin0, xin1)[mt], Bmx[mt], mxn, Bmxn, mss[mt], Bmss[mt], banks[mt], pb[mt], Bmhs[mt], mjunk, mt)
    for mt in range(2):
        bk, Bb = banks[2 + mt], pb[2 + mt]
        for kc in range(8):
            kk.pe(mm(bk[:, :], mhT[:, kc, mt * 128:(mt + 1) * 128], wmkv[:, kc, :], kc == 0, kc == 7), reads=[Bmhs[mt], Bwmkv], writes=[Bb])
        kk.act(f_act(mstg[mt][:], bk[:, :], AF.Copy), reads=[Bb], writes=[Bmstg[mt]])
        for h in range(4):
            kk.dve(f_copy(memv_pad[:, mt, h, (h % 2) * 64:(h % 2) * 64 + 64], mstg[mt][:, 256 + h * 64:256 + (h + 1) * 64]), reads=[Bmstg[mt]], writes=[Bmvp])
        kk.dma("sp", O["memk"][mt * 128:(mt + 1) * 128, :], mstg[mt][:, 0:256], reads=[Bmstg[mt]])
        kk.dma("sp", O["memv"][mt * 128:(mt + 1) * 128, :], mstg[mt][:, 256:512], reads=[Bmstg[mt]])
    for hp in range(2):
        bk, Bb = banks[4 + hp], pb[4 + hp]
        for kc in range(8):
            kk.pe(mm(bk[:, 0:256], wmkv[:, kc, hp * 128:(hp + 1) * 128], mhT[:, kc, :], kc == 0, kc == 7), reads=Bmhs + [Bwmkv], writes=[Bb])
        kk.dve(f_copy(memkT[:, hp, :], bk[:, 0:256]), reads=[Bb], writes=[BmkT])

    mpT = [ar.alloc([128, 2, 512], BF16, "mpT%d" % i) for i in range(2)]
    BmpT = [Buf("mpT0"), Buf("mpT1")]
    mrs = [ar.alloc([128, 512], F32, "mrs%d" % i) for i in range(2)]
    Bmrs = [Buf("mrs0"), Buf("mrs1")]
    it = 0
    for tc in range(4):
        o, n = TCH[tc]
        for hp in range(2):
            oc, Boc = banks[4 + 2 * (it % 2)], pb[4 + 2 * (it % 2)]
            oz, Boz = banks[5 + 2 * (it % 2)], pb[5 + 2 * (it % 2)]
            for mt in range(2):
                sa, Bsa = banks[2 * mt], pb[2 * mt]
                sb, Bsb = banks[2 * mt + 1], pb[2 * mt + 1]
                p_, Bp_ = mpT[mt], BmpT[mt]
                kk.pe(mm(sa[:, :], memkT[0:64, hp, mt * 128:(mt + 1) * 128], qmT[0:64, hp, o:o + n], True, True), reads=[BmkT, B_qmT[tc]], writes=[Bsa])
                kk.pe(mm(sb[:, :], memkT[64:128, hp, mt * 128:(mt + 1) * 128], qmT[64:128, hp, o:o + n], True, True), reads=[BmkT, B_qmT[tc]], writes=[Bsb])
                kk.act(f_act(p_[:, 0, :], sa[:, :], AF.Exp, scale=0.125), reads=[Bsa], writes=[Bp_])
                kk.act(f_act(p_[:, 1, :], sb[:, :], AF.Exp, scale=0.125), reads=[Bsb], writes=[Bp_])
                kk.pe(mm(oc[:, :], memv_pad[:, mt, 2 * hp, :], p_[:, 0, :], mt == 0, False), reads=[Bmvp, Bp_], writes=[Boc])
                kk.pe(mm(oc[:, :], memv_pad[:, mt, 2 * hp + 1, :], p_[:, 1, :], False, mt == 1), reads=[Bmvp, Bp_], writes=[Boc])
                kk.pe(mm(oz[:, :], onesE[:, :], p_[:, 0, :], mt == 0, False), reads=[Bones2, Bp_], writes=[Boz])
                kk.pe(mm(oz[:, :], onesO[:, :], p_[:, 1, :], False, mt == 1), reads=[Bones2, Bp_], writes=[Boz])
            r_, Br_ = mrs[it % 2], Bmrs[it % 2]
            kk.act(f_act(r_[:], oz[:, :], AF.Ln), reads=[Boz], writes=[Br_])
            kk.act(f_act(r_[:], r_[:], AF.Exp, scale=-1.0), reads=[Br_], writes=[Br_])
            kk.dve(f_tt(omT[:, hp, o:o + n], oc[:, :], r_[:], ALU.mult), reads=[Boc, Br_], writes=[B_omT[tc]])
            it += 1
    NCM = 4
    cmk = [ar.alloc([128, 2, 256], BF16, "cmk%d" % i) for i in range(NCM)]
    Bcmk = [Buf("cmk%d" % i) for i in range(NCM)]
    cmv = [ar.alloc([128, 2, 256], BF16, "cmv%d" % i) for i in range(NCM)]
    Bcmv = [Buf("cmv%d" % i) for i in range(NCM)]
    cmvp = [ar.alloc([128, 2, 4, 128], BF16, "cmvp%d" % i) for i in range(2)]
    Bcmvp = [Buf("cmvp0"), Buf("cmvp1")]
    kTs = [ar.alloc([128, 2, 256], BF16, "kTs%d" % i) for i in range(2)]
    BkTs = [Buf("kTs0"), Buf("kTs1")]
    pTs = [ar.alloc([128, 2, 32], BF16, "pTs%d" % i) for i in range(2)]
    BpTs = [Buf("pTs0"), Buf("pTs1")]
    for i in range(2):
        kk.pool(f_memset(cmvp[i][:], 0.0), writes=[Bcmvp[i]])
    ocs, Bocs = banks[6], pb[6]
    ozs, Bozs = banks[7], pb[7]

    def load_cm(s2):
        r = s2 % NCM
        kk.dma("pool", cmk[r][:], I["cmem_k"][s2].rearrange("(t p) c -> p t c", p=128), writes=[Bcmk[r]])
        kk.dma("pool", cmv[r][:], I["cmem_v"][s2].rearrange("(t p) c -> p t c", p=128), writes=[Bcmv[r]])

    for s2 in range(NCM - 1):
        load_cm(s2)
    for s_ in range(NSEQ):
        b = s_ % 2
        r = s_ % NCM
        if s_ + NCM - 1 < NSEQ:
            load_cm(s_ + NCM - 1)
        for h in range(4):
            kk.pool(f_copy(cmvp[b][:, :, h, (h % 2) * 64:(h % 2) * 64 + 64], cmv[r][:, :, h * 64:(h + 1) * 64]), reads=[Bcmv[r]], writes=[Bcmvp[b]])
        tb, Btb = banks[b], pb[b]
        tbf = tb[:].bitcast(BF16)
        for hp in range(2):
            for mt in range(2):
                kk.pe(f_tr(tbf[:, (hp * 2 + mt) * 128:(hp * 2 + mt + 1) * 128], cmk[r][:, mt, hp * 128:(hp + 1) * 128], ident_b[:]),
                      reads=[Bcmk[r], Bc], writes=[Btb])
        kk.dve(f_copy(kTs[b][:].rearrange("p h m -> p (h m)"), tbf[:, 0:512]), reads=[Btb], writes=[BkTs[b]])
        sa, Bsa = banks[2 + 2 * b], pb[2 + 2 * b]
        sb, Bsb = banks[3 + 2 * b], pb[3 + 2 * b]
        qs = slice(2048 + 8 * s_, 2048 + 8 * s_ + 8)
        for hp in range(2):
            for mt in range(2):
                c0 = (hp * 2 + mt) * 8
                kk.pe(mm(sa[:, c0:c0 + 8], kTs[b][0:64, hp, mt * 128:(mt + 1) * 128], qmT[0:64, hp, qs], True, True), reads=[BkTs[b], B_qmT[4]], writes=[Bsa])
                kk.pe(mm(sb[:, c0:c0 + 8], kTs[b][64:128, hp, mt * 128:(mt + 1) * 128], qmT[64:128, hp, qs], True, True), reads=[BkTs[b], B_qmT[4]], writes=[Bsb])
        kk.act(f_act(pTs[b][:, 0, :], sa[:, 0:32], AF.Exp, scale=0.125), reads=[Bsa], writes=[BpTs[b]])
        kk.act(f_act(pTs[b][:, 1, :], sb[:, 0:32], AF.Exp, scale=0.125), reads=[Bsb], writes=[BpTs[b]])
        for hp in range(2):
            oc0 = (s_ * 2 + hp) * 8
            for mt in range(2):
                c0 = (hp * 2 + mt) * 8
                kk.pe(mm(ocs[:, oc0:oc0 + 8], cmvp[b][:, mt, 2 * hp, :], pTs[b][:, 0, c0:c0 + 8], mt == 0, False), reads=[Bcmvp[b], BpTs[b]], writes=[Bocs])
                kk.pe(mm(ocs[:, oc0:oc0 + 8], cmvp[b][:, mt, 2 * hp + 1, :], pTs[b][:, 1, c0:c0 + 8], False, mt == 1), reads=[Bcmvp[b], BpTs[b]], writes=[Bocs])
            for mt in range(2):
                c0 = (hp * 2 + mt) * 8
                kk.pe(mm(ozs[:, oc0:oc0 + 8], onesE[:, :], pTs[b][:, 0, c0:c0 + 8], mt == 0, False), reads=[Bones2, BpTs[b]], writes=[Bozs])
                kk.pe(mm(ozs[:, oc0:oc0 + 8], onesO[:, :], pTs[b][:, 1, c0:c0 + 8], False, mt == 1), reads=[Bones2, BpTs[b]], writes=[Bozs])
    kk.act(f_act(mrs[0][:, 0:256], ozs[:, 0:256], AF.Ln), reads=[Bozs], writes=[Bmrs[0]])
    kk.act(f_act(mrs[0][:, 0:256], mrs[0][:, 0:256], AF.Exp, scale=-1.0), reads=[Bmrs[0]], writes=[Bmrs[0]])
    kk.dve(f_tt(omT[:, :, 2048:2176].rearrange("p h (s q) -> p h s q", q=8),
                ocs[:, 0:256].rearrange("p (s h q) -> p h s q", h=2, q=8),
                mrs[0][:, 0:256].rearrange("p (s h q) -> p h s q", h=2, q=8), ALU.mult),
           reads=[Bocs, Bmrs[0]], writes=[B_omT[4]])
    if "omT" in dbg_out:
        odbg = ar.alloc([128, 512], F32, "odbg")
        Bo = Buf("odbg")
        kk.dve(f_copy(odbg[:, 0:128], omT[:, 0, 0:128]), reads=B_omT, writes=[Bo])
        kk.dve(f_copy(odbg[:, 128:256], omT[:, 1, 1920:2048]), reads=B_omT, writes=[Bo])
        kk.dve(f_copy(odbg[:, 256:384], omT[:, 0, 2048:2176]), reads=B_omT, writes=[Bo])
        kk.dve(f_copy(odbg[:, 384:512], omT[:, 1, 2048:2176]), reads=B_omT, writes=[Bo])
        dbg("omT", odbg[:], [Bo])
    kk.barrier()
    if STOP == "p3b":
        return
    ar.reset(wmark)


    apT = [ar.alloc([128, 2, 512], BF16, "apT%d" % i) for i in range(3)]
    BapT = [Buf("apT%d" % i) for i in range(3)]
    tA = ar.alloc([128, 512], F32, "tA")
    tB = ar.alloc([128, 512], F32, "tB")
    tC = ar.alloc([128, 512], F32, "tC")
    tD = ar.alloc([128, 512], F32, "tD")
    tE = ar.alloc([128, 512], F32, "tE")
    BtA, BtB, BtC, BtD, BtE = Buf("tA"), Buf("tB"), Buf("tC"), Buf("tD"), Buf("tE")
    Sset = [((banks[0], pb[0]), (banks[1], pb[1])), ((banks[6], pb[6]), (banks[7], pb[7]))]
    O0, O1, Z0, Z1 = banks[2], banks[3], banks[4], banks[5]
    BO0, BO1, BZ0, BZ1 = pb[2], pb[3], pb[4], pb[5]
    units = []
    for h in range(4):
        for c in range(4):
            for j in range(4 * c + 4):
                units.append((h, c, j))

    def u_lo(c, j):
        return max(j - 4 * c, 0) * 128

    def emit_qk(i):
        h, c, j = units[i]
        (S0, BS0), (S1, BS1) = Sset[i % 2]
        lo = u_lo(c, j)
        q0 = c * 512
        ks = slice(j * 128, (j + 1) * 128)
        kk.pe(mm(S0[:, lo:512], kT[0:64, h, ks], qT[0:64, h, q0 + lo:q0 + 512], True, True), reads=[B_kT[j // 4], B_qT[c]], writes=[BS0])
        kk.pe(mm(S1[:, lo:512], kT[64:128, h, ks], qT[64:128, h, q0 + lo:q0 + 512], True, True), reads=[B_kT[j // 4], B_qT[c]], writes=[BS1])

    def emit_softmax(i):
        h, c, j = units[i]
        (S0, BS0), (S1, BS1) = Sset[i % 2]
        lo = u_lo(c, j)
        jj = j - 4 * c
        pT_, BpT_ = apT[i % 3], BapT[i % 3]
        kk.act(f_act(pT_[:, 0, lo:512], S0[:, lo:512], AF.Exp, scale=0.125), reads=[BS0], writes=[BpT_])
        kk.act(f_act(pT_[:, 1, lo:512], S1[:, lo:512], AF.Exp, scale=0.125), reads=[BS1], writes=[BpT_])
        for m in range(2):
            if jj >= 0:
                kk.dve(f_tt(pT_[:, m, lo:lo + 128], pT_[:, m, lo:lo + 128], T0[:, h, :], ALU.mult), reads=[BpT_, Bt], writes=[BpT_])
                if jj < 3:
                    kk.dve(f_tt(pT_[:, m, lo + 128:lo + 256], pT_[:, m, lo + 128:lo + 256], T1[:, h, :], ALU.mult), reads=[BpT_, Bt], writes=[BpT_])
            elif jj == -1:
                kk.dve(f_tt(pT_[:, m, 0:128], pT_[:, m, 0:128], T1[:, h, :], ALU.mult), reads=[BpT_, Bt], writes=[BpT_])

    def emit_pv(i):
        h, c, j = units[i]
        lo = u_lo(c, j)
        nj = 4 * c + 4
        pT_, BpT_ = apT[i % 3], BapT[i % 3]
        vv = v_bf[:, j, h * 128:(h + 1) * 128]
        for m, (Ob, BOb, Zb, BZb) in enumerate(((O0, BO0, Z0, BZ0), (O1, BO1, Z1, BZ1))):
            kk.pe(mm(Ob[:, lo:512], vv, pT_[:, m, lo:512], j == 0, j == nj - 1), reads=[B_v[j], BpT_], writes=[BOb])
            kk.pe(mm(Zb[:, lo:512], ones_b[:, :], pT_[:, m, lo:512], j == 0, j == nj - 1), reads=[Bc, BpT_], writes=[BZb])

    def emit_tail(h, c, SSb, BSS):
        q0 = c * 512
        kk.act(f_act(tA[:], Z0[:, :], AF.Ln), reads=[BZ0], writes=[BtA])
        kk.act(f_act(tB[:], Z1[:, :], AF.Ln), reads=[BZ1], writes=[BtB])
        kk.act(f_act(tA[:], tA[:], AF.Exp, scale=-1.0), reads=[BtA], writes=[BtA])
        kk.act(f_act(tB[:], tB[:], AF.Exp, scale=-1.0), reads=[BtB], writes=[BtB])
        kk.dve(f_tt(tA[:], O0[:, :], tA[:], ALU.mult), reads=[BO0, BtA], writes=[BtA])
        kk.dve(f_tt(tB[:], O1[:, :], tB[:], ALU.mult), reads=[BO1, BtB], writes=[BtB])
        kk.dve(f_stt(tC[:], tB[:], neg_lam[:, 0:1], tA[:], ALU.mult, ALU.add), reads=[BtA, BtB, Bl], writes=[BtC])
        kk.act(f_act(tD[:], tC[:], AF.Square), reads=[BtC], writes=[BtD])
        kk.pe(mm(SSb[:, :], ones_f[:, :], tD[:], True, True), reads=[Bc, BtD], writes=[BSS])
        kk.act(f_act(tE[:], SSb[:, :], AF.Ln, scale=1.0 / 128, bias=eps_t[:, 0:1]), reads=[BSS, Bc], writes=[BtE])
        kk.act(f_act(tE[:], tE[:], AF.Exp, scale=-0.5), reads=[BtE], writes=[BtE])
        kk.dve(f_stt(oT[:, h, q0:q0 + 512], tC[:], sublnT[:, 0:1], tE[:], ALU.mult, ALU.mult), reads=[BtC, BtE, Bc], writes=[B_oT[c]])

    emit_qk(0)
    for i, (h, c, j) in enumerate(units):
        emit_softmax(i)
        if i + 1 < len(units):
            emit_qk(i + 1)
        emit_pv(i)
        if j == 4 * c + 3:
            (SSb, BSS), _ = Sset[i % 2]
            emit_tail(h, c, SSb, BSS)
    if "oTp" in dbg_out:
        odbg2 = ar.alloc([128, 512], F32, "odbg2")
        Bo2 = Buf("odbg2")
        kk.dve(f_copy(odbg2[:, 0:128], oT[:, 0, 0:128]), reads=B_oT, writes=[Bo2])
        kk.dve(f_copy(odbg2[:, 128:256], oT[:, 1, 640:768]), reads=B_oT, writes=[Bo2])
        kk.dve(f_copy(odbg2[:, 256:384], oT[:, 2, 1920:2048]), reads=B_oT, writes=[Bo2])
        kk.dve(f_copy(odbg2[:, 384:512], oT[:, 3, 1024:1152]), reads=B_oT, writes=[Bo2])
        dbg("oTp", odbg2[:], [Bo2])
    kk.barrier()
    if STOP == "p3c":
        return
    ar.reset(wmark)


    ptb = ar.alloc([128, NSEQ * NPAGE], I32, "ptb")
    idx = ar.alloc([128, NSEQ * NPAGE], I32, "idx")
    iotaf = ar.alloc([128, 1], F32, "iotaf")
    qpad = ar.alloc([128, 4, NSEQ, 16], BF16, "qpad")
    M15 = ar.alloc([128, 4, 2, 8], F32, "M15")
    MN = ar.alloc([128, NSEQ, 4, 2, 8], F32, "MN")
    gbc = ar.alloc([128, 128], F32, "gbc")
    NV = 14
    kvpg = [ar.alloc([128, 1024], BF16, "kvpg%d" % i) for i in range(NV)]
    Bkvpg = [Buf("kvpg%d" % i) for i in range(NV)]
    KTs = [ar.alloc([128, 4, 128], BF16, "KTs%d" % i) for i in range(2)]
    BKTs = [Buf("KTs0"), Buf("KTs1")]
    spT = [ar.alloc([128, 8, 64], BF16, "spT%d" % i) for i in range(2)]
    BspT = [Buf("spT0"), Buf("spT1")]
    pn = ar.alloc([128, 64], BF16, "pn")
    Bpn = Buf("pn")
    rz = ar.alloc([64, 1], F32, "rz")
    onr = ar.alloc([64, 512], F32, "onr")
    c2 = ar.alloc([32, 4, 128], F32, "c2")
    sq2 = ar.alloc([32, 4, 128], F32, "sq2")
    ss2 = ar.alloc([32, 8], F32, "ss2")
    on3 = ar.alloc([32, 4, 128], BF16, "on3")
    Brz, Bonr, Bc2, Bsq2, Bss2, Bon3 = Buf("rz"), Buf("onr"), Buf("c2"), Buf("sq2"), Buf("ss2"), Buf("on3")
    Bsetup = Buf("p3dsetup")
    kk.dma("sp", ptb[:], I["page_table"][0, :].partition_broadcast(128), writes=[Bsetup])
    kk.dma("sp", iotaf[:], I["iota_f"][:, :], writes=[Bsetup])
    kk.dve(f_ts(idx[:], ptb[:], 128.0, iotaf[:, 0:1], ALU.mult, ALU.add), reads=[Bsetup], writes=[Bsetup])
    kk.dve(f_memset(qpad[:], 0.0), writes=[Bsetup])
    kk.dve(f_copy(qpad[0:64, :, :, 0:8], qT[0:64, :, 2048:2176].rearrange("p h (s q) -> p h s q", q=8)), reads=[B_qT[4], Bsetup], writes=[Bsetup])
    kk.dve(f_copy(qpad[64:128, :, :, 8:16], qT[64:128, :, 2048:2176].rearrange("p h (s q) -> p h s q", q=8)), reads=[B_qT[4], Bsetup], writes=[Bsetup])
    for c in range(2):
        kk.dve(f_copy(M15[:, :, c, :], T1[:, :, 0:8]), reads=[Bt], writes=[Bsetup])
    for h in range(4):
        for c in range(2):
            kk.dve(f_tt(MN[:, :, h, c, :], T0[:, h, :].rearrange("p (s q) -> p s q", q=8), bdiag[:].unsqueeze(2).to_broadcast([128, NSEQ, 8]), ALU.mult),
                   reads=[Bt, Bc], writes=[Bsetup])
    kk.dma("sp", gbc[:], I["subln_g"].partition_broadcast(128), writes=[Bsetup])
    kk.dve(f_ts(gbc[:], gbc[:], 1.0 - LAM_INIT, None, ALU.mult), reads=[Bsetup], writes=[Bsetup])
    OS, BOS = banks[2], pb[2]
    ZS, BZS = banks[3], pb[3]
    C2b, BC2 = banks[4], pb[4]
    TTb, BTT = banks[5], pb[5]
    steps = [(s_, j) for s_ in range(NSEQ) for j in range(NPAGE)]

    def pg_bufs(n):
        return kvpg[n % NV], Bkvpg[n % NV], banks[n % 2], pb[n % 2], KTs[n % 2], BKTs[n % 2]

    def emit_gather(n):
        s_, j = steps[n]
        kvb, Bkvb = kvpg[n % NV], Bkvpg[n % NV]
        col = s_ * NPAGE + j
        kk.op("pool", (lambda e, kvb=kvb, col=col: e.indirect_dma_start(
            out=kvb[:, :], out_offset=None, in_=I["cache_kv"][:, :],
            in_offset=bass.IndirectOffsetOnAxis(ap=idx[:, col:col + 1], axis=0))), reads=[Bsetup], writes=[Bkvb], dma=True)

    def emit_tr(n):
        kvb, Bkvb, tb, Btb, kt_, Bkt_ = pg_bufs(n)
        tbf = tb[:].bitcast(BF16)
        for h in range(4):
            kk.pe(f_tr(tbf[:, h * 128:(h + 1) * 128], kvb[:, h * 128:(h + 1) * 128], ident_b[:]), reads=[Bkvb, Bc], writes=[Btb])
        kk.dve(f_copy(kt_[:].rearrange("p h k -> p (h k)"), tbf[:, 0:512]), reads=[Btb], writes=[Bkt_])

    def emit_qk_s(n):
        s_, j = steps[n]
        kvb, Bkvb, tb, Btb, kt_, Bkt_ = pg_bufs(n)
        Sb, BSb = banks[6 + (j // 8)], pb[6 + (j // 8)]
        for h in range(4):
            c0 = (j % 8) * 64 + h * 16
            kk.pe(mm(Sb[:, c0:c0 + 16], kt_[:, h, :], qpad[:, h, s_, :], True, True), reads=[Bkt_, Bsetup], writes=[BSb])

    NPRE = NV - 8
    for n in range(min(NPRE, len(steps))):
        emit_gather(n)
    emit_tr(0)
    for n, (s_, j) in enumerate(steps):
        if n + NPRE < len(steps):
            emit_gather(n + NPRE)
        if n + 1 < len(steps):
            emit_tr(n + 1)
        emit_qk_s(n)
        if j % 8 == 7:
            half = j // 8
            Sb, BSb = banks[6 + half], pb[6 + half]
            sp_, Bsp_ = spT[half], BspT[half]
            kk.act(f_act(sp_[:].rearrange("p j c -> p (j c)"), Sb[:, :], AF.Exp, scale=0.125), reads=[BSb], writes=[Bsp_])
            if half == 1:
                kk.dve(f_tt(sp_[:, 7, :], sp_[:, 7, :], M15[:].rearrange("p h c q -> p (h c q)"), ALU.mult), reads=[Bsp_, Bsetup], writes=[Bsp_])
            for jj in range(8):
                jp = half * 8 + jj
                m_ = n - 7 + jj
                kvb, Bkvb = kvpg[m_ % NV], Bkvpg[m_ % NV]
                kk.pe(mm(OS[0:64, :], sp_[:, jj, :], kvb[:, 512:1024], jp == 0, False), reads=[Bsp_, Bkvb], writes=[BOS])
                kk.pe(mm(ZS[0:64, 0:1], sp_[:, jj, :], ones_b[:, 0:1], jp == 0, False), reads=[Bsp_, Bc], writes=[BZS])
        if j != NPAGE - 1:
            continue
        Sb, BSb = banks[6], pb[6]
        for h in range(4):
            kk.pe(mm(Sb[:, h * 16:(h + 1) * 16], kT[:, h, 2048:2176], qpad[:, h, s_, :], True, True), reads=[B_kT[4], Bsetup], writes=[BSb])
        kk.act(f_act(pn[:], Sb[:, 0:64], AF.Exp, scale=0.125), reads=[BSb], writes=[Bpn])
        kk.dve(f_tt(pn[:], pn[:], MN[:, s_].rearrange("p h c q -> p (h c q)"), ALU.mult), reads=[Bpn, Bsetup], writes=[Bpn])
        kk.pe(mm(OS[0:64, :], pn[:, :], v_bf[:, 16, :], False, True), reads=[Bpn, B_v[16]], writes=[BOS])
        kk.pe(mm(ZS[0:64, 0:1], pn[:, :], ones_b[:, 0:1], False, True), reads=[Bpn, Bc], writes=[BZS])
        kk.dve(f_recip(rz[:], ZS[0:64, 0:1]), reads=[BZS], writes=[Brz])
        kk.dve(f_ts(onr[:], OS[0:64, :], rz[:, 0:1], None, ALU.mult), reads=[BOS, Brz], writes=[Bonr])
        kk.pe(mm(C2b[0:32, :], comb[:, :], onr[:, :], True, True), reads=[Bl, Bonr], writes=[BC2])
        kk.dve(f_copy(c2[:].rearrange("p h e -> p (h e)"), C2b[0:32, :]), reads=[BC2], writes=[Bc2])
        kk.act(f_act(sq2[:], c2[:], AF.Square), reads=[Bc2], writes=[Bsq2])
        kk.dve(lambda e: e.tensor_reduce(out=ss2[:, 0:4], in_=sq2[:], axis=AX.X, op=ALU.add), reads=[Bsq2], writes=[Bss2])
        kk.act(f_act(ss2[:, 4:8], ss2[:, 0:4], AF.Sqrt, scale=1.0 / 128, bias=eps_t[0:32, 0:1]), reads=[Bss2, Bc], writes=[Bss2])
        kk.dve(f_recip(ss2[:, 4:8], ss2[:, 4:8]), reads=[Bss2], writes=[Bss2])
        kk.dve(f_tt(c2[:], c2[:], ss2[:, 4:8].unsqueeze(2).to_broadcast([32, 4, 128]), ALU.mult), reads=[Bc2, Bss2], writes=[Bc2])
        kk.dve(f_tt(on3[:], c2[:], gbc[0:32, :].unsqueeze(1).to_broadcast([32, 4, 128]), ALU.mult), reads=[Bc2, Bsetup], writes=[Bon3])
        ttf = TTb[:].bitcast(BF16)
        for h in range(4):
            kk.pe(f_tr(ttf[:, h * 32:(h + 1) * 32], on3[:, h, :], ident_b[0:32, 0:32]), reads=[Bon3, Bc], writes=[BTT])
        for h in range(4):
            kk.dve(f_copy(oT[:, h, 2048 + 8 * s_:2048 + 8 * s_ + 8], ttf[:, h * 32 + h * 8:h * 32 + h * 8 + 8]), reads=[BTT], writes=[B_oT[4]])
    if "oTs" in dbg_out:
        odbg3 = ar.alloc([128, 512], F32, "odbg3")
        Bo3 = Buf("odbg3")
        kk.dve(f_copy(odbg3[:].rearrange("p (h t) -> p h t", h=4), oT[:, :, 2048:2176]), reads=B_oT, writes=[Bo3])
        dbg("oTs", odbg3[:], [Bo3])
    kk.barrier()
    if STOP == "p3d":
        return
    ar.reset(wmark)


    ar.reset(omark)
    mergedT = ar.alloc([128, 8, NTOK], BF16, "mergedT")
    B_mg = [Buf("mg%d" % i) for i in range(len(TCH))]
    p4mark = ar.mark()
    wg = [ar.alloc([128, 8, 3, 128], BF16, "wg%d" % i) for i in range(2)]
    Bwg = [Buf("wg0"), Buf("wg1")]
    wbr = [ar.alloc([128, 8, 128], BF16, "wbr%d" % i) for i in range(2)]
    Bwbr = [Buf("wbr0"), Buf("wbr1")]
    sg = [ar.alloc([128, 512], F32, "sg%d" % i) for i in range(6)]
    Bsg = [Buf("sg%d" % i) for i in range(6)]
    mt_ = [ar.alloc([128, 512], F32, "mtmp%d" % i) for i in range(4)]
    Bmt = [Buf("mtmp%d" % i) for i in range(4)]
    bankctr = [0]

    def rbank():
        b = bankctr[0] % 8
        bankctr[0] += 1
        return banks[b], pb[b]

    def load_p4(fc):
        b = fc % 2
        for g in range(3):
            c0 = 2048 + g * 1024 + fc * 128
            kk.dma("pool", wg[b][:, :, g, :], I["w_in"][:, c0:c0 + 128].rearrange("(k p) c -> p k c", p=128), writes=[Bwg[b]])
        kk.dma("pool", wbr[b][:, 0:4, :], I["w_br_attn"][:, fc * 128:(fc + 1) * 128].rearrange("(k p) c -> p k c", p=128), writes=[Bwbr[b]])
        kk.dma("pool", wbr[b][:, 4:6, :], I["w_br_pool"][:, fc * 128:(fc + 1) * 128].rearrange("(k p) c -> p k c", p=128), writes=[Bwbr[b]])
        kk.dma("pool", wbr[b][:, 6:8, :], I["w_br_mem"][:, fc * 128:(fc + 1) * 128].rearrange("(k p) c -> p k c", p=128), writes=[Bwbr[b]])

    load_p4(0)
    un = 0
    for fc in range(8):
        if fc + 1 < 8:
            load_p4(fc + 1)
        b = fc % 2
        for tc in range(len(TCH)):
            o, n = TCH[tc]
            hreads = [B_hT[t] for t in tiles_of(tc)]
            prods = []
            for g in range(3):
                gb, Bgb = rbank()
                for kc in range(8):
                    kk.pe(mm(gb[:, 0:n], wg[b][:, kc, g, :], hT[:, kc, o:o + n], kc == 0, kc == 7), reads=[Bwg[b]] + hreads, writes=[Bgb])
                bb, Bbb = rbank()
                if g == 0:
                    for h in range(4):
                        kk.pe(mm(bb[:, 0:n], wbr[b][:, h, :], oT[:, h, o:o + n], h == 0, h == 3), reads=[Bwbr[b], B_oT[tc]], writes=[Bbb])
                elif g == 1:
                    for ch in range(2):
                        kk.pe(mm(bb[:, 0:n], wbr[b][:, 4 + ch, :], poolT[:, ch, o:o + n], ch == 0, ch == 1), reads=[Bwbr[b], B_poolT[tc]], writes=[Bbb])
                else:
                    for hp in range(2):
                        kk.pe(mm(bb[:, 0:n], wbr[b][:, 6 + hp, :], omT[:, hp, o:o + n], hp == 0, hp == 1), reads=[Bwbr[b], B_omT[tc]], writes=[Bbb])
                si = (un * 3 + g) % 6
                kk.act(f_act(sg[si][:, 0:n], gb[:, 0:n], AF.Sigmoid), reads=[Bgb], writes=[Bsg[si]])
                kk.dve(f_tt(sg[si][:, 0:n], sg[si][:, 0:n], bb[:, 0:n], ALU.mult), reads=[Bsg[si], Bbb], writes=[Bsg[si]])
                prods.append(si)
            mi = un % 4
            kk.dve(f_tt(mt_[mi][:, 0:n], sg[prods[0]][:, 0:n], sg[prods[1]][:, 0:n], ALU.add), reads=[Bsg[prods[0]], Bsg[prods[1]]], writes=[Bmt[mi]])
            kk.dve(f_tt(mergedT[:, fc, o:o + n], mt_[mi][:, 0:n], sg[prods[2]][:, 0:n], ALU.add), reads=[Bmt[mi], Bsg[prods[2]]], writes=[B_mg[tc]])
            un += 1
    if "mergedT" in dbg_out:
        mdbg = ar.alloc([128, 512], F32, "mdbg")
        Bm_ = Buf("mdbg")
        kk.dve(f_copy(mdbg[:, 0:128], mergedT[:, 0, 0:128]), reads=B_mg, writes=[Bm_])
        kk.dve(f_copy(mdbg[:, 128:256], mergedT[:, 7, 1024:1152]), reads=B_mg, writes=[Bm_])
        kk.dve(f_copy(mdbg[:, 256:384], mergedT[:, 3, 2048:2176]), reads=B_mg, writes=[Bm_])
        kk.dve(f_copy(mdbg[:, 384:512], mergedT[:, 5, 2048:2176]), reads=B_mg, writes=[Bm_])
        dbg("mergedT", mdbg[:], [Bm_])
    kk.barrier()
    if STOP == "p4":
        return
    ar.reset(p4mark)

    arA = Arena(nc, amark, omark)
    wout = arA.alloc([128, 8, D], BF16, "wout")
    Bwout = Buf("wout")
    wd = arA.alloc([128, NFF, D], BF16, "wd")
    Bwd = [Buf("wd%d" % i) for i in range(NFF)]
    wgu = [arA.alloc([128, 8, 2, 128], BF16, "wgu%d" % i) for i in range(2)]
    Bwgu = [Buf("wgu%d" % i) for i in range(3)]
    stc = [ar.alloc([32, 512], F32, "stc%d" % i) for i in range(2)]
    stT = ar.alloc([128, NFF, NSEQ, 2], F32, "stT")
    cs = ar.alloc([128, NFF, 34], F32, "cs")
    halo = [ar.alloc([128, NFF, 2], F32, "halo%d" % i) for i in range(2)]
    Bstc, BstT, Bcs, Bhalo = [Buf("stc0"), Buf("stc1")], Buf("stT"), Buf("cs"), [Buf("halo0"), Buf("halo1")]
    for k2 in range(2):
        kk.dma("pool", wout[:, k2 * 4:(k2 + 1) * 4, :], I["w_out"][k2 * 512:(k2 + 1) * 512, :].rearrange("(k p) c -> p k c", p=128), writes=[Bwout])
    kk.dve(f_memset(halo[0][:], 0.0), writes=[Bhalo[0]])
    for q4 in range(6):
        nf = min(4, NFF - q4 * 4)
        kk.dma("sp", stc[q4 % 2][:, 0:nf * 128], I["state_conv"][:, q4 * 512:q4 * 512 + nf * 128], writes=[Bstc[q4 % 2]])
        for i in range(nf):
            fcx = q4 * 4 + i
            bk, Bb = rbank()
            kk.pe(mm(bk[:, 0:32], stc[q4 % 2][:, i * 128:(i + 1) * 128], ident_f[0:32, 0:32], True, True), reads=[Bstc[q4 % 2], Bc], writes=[Bb])
            kk.dve(f_copy(stT[:, fcx].rearrange("p s r -> p (s r)"), bk[:, 0:32]), reads=[Bb], writes=[BstT])
    x2 = ar.alloc([128, 4, D], F32, "x2")
    Bx2 = [Buf("x2_%d" % i) for i in range(4)]
    h2T = ar.alloc([128, 8, 512], BF16, "h2T")
    Bh2 = [Buf("h2_%d" % i) for i in range(4)]
    actT = ar.alloc([128, NFF, 512], BF16, "actT")
    Bact = [Buf("act%d" % i) for i in range(NFF)]
    gS = [ar.alloc([128, 2 + 512], F32, "gS%d" % i) for i in range(2)]
    BgS = [Buf("gS0"), Buf("gS1")]
    gSs = [ar.alloc([128, NSEQ, 10], F32, "gSs%d" % i) for i in range(2)]
    BgSs = [Buf("gSs0"), Buf("gSs1")]
    c1 = [ar.alloc([128, 512], F32, "c1_%d" % i) for i in range(2)]
    Bc1 = [Buf("c1_0"), Buf("c1_1")]
    ge = [ar.alloc([128, 512], F32, "ge%d" % i) for i in range(2)]
    Bge = [Buf("ge0"), Buf("ge1")]
    xin5 = [ar.alloc([128, D], F32, "xin5_%d" % i) for i in range(2)]
    Bxin5 = [Buf("xin5_0"), Buf("xin5_1")]
    xn5 = ar.alloc([128, D], BF16, "xn5")
    Bxn5 = Buf("xn5")
    junk5 = ar.alloc([128, D], BF16, "junk5")
    Bjunk5 = Buf("junk5")
    ss5 = [ar.alloc([128, 4], F32, "ss5_%d" % i) for i in range(2)]
    Bss5 = [Buf("ss5_0"), Buf("ss5_1")]
    wgu.append(ar.alloc([128, 8, 2, 128], BF16, "wgu2"))
    yt = xin5
    Byt = Bxin5
    csr = [ar.alloc([34, 512], F32, "csr%d" % i) for i in range(2)]
    Bcsr = [Buf("csr0"), Buf("csr1")]
    for fcx in range(NFF):
        kk.dma("pool", wd[:, fcx, :], I["w_ffn_down"][fcx * 128:(fcx + 1) * 128, :], writes=[Bwd[fcx]])
    nwl = [0]
    nt5 = 0
    for gi, (t0, t1) in enumerate(GROUPS):
        ntile = t1 - t0
        smp = gi == len(GROUPS) - 1
        lastp = gi == len(GROUPS) - 2
        ntk = ntile * 128
        for li in range(ntile):
            t = t0 + li
            xb_, Bxb_ = xin5[nt5 % 2], Bxin5[nt5 % 2]
            kk.dma("sp", xb_[:], I["x_all"][t * 128:(t + 1) * 128, :], writes=[Bxb_])
            for half in range(2):
                bk, Bb = rbank()
                for kc in range(8):
                    kk.pe(mm(bk[:, :], mergedT[:, kc, t * 128:(t + 1) * 128], wout[:, kc, half * 512:(half + 1) * 512], kc == 0, kc == 7),
                          reads=[B_mg[t // 4], Bwout], writes=[Bb])
                kk.dve(f_tt(x2[:, li, half * 512:(half + 1) * 512], bk[:, :], xb_[:, half * 512:(half + 1) * 512], ALU.add), reads=[Bb, Bxb_], writes=[Bx2[li]])
            bk, Bb = rbank()
            norm_transpose(None, g2T, h2T, slice(li * 128, (li + 1) * 128), x2[:, li, :], Bx2[li], xn5, Bxn5,
                           ss5[nt5 % 2], Bss5[nt5 % 2], bk, Bb, Bh2[li], junk5, t)
            nt5 += 1
        n = ntk
        hin, hout = halo[gi % 2], halo[(gi + 1) % 2]
        Bhin, Bhout = Bhalo[gi % 2], Bhalo[(gi + 1) % 2]
        for fcx in range(NFF):
            wi = nwl[0] % 3
            nwl[0] += 1
            wflat = wgu[wi][:].rearrange("p k g c -> p (k g c)")
            if gi == 0:
                kk.dma("pool", wgu[wi][:, :, 0, :], I["w_ffn_gate"][:, fcx * 128:(fcx + 1) * 128].rearrange("(k p) c -> p k c", p=128), writes=[Bwgu[wi]])
                kk.dma("pool", wgu[wi][:, :, 1, :], I["w_ffn_up"][:, fcx * 128:(fcx + 1) * 128].rearrange("(k p) c -> p k c", p=128), writes=[Bwgu[wi]])
                kk.dma("sp", wscr[fcx], wflat, reads=[Bwgu[wi]], writes=[Bwscr[fcx]])
            else:
                kk.dma("sp", wflat, wscr[fcx], reads=[Bwscr[fcx]], writes=[Bwgu[wi]])
            bi = fcx % 2
            g_, Bg_ = gS[bi], BgS[bi]
            gs_, Bgs_ = gSs[bi], BgSs[bi]
            c_, Bc_ = c1[bi], Bc1[bi]
            e_, Be_ = ge[bi], Bge[bi]
            hreads = [Bh2[i] for i in range(ntile)]
            gb, Bgb = rbank()
            for kc in range(8):
                kk.pe(mm(gb[:, 0:n], wgu[wi][:, kc, 0, :], h2T[:, kc, 0:n], kc == 0, kc == 7), reads=[Bwgu[wi]] + hreads, writes=[Bgb])
            ub, Bub = rbank()
            for kc in range(8):
                kk.pe(mm(ub[:, 0:n], wgu[wi][:, kc, 1, :], h2T[:, kc, 0:n], kc == 0, kc == 7), reads=[Bwgu[wi]] + hreads, writes=[Bub])
            w0, w1, w2, bb_ = convw[:, 0, fcx:fcx + 1], convw[:, 1, fcx:fcx + 1], convw[:, 2, fcx:fcx + 1], convb[:, fcx:fcx + 1]
            if not smp:
                kk.act(f_act(g_[:, 2:2 + 512], gb[:, 0:512], AF.Copy), reads=[Bgb], writes=[Bg_])
                kk.dve(f_copy(g_[:, 0:2], hin[:, fcx, :]), reads=[Bhin], writes=[Bg_])
                kk.dve(f_copy(hout[:, fcx, :], g_[:, 512:514]), reads=[Bg_], writes=[Bhout])
                kk.dve(f_ts(c_[:, 0:512], g_[:, 0:512], w0, bb_, ALU.mult, ALU.add), reads=[Bg_, Bc], writes=[Bc_])
                kk.dve(f_stt(c_[:, 0:512], g_[:, 1:513], w1, c_[:, 0:512], ALU.mult, ALU.add), reads=[Bg_, Bc_, Bc], writes=[Bc_])
                kk.dve(f_stt(c_[:, 0:512], g_[:, 2:514], w2, c_[:, 0:512], ALU.mult, ALU.add), reads=[Bg_, Bc_, Bc], writes=[Bc_])
                if lastp:
                    kk.dve(f_copy(cs[:, fcx, 0:2], g_[:, 512:514]), reads=[Bg_], writes=[Bcs])
            else:
                kk.act(f_act(gs_[:, :, 2:10], gb[:, 0:128].rearrange("p (s i) -> p s i", i=8), AF.Copy), reads=[Bgb], writes=[Bgs_])
                kk.dve(f_copy(gs_[:, :, 0:2], stT[:, fcx]), reads=[BstT], writes=[Bgs_])
                cv = c_[:, 0:128].rearrange("p (s i) -> p s i", i=8)
                kk.dve(f_ts(cv, gs_[:, :, 0:8], w0, bb_, ALU.mult, ALU.add), reads=[Bgs_, Bc], writes=[Bc_])
                kk.dve(f_stt(cv, gs_[:, :, 1:9], w1, cv, ALU.mult, ALU.add), reads=[Bgs_, Bc_, Bc], writes=[Bc_])
                kk.dve(f_stt(cv, gs_[:, :, 2:10], w2, cv, ALU.mult, ALU.add), reads=[Bgs_, Bc_, Bc], writes=[Bc_])
                kk.dve(f_copy(cs[:, fcx, 2:34].rearrange("p (s r) -> p s r", r=2), gs_[:, :, 8:10]), reads=[Bgs_], writes=[Bcs])
            kk.act(f_act(e_[:, 0:n], c_[:, 0:n], AF.Gelu_apprx_tanh), reads=[Bc_], writes=[Be_])
            kk.dve(f_tt(actT[:, fcx, 0:n], e_[:, 0:n], ub[:, 0:n], ALU.mult), reads=[Be_, Bub], writes=[Bact[fcx]])
        for li in range(ntile):
            t = t0 + li
            for half in range(2):
                bk, Bb = rbank()
                for fcx in range(NFF):
                    kk.pe(mm(bk[:, :], actT[:, fcx, li * 128:(li + 1) * 128], wd[:, fcx, half * 512:(half + 1) * 512], fcx == 0, fcx == NFF - 1),
                          reads=[Bact[fcx], Bwd[fcx]], writes=[Bb])
                kk.dve(f_tt(x2[:, li, half * 512:(half + 1) * 512], bk[:, :], x2[:, li, half * 512:(half + 1) * 512], ALU.add), reads=[Bb, Bx2[li]], writes=[Bx2[li]])
            si = nt5 % 2
            nt5 += 1
            kk.act(f_act(junk5[:], x2[:, li, :], AF.Square, accum_out=ss5[si][:, 0:1]), reads=[Bx2[li]], writes=[Bss5[si], Bjunk5])
            kk.act(f_act(ss5[si][:, 1:2], ss5[si][:, 0:1], AF.Sqrt, scale=1.0 / D, bias=eps_t[:, 0:1]), reads=[Bss5[si], Bc], writes=[Bss5[si]])
            kk.dve(f_recip(ss5[si][:, 2:3], ss5[si][:, 1:2]), reads=[Bss5[si]], writes=[Bss5[si]])
            kk.dve(f_stt(yt[si][:], x2[:, li, :], ss5[si][:, 2:3], gfin[:], ALU.mult, ALU.mult), reads=[Bx2[li], Bss5[si], Bc], writes=[Byt[si]])
            kk.dma("sp", O["y_all"][t * 128:(t + 1) * 128, :], yt[si][:], reads=[Byt[si]])
    for q4 in range(6):
        bk, Bb = rbank()
        nf = min(4, NFF - q4 * 4)
        for i in range(nf):
            fcx = q4 * 4 + i
            kk.pe(mm(bk[0:34, i * 128:(i + 1) * 128], cs[:, fcx, :], ident_f[:, :], True, True), reads=[Bcs, Bc], writes=[Bb])
        kk.dve(f_copy(csr[q4 % 2][:, 0:nf * 128], bk[0:34, 0:nf * 128]), reads=[Bb], writes=[Bcsr[q4 % 2]])
        kk.dma("sp", O["conv_all"][:, q4 * 512:q4 * 512 + nf * 128], csr[q4 % 2][:, 0:nf * 128], reads=[Bcsr[q4 % 2]])

    kk.barrier()


_NC_CACHE = {}


def kernel(**inputs):
    f32 = lambda a: np.ascontiguousarray(np.asarray(a, dtype=np.float32))
    if "nc" not in _NC_CACHE:
        _NC_CACHE["nc"] = build_program()
    nc = _NC_CACHE["nc"]
    consts = host_constants()
    x_prompt = f32(inputs["x_prompt"])
    x_sample = f32(inputs["x_sample"])
    mem_prompt = f32(inputs["mem_prompt"])
    cache_kv = np.concatenate([f32(inputs["cache_k"]).reshape(-1, 512), f32(inputs["cache_v"]).reshape(-1, 512)], axis=1)
    page_table = np.ascontiguousarray(np.asarray(inputs["page_table"], dtype=np.int32))
    state_pool = f32(inputs["state_pool"])[0]
    state_conv = f32(inputs["state_ffn_conv"])[0]
    cmk = f32(inputs["cache_mem_k"])[0]
    cmv = f32(inputs["cache_mem_v"])[0]
    shared = {
        "cache_kv": cache_kv,
        "norm1_g": f32(inputs["norm1_g"])[0], "w_in": f32(inputs["w_in"])[0],
        "lam_q1": f32(inputs["lam_q1"]), "lam_k1": f32(inputs["lam_k1"]),
        "lam_q2": f32(inputs["lam_q2"]), "lam_k2": f32(inputs["lam_k2"]),
        "subln_g": f32(inputs["subln_g"])[0], "w_pool_grp": f32(inputs["w_pool_grp"])[0],
        "pool_scale": f32(inputs["pool_scale"])[0],
        "w_br_attn": f32(inputs["w_br_attn"])[0], "w_br_pool": f32(inputs["w_br_pool"])[0],
        "w_br_mem": f32(inputs["w_br_mem"])[0], "mem_norm_g": f32(inputs["mem_norm_g"])[0],
        "w_mem_kv": f32(inputs["w_mem_kv"])[0], "w_out": f32(inputs["w_out"])[0],
        "norm2_g": f32(inputs["norm2_g"])[0], "w_ffn_gate": f32(inputs["w_ffn_gate"])[0],
        "w_ffn_up": f32(inputs["w_ffn_up"])[0], "ffn_conv_w": f32(inputs["ffn_conv_w"])[0],
        "ffn_conv_b": f32(inputs["ffn_conv_b"])[0], "w_ffn_down": f32(inputs["w_ffn_down"])[0],
        "rel_bias": f32(inputs["rel_bias"]), "final_norm_g": f32(inputs["final_norm_g"]),
    }
    shared.update(consts)
    in_maps = []
    for c in range(8):
        sl = slice(NSEQ * c, NSEQ * (c + 1))
        m = dict(shared)
        m["x_all"] = np.ascontiguousarray(np.concatenate([x_prompt[c], x_sample[sl].reshape(128, D)], axis=0))
        m["mem"] = mem_prompt[c]
        m["page_table"] = np.ascontiguousarray(page_table[sl].reshape(1, NSEQ * NPAGE))
        m["state_pool"] = np.ascontiguousarray(state_pool[sl].reshape(NSEQ * 15, 256))
        m["state_conv"] = np.ascontiguousarray(state_conv[sl].reshape(NSEQ * 2, D_FF))
        m["cmem_k"] = np.ascontiguousarray(cmk[sl].reshape(NSEQ, 256, 256))
        m["cmem_v"] = np.ascontiguousarray(cmv[sl].reshape(NSEQ, 256, 256))
        in_maps.append(m)
    res = run_bass_kernel_spmd(nc, in_maps, core_ids=list(range(8)))
    R = res.results
    g = lambda k: [np.asarray(R[c][k], dtype=np.float32) for c in range(8)]
    y = g("y_all"); nk = g("newk"); nv = g("newv")
    y_prompt = np.stack([a[:2048] for a in y], 0)
    y_sample = np.concatenate([a[2048:].reshape(NSEQ, 8, D) for a in y], 0)
    nkp = np.stack([a[:2048].reshape(2048, 4, 128) for a in nk], 0)[None]
    nvp = np.stack([a[:2048].reshape(2048, 4, 128) for a in nv], 0)[None]
    nks = np.concatenate([a[2048:].reshape(NSEQ, 8, 4, 128) for a in nk], 0)[None]
    nvs = np.concatenate([a[2048:].reshape(NSEQ, 8, 4, 128) for a in nv], 0)[None]
    pp = np.stack(g("pool_p"), 0)[None]
    ps = np.concatenate(g("pool_s"), 0)[None]
    cv = g("conv_all")
    cp = np.stack([a[:2] for a in cv], 0)[None]
    cs = np.concatenate([a[2:].reshape(NSEQ, 2, D_FF) for a in cv], 0)[None]
    mk = np.stack([a.reshape(256, 4, 64) for a in g("memk")], 0)[None]
    mv = np.stack([a.reshape(256, 4, 64) for a in g("memv")], 0)[None]
    return (y_prompt, y_sample, nkp, nvp, nks, nvs, pp, ps, cp, cs, mk, mv)
```

```python
import numpy as np
from contextlib import ExitStack

import concourse.bass as bass
import concourse.mybir as mybir
from concourse.bass_utils import run_bass_kernel_spmd

F32 = mybir.dt.float32
BF16 = mybir.dt.bfloat16
I32 = mybir.dt.int32
AF = mybir.ActivationFunctionType
ALU = mybir.AluOpType
AX = mybir.AxisListType

D = 1024
NTOK = 2176
NT = 17
D_IN = 5120
D_FF = 2816
NFF = 22
EPS = 1e-6
LAM_INIT = 0.8 - 0.6
NSEQ = 16
NPAGE = 16
WZ = 384

TCH = [(0, 512), (512, 512), (1024, 512), (1536, 512), (2048, 128)]
GROUPS = [(0, 4), (4, 8), (8, 12), (12, 16), (16, 17)]


class Buf:
    __slots__ = ("name", "w", "r")

    def __init__(self, name):
        self.name = name
        self.w = None
        self.r = []


class Op:
    __slots__ = ("eng", "fn", "waits", "signal", "idx", "count", "dma", "dsem", "dval", "pre")

    def __init__(self, eng, fn, dma):
        self.eng = eng
        self.fn = fn
        self.waits = []
        self.signal = False
        self.idx = -1
        self.count = 0
        self.dma = dma
        self.dsem = None
        self.dval = 0
        self.pre = None


ENGS = ("pe", "act", "dve", "pool", "sp")
NDSEM = 24


class K:
    def __init__(self, nc, es):
        self.nc = nc
        self.ops = {e: [] for e in ENGS}
        self.waited = {e: {p: -1 for p in ENGS} for e in ENGS}
        self.waited_dma = {e: set() for e in ENGS}
        self.sem = {e: es.enter_context(nc.semaphore("s_" + e)) for e in ENGS}
        self.dsems = {q: [es.enter_context(nc.semaphore("d_%s%d" % (q, i))) for i in range(NDSEM)]
                      for q in ("sp", "pool")}
        self.ndma = {"sp": 0, "pool": 0}
        self.dma_ops = {"sp": [], "pool": []}

    def _dep(self, op, d, force=False):
        e = op.eng
        if d is None or d is op:
            return
        if d.dma:
            if id(d) in self.waited_dma[e]:
                return
            self.waited_dma[e].add(id(d))
            op.waits.append(d)
            return
        p = d.eng
        if p == "pe" and e == "pe" and not force:
            return
        if self.waited[e][p] >= d.idx:
            return
        self.waited[e][p] = d.idx
        d.signal = True
        op.waits.append(d)

    def op(self, eng, fn, reads=(), writes=(), dma=False):
        o = Op(eng, fn, dma)
        o.idx = len(self.ops[eng])
        deps = []
        for b in reads:
            if b.w is not None:
                deps.append(b.w)
        for b in writes:
            if b.w is not None:
                deps.append(b.w)
            deps.extend(b.r)
        latest = {}
        for d in deps:
            if d.dma:
                self._dep(o, d)
            elif d.eng not in latest or latest[d.eng].idx < d.idx:
                latest[d.eng] = d
        for d in latest.values():
            self._dep(o, d)
        if dma:
            n = self.ndma[eng]
            self.ndma[eng] += 1
            o.dsem = self.dsems[eng][n % NDSEM]
            o.dval = 16 * (n // NDSEM + 1)
            if n >= NDSEM:
                prev = self.dma_ops[eng][n - NDSEM]
                o.pre = prev
            self.dma_ops[eng].append(o)
        self.ops[eng].append(o)
        for b in reads:
            b.r.append(o)
        for b in writes:
            b.w = o
            b.r = []
        return o

    def pe(self, fn, reads=(), writes=()):
        return self.op("pe", fn, reads, writes)

    def act(self, fn, reads=(), writes=()):
        return self.op("act", fn, reads, writes)

    def dve(self, fn, reads=(), writes=()):
        return self.op("dve", fn, reads, writes)

    def pool(self, fn, reads=(), writes=()):
        return self.op("pool", fn, reads, writes)

    def dma(self, q, out, in_, reads=(), writes=(), **kw):
        return self.op(q, lambda e: e.dma_start(out=out, in_=in_, **kw), reads, writes, dma=True)

    def barrier(self):
        lasts = []
        for e in ENGS:
            real = [o for o in self.ops[e] if o.fn is not None and not o.dma]
            if real:
                lasts.append(real[-1])
        dmas = self.dma_ops["sp"][-NDSEM:] + self.dma_ops["pool"][-NDSEM:]
        for e in ENGS:
            o = Op(e, None, False)
            o.idx = len(self.ops[e])
            for d in lasts + dmas:
                self._dep(o, d, force=True)
            self.ops[e].append(o)

    def emit(self, block):
        for e in ENGS:
            c = 0
            for o in self.ops[e]:
                if o.signal:
                    c += 1
                    o.count = c
        def run(e, eng):
            for o in self.ops[e]:
                if o.pre is not None:
                    eng.wait_ge(o.pre.dsem, o.pre.dval)
                for d in o.waits:
                    if d.dma:
                        eng.wait_ge(d.dsem, d.dval)
                    else:
                        eng.wait_ge(self.sem[d.eng], d.count)
                if o.fn is None:
                    continue
                ins = o.fn(eng)
                if o.dma:
                    ins.then_inc(o.dsem, 16)
                elif o.signal:
                    ins.then_inc(self.sem[e], 1)

        @block.tensor
        def _(eng):
            run("pe", eng)

        @block.scalar
        def _(eng):
            run("act", eng)

        @block.vector
        def _(eng):
            run("dve", eng)

        @block.gpsimd
        def _(eng):
            run("pool", eng)

        @block.sync
        def _(eng):
            run("sp", eng)


class Arena:
    def __init__(self, nc, base, cap):
        self.nc = nc
        self.base = base
        self.cap = cap
        self.top = base
        self.n = 0

    def alloc(self, shape, dtype, name=None):
        nbytes = int(np.prod(shape[1:])) * mybir.dt.size(dtype)
        off = (self.top + 31) // 32 * 32
        assert off + nbytes <= self.cap, ("SBUF arena overflow", name, off, nbytes, self.cap)
        self.top = off + nbytes
        self.n += 1
        nm = "%s_%d_%d" % (name or "t", off, self.n)
        return self.nc.alloc_sbuf_tensor_at(nm, list(shape), dtype, offset=off)

    def mark(self):
        return self.top

    def reset(self, m):
        self.top = m


def rel_bucket_np(rel):
    n = np.maximum(rel, 0)
    max_exact = 16
    nf = np.maximum(n, 1).astype(np.float32)
    large = max_exact + (np.log(nf / max_exact) / np.log(128 / max_exact) * (32 - max_exact)).astype(np.int32)
    large = np.minimum(large, 31)
    return np.where(n < max_exact, n, large)


def host_constants():
    c = {}
    c["ident"] = np.eye(128, dtype=np.float32)
    rel = np.arange(WZ) - 128
    b = rel_bucket_np(rel)
    oh = np.zeros((32, WZ), np.float32)
    oh[b, np.arange(WZ)] = 1.0
    oh[:, rel < 0] = 0.0
    c["bucket_oh"] = oh
    c["relmask"] = np.repeat((rel >= 0).astype(np.float32)[None, :], 128, axis=0)
    pc = np.zeros((128, 2, 16), np.float32)
    for ch in range(2):
        for p in range(128):
            w = 2 ** (2 * ch + p // 64 + 1)
            pc[p, ch, :] = 1.0 / np.minimum(np.arange(16) + 1, w)
    c["poolc"] = pc
    bd = np.zeros((128, 16), np.float32)
    bd[np.arange(128), np.arange(128) // 8] = 1.0
    c["blockdiag"] = bd
    sel = np.zeros((64, 2, 32), np.float32)
    for h in range(4):
        for cc in range(2):
            for q in range(8):
                sel[h * 16 + cc * 8 + q, cc, h * 8 + q] = 1.0
    c["sel"] = sel
    c["iota_f"] = np.arange(128, dtype=np.float32).reshape(128, 1)
    return c


CONST_SHAPES = {
    "ident": ([128, 128], F32), "bucket_oh": ([32, WZ], F32), "relmask": ([128, WZ], F32),
    "poolc": ([128, 2, 16], F32), "blockdiag": ([128, 16], F32), "sel": ([64, 2, 32], F32),
    "iota_f": ([128, 1], F32),
}

IN_SHAPES = {
    "x_all": ([NTOK, D], F32), "mem": ([256, D], F32),
    "cache_kv": ([2560 * 128, 1024], F32),
    "page_table": ([1, NSEQ * NPAGE], I32),
    "state_pool": ([NSEQ * 15, 256], F32), "state_conv": ([NSEQ * 2, D_FF], F32),
    "cmem_k": ([NSEQ, 256, 256], F32), "cmem_v": ([NSEQ, 256, 256], F32),
    "norm1_g": ([D], F32), "w_in": ([D, D_IN], F32),
    "lam_q1": ([1, 64], F32), "lam_k1": ([1, 64], F32), "lam_q2": ([1, 64], F32), "lam_k2": ([1, 64], F32),
    "subln_g": ([128], F32), "w_pool_grp": ([4, 64, 64], F32), "pool_scale": ([256], F32),
    "w_br_attn": ([512, D], F32), "w_br_pool": ([256, D], F32), "w_br_mem": ([256, D], F32),
    "mem_norm_g": ([D], F32), "w_mem_kv": ([D, 512], F32), "w_out": ([D, D], F32),
    "norm2_g": ([D], F32), "w_ffn_gate": ([D, D_FF], F32), "w_ffn_up": ([D, D_FF], F32),
    "ffn_conv_w": ([3, D_FF], F32), "ffn_conv_b": ([D_FF], F32), "w_ffn_down": ([D_FF, D], F32),
    "rel_bias": ([32, 4], F32), "final_norm_g": ([D], F32),
}

OUT_SHAPES = {
    "y_all": [NTOK, D], "newk": [NTOK, 512], "newv": [NTOK, 512],
    "pool_p": [15, 256], "pool_s": [NSEQ, 15, 256],
    "conv_all": [2 + 2 * NSEQ, D_FF],
    "memk": [256, 256], "memv": [256, 256],
}


def build_program(phases=("all",), debug=None, nphys=2560):
    nc = bass.Bass("TRN2", target_bir_lowering=False)
    I = {}
    for k, (shp, dt) in {**IN_SHAPES, **CONST_SHAPES}.items():
        if k == "cache_kv":
            shp = [nphys * 128, 1024]
        I[k] = nc.dram_tensor(k, shp, dt, kind="ExternalInput").ap()
    O = {}
    for k, shp in OUT_SHAPES.items():
        O[k] = nc.dram_tensor(k, shp, F32, kind="ExternalOutput").ap()
    zscr = nc.dram_tensor("zscr", [128, 4 * WZ], F32, kind="Internal").ap()
    wscr_t = nc.dram_tensor("wscr", [NFF, 128, 2048], BF16, kind="Internal").ap()
    dbg_out = {}
    if debug:
        for k, shp in debug.items():
            dbg_out[k] = nc.dram_tensor("dbg_" + k, shp, F32, kind="ExternalOutput").ap()

    with ExitStack() as es:
        kk = K(nc, es)
        banks = [es.enter_context(nc.psum_tensor("bank%d" % i, [128, 512], F32)) for i in range(8)]
        pb = [Buf("psum%d" % i) for i in range(8)]
        block = es.enter_context(nc.Block())
        _build(nc, kk, I, O, zscr, banks, pb, dbg_out, phases, wscr_t)
        kk.emit(block)
    return nc


def _build(nc, kk, I, O, zscr, banks, pb, dbg_out, phases, wscr):
    Bwscr = [Buf("wscr%d" % i) for i in range(NFF)]
    ALL = "all" in phases
    STOP = [p for p in phases if p.startswith("p")]
    STOP = STOP[0] if STOP else None
    ar = Arena(nc, (nc.sbuf_base + 63) // 64 * 64, nc.sbuf_top)

    def mm(out, lhsT, rhs, start, stop):
        return lambda e: e.matmul(out, lhsT, rhs, start=start, stop=stop)

    def f_tt(out, in0, in1, op):
        return lambda e: e.tensor_tensor(out=out, in0=in0, in1=in1, op=op)

    def f_ts(out, in0, s1, s2, op0, op1=None):
        if op1 is None:
            return lambda e: e.tensor_scalar(out=out, in0=in0, scalar1=s1, scalar2=None, op0=op0)
        return lambda e: e.tensor_scalar(out=out, in0=in0, scalar1=s1, scalar2=s2, op0=op0, op1=op1)

    def f_stt(out, in0, scalar, in1, op0, op1):
        return lambda e: e.scalar_tensor_tensor(out=out, in0=in0, scalar=scalar, in1=in1, op0=op0, op1=op1)

    def f_copy(out, in_):
        return lambda e: e.tensor_copy(out=out, in_=in_)

    def f_act(out, in_, func, **kw):
        return lambda e: e.activation(out=out, in_=in_, func=func, **kw)

    def f_recip(out, in_):
        return lambda e: e.reciprocal(out=out, in_=in_)

    def f_memset(ap, v):
        return lambda e: e.memset(ap, v)

    def f_tr(out, in_, ident):
        return lambda e: e.transpose(out, in_, ident)

    cst = {}
    ident_f = ar.alloc([128, 128], F32, "identf")
    ident_b = ar.alloc([128, 128], BF16, "identb")
    ones_f = ar.alloc([128, 128], F32, "onesf")
    ones_b = ar.alloc([128, 128], BF16, "onesb")
    g1T = ar.alloc([128, 8], F32, "g1T")
    g2T = ar.alloc([128, 8], F32, "g2T")
    gmT = ar.alloc([128, 8], F32, "gmT")
    gfin = ar.alloc([128, D], F32, "gfin")
    sublnT = ar.alloc([128, 1], F32, "subln")
    pscaleT = ar.alloc([128, 2], F32, "pscale")
    convw = ar.alloc([128, 3, NFF], F32, "convw")
    convb = ar.alloc([128, NFF], F32, "convb")
    lamv = ar.alloc([128, 4, 64], F32, "lamv")
    lamt = ar.alloc([128, 8], F32, "lamt")
    neg_lam = ar.alloc([128, 1], F32, "neglam")
    eps_t = ar.alloc([128, 1], F32, "eps")
    poolc = ar.alloc([128, 2, 16], F32, "poolc")
    bdiag = ar.alloc([128, 16], F32, "bdiag")
    selc = ar.alloc([64, 2, 32], F32, "sel")
    comb = ar.alloc([64, 32], F32, "comb")
    Bc = Buf("consts")

    kk.dma("sp", ident_f[:], I["ident"][:, :], writes=[Bc])
    kk.dma("pool", ident_b[:], I["ident"][:, :], writes=[Bc])
    kk.dve(lambda e: e.memset(ones_f[:], 1.0), writes=[Bc])
    kk.dve(lambda e: e.memset(ones_b[:], 1.0), writes=[Bc])
    kk.dve(lambda e: e.memset(eps_t[:], EPS), writes=[Bc])
    for t, src in ((g1T, "norm1_g"), (gmT, "mem_norm_g")):
        kk.dma("sp", t[:], I[src].rearrange("(k p) -> p k", p=128), writes=[Bc], allow_slow_non_contiguous=True)
    kk.dma("sp", gfin[:], I["final_norm_g"].partition_broadcast(128), writes=[Bc])
    kk.dma("sp", sublnT[:], I["subln_g"].rearrange("(p o) -> p o", o=1), writes=[Bc], allow_slow_non_contiguous=True)
    kk.dma("sp", pscaleT[:], I["pool_scale"].rearrange("(k p) -> p k", p=128), writes=[Bc], allow_slow_non_contiguous=True)
    for i, nm in enumerate(("lam_q1", "lam_k1", "lam_q2", "lam_k2")):
        kk.dma("sp", lamv[:, i, :], I[nm][0, :].partition_broadcast(128), writes=[Bc])
    kk.dma("sp", poolc[:], I["poolc"][:, :, :], writes=[Bc])
    kk.dma("sp", bdiag[:], I["blockdiag"][:, :], writes=[Bc])
    kk.dma("sp", selc[:], I["sel"][:, :, :], writes=[Bc])
    Bl = Buf("lam")
    kk.dve(lambda e: e.tensor_tensor(out=lamv[:, 0, :], in0=lamv[:, 0, :], in1=lamv[:, 1, :], op=ALU.mult), reads=[Bc], writes=[Bl])
    kk.dve(lambda e: e.tensor_tensor(out=lamv[:, 2, :], in0=lamv[:, 2, :], in1=lamv[:, 3, :], op=ALU.mult), reads=[Bl], writes=[Bl])
    kk.dve(lambda e: e.tensor_reduce(out=lamt[:, 0:1], in_=lamv[:, 0, :], axis=AX.X, op=ALU.add), reads=[Bl], writes=[Bl])
    kk.dve(lambda e: e.tensor_reduce(out=lamt[:, 1:2], in_=lamv[:, 2, :], axis=AX.X, op=ALU.add), reads=[Bl], writes=[Bl])
    kk.act(lambda e: e.activation(out=lamt[:, 2:4], in_=lamt[:, 0:2], func=AF.Exp), reads=[Bl], writes=[Bl])
    kk.dve(lambda e: e.tensor_tensor(out=lamt[:, 4:5], in0=lamt[:, 3:4], in1=lamt[:, 2:3], op=ALU.subtract), reads=[Bl], writes=[Bl])
    kk.dve(lambda e: e.tensor_scalar(out=neg_lam[:], in0=lamt[:, 4:5], scalar1=-LAM_INIT, scalar2=None, op0=ALU.add), reads=[Bl], writes=[Bl])
    kk.dve(lambda e: e.scalar_tensor_tensor(out=comb[:], in0=selc[:, 1, :], scalar=neg_lam[0:64, 0:1], in1=selc[:, 0, :],
                                            op0=ALU.mult, op1=ALU.add), reads=[Bl, Bc], writes=[Bl])
    kk.dve(lambda e: e.tensor_scalar(out=sublnT[:], in0=sublnT[:], scalar1=1.0 - LAM_INIT, scalar2=None, op0=ALU.mult), reads=[Bc], writes=[Bc])

    T0 = ar.alloc([128, 4, 128], F32, "T0")
    T1 = ar.alloc([128, 4, 128], F32, "T1")
    cmark = ar.mark()
    art = Arena(nc, nc.sbuf_top - 12 * 1024, nc.sbuf_top)
    rb = art.alloc([32, 4], F32, "rb")
    rbrep = art.alloc([32, 4, 128], F32, "rbrep")
    oh = art.alloc([32, WZ], F32, "oh")
    relmask = art.alloc([128, WZ], F32, "relmask")
    erow = art.alloc([128, 4, WZ], F32, "erow")
    Bt = Buf("T")
    kk.dma("sp", rb[:], I["rel_bias"][:, :], writes=[Bt])
    kk.dma("sp", oh[:], I["bucket_oh"][:, :], writes=[Bt])
    kk.dma("sp", relmask[:], I["relmask"][:, :], writes=[Bt])
    for h in range(4):
        kk.dve(lambda e, h=h: e.tensor_copy(out=rbrep[:, h, :], in_=rb[:, h:h + 1].to_broadcast([32, 128])), reads=[Bt], writes=[Bt])
    for h in range(4):
        kk.pe(mm(banks[0][:, 0:WZ], rbrep[:, h, :], oh[:, :], True, True), reads=[Bt], writes=[pb[0]])
        kk.dve(lambda e, h=h: e.tensor_scalar(out=erow[:, h, 0:1], in0=banks[0][:, WZ - 1:WZ], scalar1=-1.0, scalar2=None, op0=ALU.mult),
               reads=[pb[0]], writes=[Bt])
        kk.act(lambda e, h=h: e.activation(out=erow[:, h, 1:WZ], in_=banks[0][:, 1:WZ], func=AF.Exp, bias=erow[:, h, 0:1]),
               reads=[pb[0], Bt], writes=[Bt])
        kk.dve(lambda e, h=h: e.tensor_tensor(out=erow[:, h, :], in0=erow[:, h, :], in1=relmask[:, :], op=ALU.mult), reads=[Bt], writes=[Bt])
    Bz = Buf("zscr")
    kk.dma("sp", zscr[:, :], erow[:].rearrange("p h w -> p (h w)"), reads=[Bt], writes=[Bz])
    for h in range(4):
        s0 = bass.AP(zscr.tensor, h * WZ + 128, [[4 * WZ - 1, 128], [1, 128]])
        s1 = bass.AP(zscr.tensor, h * WZ + 256, [[4 * WZ - 1, 128], [1, 128]])
        kk.dma("sp", T0[:, h, :], s0, reads=[Bz], writes=[Bt])
        kk.dma("sp", T1[:, h, :], s1, reads=[Bz], writes=[Bt])

    def dbg(name, ap, rd):
        if name in dbg_out:
            kk.dma("sp", dbg_out[name], ap, reads=rd)

    dbg("T0", T0[:].rearrange("p h w -> p (h w)"), [Bt])
    dbg("T1", T1[:].rearrange("p h w -> p (h w)"), [Bt])
    dbg("neglam", neg_lam[:], [Bl])
    if STOP == "p0":
        kk.barrier()
        return
    ar.reset(cmark)


    amark = ar.mark()
    hT = ar.alloc([128, 8, NTOK], BF16, "hT")
    oT = ar.alloc([128, 4, NTOK], BF16, "oT")
    poolT = ar.alloc([128, 2, NTOK], BF16, "poolT")
    omT = ar.alloc([128, 2, NTOK], BF16, "omT")
    omark = ar.mark()
    qT = ar.alloc([128, 4, NTOK], BF16, "qT")
    kT = ar.alloc([128, 4, NTOK], BF16, "kT")
    v_bf = ar.alloc([128, NT, 512], BF16, "vbf")
    qmT = ar.alloc([128, 2, NTOK], BF16, "qmT")
    B_hT = [Buf("hT%d" % t) for t in range(NT)]
    B_qT = [Buf("qT%d" % i) for i in range(len(TCH))]
    B_kT = [Buf("kT%d" % i) for i in range(len(TCH))]
    B_qmT = [Buf("qmT%d" % i) for i in range(len(TCH))]
    B_v = [Buf("v%d" % t) for t in range(NT)]
    B_oT = [Buf("oT%d" % i) for i in range(len(TCH))]
    B_poolT = [Buf("poolT%d" % i) for i in range(len(TCH))]
    B_omT = [Buf("omT%d" % i) for i in range(len(TCH))]
    wmark = ar.mark()

    def tiles_of(tc):
        o, n = TCH[tc]
        return list(range(o // 128, (o + n) // 128))

    Bc5x = []

    def norm_stats(src_rows, xin, Bx, xn, Bxn, ss, Bss, junk):
        if src_rows is not None:
            kk.dma("sp", xin[:], src_rows, writes=[Bx])
        kk.act(f_act(junk[:], xin[:], AF.Square, accum_out=ss[:, 0:1]), reads=[Bx], writes=[Bss, Bjunk])
        kk.act(f_act(ss[:, 1:2], ss[:, 0:1], AF.Sqrt, scale=1.0 / D, bias=eps_t[:, 0:1]), reads=[Bss, Bc], writes=[Bss])
        kk.dve(f_recip(ss[:, 2:3], ss[:, 1:2]), reads=[Bss], writes=[Bss])
        kk.dve(f_ts(xn[:], xin[:], ss[:, 2:3], None, ALU.mult), reads=[Bx, Bss], writes=[Bxn])

    def norm_tr(gT, dst, dst_cols, xn, Bxn, bank, Bbank, Bdst):
        pbf = bank[:].bitcast(BF16)
        for kc in range(8):
            kk.pe(f_tr(pbf[:, kc * 128:(kc + 1) * 128], xn[:, kc * 128:(kc + 1) * 128], ident_b[:]), reads=[Bxn, Bc], writes=[Bbank])
        kk.dve(f_tt(dst[:, :, dst_cols], pbf[:, 0:1024].rearrange("p (k t) -> p k t", k=8),
                    gT[:, :].unsqueeze(2).to_broadcast([128, 8, 128]), ALU.mult),
               reads=[Bbank, Bc] + Bc5x, writes=[Bdst])

    def norm_transpose(src_rows, gT, dst, dst_cols, xin, Bx, xn, Bxn, ss, Bss, bank, Bbank, Bdst, junk, i):
        norm_stats(src_rows, xin, Bx, xn, Bxn, ss, Bss, junk)
        norm_tr(gT, dst, dst_cols, xn, Bxn, bank, Bbank, Bdst)

    xins = [ar.alloc([128, D], F32, "xin%d" % i) for i in range(3)]
    Bxins = [Buf("xin%d" % i) for i in range(3)]
    xns = [ar.alloc([128, D], BF16, "xn%d" % i) for i in range(2)]
    Bxns = [Buf("xn%d" % i) for i in range(2)]
    sss = [ar.alloc([128, 4], F32, "ss%d" % i) for i in range(3)]
    Bsss = [Buf("ss%d" % i) for i in range(3)]
    junk = ar.alloc([128, D], BF16, "junk")
    Bjunk = Buf("junk")
    for t in range(NT):
        norm_transpose(I["x_all"][t * 128:(t + 1) * 128, :], g1T, hT, slice(t * 128, (t + 1) * 128),
                       xins[t % 3], Bxins[t % 3], xns[t % 2], Bxns[t % 2], sss[t % 3], Bsss[t % 3],
                       banks[t % 2], pb[t % 2], B_hT[t], junk, t)
    if "hT" in dbg_out:
        hdbg = ar.alloc([128, 8 * 128], F32, "hdbg")
        Bh = Buf("hdbg")
        kk.dve(lambda e: e.tensor_copy(out=hdbg[:].rearrange("p (k t) -> p k t", k=8), in_=hT[:, :, 2048:2176]), reads=B_hT, writes=[Bh])
        dbg("hT", hdbg[:], [Bh])
    kk.barrier()
    if STOP == "p1":
        return
    ar.reset(wmark)

    Bc5 = Buf("consts5")
    kk.dma("sp", g2T[:], I["norm2_g"].rearrange("(k p) -> p k", p=128), writes=[Bc5], allow_slow_non_contiguous=True)
    kk.dma("sp", convw[:], I["ffn_conv_w"].rearrange("j (c p) -> p j c", p=128), writes=[Bc5], allow_slow_non_contiguous=True)
    kk.dma("sp", convb[:], I["ffn_conv_b"].rearrange("(c p) -> p c", p=128), writes=[Bc5], allow_slow_non_contiguous=True)
    Bc5x.append(Bc5)
    wps = [ar.alloc([128, 8, 512], BF16, "wp%d" % i) for i in range(2)]
    Bwps = [Buf("wp%d" % i) for i in range(2)]
    stg = [ar.alloc([128, 512], F32, "stg%d" % i) for i in range(3)]
    Bstg = [Buf("stg%d" % i) for i in range(3)]
    nstg = [0]
    Eb = ar.alloc([128, 15 + 2048], F32, "Eb")
    Es = ar.alloc([128, NSEQ, 23], F32, "Es")
    W1 = ar.alloc([128, 15 + 2048], F32, "W1")
    W1s = ar.alloc([128, NSEQ, 23], F32, "W1s")
    W2 = ar.alloc([128, 15 + 2048], F32, "W2")
    W2s = ar.alloc([128, NSEQ, 23], F32, "W2s")
    dTb = ar.alloc([128, NTOK], BF16, "dTb")
    tmp16 = ar.alloc([128, 16], F32, "tmp16")
    bdw = ar.alloc([128, 128], BF16, "bdw")
    stp = ar.alloc([120, 2, 256], F32, "stp")
    BE, BW1, BW2, BdT, Bbdw, Bstp, Bt16 = Buf("E"), Buf("W1"), Buf("W2"), Buf("dT"), Buf("bdw"), Buf("stp"), Buf("t16")

    def load_wpiece(i, c0):
        kk.dma("pool", wps[i][:], I["w_in"][:, c0:c0 + 512].rearrange("(k p) c -> p k c", p=128), writes=[Bwps[i]])

    def fm_group(wp, Bwp, col0, tc, bank, Bbank):
        o, n = TCH[tc]
        for kc in range(8):
            kk.pe(mm(bank[:, 0:n], wp[:, kc, col0:col0 + 128], hT[:, kc, o:o + n], kc == 0, kc == 7),
                  reads=[Bwp] + [B_hT[t] for t in tiles_of(tc)], writes=[Bbank])

    def tm_group(wp, Bwp, t, bank, Bbank, ncols=512, c0=0):
        for kc in range(8):
            kk.pe(mm(bank[:, 0:ncols], hT[:, kc, t * 128:(t + 1) * 128], wp[:, kc, c0:c0 + ncols], kc == 0, kc == 7),
                  reads=[Bwp, B_hT[t]], writes=[Bbank])

    nb = [0]

    def next_bank():
        b = nb[0] % 4
        nb[0] += 1
        return banks[b], pb[b]

    ev = [0]

    def evac_copy(out_ap, in_ap, reads, writes):
        ev[0] += 1
        if ev[0] % 2 == 0:
            kk.dve(lambda e: e.tensor_copy(out=out_ap, in_=in_ap), reads=reads, writes=writes)
        else:
            kk.act(lambda e: e.activation(out=out_ap, in_=in_ap, func=AF.Copy), reads=reads, writes=writes)

    wps.append(ar.alloc([128, 8, 512], BF16, "wp2"))
    Bwps.append(Buf("wp2"))
    WU, WQ, WK, WV = 0, 1, 2, 1
    load_wpiece(WU, 1536)
    load_wpiece(WQ, 0)
    load_wpiece(WK, 512)

    def pool_gen():
        kk.dve(f_memset(Eb[:, 0:15], 0.0), writes=[BE])
        kk.dve(f_memset(bdw[:], 0.0), writes=[Bbdw])
        kk.dma("sp", stp[:, 0, :], I["state_pool"][0:120, :], writes=[Bstp])
        kk.dma("sp", stp[:, 1, :], I["state_pool"][120:240, :], writes=[Bstp])
        yield
        for ch in range(2):
            for tc in range(len(TCH)):
                o, n = TCH[tc]
                bk, Bb = next_bank()
                fm_group(wps[WU], Bwps[WU], ch * 128, tc, bk, Bb)
                if tc < 4:
                    evac_copy(Eb[:, 15 + o:15 + o + n], bk[:, 0:n], [Bb], [BE])
                else:
                    evac_copy(Es[:, :, 15:23], bk[:, 0:128].rearrange("p (s i) -> p s i", i=8), [Bb], [BE])
                yield
            for j in range(2):
                bk, Bb = next_bank()
                kk.pe(mm(bk[:, 0:120], stp[:, j, ch * 128:(ch + 1) * 128], ident_f[0:120, 0:120], True, True), reads=[Bstp, Bc], writes=[Bb])
                evac_copy(Es[:, j * 8:(j + 1) * 8, 0:15], bk[:, 0:120].rearrange("p (s r) -> p s r", r=15), [Bb], [BE])
                yield

            def dbl(dst, dsts, src, srcs, sh, first):
                lo = 2 * sh - 1
                kk.dve(f_tt(dst[:, lo:], src[:, lo:], src[:, lo - sh:15 + 2048 - sh], ALU.add),
                       reads=[first], writes=[BW1 if dst is W1 else BW2])
                kk.dve(f_tt(dsts[:, :, lo:], srcs[:, :, lo:], srcs[:, :, lo - sh:23 - sh], ALU.add),
                       reads=[first], writes=[BW1 if dst is W1 else BW2])
            dbl(W1, W1s, Eb, Es, 1, BE)
            yield
            dbl(W2, W2s, W1, W1s, 2, BW1)
            yield
            if ch == 1:
                dbl(W1, W1s, W2, W2s, 4, BW2)
                yield
                dbl(W2, W2s, W1, W1s, 8, BW1)
                yield
            for half, (Wb, Wbs, BWb) in enumerate(((W1, W1s, BW1), (W2, W2s, BW2))):
                ps = slice(half * 64, (half + 1) * 64)
                kk.dve(f_stt(dTb[ps, 0:2048], Wb[ps, 15:15 + 2048], poolc[ps, ch, 15:16], Eb[ps, 15:15 + 2048], ALU.mult, ALU.subtract),
                       reads=[BWb, BE, Bc], writes=[BdT])
                yield
                kk.dve(f_tt(tmp16[ps, :], Wb[ps, 15:31], poolc[ps, ch, :], ALU.mult), reads=[BWb, Bc], writes=[Bt16])
                kk.dve(f_tt(dTb[ps, 0:16], tmp16[ps, :], Eb[ps, 15:31], ALU.subtract), reads=[Bt16, BE, BdT], writes=[BdT])
                kk.dve(f_stt(dTb[ps, 2048:2176].rearrange("p (s i) -> p s i", i=8), Wbs[ps, :, 15:23], poolc[ps, ch, 15:16],
                             Es[ps, :, 15:23], ALU.mult, ALU.subtract),
                       reads=[BWb, BE, Bc, BdT], writes=[BdT])
                yield
            for half in range(2):
                ps = slice(half * 64, (half + 1) * 64)
                kk.dma("pool", bdw[ps, half * 64:(half + 1) * 64], I["w_pool_grp"][2 * ch + half, :, :], reads=[Bbdw], writes=[Bbdw])
            for tc in range(len(TCH)):
                o, n = TCH[tc]
                bk, Bb = next_bank()
                kk.pe(mm(bk[:, 0:n], bdw[:, :], dTb[:, o:o + n], True, True), reads=[Bbdw, BdT], writes=[Bb])
                kk.dve(f_ts(poolT[:, ch, o:o + n], bk[:, 0:n], pscaleT[:, ch:ch + 1], None, ALU.mult), reads=[Bb, Bc], writes=[B_poolT[tc]])
                yield

    pgen = pool_gen()

    def pstep():
        next(pgen, None)

    for hp in range(2):
        for tc in range(len(TCH)):
            o, n = TCH[tc]
            bk, Bb = next_bank()
            fm_group(wps[WU], Bwps[WU], 256 + hp * 128, tc, bk, Bb)
            evac_copy(qmT[:, hp, o:o + n], bk[:, 0:n], [Bb], [B_qmT[tc]])
    for t in (15, 16):
        bk, Bb = next_bank()
        tm_group(wps[WU], Bwps[WU], t, bk, Bb, ncols=256, c0=0)
        si = nstg[0] % 3
        nstg[0] += 1
        evac_copy(stg[si][:, 0:256], bk[:, 0:256], [Bb], [Bstg[si]])
        if t == 15:
            kk.dma("sp", O["pool_p"][:, :], stg[si][113:128, 0:256], reads=[Bstg[si]])
        else:
            for s_ in range(NSEQ):
                kk.dma("sp", O["pool_s"][s_, 7:15, :], stg[si][s_ * 8:(s_ + 1) * 8, 0:256], reads=[Bstg[si]])
    kk.dma("sp", O["pool_s"][:, 0:7, :], I["state_pool"].rearrange("(s r) c -> s r c", r=15)[:, 8:15, :])
    for h in range(4):
        for tc in range(len(TCH)):
            o, n = TCH[tc]
            bk, Bb = next_bank()
            fm_group(wps[WQ], Bwps[WQ], h * 128, tc, bk, Bb)
            evac_copy(qT[:, h, o:o + n], bk[:, 0:n], [Bb], [B_qT[tc]])
            pstep()
    load_wpiece(WV, 1024)
    for h in range(4):
        for tc in range(len(TCH)):
            o, n = TCH[tc]
            bk, Bb = next_bank()
            fm_group(wps[WK], Bwps[WK], h * 128, tc, bk, Bb)
            evac_copy(kT[:, h, o:o + n], bk[:, 0:n], [Bb], [B_kT[tc]])
            pstep()
    for t in range(NT):
        bk, Bb = next_bank()
        tm_group(wps[WK], Bwps[WK], t, bk, Bb)
        si = nstg[0] % 3
        nstg[0] += 1
        evac_copy(stg[si][:], bk[:, :], [Bb], [Bstg[si]])
        kk.dma("sp", O["newk"][t * 128:(t + 1) * 128, :], stg[si][:], reads=[Bstg[si]])
        pstep()
    for t in range(NT):
        bk, Bb = next_bank()
        tm_group(wps[WV], Bwps[WV], t, bk, Bb)
        si = nstg[0] % 3
        nstg[0] += 1
        kk.act(f_act(stg[si][:], bk[:, :], AF.Copy), reads=[Bb], writes=[Bstg[si]])
        kk.dve(f_copy(v_bf[:, t, :], stg[si][:]), reads=[Bstg[si]], writes=[B_v[t]])
        kk.dma("sp", O["newv"][t * 128:(t + 1) * 128, :], stg[si][:], reads=[Bstg[si]])
        pstep()
    for _ in pgen:
        pass
    if "poolT" in dbg_out:
        pdbg = ar.alloc([128, 2 * 256], F32, "pdbg")
        Bp = Buf("pdbg")
        kk.dve(lambda e: e.tensor_copy(out=pdbg[:, 0:128], in_=poolT[:, 0, 0:128]), reads=B_poolT, writes=[Bp])
        kk.dve(lambda e: e.tensor_copy(out=pdbg[:, 128:256], in_=poolT[:, 1, 0:128]), reads=B_poolT, writes=[Bp])
        kk.dve(lambda e: e.tensor_copy(out=pdbg[:, 256:384], in_=poolT[:, 0, 2048:2176]), reads=B_poolT, writes=[Bp])
        kk.dve(lambda e: e.tensor_copy(out=pdbg[:, 384:512], in_=poolT[:, 1, 2048:2176]), reads=B_poolT, writes=[Bp])
        dbg("poolT", pdbg[:], [Bp])
    kk.barrier()
    if STOP == "p2":
        return
    ar.reset(wmark)


    P3 = ALL or "p3" in phases
    xin0 = ar.alloc([128, D], F32, "mxin0")
    xin1 = ar.alloc([128, D], F32, "mxin1")
    mxn = ar.alloc([128, D], BF16, "mxn")
    mjunk = ar.alloc([128, D], BF16, "mjunk")
    mss = [ar.alloc([128, 4], F32, "mss%d" % i) for i in range(2)]
    mhT = ar.alloc([128, 8, 256], BF16, "mhT")
    wmkv = ar.alloc([128, 8, 512], BF16, "wmkv")
    memkT = ar.alloc([128, 2, 256], BF16, "memkT")
    memv_pad = ar.alloc([128, 2, 4, 128], BF16, "memvpad")
    onesE = ar.alloc([128, 128], BF16, "onesE")
    onesO = ar.alloc([128, 128], BF16, "onesO")
    mstg = [ar.alloc([128, 512], F32, "mstg%d" % i) for i in range(2)]
    Bmx = [Buf("mx0"), Buf("mx1")]
    Bmxn, Bmhs, Bwmkv, BmkT, Bmvp, Bones2 = Buf("mxn"), [Buf("mh0"), Buf("mh1")], Buf("wmkv"), Buf("memkT"), Buf("memvpad"), Buf("ones2")
    Bmss = [Buf("mss0"), Buf("mss1")]
    Bmstg = [Buf("mstg0"), Buf("mstg1")]
    kk.dma("pool", wmkv[:], I["w_mem_kv"].rearrange("(k p) c -> p k c", p=128), writes=[Bwmkv])
    kk.dve(f_memset(memv_pad[:], 0.0), writes=[Bmvp])
    kk.dve(f_memset(onesE[:], 0.0), writes=[Bones2])
    kk.dve(f_memset(onesO[:], 0.0), writes=[Bones2])
    kk.dve(f_memset(onesE[:, 0:64], 1.0), writes=[Bones2])
    kk.dve(f_memset(onesO[:, 64:128], 1.0), writes=[Bones2])
    for mt in range(2):
        norm_transpose(I["mem"][mt * 128:(mt + 1) * 128, :], gmT, mhT, slice(mt * 128, (mt + 1) * 128),
                       (xin0, xin1)[mt], Bmx[mt], mxn, Bmxn, mss[mt], Bmss[mt], banks[mt], pb[mt], Bmhs[mt], mjunk, mt)
    for mt in range(2):
        bk, Bb = banks[2 + mt], pb[2 + mt]
        for kc in range(8):
            kk.pe(mm(bk[:, :], mhT[:, kc, mt * 128:(mt + 1) * 128], wmkv[:, kc, :], kc == 0, kc == 7), reads=[Bmhs[mt], Bwmkv], writes=[Bb])
        kk.act(f_act(mstg[mt][:], bk[:, :], AF.Copy), reads=[Bb], writes=[Bmstg[mt]])
        for h in range(4):
            kk.dve(f_copy(memv_pad[:, mt, h, (h % 2) * 64:(h % 2) * 64 + 64], mstg[mt][:, 256 + h * 64:256 + (h + 1) * 64]), reads=[Bmstg[mt]], writes=[Bmvp])
        kk.dma("sp", O["memk"][mt * 128:(mt + 1) * 128, :], mstg[mt][:, 0:256], reads=[Bmstg[mt]])
        kk.dma("sp", O["memv"][mt * 128:(mt + 1) * 128, :], mstg[mt][:, 256:512], reads=[Bmstg[mt]])
    for hp in range(2):
        bk, Bb = banks[4 + hp], pb[4 + hp]
        for kc in range(8):
            kk.pe(mm(bk[:, 0:256], wmkv[:, kc, hp * 128:(hp + 1) * 128], mhT[:, kc, :], kc == 0, kc == 7), reads=Bmhs + [Bwmkv], writes=[Bb])
        kk.dve(f_copy(memkT[:, hp, :], bk[:, 0:256]), reads=[Bb], writes=[BmkT])

    mpT = [ar.alloc([128, 2, 512], BF16, "mpT%d" % i) for i in range(2)]
    BmpT = [Buf("mpT0"), Buf("mpT1")]
    mrs = [ar.alloc([128, 512], F32, "mrs%d" % i) for i in range(2)]
    Bmrs = [Buf("mrs0"), Buf("mrs1")]
    it = 0
    for tc in range(4):
        o, n = TCH[tc]
        for hp in range(2):
            oc, Boc = banks[4 + 2 * (it % 2)], pb[4 + 2 * (it % 2)]
            oz, Boz = banks[5 + 2 * (it % 2)], pb[5 + 2 * (it % 2)]
            for mt in range(2):
                sa, Bsa = banks[2 * mt], pb[2 * mt]
                sb, Bsb = banks[2 * mt + 1], pb[2 * mt + 1]
                p_, Bp_ = mpT[mt], BmpT[mt]
                kk.pe(mm(sa[:, :], memkT[0:64, hp, mt * 128:(mt + 1) * 128], qmT[0:64, hp, o:o + n], True, True), reads=[BmkT, B_qmT[tc]], writes=[Bsa])
                kk.pe(mm(sb[:, :], memkT[64:128, hp, mt * 128:(mt + 1) * 128], qmT[64:128, hp, o:o + n], True, True), reads=[BmkT, B_qmT[tc]], writes=[Bsb])
                kk.act(f_act(p_[:, 0, :], sa[:, :], AF.Exp, scale=0.125), reads=[Bsa], writes=[Bp_])
                kk.act(f_act(p_[:, 1, :], sb[:, :], AF.Exp, scale=0.125), reads=[Bsb], writes=[Bp_])
                kk.pe(mm(oc[:, :], memv_pad[:, mt, 2 * hp, :], p_[:, 0, :], mt == 0, False), reads=[Bmvp, Bp_], writes=[Boc])
                kk.pe(mm(oc[:, :], memv_pad[:, mt, 2 * hp + 1, :], p_[:, 1, :], False, mt == 1), reads=[Bmvp, Bp_], writes=[Boc])
                kk.pe(mm(oz[:, :], onesE[:, :], p_[:, 0, :], mt == 0, False), reads=[Bones2, Bp_], writes=[Boz])
                kk.pe(mm(oz[:, :], onesO[:, :], p_[:, 1, :], False, mt == 1), reads=[Bones2, Bp_], writes=[Boz])
            r_, Br_ = mrs[it % 2], Bmrs[it % 2]
            kk.act(f_act(r_[:], oz[:, :], AF.Ln), reads=[Boz], writes=[Br_])
            kk.act(f_act(r_[:], r_[:], AF.Exp, scale=-1.0), reads=[Br_], writes=[Br_])
            kk.dve(f_tt(omT[:, hp, o:o + n], oc[:, :], r_[:], ALU.mult), reads=[Boc, Br_], writes=[B_omT[tc]])
            it += 1
    NCM = 4
    cmk = [ar.alloc([128, 2, 256], BF16, "cmk%d" % i) for i in range(NCM)]
    Bcmk = [Buf("cmk%d" % i) for i in range(NCM)]
    cmv = [ar.alloc([128, 2, 256], BF16, "cmv%d" % i) for i in range(NCM)]
    Bcmv = [Buf("cmv%d" % i) for i in range(NCM)]
    cmvp = [ar.alloc([128, 2, 4, 128], BF16, "cmvp%d" % i) for i in range(2)]
    Bcmvp = [Buf("cmvp0"), Buf("cmvp1")]
    kTs = [ar.alloc([128, 2, 256], BF16, "kTs%d" % i) for i in range(2)]
    BkTs = [Buf("kTs0"), Buf("kTs1")]
    pTs = [ar.alloc([128, 2, 32], BF16, "pTs%d" % i) for i in range(2)]
    BpTs = [Buf("pTs0"), Buf("pTs1")]
    for i in range(2):
        kk.pool(f_memset(cmvp[i][:], 0.0), writes=[Bcmvp[i]])
    ocs, Bocs = banks[6], pb[6]
    ozs, Bozs = banks[7], pb[7]

    def load_cm(s2):
        r = s2 % NCM
        kk.dma("pool", cmk[r][:], I["cmem_k"][s2].rearrange("(t p) c -> p t c", p=128), writes=[Bcmk[r]])
        kk.dma("pool", cmv[r][:], I["cmem_v"][s2].rearrange("(t p) c -> p t c", p=128), writes=[Bcmv[r]])

    for s2 in range(NCM - 1):
        load_cm(s2)
    for s_ in range(NSEQ):
        b = s_ % 2
        r = s_ % NCM
        if s_ + NCM - 1 < NSEQ:
            load_cm(s_ + NCM - 1)
        for h in range(4):
            kk.pool(f_copy(cmvp[b][:, :, h, (h % 2) * 64:(h % 2) * 64 + 64], cmv[r][:, :, h * 64:(h + 1) * 64]), reads=[Bcmv[r]], writes=[Bcmvp[b]])
        tb, Btb = banks[b], pb[b]
        tbf = tb[:].bitcast(BF16)
        for hp in range(2):
            for mt in range(2):
                kk.pe(f_tr(tbf[:, (hp * 2 + mt) * 128:(hp * 2 + mt + 1) * 128], cmk[r][:, mt, hp * 128:(hp + 1) * 128], ident_b[:]),
                      reads=[Bcmk[r], Bc], writes=[Btb])
        kk.dve(f_copy(kTs[b][:].rearrange("p h m -> p (h m)"), tbf[:, 0:512]), reads=[Btb], writes=[BkTs[b]])
        sa, Bsa = banks[2 + 2 * b], pb[2 + 2 * b]
        sb, Bsb = banks[3 + 2 * b], pb[3 + 2 * b]
        qs = slice(2048 + 8 * s_, 2048 + 8 * s_ + 8)
        for hp in range(2):
            for mt in range(2):
                c0 = (hp * 2 + mt) * 8
                kk.pe(mm(sa[:, c0:c0 + 8], kTs[b][0:64, hp, mt * 128:(mt + 1) * 128], qmT[0:64, hp, qs], True, True), reads=[BkTs[b], B_qmT[4]], writes=[Bsa])
                kk.pe(mm(sb[:, c0:c0 + 8], kTs[b][64:128, hp, mt * 128:(mt + 1) * 128], qmT[64:128, hp, qs], True, True), reads=[BkTs[b], B_qmT[4]], writes=[Bsb])
        kk.act(f_act(pTs[b][:, 0, :], sa[:, 0:32], AF.Exp, scale=0.125), reads=[Bsa], writes=[BpTs[b]])
        kk.act(f_act(pTs[b][:, 1, :], sb[:, 0:32], AF.Exp, scale=0.125), reads=[Bsb], writes=[BpTs[b]])
        for hp in range(2):
            oc0 = (s_ * 2 + hp) * 8
            for mt in range(2):
                c0 = (hp * 2 + mt) * 8
                kk.pe(mm(ocs[:, oc0:oc0 + 8], cmvp[b][:, mt, 2 * hp, :], pTs[b][:, 0, c0:c0 + 8], mt == 0, False), reads=[Bcmvp[b], BpTs[b]], writes=[Bocs])
                kk.pe(mm(ocs[:, oc0:oc0 + 8], cmvp[b][:, mt, 2 * hp + 1, :], pTs[b][:, 1, c0:c0 + 8], False, mt == 1), reads=[Bcmvp[b], BpTs[b]], writes=[Bocs])
            for mt in range(2):
                c0 = (hp * 2 + mt) * 8
                kk.pe(mm(ozs[:, oc0:oc0 + 8], onesE[:, :], pTs[b][:, 0, c0:c0 + 8], mt == 0, False), reads=[Bones2, BpTs[b]], writes=[Bozs])
                kk.pe(mm(ozs[:, oc0:oc0 + 8], onesO[:, :], pTs[b][:, 1, c0:c0 + 8], False, mt == 1), reads=[Bones2, BpTs[b]], writes=[Bozs])
    kk.act(f_act(mrs[0][:, 0:256], ozs[:, 0:256], AF.Ln), reads=[Bozs], writes=[Bmrs[0]])
    kk.act(f_act(mrs[0][:, 0:256], mrs[0][:, 0:256], AF.Exp, scale=-1.0), reads=[Bmrs[0]], writes=[Bmrs[0]])
    kk.dve(f_tt(omT[:, :, 2048:2176].rearrange("p h (s q) -> p h s q", q=8),
                ocs[:, 0:256].rearrange("p (s h q) -> p h s q", h=2, q=8),
                mrs[0][:, 0:256].rearrange("p (s h q) -> p h s q", h=2, q=8), ALU.mult),
           reads=[Bocs, Bmrs[0]], writes=[B_omT[4]])
    if "omT" in dbg_out:
        odbg = ar.alloc([128, 512], F32, "odbg")
        Bo = Buf("odbg")
        kk.dve(f_copy(odbg[:, 0:128], omT[:, 0, 0:128]), reads=B_omT, writes=[Bo])
        kk.dve(f_copy(odbg[:, 128:256], omT[:, 1, 1920:2048]), reads=B_omT, writes=[Bo])
        kk.dve(f_copy(odbg[:, 256:384], omT[:, 0, 2048:2176]), reads=B_omT, writes=[Bo])
        kk.dve(f_copy(odbg[:, 384:512], omT[:, 1, 2048:2176]), reads=B_omT, writes=[Bo])
        dbg("omT", odbg[:], [Bo])
    kk.barrier()
    if STOP == "p3b":
        return
    ar.reset(wmark)


    apT = [ar.alloc([128, 2, 512], BF16, "apT%d" % i) for i in range(3)]
    BapT = [Buf("apT%d" % i) for i in range(3)]
    tA = ar.alloc([128, 512], F32, "tA")
    tB = ar.alloc([128, 512], F32, "tB")
    tC = ar.alloc([128, 512], F32, "tC")
    tD = ar.alloc([128, 512], F32, "tD")
    tE = ar.alloc([128, 512], F32, "tE")
    BtA, BtB, BtC, BtD, BtE = Buf("tA"), Buf("tB"), Buf("tC"), Buf("tD"), Buf("tE")
    Sset = [((banks[0], pb[0]), (banks[1], pb[1])), ((banks[6], pb[6]), (banks[7], pb[7]))]
    O0, O1, Z0, Z1 = banks[2], banks[3], banks[4], banks[5]
    BO0, BO1, BZ0, BZ1 = pb[2], pb[3], pb[4], pb[5]
    units = []
    for h in range(4):
        for c in range(4):
            for j in range(4 * c + 4):
                units.append((h, c, j))

    def u_lo(c, j):
        return max(j - 4 * c, 0) * 128

    def emit_qk(i):
        h, c, j = units[i]
        (S0, BS0), (S1, BS1) = Sset[i % 2]
        lo = u_lo(c, j)
        q0 = c * 512
        ks = slice(j * 128, (j + 1) * 128)
        kk.pe(mm(S0[:, lo:512], kT[0:64, h, ks], qT[0:64, h, q0 + lo:q0 + 512], True, True), reads=[B_kT[j // 4], B_qT[c]], writes=[BS0])
        kk.pe(mm(S1[:, lo:512], kT[64:128, h, ks], qT[64:128, h, q0 + lo:q0 + 512], True, True), reads=[B_kT[j // 4], B_qT[c]], writes=[BS1])

    def emit_softmax(i):
        h, c, j = units[i]
        (S0, BS0), (S1, BS1) = Sset[i % 2]
        lo = u_lo(c, j)
        jj = j - 4 * c
        pT_, BpT_ = apT[i % 3], BapT[i % 3]
        kk.act(f_act(pT_[:, 0, lo:512], S0[:, lo:512], AF.Exp, scale=0.125), reads=[BS0], writes=[BpT_])
        kk.act(f_act(pT_[:, 1, lo:512], S1[:, lo:512], AF.Exp, scale=0.125), reads=[BS1], writes=[BpT_])
        for m in range(2):
            if jj >= 0:
                kk.dve(f_tt(pT_[:, m, lo:lo + 128], pT_[:, m, lo:lo + 128], T0[:, h, :], ALU.mult), reads=[BpT_, Bt], writes=[BpT_])
                if jj < 3:
                    kk.dve(f_tt(pT_[:, m, lo + 128:lo + 256], pT_[:, m, lo + 128:lo + 256], T1[:, h, :], ALU.mult), reads=[BpT_, Bt], writes=[BpT_])
            elif jj == -1:
                kk.dve(f_tt(pT_[:, m, 0:128], pT_[:, m, 0:128], T1[:, h, :], ALU.mult), reads=[BpT_, Bt], writes=[BpT_])

    def emit_pv(i):
        h, c, j = units[i]
        lo = u_lo(c, j)
        nj = 4 * c + 4
        pT_, BpT_ = apT[i % 3], BapT[i % 3]
        vv = v_bf[:, j, h * 128:(h + 1) * 128]
        for m, (Ob, BOb, Zb, BZb) in enumerate(((O0, BO0, Z0, BZ0), (O1, BO1, Z1, BZ1))):
            kk.pe(mm(Ob[:, lo:512], vv, pT_[:, m, lo:512], j == 0, j == nj - 1), reads=[B_v[j], BpT_], writes=[BOb])
            kk.pe(mm(Zb[:, lo:512], ones_b[:, :], pT_[:, m, lo:512], j == 0, j == nj - 1), reads=[Bc, BpT_], writes=[BZb])

    def emit_tail(h, c, SSb, BSS):
        q0 = c * 512
        kk.act(f_act(tA[:], Z0[:, :], AF.Ln), reads=[BZ0], writes=[BtA])
        kk.act(f_act(tB[:], Z1[:, :], AF.Ln), reads=[BZ1], writes=[BtB])
        kk.act(f_act(tA[:], tA[:], AF.Exp, scale=-1.0), reads=[BtA], writes=[BtA])
        kk.act(f_act(tB[:], tB[:], AF.Exp, scale=-1.0), reads=[BtB], writes=[BtB])
        kk.dve(f_tt(tA[:], O0[:, :], tA[:], ALU.mult), reads=[BO0, BtA], writes=[BtA])
        kk.dve(f_tt(tB[:], O1[:, :], tB[:], ALU.mult), reads=[BO1, BtB], writes=[BtB])
        yield
        kk.dve(f_stt(tC[:], tB[:], neg_lam[:, 0:1], tA[:], ALU.mult, ALU.add), reads=[BtA, BtB, Bl], writes=[BtC])
        kk.act(f_act(tD[:], tC[:], AF.Square), reads=[BtC], writes=[BtD])
        yield
        kk.pe(mm(SSb[:, :], ones_f[:, :], tD[:], True, True), reads=[Bc, BtD], writes=[BSS])
        kk.act(f_act(tE[:], SSb[:, :], AF.Ln, scale=1.0 / 128, bias=eps_t[:, 0:1]), reads=[BSS, Bc], writes=[BtE])
        kk.act(f_act(tE[:], tE[:], AF.Exp, scale=-0.5), reads=[BtE], writes=[BtE])
        yield
        kk.dve(f_stt(oT[:, h, q0:q0 + 512], tC[:], sublnT[:, 0:1], tE[:], ALU.mult, ALU.mult), reads=[BtC, BtE, Bc], writes=[B_oT[c]])

    tail_gen = [None]

    def tail_step():
        if tail_gen[0] is not None:
            if next(tail_gen[0], "done") == "done":
                tail_gen[0] = None

    emit_qk(0)
    for i, (h, c, j) in enumerate(units):
        emit_softmax(i)
        if i + 1 < len(units):
            emit_qk(i + 1)
        emit_pv(i)
        tail_step()
        if j == 4 * c + 3:
            while tail_gen[0] is not None:
                tail_step()
            (SSb, BSS), _ = Sset[i % 2]
            tail_gen[0] = emit_tail(h, c, SSb, BSS)
            tail_step()
    while tail_gen[0] is not None:
        tail_step()
    if "oTp" in dbg_out:
        odbg2 = ar.alloc([128, 512], F32, "odbg2")
        Bo2 = Buf("odbg2")
        kk.dve(f_copy(odbg2[:, 0:128], oT[:, 0, 0:128]), reads=B_oT, writes=[Bo2])
        kk.dve(f_copy(odbg2[:, 128:256], oT[:, 1, 640:768]), reads=B_oT, writes=[Bo2])
        kk.dve(f_copy(odbg2[:, 256:384], oT[:, 2, 1920:2048]), reads=B_oT, writes=[Bo2])
        kk.dve(f_copy(odbg2[:, 384:512], oT[:, 3, 1024:1152]), reads=B_oT, writes=[Bo2])
        dbg("oTp", odbg2[:], [Bo2])
    kk.barrier()
    if STOP == "p3c":
        return
    ar.reset(wmark)


    ptb = ar.alloc([128, NSEQ * NPAGE], I32, "ptb")
    idx = ar.alloc([128, NSEQ * NPAGE], I32, "idx")
    iotaf = ar.alloc([128, 1], F32, "iotaf")
    qpad = ar.alloc([128, 4, NSEQ, 16], BF16, "qpad")
    M15 = ar.alloc([128, 4, 2, 8], F32, "M15")
    MN = ar.alloc([128, NSEQ, 4, 2, 8], F32, "MN")
    gbc = ar.alloc([128, 128], F32, "gbc")
    NV = 14
    kvpg = [ar.alloc([128, 1024], BF16, "kvpg%d" % i) for i in range(NV)]
    Bkvpg = [Buf("kvpg%d" % i) for i in range(NV)]
    KTs = [ar.alloc([128, 4, 128], BF16, "KTs%d" % i) for i in range(2)]
    BKTs = [Buf("KTs0"), Buf("KTs1")]
    spT = [ar.alloc([128, 8, 64], BF16, "spT%d" % i) for i in range(2)]
    BspT = [Buf("spT0"), Buf("spT1")]
    pn = ar.alloc([128, 64], BF16, "pn")
    Bpn = Buf("pn")
    spsum = [ar.alloc([128, 64], F32, "spsum%d" % i) for i in range(2)]
    Bspsum = [Buf("spsum0"), Buf("spsum1")]
    pnf = ar.alloc([128, 64], F32, "pnf")
    Bpnf = Buf("pnf")
    rz = ar.alloc([64, 1], F32, "rz")
    onr = ar.alloc([64, 512], F32, "onr")
    c2 = ar.alloc([32, 4, 128], F32, "c2")
    sq2 = ar.alloc([32, 4, 128], F32, "sq2")
    ss2 = ar.alloc([32, 8], F32, "ss2")
    on3 = ar.alloc([32, 4, 128], BF16, "on3")
    Brz, Bonr, Bc2, Bsq2, Bss2, Bon3 = Buf("rz"), Buf("onr"), Buf("c2"), Buf("sq2"), Buf("ss2"), Buf("on3")
    Bsetup = Buf("p3dsetup")
    kk.dma("sp", ptb[:], I["page_table"][0, :].partition_broadcast(128), writes=[Bsetup])
    kk.dma("sp", iotaf[:], I["iota_f"][:, :], writes=[Bsetup])
    kk.dve(f_ts(idx[:], ptb[:], 128.0, iotaf[:, 0:1], ALU.mult, ALU.add), reads=[Bsetup], writes=[Bsetup])
    kk.dve(f_memset(qpad[:], 0.0), writes=[Bsetup])
    kk.dve(f_copy(qpad[0:64, :, :, 0:8], qT[0:64, :, 2048:2176].rearrange("p h (s q) -> p h s q", q=8)), reads=[B_qT[4], Bsetup], writes=[Bsetup])
    kk.dve(f_copy(qpad[64:128, :, :, 8:16], qT[64:128, :, 2048:2176].rearrange("p h (s q) -> p h s q", q=8)), reads=[B_qT[4], Bsetup], writes=[Bsetup])
    for c in range(2):
        kk.dve(f_copy(M15[:, :, c, :], T1[:, :, 0:8]), reads=[Bt], writes=[Bsetup])
    for h in range(4):
        for c in range(2):
            kk.dve(f_tt(MN[:, :, h, c, :], T0[:, h, :].rearrange("p (s q) -> p s q", q=8), bdiag[:].unsqueeze(2).to_broadcast([128, NSEQ, 8]), ALU.mult),
                   reads=[Bt, Bc], writes=[Bsetup])
    kk.dma("sp", gbc[:], I["subln_g"].partition_broadcast(128), writes=[Bsetup])
    kk.dve(f_ts(gbc[:], gbc[:], 1.0 - LAM_INIT, None, ALU.mult), reads=[Bsetup], writes=[Bsetup])
    OS, BOS = banks[2], pb[2]
    ZS, BZS = banks[3], pb[3]
    C2b, BC2 = banks[4], pb[4]
    TTb, BTT = banks[5], pb[5]
    steps = [(s_, j) for s_ in range(NSEQ) for j in range(NPAGE)]

    def pg_bufs(n):
        return kvpg[n % NV], Bkvpg[n % NV], banks[n % 2], pb[n % 2], KTs[n % 2], BKTs[n % 2]

    def emit_gather(n):
        s_, j = steps[n]
        kvb, Bkvb = kvpg[n % NV], Bkvpg[n % NV]
        col = s_ * NPAGE + j
        kk.op("pool", (lambda e, kvb=kvb, col=col: e.indirect_dma_start(
            out=kvb[:, :], out_offset=None, in_=I["cache_kv"][:, :],
            in_offset=bass.IndirectOffsetOnAxis(ap=idx[:, col:col + 1], axis=0))), reads=[Bsetup], writes=[Bkvb], dma=True)

    def emit_tr(n):
        kvb, Bkvb, tb, Btb, kt_, Bkt_ = pg_bufs(n)
        tbf = tb[:].bitcast(BF16)
        for h in range(4):
            kk.pe(f_tr(tbf[:, h * 128:(h + 1) * 128], kvb[:, h * 128:(h + 1) * 128], ident_b[:]), reads=[Bkvb, Bc], writes=[Btb])
        kk.dve(f_copy(kt_[:].rearrange("p h k -> p (h k)"), tbf[:, 0:512]), reads=[Btb], writes=[Bkt_])

    def emit_qk_s(n):
        s_, j = steps[n]
        kvb, Bkvb, tb, Btb, kt_, Bkt_ = pg_bufs(n)
        Sb, BSb = banks[6 + (j // 8)], pb[6 + (j // 8)]
        for h in range(4):
            c0 = (j % 8) * 64 + h * 16
            kk.pe(mm(Sb[:, c0:c0 + 16], kt_[:, h, :], qpad[:, h, s_, :], True, True), reads=[Bkt_, Bsetup], writes=[BSb])

    def sample_tail(s_):
        kk.dve(f_recip(rz[:], ZS[0:64, 0:1]), reads=[BZS], writes=[Brz])
        kk.dve(f_ts(onr[:], OS[0:64, :], rz[:, 0:1], None, ALU.mult), reads=[BOS, Brz], writes=[Bonr])
        yield
        kk.pe(mm(C2b[0:32, :], comb[:, :], onr[:, :], True, True), reads=[Bl, Bonr], writes=[BC2])
        yield
        kk.dve(f_copy(c2[:].rearrange("p h e -> p (h e)"), C2b[0:32, :]), reads=[BC2], writes=[Bc2])
        kk.act(f_act(sq2[:], c2[:], AF.Square), reads=[Bc2], writes=[Bsq2])
        yield
        kk.dve(lambda e: e.tensor_reduce(out=ss2[:, 0:4], in_=sq2[:], axis=AX.X, op=ALU.add), reads=[Bsq2], writes=[Bss2])
        kk.act(f_act(ss2[:, 4:8], ss2[:, 0:4], AF.Ln, scale=1.0 / 128, bias=eps_t[0:32, 0:1]), reads=[Bss2, Bc], writes=[Bss2])
        kk.act(f_act(ss2[:, 4:8], ss2[:, 4:8], AF.Exp, scale=-0.5), reads=[Bss2], writes=[Bss2])
        yield
        kk.dve(f_tt(c2[:], c2[:], ss2[:, 4:8].unsqueeze(2).to_broadcast([32, 4, 128]), ALU.mult), reads=[Bc2, Bss2], writes=[Bc2])
        kk.dve(f_tt(on3[:], c2[:], gbc[0:32, :].unsqueeze(1).to_broadcast([32, 4, 128]), ALU.mult), reads=[Bc2, Bsetup], writes=[Bon3])
        yield
        ttf = TTb[:].bitcast(BF16)
        for h in range(4):
            kk.pe(f_tr(ttf[:, h * 32:(h + 1) * 32], on3[:, h, :], ident_b[0:32, 0:32]), reads=[Bon3, Bc], writes=[BTT])
        yield
        for h in range(4):
            kk.dve(f_copy(oT[:, h, 2048 + 8 * s_:2048 + 8 * s_ + 8], ttf[:, h * 32 + h * 8:h * 32 + h * 8 + 8]), reads=[BTT], writes=[B_oT[4]])

    stail = [None]

    def stail_step():
        if stail[0] is not None:
            if next(stail[0], "done") == "done":
                stail[0] = None

    NPRE = NV - 8
    for n in range(min(NPRE, len(steps))):
        emit_gather(n)
    emit_tr(0)
    for n, (s_, j) in enumerate(steps):
        if n + NPRE < len(steps):
            emit_gather(n + NPRE)
        if n + 1 < len(steps):
            emit_tr(n + 1)
        emit_qk_s(n)
        stail_step()
        if j % 8 == 7:
            half = j // 8
            Sb, BSb = banks[6 + half], pb[6 + half]
            sp_, Bsp_ = spT[half], BspT[half]
            kk.act(f_act(sp_[:].rearrange("p j c -> p (j c)"), Sb[:, :], AF.Exp, scale=0.125), reads=[BSb], writes=[Bsp_])
            if half == 1:
                kk.dve(f_tt(sp_[:, 7, :], sp_[:, 7, :], M15[:].rearrange("p h c q -> p (h c q)"), ALU.mult), reads=[Bsp_, Bsetup], writes=[Bsp_])
            for jj in range(8):
                jp = half * 8 + jj
                m_ = n - 7 + jj
                kvb, Bkvb = kvpg[m_ % NV], Bkvpg[m_ % NV]
                kk.pe(mm(OS[0:64, :], sp_[:, jj, :], kvb[:, 512:1024], jp == 0, False), reads=[Bsp_, Bkvb], writes=[BOS])
                kk.pe(mm(ZS[0:64, 0:1], sp_[:, jj, :], ones_b[:, 0:1], jp == 0, False), reads=[Bsp_, Bc], writes=[BZS])
        if j != NPAGE - 1:
            continue
        Sb, BSb = banks[6], pb[6]
        for h in range(4):
            kk.pe(mm(Sb[:, h * 16:(h + 1) * 16], kT[:, h, 2048:2176], qpad[:, h, s_, :], True, True), reads=[B_kT[4], Bsetup], writes=[BSb])
        kk.act(f_act(pn[:], Sb[:, 0:64], AF.Exp, scale=0.125), reads=[BSb], writes=[Bpn])
        kk.dve(f_tt(pn[:], pn[:], MN[:, s_].rearrange("p h c q -> p (h c q)"), ALU.mult), reads=[Bpn, Bsetup], writes=[Bpn])
        kk.pe(mm(OS[0:64, :], pn[:, :], v_bf[:, 16, :], False, True), reads=[Bpn, B_v[16]], writes=[BOS])
        kk.pe(mm(ZS[0:64, 0:1], pn[:, :], ones_b[:, 0:1], False, True), reads=[Bpn, Bc], writes=[BZS])
        while stail[0] is not None:
            stail_step()
        stail[0] = sample_tail(s_)
        stail_step()
    while stail[0] is not None:
        stail_step()
    if "oTs" in dbg_out:
        odbg3 = ar.alloc([128, 512], F32, "odbg3")
        Bo3 = Buf("odbg3")
        kk.dve(f_copy(odbg3[:].rearrange("p (h t) -> p h t", h=4), oT[:, :, 2048:2176]), reads=B_oT, writes=[Bo3])
        dbg("oTs", odbg3[:], [Bo3])
    kk.barrier()
    if STOP == "p3d":
        return
    ar.reset(wmark)


    ar.reset(omark)
    mergedT = ar.alloc([128, 8, NTOK], BF16, "mergedT")
    B_mg = [Buf("mg%d" % i) for i in range(len(TCH))]
    p4mark = ar.mark()
    wg = [ar.alloc([128, 8, 3, 128], BF16, "wg%d" % i) for i in range(2)]
    Bwg = [Buf("wg0"), Buf("wg1")]
    wbr = [ar.alloc([128, 8, 128], BF16, "wbr%d" % i) for i in range(2)]
    Bwbr = [Buf("wbr0"), Buf("wbr1")]
    sg = [ar.alloc([128, 512], F32, "sg%d" % i) for i in range(6)]
    Bsg = [Buf("sg%d" % i) for i in range(6)]
    mt_ = [ar.alloc([128, 512], F32, "mtmp%d" % i) for i in range(4)]
    Bmt = [Buf("mtmp%d" % i) for i in range(4)]
    bankctr = [0]

    def rbank():
        b = bankctr[0] % 8
        bankctr[0] += 1
        return banks[b], pb[b]

    def load_p4(fc):
        b = fc % 2
        for g in range(3):
            c0 = 2048 + g * 1024 + fc * 128
            kk.dma("pool", wg[b][:, :, g, :], I["w_in"][:, c0:c0 + 128].rearrange("(k p) c -> p k c", p=128), writes=[Bwg[b]])
        kk.dma("pool", wbr[b][:, 0:4, :], I["w_br_attn"][:, fc * 128:(fc + 1) * 128].rearrange("(k p) c -> p k c", p=128), writes=[Bwbr[b]])
        kk.dma("pool", wbr[b][:, 4:6, :], I["w_br_pool"][:, fc * 128:(fc + 1) * 128].rearrange("(k p) c -> p k c", p=128), writes=[Bwbr[b]])
        kk.dma("pool", wbr[b][:, 6:8, :], I["w_br_mem"][:, fc * 128:(fc + 1) * 128].rearrange("(k p) c -> p k c", p=128), writes=[Bwbr[b]])

    load_p4(0)
    un = 0
    for fc in range(8):
        if fc + 1 < 8:
            load_p4(fc + 1)
        b = fc % 2
        for tc in range(len(TCH)):
            o, n = TCH[tc]
            hreads = [B_hT[t] for t in tiles_of(tc)]
            prods = []
            for g in range(3):
                gb, Bgb = rbank()
                for kc in range(8):
                    kk.pe(mm(gb[:, 0:n], wg[b][:, kc, g, :], hT[:, kc, o:o + n], kc == 0, kc == 7), reads=[Bwg[b]] + hreads, writes=[Bgb])
                bb, Bbb = rbank()
                if g == 0:
                    for h in range(4):
                        kk.pe(mm(bb[:, 0:n], wbr[b][:, h, :], oT[:, h, o:o + n], h == 0, h == 3), reads=[Bwbr[b], B_oT[tc]], writes=[Bbb])
                elif g == 1:
                    for ch in range(2):
                        kk.pe(mm(bb[:, 0:n], wbr[b][:, 4 + ch, :], poolT[:, ch, o:o + n], ch == 0, ch == 1), reads=[Bwbr[b], B_poolT[tc]], writes=[Bbb])
                else:
                    for hp in range(2):
                        kk.pe(mm(bb[:, 0:n], wbr[b][:, 6 + hp, :], omT[:, hp, o:o + n], hp == 0, hp == 1), reads=[Bwbr[b], B_omT[tc]], writes=[Bbb])
                si = (un * 3 + g) % 6
                kk.act(f_act(sg[si][:, 0:n], gb[:, 0:n], AF.Sigmoid), reads=[Bgb], writes=[Bsg[si]])
                kk.dve(f_tt(sg[si][:, 0:n], sg[si][:, 0:n], bb[:, 0:n], ALU.mult), reads=[Bsg[si], Bbb], writes=[Bsg[si]])
                prods.append(si)
            mi = un % 4
            kk.dve(f_tt(mt_[mi][:, 0:n], sg[prods[0]][:, 0:n], sg[prods[1]][:, 0:n], ALU.add), reads=[Bsg[prods[0]], Bsg[prods[1]]], writes=[Bmt[mi]])
            kk.dve(f_tt(mergedT[:, fc, o:o + n], mt_[mi][:, 0:n], sg[prods[2]][:, 0:n], ALU.add), reads=[Bmt[mi], Bsg[prods[2]]], writes=[B_mg[tc]])
            un += 1
    if "mergedT" in dbg_out:
        mdbg = ar.alloc([128, 512], F32, "mdbg")
        Bm_ = Buf("mdbg")
        kk.dve(f_copy(mdbg[:, 0:128], mergedT[:, 0, 0:128]), reads=B_mg, writes=[Bm_])
        kk.dve(f_copy(mdbg[:, 128:256], mergedT[:, 7, 1024:1152]), reads=B_mg, writes=[Bm_])
        kk.dve(f_copy(mdbg[:, 256:384], mergedT[:, 3, 2048:2176]), reads=B_mg, writes=[Bm_])
        kk.dve(f_copy(mdbg[:, 384:512], mergedT[:, 5, 2048:2176]), reads=B_mg, writes=[Bm_])
        dbg("mergedT", mdbg[:], [Bm_])
    kk.barrier()
    if STOP == "p4":
        return
    ar.reset(p4mark)

    arA = Arena(nc, amark, omark)
    wout = arA.alloc([128, 8, D], BF16, "wout")
    Bwout = Buf("wout")
    wd = arA.alloc([128, NFF, D], BF16, "wd")
    Bwd = [Buf("wd%d" % i) for i in range(NFF)]
    wgu = [arA.alloc([128, 8, 2, 128], BF16, "wgu%d" % i) for i in range(2)]
    Bwgu = [Buf("wgu%d" % i) for i in range(3)]
    stc = [ar.alloc([32, 512], F32, "stc%d" % i) for i in range(2)]
    stT = ar.alloc([128, NFF, NSEQ, 2], F32, "stT")
    cs = ar.alloc([128, NFF, 34], F32, "cs")
    halo = [ar.alloc([128, NFF, 2], F32, "halo%d" % i) for i in range(2)]
    Bstc, BstT, Bcs, Bhalo = [Buf("stc0"), Buf("stc1")], Buf("stT"), Buf("cs"), [Buf("halo0"), Buf("halo1")]
    for k2 in range(2):
        kk.dma("pool", wout[:, k2 * 4:(k2 + 1) * 4, :], I["w_out"][k2 * 512:(k2 + 1) * 512, :].rearrange("(k p) c -> p k c", p=128), writes=[Bwout])
    kk.dve(f_memset(halo[0][:], 0.0), writes=[Bhalo[0]])
    for q4 in range(6):
        nf = min(4, NFF - q4 * 4)
        kk.dma("sp", stc[q4 % 2][:, 0:nf * 128], I["state_conv"][:, q4 * 512:q4 * 512 + nf * 128], writes=[Bstc[q4 % 2]])
        for i in range(nf):
            fcx = q4 * 4 + i
            bk, Bb = rbank()
            kk.pe(mm(bk[:, 0:32], stc[q4 % 2][:, i * 128:(i + 1) * 128], ident_f[0:32, 0:32], True, True), reads=[Bstc[q4 % 2], Bc], writes=[Bb])
            kk.dve(f_copy(stT[:, fcx].rearrange("p s r -> p (s r)"), bk[:, 0:32]), reads=[Bb], writes=[BstT])
    x2 = ar.alloc([128, 4, D], F32, "x2")
    Bx2 = [Buf("x2_%d" % i) for i in range(4)]
    h2T = ar.alloc([128, 8, 512], BF16, "h2T")
    Bh2 = [Buf("h2_%d" % i) for i in range(4)]
    actT = ar.alloc([128, NFF, 512], BF16, "actT")
    Bact = [Buf("act%d" % i) for i in range(NFF)]
    gS = [ar.alloc([128, 2 + 512], F32, "gS%d" % i) for i in range(2)]
    BgS = [Buf("gS0"), Buf("gS1")]
    gSs = [ar.alloc([128, NSEQ, 10], F32, "gSs%d" % i) for i in range(2)]
    BgSs = [Buf("gSs0"), Buf("gSs1")]
    c1 = [ar.alloc([128, 512], F32, "c1_%d" % i) for i in range(2)]
    Bc1 = [Buf("c1_0"), Buf("c1_1")]
    ge = [ar.alloc([128, 512], F32, "ge%d" % i) for i in range(2)]
    Bge = [Buf("ge0"), Buf("ge1")]
    xin5 = [ar.alloc([128, D], F32, "xin5_%d" % i) for i in range(2)]
    Bxin5 = [Buf("xin5_0"), Buf("xin5_1")]
    xn5 = ar.alloc([128, D], BF16, "xn5")
    Bxn5 = Buf("xn5")
    junk5 = ar.alloc([128, D], BF16, "junk5")
    Bjunk5 = Buf("junk5")
    ss5 = [ar.alloc([128, 4], F32, "ss5_%d" % i) for i in range(2)]
    Bss5 = [Buf("ss5_0"), Buf("ss5_1")]
    wgu.append(ar.alloc([128, 8, 2, 128], BF16, "wgu2"))
    yt = None
    csr = [ar.alloc([34, 512], F32, "csr%d" % i) for i in range(2)]
    Bcsr = [Buf("csr0"), Buf("csr1")]
    for fcx in range(NFF):
        kk.dma("pool", wd[:, fcx, :], I["w_ffn_down"][fcx * 128:(fcx + 1) * 128, :], writes=[Bwd[fcx]])
    def emit_p5a(gi5, li):
        t0_, t1_ = GROUPS[gi5]
        t = t0_ + li
        k5 = nt5c[0]
        nt5c[0] += 1
        xb_, Bxb_ = xin5[k5 % 2], Bxin5[k5 % 2]
        kk.dma("sp", xb_[:], I["x_all"][t * 128:(t + 1) * 128, :], writes=[Bxb_])
        for half in range(2):
            bk, Bb = rbank()
            for kc in range(8):
                kk.pe(mm(bk[:, :], mergedT[:, kc, t * 128:(t + 1) * 128], wout[:, kc, half * 512:(half + 1) * 512], kc == 0, kc == 7),
                      reads=[B_mg[t // 4], Bwout], writes=[Bb])
            kk.dve(f_tt(x2[:, li, half * 512:(half + 1) * 512], bk[:, :], xb_[:, half * 512:(half + 1) * 512], ALU.add), reads=[Bb, Bxb_], writes=[Bx2[li]])
        norm_stats(None, x2[:, li, :], Bx2[li], xn5, Bxn5, ss5[k5 % 2], Bss5[k5 % 2], junk5)

    def emit_p5b(gi5, li):
        bk, Bb = rbank()
        norm_tr(g2T, h2T, slice(li * 128, (li + 1) * 128), xn5, Bxn5, bk, Bb, Bh2[li])

    def emit_p5(gi5, li):
        emit_p5a(gi5, li)
        emit_p5b(gi5, li)

    def emit_p7(gi7, li):
        t0_, t1_ = GROUPS[gi7]
        t = t0_ + li
        for half in range(2):
            bk, Bb = rbank()
            for fcx in range(NFF):
                kk.pe(mm(bk[:, :], actT[:, fcx, li * 128:(li + 1) * 128], wd[:, fcx, half * 512:(half + 1) * 512], fcx == 0, fcx == NFF - 1),
                      reads=[Bact[fcx], Bwd[fcx]], writes=[Bb])
            kk.dve(f_tt(x2[:, li, half * 512:(half + 1) * 512], bk[:, :], x2[:, li, half * 512:(half + 1) * 512], ALU.add), reads=[Bb, Bx2[li]], writes=[Bx2[li]])
        si = nt7c[0] % 2
        nt7c[0] += 1
        yb, Byb = y7[si], By7[si]
        kk.act(f_act(junk5[:], x2[:, li, :], AF.Square, accum_out=ss7[si][:, 0:1]), reads=[Bx2[li]], writes=[Bss7[si], Bjunk5])
        kk.act(f_act(ss7[si][:, 1:2], ss7[si][:, 0:1], AF.Sqrt, scale=1.0 / D, bias=eps_t[:, 0:1]), reads=[Bss7[si], Bc], writes=[Bss7[si]])
        kk.dve(f_recip(ss7[si][:, 2:3], ss7[si][:, 1:2]), reads=[Bss7[si]], writes=[Bss7[si]])
        kk.dve(f_stt(yb[:], x2[:, li, :], ss7[si][:, 2:3], gfin[:], ALU.mult, ALU.mult), reads=[Bx2[li], Bss7[si], Bc], writes=[Byb])
        kk.dma("sp", O["y_all"][t * 128:(t + 1) * 128, :], yb[:], reads=[Byb])

    nt5c = [0]
    nt7c = [0]
    ss7 = [ar.alloc([128, 4], F32, "ss7_%d" % i) for i in range(2)]
    Bss7 = [Buf("ss7_0"), Buf("ss7_1")]
    y7 = [ar.alloc([128, D], F32, "y7_0")] * 2
    By7 = [Buf("y7_0")] * 2
    for li in range(GROUPS[0][1] - GROUPS[0][0]):
        emit_p5(0, li)
    nwl = [0]
    nt5 = 0
    for gi, (t0, t1) in enumerate(GROUPS):
        ntile = t1 - t0
        smp = gi == len(GROUPS) - 1
        lastp = gi == len(GROUPS) - 2
        ntk = ntile * 128
        n = ntk
        hin, hout = halo[gi % 2], halo[(gi + 1) % 2]
        Bhin, Bhout = Bhalo[gi % 2], Bhalo[(gi + 1) % 2]
        for fcx in range(NFF):
            wi = nwl[0] % 3
            nwl[0] += 1
            wflat = wgu[wi][:].rearrange("p k g c -> p (k g c)")
            if gi == 0:
                kk.dma("pool", wgu[wi][:, :, 0, :], I["w_ffn_gate"][:, fcx * 128:(fcx + 1) * 128].rearrange("(k p) c -> p k c", p=128), writes=[Bwgu[wi]])
                kk.dma("pool", wgu[wi][:, :, 1, :], I["w_ffn_up"][:, fcx * 128:(fcx + 1) * 128].rearrange("(k p) c -> p k c", p=128), writes=[Bwgu[wi]])
                kk.dma("sp", wscr[fcx], wflat, reads=[Bwgu[wi]], writes=[Bwscr[fcx]])
            else:
                kk.dma("sp", wflat, wscr[fcx], reads=[Bwscr[fcx]], writes=[Bwgu[wi]])
            bi = fcx % 2
            g_, Bg_ = gS[bi], BgS[bi]
            gs_, Bgs_ = gSs[bi], BgSs[bi]
            c_, Bc_ = c1[bi], Bc1[bi]
            e_, Be_ = ge[bi], Bge[bi]
            hreads = [Bh2[i] for i in range(ntile)]
            gb, Bgb = rbank()
            for kc in range(8):
                kk.pe(mm(gb[:, 0:n], wgu[wi][:, kc, 0, :], h2T[:, kc, 0:n], kc == 0, kc == 7), reads=[Bwgu[wi]] + hreads, writes=[Bgb])
            ub, Bub = rbank()
            for kc in range(8):
                kk.pe(mm(ub[:, 0:n], wgu[wi][:, kc, 1, :], h2T[:, kc, 0:n], kc == 0, kc == 7), reads=[Bwgu[wi]] + hreads, writes=[Bub])
            w0, w1, w2, bb_ = convw[:, 0, fcx:fcx + 1], convw[:, 1, fcx:fcx + 1], convw[:, 2, fcx:fcx + 1], convb[:, fcx:fcx + 1]
            if not smp:
                kk.act(f_act(g_[:, 2:2 + 512], gb[:, 0:512], AF.Copy), reads=[Bgb], writes=[Bg_])
                kk.dve(f_copy(g_[:, 0:2], hin[:, fcx, :]), reads=[Bhin], writes=[Bg_])
                kk.dve(f_copy(hout[:, fcx, :], g_[:, 512:514]), reads=[Bg_], writes=[Bhout])
                kk.dve(f_ts(c_[:, 0:512], g_[:, 0:512], w0, bb_, ALU.mult, ALU.add), reads=[Bg_, Bc, Bc5], writes=[Bc_])
                kk.dve(f_stt(c_[:, 0:512], g_[:, 1:513], w1, c_[:, 0:512], ALU.mult, ALU.add), reads=[Bg_, Bc_, Bc], writes=[Bc_])
                kk.dve(f_stt(c_[:, 0:512], g_[:, 2:514], w2, c_[:, 0:512], ALU.mult, ALU.add), reads=[Bg_, Bc_, Bc], writes=[Bc_])
                if lastp:
                    kk.dve(f_copy(cs[:, fcx, 0:2], g_[:, 512:514]), reads=[Bg_], writes=[Bcs])
            else:
                kk.act(f_act(gs_[:, :, 2:10], gb[:, 0:128].rearrange("p (s i) -> p s i", i=8), AF.Copy), reads=[Bgb], writes=[Bgs_])
                kk.dve(f_copy(gs_[:, :, 0:2], stT[:, fcx]), reads=[BstT], writes=[Bgs_])
                cv = c_[:, 0:128].rearrange("p (s i) -> p s i", i=8)
                kk.dve(f_ts(cv, gs_[:, :, 0:8], w0, bb_, ALU.mult, ALU.add), reads=[Bgs_, Bc, Bc5], writes=[Bc_])
                kk.dve(f_stt(cv, gs_[:, :, 1:9], w1, cv, ALU.mult, ALU.add), reads=[Bgs_, Bc_, Bc], writes=[Bc_])
                kk.dve(f_stt(cv, gs_[:, :, 2:10], w2, cv, ALU.mult, ALU.add), reads=[Bgs_, Bc_, Bc], writes=[Bc_])
                kk.dve(f_copy(cs[:, fcx, 2:34].rearrange("p (s r) -> p s r", r=2), gs_[:, :, 8:10]), reads=[Bgs_], writes=[Bcs])
            kk.act(f_act(e_[:, 0:n], c_[:, 0:n], AF.Gelu_apprx_tanh), reads=[Bc_], writes=[Be_])
            kk.dve(f_tt(actT[:, fcx, 0:n], e_[:, 0:n], ub[:, 0:n], ALU.mult), reads=[Be_, Bub], writes=[Bact[fcx]])
        nnext = (GROUPS[gi + 1][1] - GROUPS[gi + 1][0]) if gi + 1 < len(GROUPS) else 0
        pend_b = None
        for li in range(max(ntile, nnext)):
            if li < ntile:
                emit_p7(gi, li)
            if pend_b is not None:
                emit_p5b(gi + 1, pend_b)
                pend_b = None
            if li < nnext:
                emit_p5a(gi + 1, li)
                pend_b = li
        if pend_b is not None:
            emit_p5b(gi + 1, pend_b)
    for q4 in range(6):
        bk, Bb = rbank()
        nf = min(4, NFF - q4 * 4)
        for i in range(nf):
            fcx = q4 * 4 + i
            kk.pe(mm(bk[0:34, i * 128:(i + 1) * 128], cs[:, fcx, :], ident_f[:, :], True, True), reads=[Bcs, Bc], writes=[Bb])
        kk.dve(f_copy(csr[q4 % 2][:, 0:nf * 128], bk[0:34, 0:nf * 128]), reads=[Bb], writes=[Bcsr[q4 % 2]])
        kk.dma("sp", O["conv_all"][:, q4 * 512:q4 * 512 + nf * 128], csr[q4 % 2][:, 0:nf * 128], reads=[Bcsr[q4 % 2]])

    kk.barrier()


_NC_CACHE = {}


def kernel(**inputs):
    f32 = lambda a: np.ascontiguousarray(np.asarray(a, dtype=np.float32))
    if "nc" not in _NC_CACHE:
        _NC_CACHE["nc"] = build_program()
    nc = _NC_CACHE["nc"]
    consts = host_constants()
    x_prompt = f32(inputs["x_prompt"])
    x_sample = f32(inputs["x_sample"])
    mem_prompt = f32(inputs["mem_prompt"])
    cache_kv = np.concatenate([f32(inputs["cache_k"]).reshape(-1, 512), f32(inputs["cache_v"]).reshape(-1, 512)], axis=1)
    page_table = np.ascontiguousarray(np.asarray(inputs["page_table"], dtype=np.int32))
    state_pool = f32(inputs["state_pool"])[0]
    state_conv = f32(inputs["state_ffn_conv"])[0]
    cmk = f32(inputs["cache_mem_k"])[0]
    cmv = f32(inputs["cache_mem_v"])[0]
    shared = {
        "cache_kv": cache_kv,
        "norm1_g": f32(inputs["norm1_g"])[0], "w_in": f32(inputs["w_in"])[0],
        "lam_q1": f32(inputs["lam_q1"]), "lam_k1": f32(inputs["lam_k1"]),
        "lam_q2": f32(inputs["lam_q2"]), "lam_k2": f32(inputs["lam_k2"]),
        "subln_g": f32(inputs["subln_g"])[0], "w_pool_grp": f32(inputs["w_pool_grp"])[0],
        "pool_scale": f32(inputs["pool_scale"])[0],
        "w_br_attn": f32(inputs["w_br_attn"])[0], "w_br_pool": f32(inputs["w_br_pool"])[0],
        "w_br_mem": f32(inputs["w_br_mem"])[0], "mem_norm_g": f32(inputs["mem_norm_g"])[0],
        "w_mem_kv": f32(inputs["w_mem_kv"])[0], "w_out": f32(inputs["w_out"])[0],
        "norm2_g": f32(inputs["norm2_g"])[0], "w_ffn_gate": f32(inputs["w_ffn_gate"])[0],
        "w_ffn_up": f32(inputs["w_ffn_up"])[0], "ffn_conv_w": f32(inputs["ffn_conv_w"])[0],
        "ffn_conv_b": f32(inputs["ffn_conv_b"])[0], "w_ffn_down": f32(inputs["w_ffn_down"])[0],
        "rel_bias": f32(inputs["rel_bias"]), "final_norm_g": f32(inputs["final_norm_g"]),
    }
    shared.update(consts)
    in_maps = []
    for c in range(8):
        sl = slice(NSEQ * c, NSEQ * (c + 1))
        m = dict(shared)
        m["x_all"] = np.ascontiguousarray(np.concatenate([x_prompt[c], x_sample[sl].reshape(128, D)], axis=0))
        m["mem"] = mem_prompt[c]
        m["page_table"] = np.ascontiguousarray(page_table[sl].reshape(1, NSEQ * NPAGE))
        m["state_pool"] = np.ascontiguousarray(state_pool[sl].reshape(NSEQ * 15, 256))
        m["state_conv"] = np.ascontiguousarray(state_conv[sl].reshape(NSEQ * 2, D_FF))
        m["cmem_k"] = np.ascontiguousarray(cmk[sl].reshape(NSEQ, 256, 256))
        m["cmem_v"] = np.ascontiguousarray(cmv[sl].reshape(NSEQ, 256, 256))
        in_maps.append(m)
    res = run_bass_kernel_spmd(nc, in_maps, core_ids=list(range(8)))
    R = res.results
    g = lambda k: [np.asarray(R[c][k], dtype=np.float32) for c in range(8)]
    y = g("y_all"); nk = g("newk"); nv = g("newv")
    y_prompt = np.stack([a[:2048] for a in y], 0)
    y_sample = np.concatenate([a[2048:].reshape(NSEQ, 8, D) for a in y], 0)
    nkp = np.stack([a[:2048].reshape(2048, 4, 128) for a in nk], 0)[None]
    nvp = np.stack([a[:2048].reshape(2048, 4, 128) for a in nv], 0)[None]
    nks = np.concatenate([a[2048:].reshape(NSEQ, 8, 4, 128) for a in nk], 0)[None]
    nvs = np.concatenate([a[2048:].reshape(NSEQ, 8, 4, 128) for a in nv], 0)[None]
    pp = np.stack(g("pool_p"), 0)[None]
    ps = np.concatenate(g("pool_s"), 0)[None]
    cv = g("conv_all")
    cp = np.stack([a[:2] for a in cv], 0)[None]
    cs = np.concatenate([a[2:].reshape(NSEQ, 2, D_FF) for a in cv], 0)[None]
    mk = np.stack([a.reshape(256, 4, 64) for a in g("memk")], 0)[None]
    mv = np.stack([a.reshape(256, 4, 64) for a in g("memv")], 0)[None]
    return (y_prompt, y_sample, nkp, nvp, nks, nvs, pp, ps, cp, cs, mk, mv)
```

```python
import numpy as np
from contextlib import ExitStack

import concourse.bass as bass
import concourse.mybir as mybir
from concourse.bass_utils import run_bass_kernel_spmd

F32 = mybir.dt.float32
BF16 = mybir.dt.bfloat16
I32 = mybir.dt.int32
AF = mybir.ActivationFunctionType
ALU = mybir.AluOpType
AX = mybir.AxisListType

D = 1024
NTOK = 2176
NT = 17
D_IN = 5120
D_FF = 2816
NFF = 22
EPS = 1e-6
LAM_INIT = 0.8 - 0.6
NSEQ = 16
NPAGE = 16
WZ = 384

TCH = [(0, 512), (512, 512), (1024, 512), (1536, 512), (2048, 128)]
GROUPS = [(0, 4), (4, 8), (8, 12), (12, 16), (16, 17)]


class Buf:
    __slots__ = ("name", "w", "r")

    def __init__(self, name):
        self.name = name
        self.w = None
        self.r = []


class Op:
    __slots__ = ("eng", "fn", "waits", "signal", "idx", "count", "dma", "dsem", "dval", "pre")

    def __init__(self, eng, fn, dma):
        self.eng = eng
        self.fn = fn
        self.waits = []
        self.signal = False
        self.idx = -1
        self.count = 0
        self.dma = dma
        self.dsem = None
        self.dval = 0
        self.pre = None


ENGS = ("pe", "act", "dve", "pool", "sp")
NDSEM = 24


class K:
    def __init__(self, nc, es):
        self.nc = nc
        self.ops = {e: [] for e in ENGS}
        self.waited = {e: {p: -1 for p in ENGS} for e in ENGS}
        self.waited_dma = {e: set() for e in ENGS}
        self.sem = {e: es.enter_context(nc.semaphore("s_" + e)) for e in ENGS}
        self.dsems = {q: [es.enter_context(nc.semaphore("d_%s%d" % (q, i))) for i in range(NDSEM)]
                      for q in ("sp", "pool")}
        self.ndma = {"sp": 0, "pool": 0}
        self.dma_ops = {"sp": [], "pool": []}

    def _dep(self, op, d, force=False):
        e = op.eng
        if d is None or d is op:
            return
        if d.dma:
            if id(d) in self.waited_dma[e]:
                return
            self.waited_dma[e].add(id(d))
            op.waits.append(d)
            return
        p = d.eng
        if p == "pe" and e == "pe" and not force:
            return
        if self.waited[e][p] >= d.idx:
            return
        self.waited[e][p] = d.idx
        d.signal = True
        op.waits.append(d)

    def op(self, eng, fn, reads=(), writes=(), dma=False):
        o = Op(eng, fn, dma)
        o.idx = len(self.ops[eng])
        deps = []
        for b in reads:
            if b.w is not None:
                deps.append(b.w)
        for b in writes:
            if b.w is not None:
                deps.append(b.w)
            deps.extend(b.r)
        latest = {}
        for d in deps:
            if d.dma:
                self._dep(o, d)
            elif d.eng not in latest or latest[d.eng].idx < d.idx:
                latest[d.eng] = d
        for d in latest.values():
            self._dep(o, d)
        if dma:
            n = self.ndma[eng]
            self.ndma[eng] += 1
            o.dsem = self.dsems[eng][n % NDSEM]
            o.dval = 16 * (n // NDSEM + 1)
            if n >= NDSEM:
                prev = self.dma_ops[eng][n - NDSEM]
                o.pre = prev
            self.dma_ops[eng].append(o)
        self.ops[eng].append(o)
        for b in reads:
            b.r.append(o)
        for b in writes:
            b.w = o
            b.r = []
        return o

    def pe(self, fn, reads=(), writes=()):
        return self.op("pe", fn, reads, writes)

    def act(self, fn, reads=(), writes=()):
        return self.op("act", fn, reads, writes)

    def dve(self, fn, reads=(), writes=()):
        return self.op("dve", fn, reads, writes)

    def pool(self, fn, reads=(), writes=()):
        return self.op("pool", fn, reads, writes)

    def dma(self, q, out, in_, reads=(), writes=(), **kw):
        return self.op(q, lambda e: e.dma_start(out=out, in_=in_, **kw), reads, writes, dma=True)

    def barrier(self):
        lasts = []
        for e in ENGS:
            real = [o for o in self.ops[e] if o.fn is not None and not o.dma]
            if real:
                lasts.append(real[-1])
        dmas = self.dma_ops["sp"][-NDSEM:] + self.dma_ops["pool"][-NDSEM:]
        for e in ENGS:
            o = Op(e, None, False)
            o.idx = len(self.ops[e])
            for d in lasts + dmas:
                self._dep(o, d, force=True)
            self.ops[e].append(o)

    def emit(self, block):
        for e in ENGS:
            c = 0
            for o in self.ops[e]:
                if o.signal:
                    c += 1
                    o.count = c
        def run(e, eng):
            for o in self.ops[e]:
                if o.pre is not None:
                    eng.wait_ge(o.pre.dsem, o.pre.dval)
                for d in o.waits:
                    if d.dma:
                        eng.wait_ge(d.dsem, d.dval)
                    else:
                        eng.wait_ge(self.sem[d.eng], d.count)
                if o.fn is None:
                    continue
                ins = o.fn(eng)
                if o.dma:
                    ins.then_inc(o.dsem, 16)
                elif o.signal:
                    ins.then_inc(self.sem[e], 1)

        @block.tensor
        def _(eng):
            run("pe", eng)

        @block.scalar
        def _(eng):
            run("act", eng)

        @block.vector
        def _(eng):
            run("dve", eng)

        @block.gpsimd
        def _(eng):
            run("pool", eng)

        @block.sync
        def _(eng):
            run("sp", eng)


class Arena:
    def __init__(self, nc, base, cap):
        self.nc = nc
        self.base = base
        self.cap = cap
        self.top = base
        self.n = 0

    def alloc(self, shape, dtype, name=None):
        nbytes = int(np.prod(shape[1:])) * mybir.dt.size(dtype)
        off = (self.top + 31) // 32 * 32
        assert off + nbytes <= self.cap, ("SBUF arena overflow", name, off, nbytes, self.cap)
        self.top = off + nbytes
        self.n += 1
        nm = "%s_%d_%d" % (name or "t", off, self.n)
        return self.nc.alloc_sbuf_tensor_at(nm, list(shape), dtype, offset=off)

    def mark(self):
        return self.top

    def reset(self, m):
        self.top = m


def rel_bucket_np(rel):
    n = np.maximum(rel, 0)
    max_exact = 16
    nf = np.maximum(n, 1).astype(np.float32)
    large = max_exact + (np.log(nf / max_exact) / np.log(128 / max_exact) * (32 - max_exact)).astype(np.int32)
    large = np.minimum(large, 31)
    return np.where(n < max_exact, n, large)


def host_constants():
    c = {}
    c["ident"] = np.eye(128, dtype=np.float32)
    rel = np.arange(WZ) - 128
    b = rel_bucket_np(rel)
    oh = np.zeros((32, WZ), np.float32)
    oh[b, np.arange(WZ)] = 1.0
    oh[:, rel < 0] = 0.0
    c["bucket_oh"] = oh
    c["relmask"] = np.repeat((rel >= 0).astype(np.float32)[None, :], 128, axis=0)
    pc = np.zeros((128, 2, 16), np.float32)
    for ch in range(2):
        for p in range(128):
            w = 2 ** (2 * ch + p // 64 + 1)
            pc[p, ch, :] = 1.0 / np.minimum(np.arange(16) + 1, w)
    c["poolc"] = pc
    bd = np.zeros((128, 16), np.float32)
    bd[np.arange(128), np.arange(128) // 8] = 1.0
    c["blockdiag"] = bd
    sel = np.zeros((64, 2, 32), np.float32)
    for h in range(4):
        for cc in range(2):
            for q in range(8):
                sel[h * 16 + cc * 8 + q, cc, h * 8 + q] = 1.0
    c["sel"] = sel
    c["iota_f"] = np.arange(128, dtype=np.float32).reshape(128, 1)
    return c


CONST_SHAPES = {
    "ident": ([128, 128], F32), "bucket_oh": ([32, WZ], F32), "relmask": ([128, WZ], F32),
    "poolc": ([128, 2, 16], F32), "blockdiag": ([128, 16], F32), "sel": ([64, 2, 32], F32),
    "iota_f": ([128, 1], F32),
}

IN_SHAPES = {
    "x_all": ([NTOK, D], F32), "mem": ([256, D], F32),
    "cache_kv": ([2560 * 128, 1024], F32),
    "page_table": ([1, NSEQ * NPAGE], I32),
    "state_pool": ([NSEQ * 15, 256], F32), "state_conv": ([NSEQ * 2, D_FF], F32),
    "cmem_k": ([NSEQ, 256, 256], F32), "cmem_v": ([NSEQ, 256, 256], F32),
    "norm1_g": ([D], F32), "w_in": ([D, D_IN], F32),
    "lam_q1": ([1, 64], F32), "lam_k1": ([1, 64], F32), "lam_q2": ([1, 64], F32), "lam_k2": ([1, 64], F32),
    "subln_g": ([128], F32), "w_pool_grp": ([4, 64, 64], F32), "pool_scale": ([256], F32),
    "w_br_attn": ([512, D], F32), "w_br_pool": ([256, D], F32), "w_br_mem": ([256, D], F32),
    "mem_norm_g": ([D], F32), "w_mem_kv": ([D, 512], F32), "w_out": ([D, D], F32),
    "norm2_g": ([D], F32), "w_ffn_gate": ([D, D_FF], F32), "w_ffn_up": ([D, D_FF], F32),
    "ffn_conv_w": ([3, D_FF], F32), "ffn_conv_b": ([D_FF], F32), "w_ffn_down": ([D_FF, D], F32),
    "rel_bias": ([32, 4], F32), "final_norm_g": ([D], F32),
}

OUT_SHAPES = {
    "y_all": [NTOK, D], "newk": [NTOK, 512], "newv": [NTOK, 512],
    "pool_p": [15, 256], "pool_s": [NSEQ, 15, 256],
    "conv_all": [2 + 2 * NSEQ, D_FF],
    "memk": [256, 256], "memv": [256, 256],
}


def build_program(phases=("all",), debug=None, nphys=2560):
    nc = bass.Bass("TRN2", target_bir_lowering=False)
    I = {}
    for k, (shp, dt) in {**IN_SHAPES, **CONST_SHAPES}.items():
        if k == "cache_kv":
            shp = [nphys * 128, 1024]
        I[k] = nc.dram_tensor(k, shp, dt, kind="ExternalInput").ap()
    O = {}
    for k, shp in OUT_SHAPES.items():
        O[k] = nc.dram_tensor(k, shp, F32, kind="ExternalOutput").ap()
    zscr = nc.dram_tensor("zscr", [128, 4 * WZ], F32, kind="Internal").ap()
    wscr_t = nc.dram_tensor("wscr", [NFF, 128, 2048], BF16, kind="Internal").ap()
    dbg_out = {}
    if debug:
        for k, shp in debug.items():
            dbg_out[k] = nc.dram_tensor("dbg_" + k, shp, F32, kind="ExternalOutput").ap()

    with ExitStack() as es:
        kk = K(nc, es)
        banks = [es.enter_context(nc.psum_tensor("bank%d" % i, [128, 512], F32)) for i in range(8)]
        pb = [Buf("psum%d" % i) for i in range(8)]
        block = es.enter_context(nc.Block())
        _build(nc, kk, I, O, zscr, banks, pb, dbg_out, phases, wscr_t)
        kk.emit(block)
    return nc


def _build(nc, kk, I, O, zscr, banks, pb, dbg_out, phases, wscr):
    Bwscr = [Buf("wscr%d" % i) for i in range(NFF)]
    ALL = "all" in phases
    STOP = [p for p in phases if p.startswith("p")]
    STOP = STOP[0] if STOP else None
    ar = Arena(nc, (nc.sbuf_base + 63) // 64 * 64, nc.sbuf_top)

    def mm(out, lhsT, rhs, start, stop):
        return lambda e: e.matmul(out, lhsT, rhs, start=start, stop=stop)

    def f_tt(out, in0, in1, op):
        return lambda e: e.tensor_tensor(out=out, in0=in0, in1=in1, op=op)

    def f_ts(out, in0, s1, s2, op0, op1=None):
        if op1 is None:
            return lambda e: e.tensor_scalar(out=out, in0=in0, scalar1=s1, scalar2=None, op0=op0)
        return lambda e: e.tensor_scalar(out=out, in0=in0, scalar1=s1, scalar2=s2, op0=op0, op1=op1)

    def f_stt(out, in0, scalar, in1, op0, op1):
        return lambda e: e.scalar_tensor_tensor(out=out, in0=in0, scalar=scalar, in1=in1, op0=op0, op1=op1)

    def f_copy(out, in_):
        return lambda e: e.tensor_copy(out=out, in_=in_)

    def f_act(out, in_, func, **kw):
        return lambda e: e.activation(out=out, in_=in_, func=func, **kw)

    def f_recip(out, in_):
        return lambda e: e.reciprocal(out=out, in_=in_)

    def f_memset(ap, v):
        return lambda e: e.memset(ap, v)

    def f_tr(out, in_, ident):
        return lambda e: e.transpose(out, in_, ident)

    cst = {}
    ident_f = ar.alloc([128, 128], F32, "identf")
    ident_b = ar.alloc([128, 128], BF16, "identb")
    ones_f = ar.alloc([128, 128], F32, "onesf")
    ones_b = ar.alloc([128, 128], BF16, "onesb")
    g1T = ar.alloc([128, 8], F32, "g1T")
    g2T = ar.alloc([128, 8], F32, "g2T")
    gmT = ar.alloc([128, 8], F32, "gmT")
    gfin = ar.alloc([128, D], F32, "gfin")
    sublnT = ar.alloc([128, 1], F32, "subln")
    pscaleT = ar.alloc([128, 2], F32, "pscale")
    convw = ar.alloc([128, 3, NFF], F32, "convw")
    convb = ar.alloc([128, NFF], F32, "convb")
    lamv = ar.alloc([128, 4, 64], F32, "lamv")
    lamt = ar.alloc([128, 8], F32, "lamt")
    neg_lam = ar.alloc([128, 1], F32, "neglam")
    eps_t = ar.alloc([128, 1], F32, "eps")
    poolc = ar.alloc([128, 2, 16], F32, "poolc")
    bdiag = ar.alloc([128, 16], F32, "bdiag")
    selc = ar.alloc([64, 2, 32], F32, "sel")
    comb = ar.alloc([64, 32], F32, "comb")
    Bc = Buf("consts")

    kk.dma("sp", ident_f[:], I["ident"][:, :], writes=[Bc])
    kk.dma("pool", ident_b[:], I["ident"][:, :], writes=[Bc])
    kk.dve(lambda e: e.memset(ones_f[:], 1.0), writes=[Bc])
    kk.dve(lambda e: e.memset(ones_b[:], 1.0), writes=[Bc])
    kk.dve(lambda e: e.memset(eps_t[:], EPS), writes=[Bc])
    for t, src in ((g1T, "norm1_g"), (gmT, "mem_norm_g")):
        kk.dma("sp", t[:], I[src].rearrange("(k p) -> p k", p=128), writes=[Bc], allow_slow_non_contiguous=True)
    kk.dma("sp", gfin[:], I["final_norm_g"].partition_broadcast(128), writes=[Bc])
    kk.dma("sp", sublnT[:], I["subln_g"].rearrange("(p o) -> p o", o=1), writes=[Bc], allow_slow_non_contiguous=True)
    kk.dma("sp", pscaleT[:], I["pool_scale"].rearrange("(k p) -> p k", p=128), writes=[Bc], allow_slow_non_contiguous=True)
    for i, nm in enumerate(("lam_q1", "lam_k1", "lam_q2", "lam_k2")):
        kk.dma("sp", lamv[:, i, :], I[nm][0, :].partition_broadcast(128), writes=[Bc])
    kk.dma("sp", poolc[:], I["poolc"][:, :, :], writes=[Bc])
    kk.dma("sp", bdiag[:], I["blockdiag"][:, :], writes=[Bc])
    kk.dma("sp", selc[:], I["sel"][:, :, :], writes=[Bc])
    Bl = Buf("lam")
    kk.dve(lambda e: e.tensor_tensor(out=lamv[:, 0, :], in0=lamv[:, 0, :], in1=lamv[:, 1, :], op=ALU.mult), reads=[Bc], writes=[Bl])
    kk.dve(lambda e: e.tensor_tensor(out=lamv[:, 2, :], in0=lamv[:, 2, :], in1=lamv[:, 3, :], op=ALU.mult), reads=[Bl], writes=[Bl])
    kk.dve(lambda e: e.tensor_reduce(out=lamt[:, 0:1], in_=lamv[:, 0, :], axis=AX.X, op=ALU.add), reads=[Bl], writes=[Bl])
    kk.dve(lambda e: e.tensor_reduce(out=lamt[:, 1:2], in_=lamv[:, 2, :], axis=AX.X, op=ALU.add), reads=[Bl], writes=[Bl])
    kk.act(lambda e: e.activation(out=lamt[:, 2:4], in_=lamt[:, 0:2], func=AF.Exp), reads=[Bl], writes=[Bl])
    kk.dve(lambda e: e.tensor_tensor(out=lamt[:, 4:5], in0=lamt[:, 3:4], in1=lamt[:, 2:3], op=ALU.subtract), reads=[Bl], writes=[Bl])
    kk.dve(lambda e: e.tensor_scalar(out=neg_lam[:], in0=lamt[:, 4:5], scalar1=-LAM_INIT, scalar2=None, op0=ALU.add), reads=[Bl], writes=[Bl])
    kk.dve(lambda e: e.scalar_tensor_tensor(out=comb[:], in0=selc[:, 1, :], scalar=neg_lam[0:64, 0:1], in1=selc[:, 0, :],
                                            op0=ALU.mult, op1=ALU.add), reads=[Bl, Bc], writes=[Bl])
    kk.dve(lambda e: e.tensor_scalar(out=sublnT[:], in0=sublnT[:], scalar1=1.0 - LAM_INIT, scalar2=None, op0=ALU.mult), reads=[Bc], writes=[Bc])

    T0 = ar.alloc([128, 4, 128], F32, "T0")
    T1 = ar.alloc([128, 4, 128], F32, "T1")
    cmark = ar.mark()
    art = Arena(nc, nc.sbuf_top - 12 * 1024, nc.sbuf_top)
    rb = art.alloc([32, 4], F32, "rb")
    rbrep = art.alloc([32, 4, 128], F32, "rbrep")
    oh = art.alloc([32, WZ], F32, "oh")
    relmask = art.alloc([128, WZ], F32, "relmask")
    erow = art.alloc([128, 4, WZ], F32, "erow")
    Bt = Buf("T")
    kk.dma("sp", rb[:], I["rel_bias"][:, :], writes=[Bt])
    kk.dma("sp", oh[:], I["bucket_oh"][:, :], writes=[Bt])
    kk.dma("sp", relmask[:], I["relmask"][:, :], writes=[Bt])
    for h in range(4):
        kk.dve(lambda e, h=h: e.tensor_copy(out=rbrep[:, h, :], in_=rb[:, h:h + 1].to_broadcast([32, 128])), reads=[Bt], writes=[Bt])
    for h in range(4):
        kk.pe(mm(banks[0][:, 0:WZ], rbrep[:, h, :], oh[:, :], True, True), reads=[Bt], writes=[pb[0]])
        kk.dve(lambda e, h=h: e.tensor_scalar(out=erow[:, h, 0:1], in0=banks[0][:, WZ - 1:WZ], scalar1=-1.0, scalar2=None, op0=ALU.mult),
               reads=[pb[0]], writes=[Bt])
        kk.act(lambda e, h=h: e.activation(out=erow[:, h, 1:WZ], in_=banks[0][:, 1:WZ], func=AF.Exp, bias=erow[:, h, 0:1]),
               reads=[pb[0], Bt], writes=[Bt])
        kk.dve(lambda e, h=h: e.tensor_tensor(out=erow[:, h, :], in0=erow[:, h, :], in1=relmask[:, :], op=ALU.mult), reads=[Bt], writes=[Bt])
    Bz = Buf("zscr")
    kk.dma("sp", zscr[:, :], erow[:].rearrange("p h w -> p (h w)"), reads=[Bt], writes=[Bz])
    for h in range(4):
        s0 = bass.AP(zscr.tensor, h * WZ + 128, [[4 * WZ - 1, 128], [1, 128]])
        s1 = bass.AP(zscr.tensor, h * WZ + 256, [[4 * WZ - 1, 128], [1, 128]])
        kk.dma("sp", T0[:, h, :], s0, reads=[Bz], writes=[Bt])
        kk.dma("sp", T1[:, h, :], s1, reads=[Bz], writes=[Bt])

    def dbg(name, ap, rd):
        if name in dbg_out:
            kk.dma("sp", dbg_out[name], ap, reads=rd)

    dbg("T0", T0[:].rearrange("p h w -> p (h w)"), [Bt])
    dbg("T1", T1[:].rearrange("p h w -> p (h w)"), [Bt])
    dbg("neglam", neg_lam[:], [Bl])
    if STOP == "p0":
        kk.barrier()
        return
    ar.reset(cmark)


    amark = ar.mark()
    hT = ar.alloc([128, 8, NTOK], BF16, "hT")
    oT = ar.alloc([128, 4, NTOK], BF16, "oT")
    poolT = ar.alloc([128, 2, NTOK], BF16, "poolT")
    omT = ar.alloc([128, 2, NTOK], BF16, "omT")
    omark = ar.mark()
    qT = ar.alloc([128, 4, NTOK], BF16, "qT")
    kT = ar.alloc([128, 4, NTOK], BF16, "kT")
    v_bf = ar.alloc([128, NT, 512], BF16, "vbf")
    qmT = ar.alloc([128, 2, NTOK], BF16, "qmT")
    B_hT = [Buf("hT%d" % t) for t in range(NT)]
    B_qT = [Buf("qT%d" % i) for i in range(len(TCH))]
    B_kT = [Buf("kT%d" % i) for i in range(len(TCH))]
    B_qmT = [Buf("qmT%d" % i) for i in range(len(TCH))]
    B_v = [Buf("v%d" % t) for t in range(NT)]
    B_oT = [Buf("oT%d" % i) for i in range(len(TCH))]
    B_poolT = [Buf("poolT%d" % i) for i in range(len(TCH))]
    B_omT = [Buf("omT%d" % i) for i in range(len(TCH))]
    wmark = ar.mark()

    def tiles_of(tc):
        o, n = TCH[tc]
        return list(range(o // 128, (o + n) // 128))

    Bc5x = []

    def norm_stats(src_rows, xin, Bx, xn, Bxn, ss, Bss, junk):
        if src_rows is not None:
            kk.dma("sp", xin[:], src_rows, writes=[Bx])
        kk.act(f_act(junk[:], xin[:], AF.Square, accum_out=ss[:, 0:1]), reads=[Bx], writes=[Bss, Bjunk])
        kk.act(f_act(ss[:, 1:2], ss[:, 0:1], AF.Sqrt, scale=1.0 / D, bias=eps_t[:, 0:1]), reads=[Bss, Bc], writes=[Bss])
        kk.dve(f_recip(ss[:, 2:3], ss[:, 1:2]), reads=[Bss], writes=[Bss])
        kk.dve(f_ts(xn[:], xin[:], ss[:, 2:3], None, ALU.mult), reads=[Bx, Bss], writes=[Bxn])

    def norm_tr(gT, dst, dst_cols, xn, Bxn, bank, Bbank, Bdst):
        pbf = bank[:].bitcast(BF16)
        for kc in range(8):
            kk.pe(f_tr(pbf[:, kc * 128:(kc + 1) * 128], xn[:, kc * 128:(kc + 1) * 128], ident_b[:]), reads=[Bxn, Bc], writes=[Bbank])
        kk.dve(f_tt(dst[:, :, dst_cols], pbf[:, 0:1024].rearrange("p (k t) -> p k t", k=8),
                    gT[:, :].unsqueeze(2).to_broadcast([128, 8, 128]), ALU.mult),
               reads=[Bbank, Bc] + Bc5x, writes=[Bdst])

    def norm_transpose(src_rows, gT, dst, dst_cols, xin, Bx, xn, Bxn, ss, Bss, bank, Bbank, Bdst, junk, i):
        norm_stats(src_rows, xin, Bx, xn, Bxn, ss, Bss, junk)
        norm_tr(gT, dst, dst_cols, xn, Bxn, bank, Bbank, Bdst)

    xins = [ar.alloc([128, D], F32, "xin%d" % i) for i in range(3)]
    Bxins = [Buf("xin%d" % i) for i in range(3)]
    xns = [ar.alloc([128, D], BF16, "xn%d" % i) for i in range(2)]
    Bxns = [Buf("xn%d" % i) for i in range(2)]
    sss = [ar.alloc([128, 4], F32, "ss%d" % i) for i in range(3)]
    Bsss = [Buf("ss%d" % i) for i in range(3)]
    junk = ar.alloc([128, D], BF16, "junk")
    Bjunk = Buf("junk")
    for t in range(NT):
        norm_transpose(I["x_all"][t * 128:(t + 1) * 128, :], g1T, hT, slice(t * 128, (t + 1) * 128),
                       xins[t % 3], Bxins[t % 3], xns[t % 2], Bxns[t % 2], sss[t % 3], Bsss[t % 3],
                       banks[t % 2], pb[t % 2], B_hT[t], junk, t)
    if "hT" in dbg_out:
        hdbg = ar.alloc([128, 8 * 128], F32, "hdbg")
        Bh = Buf("hdbg")
        kk.dve(lambda e: e.tensor_copy(out=hdbg[:].rearrange("p (k t) -> p k t", k=8), in_=hT[:, :, 2048:2176]), reads=B_hT, writes=[Bh])
        dbg("hT", hdbg[:], [Bh])
    kk.barrier()
    if STOP == "p1":
        return
    ar.reset(wmark)

    Bc5 = Buf("consts5")
    kk.dma("sp", g2T[:], I["norm2_g"].rearrange("(k p) -> p k", p=128), writes=[Bc5], allow_slow_non_contiguous=True)
    kk.dma("sp", convw[:], I["ffn_conv_w"].rearrange("j (c p) -> p j c", p=128), writes=[Bc5], allow_slow_non_contiguous=True)
    kk.dma("sp", convb[:], I["ffn_conv_b"].rearrange("(c p) -> p c", p=128), writes=[Bc5], allow_slow_non_contiguous=True)
    Bc5x.append(Bc5)
    wps = [ar.alloc([128, 8, 512], BF16, "wp%d" % i) for i in range(2)]
    Bwps = [Buf("wp%d" % i) for i in range(2)]
    stg = [ar.alloc([128, 512], F32, "stg%d" % i) for i in range(3)]
    Bstg = [Buf("stg%d" % i) for i in range(3)]
    nstg = [0]
    Eb = ar.alloc([128, 15 + 2048], F32, "Eb")
    Es = ar.alloc([128, NSEQ, 23], F32, "Es")
    W1 = ar.alloc([128, 15 + 2048], F32, "W1")
    W1s = ar.alloc([128, NSEQ, 23], F32, "W1s")
    W2 = ar.alloc([128, 15 + 2048], F32, "W2")
    W2s = ar.alloc([128, NSEQ, 23], F32, "W2s")
    dTb = ar.alloc([128, NTOK], BF16, "dTb")
    tmp16 = ar.alloc([128, 16], F32, "tmp16")
    bdw = ar.alloc([128, 128], BF16, "bdw")
    stp = ar.alloc([120, 2, 256], F32, "stp")
    BE, BW1, BW2, BdT, Bbdw, Bstp, Bt16 = Buf("E"), Buf("W1"), Buf("W2"), Buf("dT"), Buf("bdw"), Buf("stp"), Buf("t16")

    def load_wpiece(i, c0):
        kk.dma("pool", wps[i][:], I["w_in"][:, c0:c0 + 512].rearrange("(k p) c -> p k c", p=128), writes=[Bwps[i]])

    def fm_group(wp, Bwp, col0, tc, bank, Bbank):
        o, n = TCH[tc]
        for kc in range(8):
            kk.pe(mm(bank[:, 0:n], wp[:, kc, col0:col0 + 128], hT[:, kc, o:o + n], kc == 0, kc == 7),
                  reads=[Bwp] + [B_hT[t] for t in tiles_of(tc)], writes=[Bbank])

    def tm_group(wp, Bwp, t, bank, Bbank, ncols=512, c0=0):
        for kc in range(8):
            kk.pe(mm(bank[:, 0:ncols], hT[:, kc, t * 128:(t + 1) * 128], wp[:, kc, c0:c0 + ncols], kc == 0, kc == 7),
                  reads=[Bwp, B_hT[t]], writes=[Bbank])

    nb = [0]

    def next_bank():
        b = nb[0] % 4
        nb[0] += 1
        return banks[b], pb[b]

    ev = [0]

    def evac_copy(out_ap, in_ap, reads, writes):
        ev[0] += 1
        if ev[0] % 2 == 0:
            kk.dve(lambda e: e.tensor_copy(out=out_ap, in_=in_ap), reads=reads, writes=writes)
        else:
            kk.act(lambda e: e.activation(out=out_ap, in_=in_ap, func=AF.Copy), reads=reads, writes=writes)

    wps.append(ar.alloc([128, 8, 512], BF16, "wp2"))
    Bwps.append(Buf("wp2"))
    WU, WQ, WK, WV = 0, 1, 2, 1
    load_wpiece(WU, 1536)
    load_wpiece(WQ, 0)
    load_wpiece(WK, 512)

    def pool_gen():
        kk.dve(f_memset(Eb[:, 0:15], 0.0), writes=[BE])
        kk.dve(f_memset(bdw[:], 0.0), writes=[Bbdw])
        kk.dma("sp", stp[:, 0, :], I["state_pool"][0:120, :], writes=[Bstp])
        kk.dma("sp", stp[:, 1, :], I["state_pool"][120:240, :], writes=[Bstp])
        yield
        for ch in range(2):
            for tc in range(len(TCH)):
                o, n = TCH[tc]
                bk, Bb = next_bank()
                fm_group(wps[WU], Bwps[WU], ch * 128, tc, bk, Bb)
                if tc < 4:
                    evac_copy(Eb[:, 15 + o:15 + o + n], bk[:, 0:n], [Bb], [BE])
                else:
                    evac_copy(Es[:, :, 15:23], bk[:, 0:128].rearrange("p (s i) -> p s i", i=8), [Bb], [BE])
                yield
            for j in range(2):
                bk, Bb = next_bank()
                kk.pe(mm(bk[:, 0:120], stp[:, j, ch * 128:(ch + 1) * 128], ident_f[0:120, 0:120], True, True), reads=[Bstp, Bc], writes=[Bb])
                evac_copy(Es[:, j * 8:(j + 1) * 8, 0:15], bk[:, 0:120].rearrange("p (s r) -> p s r", r=15), [Bb], [BE])
                yield

            def dbl(dst, dsts, src, srcs, sh, first):
                lo = 2 * sh - 1
                kk.dve(f_tt(dst[:, lo:], src[:, lo:], src[:, lo - sh:15 + 2048 - sh], ALU.add),
                       reads=[first], writes=[BW1 if dst is W1 else BW2])
                kk.dve(f_tt(dsts[:, :, lo:], srcs[:, :, lo:], srcs[:, :, lo - sh:23 - sh], ALU.add),
                       reads=[first], writes=[BW1 if dst is W1 else BW2])
            dbl(W1, W1s, Eb, Es, 1, BE)
            yield
            dbl(W2, W2s, W1, W1s, 2, BW1)
            yield
            if ch == 1:
                dbl(W1, W1s, W2, W2s, 4, BW2)
                yield
                dbl(W2, W2s, W1, W1s, 8, BW1)
                yield
            for half, (Wb, Wbs, BWb) in enumerate(((W1, W1s, BW1), (W2, W2s, BW2))):
                ps = slice(half * 64, (half + 1) * 64)
                kk.dve(f_stt(dTb[ps, 0:2048], Wb[ps, 15:15 + 2048], poolc[ps, ch, 15:16], Eb[ps, 15:15 + 2048], ALU.mult, ALU.subtract),
                       reads=[BWb, BE, Bc], writes=[BdT])
                yield
                kk.dve(f_tt(tmp16[ps, :], Wb[ps, 15:31], poolc[ps, ch, :], ALU.mult), reads=[BWb, Bc], writes=[Bt16])
                kk.dve(f_tt(dTb[ps, 0:16], tmp16[ps, :], Eb[ps, 15:31], ALU.subtract), reads=[Bt16, BE, BdT], writes=[BdT])
                kk.dve(f_stt(dTb[ps, 2048:2176].rearrange("p (s i) -> p s i", i=8), Wbs[ps, :, 15:23], poolc[ps, ch, 15:16],
                             Es[ps, :, 15:23], ALU.mult, ALU.subtract),
                       reads=[BWb, BE, Bc, BdT], writes=[BdT])
                yield
            for half in range(2):
                ps = slice(half * 64, (half + 1) * 64)
                kk.dma("pool", bdw[ps, half * 64:(half + 1) * 64], I["w_pool_grp"][2 * ch + half, :, :], reads=[Bbdw], writes=[Bbdw])
            for tc in range(len(TCH)):
                o, n = TCH[tc]
                bk, Bb = next_bank()
                kk.pe(mm(bk[:, 0:n], bdw[:, :], dTb[:, o:o + n], True, True), reads=[Bbdw, BdT], writes=[Bb])
                kk.dve(f_ts(poolT[:, ch, o:o + n], bk[:, 0:n], pscaleT[:, ch:ch + 1], None, ALU.mult), reads=[Bb, Bc], writes=[B_poolT[tc]])
                yield

    pgen = pool_gen()

    def pstep():
        next(pgen, None)

    for hp in range(2):
        for tc in range(len(TCH)):
            o, n = TCH[tc]
            bk, Bb = next_bank()
            fm_group(wps[WU], Bwps[WU], 256 + hp * 128, tc, bk, Bb)
            evac_copy(qmT[:, hp, o:o + n], bk[:, 0:n], [Bb], [B_qmT[tc]])
    for t in (15, 16):
        bk, Bb = next_bank()
        tm_group(wps[WU], Bwps[WU], t, bk, Bb, ncols=256, c0=0)
        si = nstg[0] % 3
        nstg[0] += 1
        evac_copy(stg[si][:, 0:256], bk[:, 0:256], [Bb], [Bstg[si]])
        if t == 15:
            kk.dma("sp", O["pool_p"][:, :], stg[si][113:128, 0:256], reads=[Bstg[si]])
        else:
            for s_ in range(NSEQ):
                kk.dma("sp", O["pool_s"][s_, 7:15, :], stg[si][s_ * 8:(s_ + 1) * 8, 0:256], reads=[Bstg[si]])
    kk.dma("sp", O["pool_s"][:, 0:7, :], I["state_pool"].rearrange("(s r) c -> s r c", r=15)[:, 8:15, :])
    for h in range(4):
        for tc in range(len(TCH)):
            o, n = TCH[tc]
            bk, Bb = next_bank()
            fm_group(wps[WQ], Bwps[WQ], h * 128, tc, bk, Bb)
            evac_copy(qT[:, h, o:o + n], bk[:, 0:n], [Bb], [B_qT[tc]])
            pstep()
    load_wpiece(WV, 1024)
    for h in range(4):
        for tc in range(len(TCH)):
            o, n = TCH[tc]
            bk, Bb = next_bank()
            fm_group(wps[WK], Bwps[WK], h * 128, tc, bk, Bb)
            evac_copy(kT[:, h, o:o + n], bk[:, 0:n], [Bb], [B_kT[tc]])
            pstep()
    for t in range(NT):
        bk, Bb = next_bank()
        tm_group(wps[WK], Bwps[WK], t, bk, Bb)
        si = nstg[0] % 3
        nstg[0] += 1
        evac_copy(stg[si][:], bk[:, :], [Bb], [Bstg[si]])
        kk.dma("sp", O["newk"][t * 128:(t + 1) * 128, :], stg[si][:], reads=[Bstg[si]])
        pstep()
    for t in range(NT):
        bk, Bb = next_bank()
        tm_group(wps[WV], Bwps[WV], t, bk, Bb)
        si = nstg[0] % 3
        nstg[0] += 1
        kk.act(f_act(stg[si][:], bk[:, :], AF.Copy), reads=[Bb], writes=[Bstg[si]])
        kk.dve(f_copy(v_bf[:, t, :], stg[si][:]), reads=[Bstg[si]], writes=[B_v[t]])
        kk.dma("sp", O["newv"][t * 128:(t + 1) * 128, :], stg[si][:], reads=[Bstg[si]])
        pstep()
    for _ in pgen:
        pass
    if "poolT" in dbg_out:
        pdbg = ar.alloc([128, 2 * 256], F32, "pdbg")
        Bp = Buf("pdbg")
        kk.dve(lambda e: e.tensor_copy(out=pdbg[:, 0:128], in_=poolT[:, 0, 0:128]), reads=B_poolT, writes=[Bp])
        kk.dve(lambda e: e.tensor_copy(out=pdbg[:, 128:256], in_=poolT[:, 1, 0:128]), reads=B_poolT, writes=[Bp])
        kk.dve(lambda e: e.tensor_copy(out=pdbg[:, 256:384], in_=poolT[:, 0, 2048:2176]), reads=B_poolT, writes=[Bp])
        kk.dve(lambda e: e.tensor_copy(out=pdbg[:, 384:512], in_=poolT[:, 1, 2048:2176]), reads=B_poolT, writes=[Bp])
        dbg("poolT", pdbg[:], [Bp])
    kk.barrier()
    if STOP == "p2":
        return
    ar.reset(wmark)


    P3 = ALL or "p3" in phases
    xin0 = ar.alloc([128, D], F32, "mxin0")
    xin1 = ar.alloc([128, D], F32, "mxin1")
    mxn = ar.alloc([128, D], BF16, "mxn")
    mjunk = ar.alloc([128, D], BF16, "mjunk")
    mss = [ar.alloc([128, 4], F32, "mss%d" % i) for i in range(2)]
    mhT = ar.alloc([128, 8, 256], BF16, "mhT")
    wmkv = ar.alloc([128, 8, 512], BF16, "wmkv")
    memkT = ar.alloc([128, 2, 256], BF16, "memkT")
    memv_pad = ar.alloc([128, 2, 4, 128], BF16, "memvpad")
    onesE = ar.alloc([128, 128], BF16, "onesE")
    onesO = ar.alloc([128, 128], BF16, "onesO")
    mstg = [ar.alloc([128, 512], F32, "mstg%d" % i) for i in range(2)]
    Bmx = [Buf("mx0"), Buf("mx1")]
    Bmxn, Bmhs, Bwmkv, BmkT, Bmvp, Bones2 = Buf("mxn"), [Buf("mh0"), Buf("mh1")], Buf("wmkv"), Buf("memkT"), Buf("memvpad"), Buf("ones2")
    Bmss = [Buf("mss0"), Buf("mss1")]
    Bmstg = [Buf("mstg0"), Buf("mstg1")]
    kk.dma("pool", wmkv[:], I["w_mem_kv"].rearrange("(k p) c -> p k c", p=128), writes=[Bwmkv])
    kk.dve(f_memset(memv_pad[:], 0.0), writes=[Bmvp])
    kk.dve(f_memset(onesE[:], 0.0), writes=[Bones2])
    kk.dve(f_memset(onesO[:], 0.0), writes=[Bones2])
    kk.dve(f_memset(onesE[:, 0:64], 1.0), writes=[Bones2])
    kk.dve(f_memset(onesO[:, 64:128], 1.0), writes=[Bones2])
    for mt in range(2):
        norm_transpose(I["mem"][mt * 128:(mt + 1) * 128, :], gmT, mhT, slice(mt * 128, (mt + 1) * 128),
                       (xin0, xin1)[mt], Bmx[mt], mxn, Bmxn, mss[mt], Bmss[mt], banks[mt], pb[mt], Bmhs[mt], mjunk, mt)
    for mt in range(2):
        bk, Bb = banks[2 + mt], pb[2 + mt]
        for kc in range(8):
            kk.pe(mm(bk[:, :], mhT[:, kc, mt * 128:(mt + 1) * 128], wmkv[:, kc, :], kc == 0, kc == 7), reads=[Bmhs[mt], Bwmkv], writes=[Bb])
        kk.act(f_act(mstg[mt][:], bk[:, :], AF.Copy), reads=[Bb], writes=[Bmstg[mt]])
        for h in range(4):
            kk.dve(f_copy(memv_pad[:, mt, h, (h % 2) * 64:(h % 2) * 64 + 64], mstg[mt][:, 256 + h * 64:256 + (h + 1) * 64]), reads=[Bmstg[mt]], writes=[Bmvp])
        kk.dma("sp", O["memk"][mt * 128:(mt + 1) * 128, :], mstg[mt][:, 0:256], reads=[Bmstg[mt]])
        kk.dma("sp", O["memv"][mt * 128:(mt + 1) * 128, :], mstg[mt][:, 256:512], reads=[Bmstg[mt]])
    for hp in range(2):
        bk, Bb = banks[4 + hp], pb[4 + hp]
        for kc in range(8):
            kk.pe(mm(bk[:, 0:256], wmkv[:, kc, hp * 128:(hp + 1) * 128], mhT[:, kc, :], kc == 0, kc == 7), reads=Bmhs + [Bwmkv], writes=[Bb])
        kk.dve(f_copy(memkT[:, hp, :], bk[:, 0:256]), reads=[Bb], writes=[BmkT])

    mpT = [ar.alloc([128, 2, 512], BF16, "mpT%d" % i) for i in range(2)]
    BmpT = [Buf("mpT0"), Buf("mpT1")]
    mrs = [ar.alloc([128, 512], F32, "mrs%d" % i) for i in range(2)]
    Bmrs = [Buf("mrs0"), Buf("mrs1")]
    it = 0
    for tc in range(4):
        o, n = TCH[tc]
        for hp in range(2):
            oc, Boc = banks[4 + 2 * (it % 2)], pb[4 + 2 * (it % 2)]
            oz, Boz = banks[5 + 2 * (it % 2)], pb[5 + 2 * (it % 2)]
            for mt in range(2):
                sa, Bsa = banks[2 * mt], pb[2 * mt]
                sb, Bsb = banks[2 * mt + 1], pb[2 * mt + 1]
                p_, Bp_ = mpT[mt], BmpT[mt]
                kk.pe(mm(sa[:, :], memkT[0:64, hp, mt * 128:(mt + 1) * 128], qmT[0:64, hp, o:o + n], True, True), reads=[BmkT, B_qmT[tc]], writes=[Bsa])
                kk.pe(mm(sb[:, :], memkT[64:128, hp, mt * 128:(mt + 1) * 128], qmT[64:128, hp, o:o + n], True, True), reads=[BmkT, B_qmT[tc]], writes=[Bsb])
                kk.act(f_act(p_[:, 0, :], sa[:, :], AF.Exp, scale=0.125), reads=[Bsa], writes=[Bp_])
                kk.act(f_act(p_[:, 1, :], sb[:, :], AF.Exp, scale=0.125), reads=[Bsb], writes=[Bp_])
                kk.pe(mm(oc[:, :], memv_pad[:, mt, 2 * hp, :], p_[:, 0, :], mt == 0, False), reads=[Bmvp, Bp_], writes=[Boc])
                kk.pe(mm(oc[:, :], memv_pad[:, mt, 2 * hp + 1, :], p_[:, 1, :], False, mt == 1), reads=[Bmvp, Bp_], writes=[Boc])
                kk.pe(mm(oz[:, :], onesE[:, :], p_[:, 0, :], mt == 0, False), reads=[Bones2, Bp_], writes=[Boz])
                kk.pe(mm(oz[:, :], onesO[:, :], p_[:, 1, :], False, mt == 1), reads=[Bones2, Bp_], writes=[Boz])
            r_, Br_ = mrs[it % 2], Bmrs[it % 2]
            kk.act(f_act(r_[:], oz[:, :], AF.Ln), reads=[Boz], writes=[Br_])
            kk.act(f_act(r_[:], r_[:], AF.Exp, scale=-1.0), reads=[Br_], writes=[Br_])
            kk.dve(f_tt(omT[:, hp, o:o + n], oc[:, :], r_[:], ALU.mult), reads=[Boc, Br_], writes=[B_omT[tc]])
            it += 1
    NCM = 4
    cmk = [ar.alloc([128, 2, 256], BF16, "cmk%d" % i) for i in range(NCM)]
    Bcmk = [Buf("cmk%d" % i) for i in range(NCM)]
    cmv = [ar.alloc([128, 2, 256], BF16, "cmv%d" % i) for i in range(NCM)]
    Bcmv = [Buf("cmv%d" % i) for i in range(NCM)]
    cmvp = [ar.alloc([128, 2, 4, 128], BF16, "cmvp%d" % i) for i in range(2)]
    Bcmvp = [Buf("cmvp0"), Buf("cmvp1")]
    kTs = [ar.alloc([128, 2, 256], BF16, "kTs%d" % i) for i in range(2)]
    BkTs = [Buf("kTs0"), Buf("kTs1")]
    pTs = [ar.alloc([128, 2, 32], BF16, "pTs%d" % i) for i in range(2)]
    BpTs = [Buf("pTs0"), Buf("pTs1")]
    for i in range(2):
        kk.pool(f_memset(cmvp[i][:], 0.0), writes=[Bcmvp[i]])
    ocs, Bocs = banks[6], pb[6]
    ozs, Bozs = banks[7], pb[7]

    def load_cm(s2):
        r = s2 % NCM
        kk.dma("pool", cmk[r][:], I["cmem_k"][s2].rearrange("(t p) c -> p t c", p=128), writes=[Bcmk[r]])
        kk.dma("pool", cmv[r][:], I["cmem_v"][s2].rearrange("(t p) c -> p t c", p=128), writes=[Bcmv[r]])

    for s2 in range(NCM - 1):
        load_cm(s2)
    for s_ in range(NSEQ):
        b = s_ % 2
        r = s_ % NCM
        if s_ + NCM - 1 < NSEQ:
            load_cm(s_ + NCM - 1)
        for h in range(4):
            kk.pool(f_copy(cmvp[b][:, :, h, (h % 2) * 64:(h % 2) * 64 + 64], cmv[r][:, :, h * 64:(h + 1) * 64]), reads=[Bcmv[r]], writes=[Bcmvp[b]])
        tb, Btb = banks[b], pb[b]
        tbf = tb[:].bitcast(BF16)
        for hp in range(2):
            for mt in range(2):
                kk.pe(f_tr(tbf[:, (hp * 2 + mt) * 128:(hp * 2 + mt + 1) * 128], cmk[r][:, mt, hp * 128:(hp + 1) * 128], ident_b[:]),
                      reads=[Bcmk[r], Bc], writes=[Btb])
        kk.dve(f_copy(kTs[b][:].rearrange("p h m -> p (h m)"), tbf[:, 0:512]), reads=[Btb], writes=[BkTs[b]])
        sa, Bsa = banks[2 + 2 * b], pb[2 + 2 * b]
        sb, Bsb = banks[3 + 2 * b], pb[3 + 2 * b]
        qs = slice(2048 + 8 * s_, 2048 + 8 * s_ + 8)
        for hp in range(2):
            for mt in range(2):
                c0 = (hp * 2 + mt) * 8
                kk.pe(mm(sa[:, c0:c0 + 8], kTs[b][0:64, hp, mt * 128:(mt + 1) * 128], qmT[0:64, hp, qs], True, True), reads=[BkTs[b], B_qmT[4]], writes=[Bsa])
                kk.pe(mm(sb[:, c0:c0 + 8], kTs[b][64:128, hp, mt * 128:(mt + 1) * 128], qmT[64:128, hp, qs], True, True), reads=[BkTs[b], B_qmT[4]], writes=[Bsb])
        kk.act(f_act(pTs[b][:, 0, :], sa[:, 0:32], AF.Exp, scale=0.125), reads=[Bsa], writes=[BpTs[b]])
        kk.act(f_act(pTs[b][:, 1, :], sb[:, 0:32], AF.Exp, scale=0.125), reads=[Bsb], writes=[BpTs[b]])
        for hp in range(2):
            oc0 = (s_ * 2 + hp) * 8
            for mt in range(2):
                c0 = (hp * 2 + mt) * 8
                kk.pe(mm(ocs[:, oc0:oc0 + 8], cmvp[b][:, mt, 2 * hp, :], pTs[b][:, 0, c0:c0 + 8], mt == 0, False), reads=[Bcmvp[b], BpTs[b]], writes=[Bocs])
                kk.pe(mm(ocs[:, oc0:oc0 + 8], cmvp[b][:, mt, 2 * hp + 1, :], pTs[b][:, 1, c0:c0 + 8], False, mt == 1), reads=[Bcmvp[b], BpTs[b]], writes=[Bocs])
            for mt in range(2):
                c0 = (hp * 2 + mt) * 8
                kk.pe(mm(ozs[:, oc0:oc0 + 8], onesE[:, :], pTs[b][:, 0, c0:c0 + 8], mt == 0, False), reads=[Bones2, BpTs[b]], writes=[Bozs])
                kk.pe(mm(ozs[:, oc0:oc0 + 8], onesO[:, :], pTs[b][:, 1, c0:c0 + 8], False, mt == 1), reads=[Bones2, BpTs[b]], writes=[Bozs])
    kk.act(f_act(mrs[0][:, 0:256], ozs[:, 0:256], AF.Ln), reads=[Bozs], writes=[Bmrs[0]])
    kk.act(f_act(mrs[0][:, 0:256], mrs[0][:, 0:256], AF.Exp, scale=-1.0), reads=[Bmrs[0]], writes=[Bmrs[0]])
    kk.dve(f_tt(omT[:, :, 2048:2176].rearrange("p h (s q) -> p h s q", q=8),
                ocs[:, 0:256].rearrange("p (s h q) -> p h s q", h=2, q=8),
                mrs[0][:, 0:256].rearrange("p (s h q) -> p h s q", h=2, q=8), ALU.mult),
           reads=[Bocs, Bmrs[0]], writes=[B_omT[4]])
    if "omT" in dbg_out:
        odbg = ar.alloc([128, 512], F32, "odbg")
        Bo = Buf("odbg")
        kk.dve(f_copy(odbg[:, 0:128], omT[:, 0, 0:128]), reads=B_omT, writes=[Bo])
        kk.dve(f_copy(odbg[:, 128:256], omT[:, 1, 1920:2048]), reads=B_omT, writes=[Bo])
        kk.dve(f_copy(odbg[:, 256:384], omT[:, 0, 2048:2176]), reads=B_omT, writes=[Bo])
        kk.dve(f_copy(odbg[:, 384:512], omT[:, 1, 2048:2176]), reads=B_omT, writes=[Bo])
        dbg("omT", odbg[:], [Bo])
    kk.barrier()
    if STOP == "p3b":
        return
    ar.reset(wmark)


    apT = [ar.alloc([128, 2, 512], BF16, "apT%d" % i) for i in range(3)]
    BapT = [Buf("apT%d" % i) for i in range(3)]
    tA = ar.alloc([128, 512], F32, "tA")
    tB = ar.alloc([128, 512], F32, "tB")
    tC = ar.alloc([128, 512], F32, "tC")
    tD = ar.alloc([128, 512], F32, "tD")
    tE = ar.alloc([128, 512], F32, "tE")
    BtA, BtB, BtC, BtD, BtE = Buf("tA"), Buf("tB"), Buf("tC"), Buf("tD"), Buf("tE")
    Sset = [((banks[0], pb[0]), (banks[1], pb[1])), ((banks[6], pb[6]), (banks[7], pb[7]))]
    O0, O1, Z0, Z1 = banks[2], banks[3], banks[4], banks[5]
    BO0, BO1, BZ0, BZ1 = pb[2], pb[3], pb[4], pb[5]
    units = []
    for h in range(4):
        for c in range(4):
            for j in range(4 * c + 4):
                units.append((h, c, j))

    def u_lo(c, j):
        return max(j - 4 * c, 0) * 128

    def emit_qk(i):
        h, c, j = units[i]
        (S0, BS0), (S1, BS1) = Sset[i % 2]
        lo = u_lo(c, j)
        q0 = c * 512
        ks = slice(j * 128, (j + 1) * 128)
        kk.pe(mm(S0[:, lo:512], kT[0:64, h, ks], qT[0:64, h, q0 + lo:q0 + 512], True, True), reads=[B_kT[j // 4], B_qT[c]], writes=[BS0])
        kk.pe(mm(S1[:, lo:512], kT[64:128, h, ks], qT[64:128, h, q0 + lo:q0 + 512], True, True), reads=[B_kT[j // 4], B_qT[c]], writes=[BS1])

    def emit_softmax(i):
        h, c, j = units[i]
        (S0, BS0), (S1, BS1) = Sset[i % 2]
        lo = u_lo(c, j)
        jj = j - 4 * c
        pT_, BpT_ = apT[i % 3], BapT[i % 3]
        kk.act(f_act(pT_[:, 0, lo:512], S0[:, lo:512], AF.Exp, scale=0.125), reads=[BS0], writes=[BpT_])
        kk.act(f_act(pT_[:, 1, lo:512], S1[:, lo:512], AF.Exp, scale=0.125), reads=[BS1], writes=[BpT_])
        t0b = T0[:, h, :].unsqueeze(1).to_broadcast([128, 2, 128])
        t1b = T1[:, h, :].unsqueeze(1).to_broadcast([128, 2, 128])
        if jj >= 0:
            kk.dve(f_tt(pT_[:, :, lo:lo + 128], pT_[:, :, lo:lo + 128], t0b, ALU.mult), reads=[BpT_, Bt], writes=[BpT_])
            if jj < 3:
                kk.dve(f_tt(pT_[:, :, lo + 128:lo + 256], pT_[:, :, lo + 128:lo + 256], t1b, ALU.mult), reads=[BpT_, Bt], writes=[BpT_])
        elif jj == -1:
            kk.dve(f_tt(pT_[:, :, 0:128], pT_[:, :, 0:128], t1b, ALU.mult), reads=[BpT_, Bt], writes=[BpT_])

    def emit_pv(i):
        h, c, j = units[i]
        lo = u_lo(c, j)
        nj = 4 * c + 4
        pT_, BpT_ = apT[i % 3], BapT[i % 3]
        vv = v_bf[:, j, h * 128:(h + 1) * 128]
        for m, (Ob, BOb, Zb, BZb) in enumerate(((O0, BO0, Z0, BZ0), (O1, BO1, Z1, BZ1))):
            kk.pe(mm(Ob[:, lo:512], vv, pT_[:, m, lo:512], j == 0, j == nj - 1), reads=[B_v[j], BpT_], writes=[BOb])
            kk.pe(mm(Zb[:, lo:512], ones_b[:, :], pT_[:, m, lo:512], j == 0, j == nj - 1), reads=[Bc, BpT_], writes=[BZb])

    def emit_tail(h, c, SSb, BSS):
        q0 = c * 512
        kk.act(f_act(tA[:], Z0[:, :], AF.Ln), reads=[BZ0], writes=[BtA])
        kk.act(f_act(tB[:], Z1[:, :], AF.Ln), reads=[BZ1], writes=[BtB])
        kk.act(f_act(tA[:], tA[:], AF.Exp, scale=-1.0), reads=[BtA], writes=[BtA])
        kk.act(f_act(tB[:], tB[:], AF.Exp, scale=-1.0), reads=[BtB], writes=[BtB])
        kk.dve(f_tt(tA[:], O0[:, :], tA[:], ALU.mult), reads=[BO0, BtA], writes=[BtA])
        kk.dve(f_tt(tB[:], O1[:, :], tB[:], ALU.mult), reads=[BO1, BtB], writes=[BtB])
        yield
        kk.dve(f_stt(tC[:], tB[:], neg_lam[:, 0:1], tA[:], ALU.mult, ALU.add), reads=[BtA, BtB, Bl], writes=[BtC])
        kk.act(f_act(tD[:], tC[:], AF.Square), reads=[BtC], writes=[BtD])
        yield
        kk.pe(mm(SSb[:, :], ones_f[:, :], tD[:], True, True), reads=[Bc, BtD], writes=[BSS])
        kk.act(f_act(tE[:], SSb[:, :], AF.Ln, scale=1.0 / 128, bias=eps_t[:, 0:1]), reads=[BSS, Bc], writes=[BtE])
        kk.act(f_act(tE[:], tE[:], AF.Exp, scale=-0.5), reads=[BtE], writes=[BtE])
        yield
        kk.dve(f_stt(oT[:, h, q0:q0 + 512], tC[:], sublnT[:, 0:1], tE[:], ALU.mult, ALU.mult), reads=[BtC, BtE, Bc], writes=[B_oT[c]])

    tail_gen = [None]

    def tail_step():
        if tail_gen[0] is not None:
            if next(tail_gen[0], "done") == "done":
                tail_gen[0] = None

    emit_qk(0)
    for i, (h, c, j) in enumerate(units):
        emit_softmax(i)
        if i + 1 < len(units):
            emit_qk(i + 1)
        emit_pv(i)
        tail_step()
        if j == 4 * c + 3:
            while tail_gen[0] is not None:
                tail_step()
            (SSb, BSS), _ = Sset[i % 2]
            tail_gen[0] = emit_tail(h, c, SSb, BSS)
            tail_step()
    while tail_gen[0] is not None:
        tail_step()
    if "oTp" in dbg_out:
        odbg2 = ar.alloc([128, 512], F32, "odbg2")
        Bo2 = Buf("odbg2")
        kk.dve(f_copy(odbg2[:, 0:128], oT[:, 0, 0:128]), reads=B_oT, writes=[Bo2])
        kk.dve(f_copy(odbg2[:, 128:256], oT[:, 1, 640:768]), reads=B_oT, writes=[Bo2])
        kk.dve(f_copy(odbg2[:, 256:384], oT[:, 2, 1920:2048]), reads=B_oT, writes=[Bo2])
        kk.dve(f_copy(odbg2[:, 384:512], oT[:, 3, 1024:1152]), reads=B_oT, writes=[Bo2])
        dbg("oTp", odbg2[:], [Bo2])
    kk.barrier()
    if STOP == "p3c":
        return
    ar.reset(wmark)


    ptb = ar.alloc([128, NSEQ * NPAGE], I32, "ptb")
    idx = ar.alloc([128, NSEQ * NPAGE], I32, "idx")
    iotaf = ar.alloc([128, 1], F32, "iotaf")
    qpad = ar.alloc([128, 4, NSEQ, 16], BF16, "qpad")
    M15 = ar.alloc([128, 4, 2, 8], F32, "M15")
    MN = ar.alloc([128, NSEQ, 4, 2, 8], F32, "MN")
    gbc = ar.alloc([128, 128], F32, "gbc")
    NV = 14
    kvpg = [ar.alloc([128, 1024], BF16, "kvpg%d" % i) for i in range(NV)]
    Bkvpg = [Buf("kvpg%d" % i) for i in range(NV)]
    KTs = [ar.alloc([128, 4, 128], BF16, "KTs%d" % i) for i in range(2)]
    BKTs = [Buf("KTs0"), Buf("KTs1")]
    spT = [ar.alloc([128, 8, 64], BF16, "spT%d" % i) for i in range(2)]
    BspT = [Buf("spT0"), Buf("spT1")]
    pn = ar.alloc([128, 64], BF16, "pn")
    Bpn = Buf("pn")
    spsum = [ar.alloc([128, 64], F32, "spsum%d" % i) for i in range(2)]
    Bspsum = [Buf("spsum0"), Buf("spsum1")]
    pnf = ar.alloc([128, 64], F32, "pnf")
    Bpnf = Buf("pnf")
    rz = ar.alloc([64, 1], F32, "rz")
    onr = ar.alloc([64, 512], F32, "onr")
    c2 = ar.alloc([32, 4, 128], F32, "c2")
    sq2 = ar.alloc([32, 4, 128], F32, "sq2")
    ss2 = ar.alloc([32, 8], F32, "ss2")
    on3 = ar.alloc([32, 4, 128], BF16, "on3")
    Brz, Bonr, Bc2, Bsq2, Bss2, Bon3 = Buf("rz"), Buf("onr"), Buf("c2"), Buf("sq2"), Buf("ss2"), Buf("on3")
    Bsetup = Buf("p3dsetup")
    kk.dma("sp", ptb[:], I["page_table"][0, :].partition_broadcast(128), writes=[Bsetup])
    kk.dma("sp", iotaf[:], I["iota_f"][:, :], writes=[Bsetup])
    kk.dve(f_ts(idx[:], ptb[:], 128.0, iotaf[:, 0:1], ALU.mult, ALU.add), reads=[Bsetup], writes=[Bsetup])
    kk.dve(f_memset(qpad[:], 0.0), writes=[Bsetup])
    kk.dve(f_copy(qpad[0:64, :, :, 0:8], qT[0:64, :, 2048:2176].rearrange("p h (s q) -> p h s q", q=8)), reads=[B_qT[4], Bsetup], writes=[Bsetup])
    kk.dve(f_copy(qpad[64:128, :, :, 8:16], qT[64:128, :, 2048:2176].rearrange("p h (s q) -> p h s q", q=8)), reads=[B_qT[4], Bsetup], writes=[Bsetup])
    for c in range(2):
        kk.dve(f_copy(M15[:, :, c, :], T1[:, :, 0:8]), reads=[Bt], writes=[Bsetup])
    for h in range(4):
        for c in range(2):
            kk.dve(f_tt(MN[:, :, h, c, :], T0[:, h, :].rearrange("p (s q) -> p s q", q=8), bdiag[:].unsqueeze(2).to_broadcast([128, NSEQ, 8]), ALU.mult),
                   reads=[Bt, Bc], writes=[Bsetup])
    kk.dma("sp", gbc[:], I["subln_g"].partition_broadcast(128), writes=[Bsetup])
    kk.dve(f_ts(gbc[:], gbc[:], 1.0 - LAM_INIT, None, ALU.mult), reads=[Bsetup], writes=[Bsetup])
    OS, BOS = banks[2], pb[2]
    ZS, BZS = banks[3], pb[3]
    C2b, BC2 = banks[4], pb[4]
    TTb, BTT = banks[5], pb[5]
    steps = [(s_, j) for s_ in range(NSEQ) for j in range(NPAGE)]

    def pg_bufs(n):
        return kvpg[n % NV], Bkvpg[n % NV], banks[n % 2], pb[n % 2], KTs[n % 2], BKTs[n % 2]

    def emit_gather(n):
        s_, j = steps[n]
        kvb, Bkvb = kvpg[n % NV], Bkvpg[n % NV]
        col = s_ * NPAGE + j
        kk.op("pool", (lambda e, kvb=kvb, col=col: e.indirect_dma_start(
            out=kvb[:, :], out_offset=None, in_=I["cache_kv"][:, :],
            in_offset=bass.IndirectOffsetOnAxis(ap=idx[:, col:col + 1], axis=0))), reads=[Bsetup], writes=[Bkvb], dma=True)

    def emit_tr(n):
        kvb, Bkvb, tb, Btb, kt_, Bkt_ = pg_bufs(n)
        tbf = tb[:].bitcast(BF16)
        for h in range(4):
            kk.pe(f_tr(tbf[:, h * 128:(h + 1) * 128], kvb[:, h * 128:(h + 1) * 128], ident_b[:]), reads=[Bkvb, Bc], writes=[Btb])
        kk.dve(f_copy(kt_[:].rearrange("p h k -> p (h k)"), tbf[:, 0:512]), reads=[Btb], writes=[Bkt_])

    def emit_qk_s(n):
        s_, j = steps[n]
        kvb, Bkvb, tb, Btb, kt_, Bkt_ = pg_bufs(n)
        Sb, BSb = banks[6 + (j // 8)], pb[6 + (j // 8)]
        for h in range(4):
            c0 = (j % 8) * 64 + h * 16
            kk.pe(mm(Sb[:, c0:c0 + 16], kt_[:, h, :], qpad[:, h, s_, :], True, True), reads=[Bkt_, Bsetup], writes=[BSb])

    def sample_tail(s_):
        kk.dve(f_recip(rz[:], ZS[0:64, 0:1]), reads=[BZS], writes=[Brz])
        kk.dve(f_ts(onr[:], OS[0:64, :], rz[:, 0:1], None, ALU.mult), reads=[BOS, Brz], writes=[Bonr])
        yield
        kk.pe(mm(C2b[0:32, :], comb[:, :], onr[:, :], True, True), reads=[Bl, Bonr], writes=[BC2])
        yield
        kk.dve(f_copy(c2[:].rearrange("p h e -> p (h e)"), C2b[0:32, :]), reads=[BC2], writes=[Bc2])
        kk.act(f_act(sq2[:], c2[:], AF.Square), reads=[Bc2], writes=[Bsq2])
        yield
        kk.dve(lambda e: e.tensor_reduce(out=ss2[:, 0:4], in_=sq2[:], axis=AX.X, op=ALU.add), reads=[Bsq2], writes=[Bss2])
        kk.act(f_act(ss2[:, 4:8], ss2[:, 0:4], AF.Ln, scale=1.0 / 128, bias=eps_t[0:32, 0:1]), reads=[Bss2, Bc], writes=[Bss2])
        kk.act(f_act(ss2[:, 4:8], ss2[:, 4:8], AF.Exp, scale=-0.5), reads=[Bss2], writes=[Bss2])
        yield
        kk.dve(f_tt(c2[:], c2[:], ss2[:, 4:8].unsqueeze(2).to_broadcast([32, 4, 128]), ALU.mult), reads=[Bc2, Bss2], writes=[Bc2])
        kk.dve(f_tt(on3[:], c2[:], gbc[0:32, :].unsqueeze(1).to_broadcast([32, 4, 128]), ALU.mult), reads=[Bc2, Bsetup], writes=[Bon3])
        yield
        ttf = TTb[:].bitcast(BF16)
        for h in range(4):
            kk.pe(f_tr(ttf[:, h * 32:(h + 1) * 32], on3[:, h, :], ident_b[0:32, 0:32]), reads=[Bon3, Bc], writes=[BTT])
        yield
        for h in range(4):
            kk.dve(f_copy(oT[:, h, 2048 + 8 * s_:2048 + 8 * s_ + 8], ttf[:, h * 32 + h * 8:h * 32 + h * 8 + 8]), reads=[BTT], writes=[B_oT[4]])

    stail = [None]

    def stail_step():
        if stail[0] is not None:
            if next(stail[0], "done") == "done":
                stail[0] = None

    NPRE = NV - 8
    for n in range(min(NPRE, len(steps))):
        emit_gather(n)
    emit_tr(0)
    for n, (s_, j) in enumerate(steps):
        if n + NPRE < len(steps):
            emit_gather(n + NPRE)
        if n + 1 < len(steps):
            emit_tr(n + 1)
        emit_qk_s(n)
        stail_step()
        if j % 8 == 7:
            half = j // 8
            Sb, BSb = banks[6 + half], pb[6 + half]
            sp_, Bsp_ = spT[half], BspT[half]
            kk.act(f_act(sp_[:].rearrange("p j c -> p (j c)"), Sb[:, :], AF.Exp, scale=0.125), reads=[BSb], writes=[Bsp_])
            if half == 1:
                kk.dve(f_tt(sp_[:, 7, :], sp_[:, 7, :], M15[:].rearrange("p h c q -> p (h c q)"), ALU.mult), reads=[Bsp_, Bsetup], writes=[Bsp_])
            for jj in range(8):
                jp = half * 8 + jj
                m_ = n - 7 + jj
                kvb, Bkvb = kvpg[m_ % NV], Bkvpg[m_ % NV]
                kk.pe(mm(OS[0:64, :], sp_[:, jj, :], kvb[:, 512:1024], jp == 0, False), reads=[Bsp_, Bkvb], writes=[BOS])
                kk.pe(mm(ZS[0:64, 0:1], sp_[:, jj, :], ones_b[:, 0:1], jp == 0, False), reads=[Bsp_, Bc], writes=[BZS])
        if j != NPAGE - 1:
            continue
        Sb, BSb = banks[6], pb[6]
        for h in range(4):
            kk.pe(mm(Sb[:, h * 16:(h + 1) * 16], kT[:, h, 2048:2176], qpad[:, h, s_, :], True, True), reads=[B_kT[4], Bsetup], writes=[BSb])
        kk.act(f_act(pn[:], Sb[:, 0:64], AF.Exp, scale=0.125), reads=[BSb], writes=[Bpn])
        kk.dve(f_tt(pn[:], pn[:], MN[:, s_].rearrange("p h c q -> p (h c q)"), ALU.mult), reads=[Bpn, Bsetup], writes=[Bpn])
        kk.pe(mm(OS[0:64, :], pn[:, :], v_bf[:, 16, :], False, True), reads=[Bpn, B_v[16]], writes=[BOS])
        kk.pe(mm(ZS[0:64, 0:1], pn[:, :], ones_b[:, 0:1], False, True), reads=[Bpn, Bc], writes=[BZS])
        while stail[0] is not None:
            stail_step()
        stail[0] = sample_tail(s_)
        stail_step()
    while stail[0] is not None:
        stail_step()
    if "oTs" in dbg_out:
        odbg3 = ar.alloc([128, 512], F32, "odbg3")
        Bo3 = Buf("odbg3")
        kk.dve(f_copy(odbg3[:].rearrange("p (h t) -> p h t", h=4), oT[:, :, 2048:2176]), reads=B_oT, writes=[Bo3])
        dbg("oTs", odbg3[:], [Bo3])
    kk.barrier()
    if STOP == "p3d":
        return
    ar.reset(wmark)


    ar.reset(omark)
    mergedT = ar.alloc([128, 8, NTOK], BF16, "mergedT")
    B_mg = [Buf("mg%d" % i) for i in range(len(TCH))]
    p4mark = ar.mark()
    wg = [ar.alloc([128, 8, 3, 128], BF16, "wg%d" % i) for i in range(2)]
    Bwg = [Buf("wg0"), Buf("wg1")]
    wbr = [ar.alloc([128, 8, 128], BF16, "wbr%d" % i) for i in range(2)]
    Bwbr = [Buf("wbr0"), Buf("wbr1")]
    sg = [ar.alloc([128, 512], F32, "sg%d" % i) for i in range(6)]
    Bsg = [Buf("sg%d" % i) for i in range(6)]
    mt_ = [ar.alloc([128, 512], F32, "mtmp%d" % i) for i in range(4)]
    Bmt = [Buf("mtmp%d" % i) for i in range(4)]
    bankctr = [0]

    def rbank():
        b = bankctr[0] % 8
        bankctr[0] += 1
        return banks[b], pb[b]

    def load_p4(fc):
        b = fc % 2
        for g in range(3):
            c0 = 2048 + g * 1024 + fc * 128
            kk.dma("pool", wg[b][:, :, g, :], I["w_in"][:, c0:c0 + 128].rearrange("(k p) c -> p k c", p=128), writes=[Bwg[b]])
        kk.dma("pool", wbr[b][:, 0:4, :], I["w_br_attn"][:, fc * 128:(fc + 1) * 128].rearrange("(k p) c -> p k c", p=128), writes=[Bwbr[b]])
        kk.dma("pool", wbr[b][:, 4:6, :], I["w_br_pool"][:, fc * 128:(fc + 1) * 128].rearrange("(k p) c -> p k c", p=128), writes=[Bwbr[b]])
        kk.dma("pool", wbr[b][:, 6:8, :], I["w_br_mem"][:, fc * 128:(fc + 1) * 128].rearrange("(k p) c -> p k c", p=128), writes=[Bwbr[b]])

    load_p4(0)
    un = 0
    for fc in range(8):
        if fc + 1 < 8:
            load_p4(fc + 1)
        b = fc % 2
        for tc in range(len(TCH)):
            o, n = TCH[tc]
            hreads = [B_hT[t] for t in tiles_of(tc)]
            prods = []
            for g in range(3):
                gb, Bgb = rbank()
                for kc in range(8):
                    kk.pe(mm(gb[:, 0:n], wg[b][:, kc, g, :], hT[:, kc, o:o + n], kc == 0, kc == 7), reads=[Bwg[b]] + hreads, writes=[Bgb])
                bb, Bbb = rbank()
                if g == 0:
                    for h in range(4):
                        kk.pe(mm(bb[:, 0:n], wbr[b][:, h, :], oT[:, h, o:o + n], h == 0, h == 3), reads=[Bwbr[b], B_oT[tc]], writes=[Bbb])
                elif g == 1:
                    for ch in range(2):
                        kk.pe(mm(bb[:, 0:n], wbr[b][:, 4 + ch, :], poolT[:, ch, o:o + n], ch == 0, ch == 1), reads=[Bwbr[b], B_poolT[tc]], writes=[Bbb])
                else:
                    for hp in range(2):
                        kk.pe(mm(bb[:, 0:n], wbr[b][:, 6 + hp, :], omT[:, hp, o:o + n], hp == 0, hp == 1), reads=[Bwbr[b], B_omT[tc]], writes=[Bbb])
                si = (un * 3 + g) % 6
                kk.act(f_act(sg[si][:, 0:n], gb[:, 0:n], AF.Sigmoid), reads=[Bgb], writes=[Bsg[si]])
                kk.dve(f_tt(sg[si][:, 0:n], sg[si][:, 0:n], bb[:, 0:n], ALU.mult), reads=[Bsg[si], Bbb], writes=[Bsg[si]])
                prods.append(si)
            mi = un % 4
            kk.dve(f_tt(mt_[mi][:, 0:n], sg[prods[0]][:, 0:n], sg[prods[1]][:, 0:n], ALU.add), reads=[Bsg[prods[0]], Bsg[prods[1]]], writes=[Bmt[mi]])
            kk.dve(f_tt(mergedT[:, fc, o:o + n], mt_[mi][:, 0:n], sg[prods[2]][:, 0:n], ALU.add), reads=[Bmt[mi], Bsg[prods[2]]], writes=[B_mg[tc]])
            un += 1
    if "mergedT" in dbg_out:
        mdbg = ar.alloc([128, 512], F32, "mdbg")
        Bm_ = Buf("mdbg")
        kk.dve(f_copy(mdbg[:, 0:128], mergedT[:, 0, 0:128]), reads=B_mg, writes=[Bm_])
        kk.dve(f_copy(mdbg[:, 128:256], mergedT[:, 7, 1024:1152]), reads=B_mg, writes=[Bm_])
        kk.dve(f_copy(mdbg[:, 256:384], mergedT[:, 3, 2048:2176]), reads=B_mg, writes=[Bm_])
        kk.dve(f_copy(mdbg[:, 384:512], mergedT[:, 5, 2048:2176]), reads=B_mg, writes=[Bm_])
        dbg("mergedT", mdbg[:], [Bm_])
    kk.barrier()
    if STOP == "p4":
        return
    ar.reset(p4mark)

    arA = Arena(nc, amark, omark)
    wout = arA.alloc([128, 8, D], BF16, "wout")
    Bwout = Buf("wout")
    wd = arA.alloc([128, NFF, D], BF16, "wd")
    Bwd = [Buf("wd%d" % i) for i in range(NFF)]
    wgu = [arA.alloc([128, 8, 2, 128], BF16, "wgu%d" % i) for i in range(2)]
    Bwgu = [Buf("wgu%d" % i) for i in range(3)]
    stc = [ar.alloc([32, 512], F32, "stc%d" % i) for i in range(2)]
    stT = ar.alloc([128, NFF, NSEQ, 2], F32, "stT")
    cs = ar.alloc([128, NFF, 34], F32, "cs")
    halo = [ar.alloc([128, NFF, 2], F32, "halo%d" % i) for i in range(2)]
    Bstc, BstT, Bcs, Bhalo = [Buf("stc0"), Buf("stc1")], Buf("stT"), Buf("cs"), [Buf("halo0"), Buf("halo1")]
    for k2 in range(2):
        kk.dma("pool", wout[:, k2 * 4:(k2 + 1) * 4, :], I["w_out"][k2 * 512:(k2 + 1) * 512, :].rearrange("(k p) c -> p k c", p=128), writes=[Bwout])
    kk.dve(f_memset(halo[0][:], 0.0), writes=[Bhalo[0]])
    for q4 in range(6):
        nf = min(4, NFF - q4 * 4)
        kk.dma("sp", stc[q4 % 2][:, 0:nf * 128], I["state_conv"][:, q4 * 512:q4 * 512 + nf * 128], writes=[Bstc[q4 % 2]])
        for i in range(nf):
            fcx = q4 * 4 + i
            bk, Bb = rbank()
            kk.pe(mm(bk[:, 0:32], stc[q4 % 2][:, i * 128:(i + 1) * 128], ident_f[0:32, 0:32], True, True), reads=[Bstc[q4 % 2], Bc], writes=[Bb])
            kk.dve(f_copy(stT[:, fcx].rearrange("p s r -> p (s r)"), bk[:, 0:32]), reads=[Bb], writes=[BstT])
    x2 = ar.alloc([128, 4, D], F32, "x2")
    Bx2 = [Buf("x2_%d" % i) for i in range(4)]
    h2T = ar.alloc([128, 8, 512], BF16, "h2T")
    Bh2 = [Buf("h2_%d" % i) for i in range(4)]
    actT = ar.alloc([128, NFF, 512], BF16, "actT")
    Bact = [Buf("act%d" % i) for i in range(NFF)]
    gS = [ar.alloc([128, 2 + 512], F32, "gS%d" % i) for i in range(2)]
    BgS = [Buf("gS0"), Buf("gS1")]
    gSs = [ar.alloc([128, NSEQ, 10], F32, "gSs%d" % i) for i in range(2)]
    BgSs = [Buf("gSs0"), Buf("gSs1")]
    c1 = [ar.alloc([128, 512], F32, "c1_%d" % i) for i in range(2)]
    Bc1 = [Buf("c1_0"), Buf("c1_1")]
    ge = [ar.alloc([128, 512], F32, "ge%d" % i) for i in range(2)]
    Bge = [Buf("ge0"), Buf("ge1")]
    xin5 = [ar.alloc([128, D], F32, "xin5_%d" % i) for i in range(2)]
    Bxin5 = [Buf("xin5_0"), Buf("xin5_1")]
    xn5 = ar.alloc([128, D], BF16, "xn5")
    Bxn5 = Buf("xn5")
    junk5 = ar.alloc([128, D], BF16, "junk5")
    Bjunk5 = Buf("junk5")
    ss5 = [ar.alloc([128, 4], F32, "ss5_%d" % i) for i in range(2)]
    Bss5 = [Buf("ss5_0"), Buf("ss5_1")]
    wgu.append(ar.alloc([128, 8, 2, 128], BF16, "wgu2"))
    yt = None
    csr = [ar.alloc([34, 512], F32, "csr%d" % i) for i in range(2)]
    Bcsr = [Buf("csr0"), Buf("csr1")]
    for fcx in range(NFF):
        kk.dma("pool", wd[:, fcx, :], I["w_ffn_down"][fcx * 128:(fcx + 1) * 128, :], writes=[Bwd[fcx]])
    def emit_p5a(gi5, li):
        t0_, t1_ = GROUPS[gi5]
        t = t0_ + li
        k5 = nt5c[0]
        nt5c[0] += 1
        xb_, Bxb_ = xin5[k5 % 2], Bxin5[k5 % 2]
        kk.dma("sp", xb_[:], I["x_all"][t * 128:(t + 1) * 128, :], writes=[Bxb_])
        for half in range(2):
            bk, Bb = rbank()
            for kc in range(8):
                kk.pe(mm(bk[:, :], mergedT[:, kc, t * 128:(t + 1) * 128], wout[:, kc, half * 512:(half + 1) * 512], kc == 0, kc == 7),
                      reads=[B_mg[t // 4], Bwout], writes=[Bb])
            kk.dve(f_tt(x2[:, li, half * 512:(half + 1) * 512], bk[:, :], xb_[:, half * 512:(half + 1) * 512], ALU.add), reads=[Bb, Bxb_], writes=[Bx2[li]])
        norm_stats(None, x2[:, li, :], Bx2[li], xn5, Bxn5, ss5[k5 % 2], Bss5[k5 % 2], junk5)

    def emit_p5b(gi5, li):
        bk, Bb = rbank()
        norm_tr(g2T, h2T, slice(li * 128, (li + 1) * 128), xn5, Bxn5, bk, Bb, Bh2[li])

    def emit_p5(gi5, li):
        emit_p5a(gi5, li)
        emit_p5b(gi5, li)

    def emit_p7(gi7, li):
        t0_, t1_ = GROUPS[gi7]
        t = t0_ + li
        for half in range(2):
            bk, Bb = rbank()
            for fcx in range(NFF):
                kk.pe(mm(bk[:, :], actT[:, fcx, li * 128:(li + 1) * 128], wd[:, fcx, half * 512:(half + 1) * 512], fcx == 0, fcx == NFF - 1),
                      reads=[Bact[fcx], Bwd[fcx]], writes=[Bb])
            kk.dve(f_tt(x2[:, li, half * 512:(half + 1) * 512], bk[:, :], x2[:, li, half * 512:(half + 1) * 512], ALU.add), reads=[Bb, Bx2[li]], writes=[Bx2[li]])
        si = nt7c[0] % 2
        nt7c[0] += 1
        yb, Byb = y7[si], By7[si]
        kk.act(f_act(junk5[:], x2[:, li, :], AF.Square, accum_out=ss7[si][:, 0:1]), reads=[Bx2[li]], writes=[Bss7[si], Bjunk5])
        kk.act(f_act(ss7[si][:, 1:2], ss7[si][:, 0:1], AF.Sqrt, scale=1.0 / D, bias=eps_t[:, 0:1]), reads=[Bss7[si], Bc], writes=[Bss7[si]])
        kk.dve(f_recip(ss7[si][:, 2:3], ss7[si][:, 1:2]), reads=[Bss7[si]], writes=[Bss7[si]])
        kk.dve(f_stt(yb[:], x2[:, li, :], ss7[si][:, 2:3], gfin[:], ALU.mult, ALU.mult), reads=[Bx2[li], Bss7[si], Bc], writes=[Byb])
        kk.dma("sp", O["y_all"][t * 128:(t + 1) * 128, :], yb[:], reads=[Byb])

    nt5c = [0]
    nt7c = [0]
    ss7 = [ar.alloc([128, 4], F32, "ss7_%d" % i) for i in range(2)]
    Bss7 = [Buf("ss7_0"), Buf("ss7_1")]
    y7 = [ar.alloc([128, D], F32, "y7_0")] * 2
    By7 = [Buf("y7_0")] * 2
    for li in range(GROUPS[0][1] - GROUPS[0][0]):
        emit_p5(0, li)
    nwl = [0]
    nt5 = 0
    for gi, (t0, t1) in enumerate(GROUPS):
        ntile = t1 - t0
        smp = gi == len(GROUPS) - 1
        lastp = gi == len(GROUPS) - 2
        ntk = ntile * 128
        n = ntk
        hin, hout = halo[gi % 2], halo[(gi + 1) % 2]
        Bhin, Bhout = Bhalo[gi % 2], Bhalo[(gi + 1) % 2]
        for fcx in range(NFF):
            wi = nwl[0] % 3
            nwl[0] += 1
            wflat = wgu[wi][:].rearrange("p k g c -> p (k g c)")
            if gi == 0:
                kk.dma("pool", wgu[wi][:, :, 0, :], I["w_ffn_gate"][:, fcx * 128:(fcx + 1) * 128].rearrange("(k p) c -> p k c", p=128), writes=[Bwgu[wi]])
                kk.dma("pool", wgu[wi][:, :, 1, :], I["w_ffn_up"][:, fcx * 128:(fcx + 1) * 128].rearrange("(k p) c -> p k c", p=128), writes=[Bwgu[wi]])
                kk.dma("sp", wscr[fcx], wflat, reads=[Bwgu[wi]], writes=[Bwscr[fcx]])
            else:
                kk.dma("sp", wflat, wscr[fcx], reads=[Bwscr[fcx]], writes=[Bwgu[wi]])
            bi = fcx % 2
            g_, Bg_ = gS[bi], BgS[bi]
            gs_, Bgs_ = gSs[bi], BgSs[bi]
            c_, Bc_ = c1[bi], Bc1[bi]
            e_, Be_ = ge[bi], Bge[bi]
            hreads = [Bh2[i] for i in range(ntile)]
            gb, Bgb = rbank()
            for kc in range(8):
                kk.pe(mm(gb[:, 0:n], wgu[wi][:, kc, 0, :], h2T[:, kc, 0:n], kc == 0, kc == 7), reads=[Bwgu[wi]] + hreads, writes=[Bgb])
            ub, Bub = rbank()
            for kc in range(8):
                kk.pe(mm(ub[:, 0:n], wgu[wi][:, kc, 1, :], h2T[:, kc, 0:n], kc == 0, kc == 7), reads=[Bwgu[wi]] + hreads, writes=[Bub])
            w0, w1, w2, bb_ = convw[:, 0, fcx:fcx + 1], convw[:, 1, fcx:fcx + 1], convw[:, 2, fcx:fcx + 1], convb[:, fcx:fcx + 1]
            if not smp:
                kk.act(f_act(g_[:, 2:2 + 512], gb[:, 0:512], AF.Copy), reads=[Bgb], writes=[Bg_])
                kk.dve(f_copy(g_[:, 0:2], hin[:, fcx, :]), reads=[Bhin], writes=[Bg_])
                kk.dve(f_copy(hout[:, fcx, :], g_[:, 512:514]), reads=[Bg_], writes=[Bhout])
                kk.dve(f_ts(c_[:, 0:512], g_[:, 0:512], w0, bb_, ALU.mult, ALU.add), reads=[Bg_, Bc, Bc5], writes=[Bc_])
                kk.dve(f_stt(c_[:, 0:512], g_[:, 1:513], w1, c_[:, 0:512], ALU.mult, ALU.add), reads=[Bg_, Bc_, Bc], writes=[Bc_])
                kk.dve(f_stt(c_[:, 0:512], g_[:, 2:514], w2, c_[:, 0:512], ALU.mult, ALU.add), reads=[Bg_, Bc_, Bc], writes=[Bc_])
                if lastp:
                    kk.dve(f_copy(cs[:, fcx, 0:2], g_[:, 512:514]), reads=[Bg_], writes=[Bcs])
            else:
                kk.act(f_act(gs_[:, :, 2:10], gb[:, 0:128].rearrange("p (s i) -> p s i", i=8), AF.Copy), reads=[Bgb], writes=[Bgs_])
                kk.dve(f_copy(gs_[:, :, 0:2], stT[:, fcx]), reads=[BstT], writes=[Bgs_])
                cv = c_[:, 0:128].rearrange("p (s i) -> p s i", i=8)
                kk.dve(f_ts(cv, gs_[:, :, 0:8], w0, bb_, ALU.mult, ALU.add), reads=[Bgs_, Bc, Bc5], writes=[Bc_])
                kk.dve(f_stt(cv, gs_[:, :, 1:9], w1, cv, ALU.mult, ALU.add), reads=[Bgs_, Bc_, Bc], writes=[Bc_])
                kk.dve(f_stt(cv, gs_[:, :, 2:10], w2, cv, ALU.mult, ALU.add), reads=[Bgs_, Bc_, Bc], writes=[Bc_])
                kk.dve(f_copy(cs[:, fcx, 2:34].rearrange("p (s r) -> p s r", r=2), gs_[:, :, 8:10]), reads=[Bgs_], writes=[Bcs])
            kk.act(f_act(e_[:, 0:n], c_[:, 0:n], AF.Gelu_apprx_tanh), reads=[Bc_], writes=[Be_])
            kk.dve(f_tt(actT[:, fcx, 0:n], e_[:, 0:n], ub[:, 0:n], ALU.mult), reads=[Be_, Bub], writes=[Bact[fcx]])
        nnext = (GROUPS[gi + 1][1] - GROUPS[gi + 1][0]) if gi + 1 < len(GROUPS) else 0
        pend_b = None
        for li in range(max(ntile, nnext)):
            if li < ntile:
                emit_p7(gi, li)
            if pend_b is not None:
                emit_p5b(gi + 1, pend_b)
                pend_b = None
            if li < nnext:
                emit_p5a(gi + 1, li)
                pend_b = li
        if pend_b is not None:
            emit_p5b(gi + 1, pend_b)
    for q4 in range(6):
        bk, Bb = rbank()
        nf = min(4, NFF - q4 * 4)
        for i in range(nf):
            fcx = q4 * 4 + i
            kk.pe(mm(bk[0:34, i * 128:(i + 1) * 128], cs[:, fcx, :], ident_f[:, :], True, True), reads=[Bcs, Bc], writes=[Bb])
        kk.dve(f_copy(csr[q4 % 2][:, 0:nf * 128], bk[0:34, 0:nf * 128]), reads=[Bb], writes=[Bcsr[q4 % 2]])
        kk.dma("sp", O["conv_all"][:, q4 * 512:q4 * 512 + nf * 128], csr[q4 % 2][:, 0:nf * 128], reads=[Bcsr[q4 % 2]])

    kk.barrier()


_NC_CACHE = {}


def kernel(**inputs):
    f32 = lambda a: np.ascontiguousarray(np.asarray(a, dtype=np.float32))
    if "nc" not in _NC_CACHE:
        _NC_CACHE["nc"] = build_program()
    nc = _NC_CACHE["nc"]
    consts = host_constants()
    x_prompt = f32(inputs["x_prompt"])
    x_sample = f32(inputs["x_sample"])
    mem_prompt = f32(inputs["mem_prompt"])
    cache_kv = np.concatenate([f32(inputs["cache_k"]).reshape(-1, 512), f32(inputs["cache_v"]).reshape(-1, 512)], axis=1)
    page_table = np.ascontiguousarray(np.asarray(inputs["page_table"], dtype=np.int32))
    state_pool = f32(inputs["state_pool"])[0]
    state_conv = f32(inputs["state_ffn_conv"])[0]
    cmk = f32(inputs["cache_mem_k"])[0]
    cmv = f32(inputs["cache_mem_v"])[0]
    shared = {
        "cache_kv": cache_kv,
        "norm1_g": f32(inputs["norm1_g"])[0], "w_in": f32(inputs["w_in"])[0],
        "lam_q1": f32(inputs["lam_q1"]), "lam_k1": f32(inputs["lam_k1"]),
        "lam_q2": f32(inputs["lam_q2"]), "lam_k2": f32(inputs["lam_k2"]),
        "subln_g": f32(inputs["subln_g"])[0], "w_pool_grp": f32(inputs["w_pool_grp"])[0],
        "pool_scale": f32(inputs["pool_scale"])[0],
        "w_br_attn": f32(inputs["w_br_attn"])[0], "w_br_pool": f32(inputs["w_br_pool"])[0],
        "w_br_mem": f32(inputs["w_br_mem"])[0], "mem_norm_g": f32(inputs["mem_norm_g"])[0],
        "w_mem_kv": f32(inputs["w_mem_kv"])[0], "w_out": f32(inputs["w_out"])[0],
        "norm2_g": f32(inputs["norm2_g"])[0], "w_ffn_gate": f32(inputs["w_ffn_gate"])[0],
        "w_ffn_up": f32(inputs["w_ffn_up"])[0], "ffn_conv_w": f32(inputs["ffn_conv_w"])[0],
        "ffn_conv_b": f32(inputs["ffn_conv_b"])[0], "w_ffn_down": f32(inputs["w_ffn_down"])[0],
        "rel_bias": f32(inputs["rel_bias"]), "final_norm_g": f32(inputs["final_norm_g"]),
    }
    shared.update(consts)
    in_maps = []
    for c in range(8):
        sl = slice(NSEQ * c, NSEQ * (c + 1))
        m = dict(shared)
        m["x_all"] = np.ascontiguousarray(np.concatenate([x_prompt[c], x_sample[sl].reshape(128, D)], axis=0))
        m["mem"] = mem_prompt[c]
        m["page_table"] = np.ascontiguousarray(page_table[sl].reshape(1, NSEQ * NPAGE))
        m["state_pool"] = np.ascontiguousarray(state_pool[sl].reshape(NSEQ * 15, 256))
        m["state_conv"] = np.ascontiguousarray(state_conv[sl].reshape(NSEQ * 2, D_FF))
        m["cmem_k"] = np.ascontiguousarray(cmk[sl].reshape(NSEQ, 256, 256))
        m["cmem_v"] = np.ascontiguousarray(cmv[sl].reshape(NSEQ, 256, 256))
        in_maps.append(m)
    res = run_bass_kernel_spmd(nc, in_maps, core_ids=list(range(8)))
    R = res.results
    g = lambda k: [np.asarray(R[c][k], dtype=np.float32) for c in range(8)]
    y = g("y_all"); nk = g("newk"); nv = g("newv")
    y_prompt = np.stack([a[:2048] for a in y], 0)
    y_sample = np.concatenate([a[2048:].reshape(NSEQ, 8, D) for a in y], 0)
    nkp = np.stack([a[:2048].reshape(2048, 4, 128) for a in nk], 0)[None]
    nvp = np.stack([a[:2048].reshape(2048, 4, 128) for a in nv], 0)[None]
    nks = np.concatenate([a[2048:].reshape(NSEQ, 8, 4, 128) for a in nk], 0)[None]
    nvs = np.concatenate([a[2048:].reshape(NSEQ, 8, 4, 128) for a in nv], 0)[None]
    pp = np.stack(g("pool_p"), 0)[None]
    ps = np.concatenate(g("pool_s"), 0)[None]
    cv = g("conv_all")
    cp = np.stack([a[:2] for a in cv], 0)[None]
    cs = np.concatenate([a[2:].reshape(NSEQ, 2, D_FF) for a in cv], 0)[None]
    mk = np.stack([a.reshape(256, 4, 64) for a in g("memk")], 0)[None]
    mv = np.stack([a.reshape(256, 4, 64) for a in g("memv")], 0)[None]
    return (y_prompt, y_sample, nkp, nvp, nks, nvs, pp, ps, cp, cs, mk, mv)
```

```python
import numpy as np
from contextlib import ExitStack

import concourse.bass as bass
import concourse.mybir as mybir
from concourse.bass_utils import run_bass_kernel_spmd

F32 = mybir.dt.float32
BF16 = mybir.dt.bfloat16
I32 = mybir.dt.int32
AF = mybir.ActivationFunctionType
ALU = mybir.AluOpType
AX = mybir.AxisListType

D = 1024
NTOK = 2176
NT = 17
D_IN = 5120
D_FF = 2816
NFF = 22
EPS = 1e-6
LAM_INIT = 0.8 - 0.6
NSEQ = 16
NPAGE = 16
WZ = 384

TCH = [(0, 512), (512, 512), (1024, 512), (1536, 512), (2048, 128)]
GROUPS = [(0, 4), (4, 8), (8, 12), (12, 16), (16, 17)]


class Buf:
    __slots__ = ("name", "w", "r")

    def __init__(self, name):
        self.name = name
        self.w = None
        self.r = []


class Op:
    __slots__ = ("eng", "fn", "waits", "signal", "idx", "count", "dma", "dsem", "dval", "pre")

    def __init__(self, eng, fn, dma):
        self.eng = eng
        self.fn = fn
        self.waits = []
        self.signal = False
        self.idx = -1
        self.count = 0
        self.dma = dma
        self.dsem = None
        self.dval = 0
        self.pre = None


ENGS = ("pe", "act", "dve", "pool", "sp")
NDSEM = 24


class K:
    def __init__(self, nc, es):
        self.nc = nc
        self.ops = {e: [] for e in ENGS}
        self.waited = {e: {p: -1 for p in ENGS} for e in ENGS}
        self.waited_dma = {e: set() for e in ENGS}
        self.sem = {e: es.enter_context(nc.semaphore("s_" + e)) for e in ENGS}
        self.dsems = {q: [es.enter_context(nc.semaphore("d_%s%d" % (q, i))) for i in range(NDSEM)]
                      for q in ("sp", "pool")}
        self.ndma = {"sp": 0, "pool": 0}
        self.dma_ops = {"sp": [], "pool": []}

    def _dep(self, op, d, force=False):
        e = op.eng
        if d is None or d is op:
            return
        if d.dma:
            if id(d) in self.waited_dma[e]:
                return
            self.waited_dma[e].add(id(d))
            op.waits.append(d)
            return
        p = d.eng
        if p == "pe" and e == "pe" and not force:
            return
        if self.waited[e][p] >= d.idx:
            return
        self.waited[e][p] = d.idx
        d.signal = True
        op.waits.append(d)

    def op(self, eng, fn, reads=(), writes=(), dma=False):
        o = Op(eng, fn, dma)
        o.idx = len(self.ops[eng])
        deps = []
        for b in reads:
            if b.w is not None:
                deps.append(b.w)
        for b in writes:
            if b.w is not None:
                deps.append(b.w)
            deps.extend(b.r)
        latest = {}
        for d in deps:
            if d.dma:
                self._dep(o, d)
            elif d.eng not in latest or latest[d.eng].idx < d.idx:
                latest[d.eng] = d
        for d in latest.values():
            self._dep(o, d)
        if dma:
            n = self.ndma[eng]
            self.ndma[eng] += 1
            o.dsem = self.dsems[eng][n % NDSEM]
            o.dval = 16 * (n // NDSEM + 1)
            if n >= NDSEM:
                prev = self.dma_ops[eng][n - NDSEM]
                o.pre = prev
            self.dma_ops[eng].append(o)
        self.ops[eng].append(o)
        for b in reads:
            b.r.append(o)
        for b in writes:
            b.w = o
            b.r = []
        return o

    def pe(self, fn, reads=(), writes=()):
        return self.op("pe", fn, reads, writes)

    def act(self, fn, reads=(), writes=()):
        return self.op("act", fn, reads, writes)

    def dve(self, fn, reads=(), writes=()):
        return self.op("dve", fn, reads, writes)

    def pool(self, fn, reads=(), writes=()):
        return self.op("pool", fn, reads, writes)

    def dma(self, q, out, in_, reads=(), writes=(), **kw):
        return self.op(q, lambda e: e.dma_start(out=out, in_=in_, **kw), reads, writes, dma=True)

    def barrier(self):
        lasts = []
        for e in ENGS:
            real = [o for o in self.ops[e] if o.fn is not None and not o.dma]
            if real:
                lasts.append(real[-1])
        dmas = self.dma_ops["sp"][-NDSEM:] + self.dma_ops["pool"][-NDSEM:]
        for e in ENGS:
            o = Op(e, None, False)
            o.idx = len(self.ops[e])
            for d in lasts + dmas:
                self._dep(o, d, force=True)
            self.ops[e].append(o)

    def emit(self, block):
        for e in ENGS:
            c = 0
            for o in self.ops[e]:
                if o.signal:
                    c += 1
                    o.count = c
        def run(e, eng):
            for o in self.ops[e]:
                if o.pre is not None:
                    eng.wait_ge(o.pre.dsem, o.pre.dval)
                for d in o.waits:
                    if d.dma:
                        eng.wait_ge(d.dsem, d.dval)
                    else:
                        eng.wait_ge(self.sem[d.eng], d.count)
                if o.fn is None:
                    continue
                ins = o.fn(eng)
                if o.dma:
                    ins.then_inc(o.dsem, 16)
                elif o.signal:
                    ins.then_inc(self.sem[e], 1)

        @block.tensor
        def _(eng):
            run("pe", eng)

        @block.scalar
        def _(eng):
            run("act", eng)

        @block.vector
        def _(eng):
            run("dve", eng)

        @block.gpsimd
        def _(eng):
            run("pool", eng)

        @block.sync
        def _(eng):
            run("sp", eng)


class Arena:
    def __init__(self, nc, base, cap):
        self.nc = nc
        self.base = base
        self.cap = cap
        self.top = base
        self.n = 0

    def alloc(self, shape, dtype, name=None):
        nbytes = int(np.prod(shape[1:])) * mybir.dt.size(dtype)
        off = (self.top + 31) // 32 * 32
        assert off + nbytes <= self.cap, ("SBUF arena overflow", name, off, nbytes, self.cap)
        self.top = off + nbytes
        self.n += 1
        nm = "%s_%d_%d" % (name or "t", off, self.n)
        return self.nc.alloc_sbuf_tensor_at(nm, list(shape), dtype, offset=off)

    def mark(self):
        return self.top

    def reset(self, m):
        self.top = m


def rel_bucket_np(rel):
    n = np.maximum(rel, 0)
    max_exact = 16
    nf = np.maximum(n, 1).astype(np.float32)
    large = max_exact + (np.log(nf / max_exact) / np.log(128 / max_exact) * (32 - max_exact)).astype(np.int32)
    large = np.minimum(large, 31)
    return np.where(n < max_exact, n, large)


def host_constants():
    c = {}
    c["ident"] = np.eye(128, dtype=np.float32)
    rel = np.arange(WZ) - 128
    b = rel_bucket_np(rel)
    oh = np.zeros((32, WZ), np.float32)
    oh[b, np.arange(WZ)] = 1.0
    oh[:, rel < 0] = 0.0
    c["bucket_oh"] = oh
    c["relmask"] = np.repeat((rel >= 0).astype(np.float32)[None, :], 128, axis=0)
    pc = np.zeros((128, 2, 16), np.float32)
    for ch in range(2):
        for p in range(128):
            w = 2 ** (2 * ch + p // 64 + 1)
            pc[p, ch, :] = 1.0 / np.minimum(np.arange(16) + 1, w)
    c["poolc"] = pc
    bd = np.zeros((128, 16), np.float32)
    bd[np.arange(128), np.arange(128) // 8] = 1.0
    c["blockdiag"] = bd
    sel = np.zeros((64, 2, 32), np.float32)
    for h in range(4):
        for cc in range(2):
            for q in range(8):
                sel[h * 16 + cc * 8 + q, cc, h * 8 + q] = 1.0
    c["sel"] = sel
    c["iota_f"] = np.arange(128, dtype=np.float32).reshape(128, 1)
    return c


CONST_SHAPES = {
    "ident": ([128, 128], F32), "bucket_oh": ([32, WZ], F32), "relmask": ([128, WZ], F32),
    "poolc": ([128, 2, 16], F32), "blockdiag": ([128, 16], F32), "sel": ([64, 2, 32], F32),
    "iota_f": ([128, 1], F32),
}

IN_SHAPES = {
    "x_all": ([NTOK, D], F32), "mem": ([256, D], F32),
    "cache_kv": ([2560 * 128, 1024], F32),
    "page_table": ([1, NSEQ * NPAGE], I32),
    "state_pool": ([NSEQ * 15, 256], F32), "state_conv": ([NSEQ * 2, D_FF], F32),
    "cmem_k": ([NSEQ, 256, 256], F32), "cmem_v": ([NSEQ, 256, 256], F32),
    "norm1_g": ([D], F32), "w_in": ([D, D_IN], F32),
    "lam_q1": ([1, 64], F32), "lam_k1": ([1, 64], F32), "lam_q2": ([1, 64], F32), "lam_k2": ([1, 64], F32),
    "subln_g": ([128], F32), "w_pool_grp": ([4, 64, 64], F32), "pool_scale": ([256], F32),
    "w_br_attn": ([512, D], F32), "w_br_pool": ([256, D], F32), "w_br_mem": ([256, D], F32),
    "mem_norm_g": ([D], F32), "w_mem_kv": ([D, 512], F32), "w_out": ([D, D], F32),
    "norm2_g": ([D], F32), "w_ffn_gate": ([D, D_FF], F32), "w_ffn_up": ([D, D_FF], F32),
    "ffn_conv_w": ([3, D_FF], F32), "ffn_conv_b": ([D_FF], F32), "w_ffn_down": ([D_FF, D], F32),
    "rel_bias": ([32, 4], F32), "final_norm_g": ([D], F32),
}

OUT_SHAPES = {
    "y_all": [NTOK, D], "newk": [NTOK, 512], "newv": [NTOK, 512],
    "pool_p": [15, 256], "pool_s": [NSEQ, 15, 256],
    "conv_all": [2 + 2 * NSEQ, D_FF],
    "memk": [256, 256], "memv": [256, 256],
}


def build_program(phases=("all",), debug=None, nphys=2560):
    nc = bass.Bass("TRN2", target_bir_lowering=False)
    I = {}
    for k, (shp, dt) in {**IN_SHAPES, **CONST_SHAPES}.items():
        if k == "cache_kv":
            shp = [nphys * 128, 1024]
        I[k] = nc.dram_tensor(k, shp, dt, kind="ExternalInput").ap()
    O = {}
    for k, shp in OUT_SHAPES.items():
        O[k] = nc.dram_tensor(k, shp, F32, kind="ExternalOutput").ap()
    zscr = nc.dram_tensor("zscr", [128, 4 * WZ], F32, kind="Internal").ap()
    wscr_t = nc.dram_tensor("wscr", [NFF, 128, 2048], BF16, kind="Internal").ap()
    dbg_out = {}
    if debug:
        for k, shp in debug.items():
            dbg_out[k] = nc.dram_tensor("dbg_" + k, shp, F32, kind="ExternalOutput").ap()

    with ExitStack() as es:
        kk = K(nc, es)
        banks = [es.enter_context(nc.psum_tensor("bank%d" % i, [128, 512], F32)) for i in range(8)]
        pb = [Buf("psum%d" % i) for i in range(8)]
        block = es.enter_context(nc.Block())
        _build(nc, kk, I, O, zscr, banks, pb, dbg_out, phases, wscr_t)
        kk.emit(block)
    return nc


def _build(nc, kk, I, O, zscr, banks, pb, dbg_out, phases, wscr):
    Bwscr = [Buf("wscr%d" % i) for i in range(NFF)]
    ALL = "all" in phases
    STOP = [p for p in phases if p.startswith("p")]
    STOP = STOP[0] if STOP else None
    ar = Arena(nc, (nc.sbuf_base + 63) // 64 * 64, nc.sbuf_top)

    def mm(out, lhsT, rhs, start, stop):
        return lambda e: e.matmul(out, lhsT, rhs, start=start, stop=stop)

    def f_tt(out, in0, in1, op):
        return lambda e: e.tensor_tensor(out=out, in0=in0, in1=in1, op=op)

    def f_ts(out, in0, s1, s2, op0, op1=None):
        if op1 is None:
            return lambda e: e.tensor_scalar(out=out, in0=in0, scalar1=s1, scalar2=None, op0=op0)
        return lambda e: e.tensor_scalar(out=out, in0=in0, scalar1=s1, scalar2=s2, op0=op0, op1=op1)

    def f_stt(out, in0, scalar, in1, op0, op1):
        return lambda e: e.scalar_tensor_tensor(out=out, in0=in0, scalar=scalar, in1=in1, op0=op0, op1=op1)

    def f_copy(out, in_):
        return lambda e: e.tensor_copy(out=out, in_=in_)

    def f_act(out, in_, func, **kw):
        return lambda e: e.activation(out=out, in_=in_, func=func, **kw)

    def f_recip(out, in_):
        return lambda e: e.reciprocal(out=out, in_=in_)

    def f_memset(ap, v):
        return lambda e: e.memset(ap, v)

    def f_tr(out, in_, ident):
        return lambda e: e.transpose(out, in_, ident)

    cst = {}
    ident_f = ar.alloc([128, 128], F32, "identf")
    ident_b = ar.alloc([128, 128], BF16, "identb")
    ones_f = ar.alloc([128, 128], F32, "onesf")
    ones_b = ar.alloc([128, 128], BF16, "onesb")
    g1T = ar.alloc([128, 8], F32, "g1T")
    g2T = ar.alloc([128, 8], F32, "g2T")
    gmT = ar.alloc([128, 8], F32, "gmT")
    gfin = ar.alloc([128, D], F32, "gfin")
    sublnT = ar.alloc([128, 1], F32, "subln")
    pscaleT = ar.alloc([128, 2], F32, "pscale")
    convw = ar.alloc([128, 3, NFF], F32, "convw")
    convb = ar.alloc([128, NFF], F32, "convb")
    lamv = ar.alloc([128, 4, 64], F32, "lamv")
    lamt = ar.alloc([128, 8], F32, "lamt")
    neg_lam = ar.alloc([128, 1], F32, "neglam")
    eps_t = ar.alloc([128, 1], F32, "eps")
    poolc = ar.alloc([128, 2, 16], F32, "poolc")
    bdiag = ar.alloc([128, 16], F32, "bdiag")
    selc = ar.alloc([64, 2, 32], F32, "sel")
    comb = ar.alloc([64, 32], F32, "comb")
    Bc = Buf("consts")

    kk.dma("sp", ident_f[:], I["ident"][:, :], writes=[Bc])
    kk.dma("pool", ident_b[:], I["ident"][:, :], writes=[Bc])
    kk.dve(lambda e: e.memset(ones_f[:], 1.0), writes=[Bc])
    kk.dve(lambda e: e.memset(ones_b[:], 1.0), writes=[Bc])
    kk.dve(lambda e: e.memset(eps_t[:], EPS), writes=[Bc])
    for t, src in ((g1T, "norm1_g"), (gmT, "mem_norm_g")):
        kk.dma("sp", t[:], I[src].rearrange("(k p) -> p k", p=128), writes=[Bc], allow_slow_non_contiguous=True)
    kk.dma("sp", gfin[:], I["final_norm_g"].partition_broadcast(128), writes=[Bc])
    kk.dma("sp", sublnT[:], I["subln_g"].rearrange("(p o) -> p o", o=1), writes=[Bc], allow_slow_non_contiguous=True)
    kk.dma("sp", pscaleT[:], I["pool_scale"].rearrange("(k p) -> p k", p=128), writes=[Bc], allow_slow_non_contiguous=True)
    for i, nm in enumerate(("lam_q1", "lam_k1", "lam_q2", "lam_k2")):
        kk.dma("sp", lamv[:, i, :], I[nm][0, :].partition_broadcast(128), writes=[Bc])
    kk.dma("sp", poolc[:], I["poolc"][:, :, :], writes=[Bc])
    kk.dma("sp", bdiag[:], I["blockdiag"][:, :], writes=[Bc])
    kk.dma("sp", selc[:], I["sel"][:, :, :], writes=[Bc])
    Bl = Buf("lam")
    kk.dve(lambda e: e.tensor_tensor(out=lamv[:, 0, :], in0=lamv[:, 0, :], in1=lamv[:, 1, :], op=ALU.mult), reads=[Bc], writes=[Bl])
    kk.dve(lambda e: e.tensor_tensor(out=lamv[:, 2, :], in0=lamv[:, 2, :], in1=lamv[:, 3, :], op=ALU.mult), reads=[Bl], writes=[Bl])
    kk.dve(lambda e: e.tensor_reduce(out=lamt[:, 0:1], in_=lamv[:, 0, :], axis=AX.X, op=ALU.add), reads=[Bl], writes=[Bl])
    kk.dve(lambda e: e.tensor_reduce(out=lamt[:, 1:2], in_=lamv[:, 2, :], axis=AX.X, op=ALU.add), reads=[Bl], writes=[Bl])
    kk.act(lambda e: e.activation(out=lamt[:, 2:4], in_=lamt[:, 0:2], func=AF.Exp), reads=[Bl], writes=[Bl])
    kk.dve(lambda e: e.tensor_tensor(out=lamt[:, 4:5], in0=lamt[:, 3:4], in1=lamt[:, 2:3], op=ALU.subtract), reads=[Bl], writes=[Bl])
    kk.dve(lambda e: e.tensor_scalar(out=neg_lam[:], in0=lamt[:, 4:5], scalar1=-LAM_INIT, scalar2=None, op0=ALU.add), reads=[Bl], writes=[Bl])
    kk.dve(lambda e: e.scalar_tensor_tensor(out=comb[:], in0=selc[:, 1, :], scalar=neg_lam[0:64, 0:1], in1=selc[:, 0, :],
                                            op0=ALU.mult, op1=ALU.add), reads=[Bl, Bc], writes=[Bl])
    kk.dve(lambda e: e.tensor_scalar(out=sublnT[:], in0=sublnT[:], scalar1=1.0 - LAM_INIT, scalar2=None, op0=ALU.mult), reads=[Bc], writes=[Bc])

    T0 = ar.alloc([128, 4, 128], F32, "T0")
    T1 = ar.alloc([128, 4, 128], F32, "T1")
    cmark = ar.mark()
    art = Arena(nc, nc.sbuf_top - 12 * 1024, nc.sbuf_top)
    rb = art.alloc([32, 4], F32, "rb")
    rbrep = art.alloc([32, 4, 128], F32, "rbrep")
    oh = art.alloc([32, WZ], F32, "oh")
    relmask = art.alloc([128, WZ], F32, "relmask")
    erow = art.alloc([128, 4, WZ], F32, "erow")
    Bt = Buf("T")
    kk.dma("sp", rb[:], I["rel_bias"][:, :], writes=[Bt])
    kk.dma("sp", oh[:], I["bucket_oh"][:, :], writes=[Bt])
    kk.dma("sp", relmask[:], I["relmask"][:, :], writes=[Bt])
    for h in range(4):
        kk.dve(lambda e, h=h: e.tensor_copy(out=rbrep[:, h, :], in_=rb[:, h:h + 1].to_broadcast([32, 128])), reads=[Bt], writes=[Bt])
    for h in range(4):
        kk.pe(mm(banks[0][:, 0:WZ], rbrep[:, h, :], oh[:, :], True, True), reads=[Bt], writes=[pb[0]])
        kk.dve(lambda e, h=h: e.tensor_scalar(out=erow[:, h, 0:1], in0=banks[0][:, WZ - 1:WZ], scalar1=-1.0, scalar2=None, op0=ALU.mult),
               reads=[pb[0]], writes=[Bt])
        kk.act(lambda e, h=h: e.activation(out=erow[:, h, 1:WZ], in_=banks[0][:, 1:WZ], func=AF.Exp, bias=erow[:, h, 0:1]),
               reads=[pb[0], Bt], writes=[Bt])
        kk.dve(lambda e, h=h: e.tensor_tensor(out=erow[:, h, :], in0=erow[:, h, :], in1=relmask[:, :], op=ALU.mult), reads=[Bt], writes=[Bt])
    Bz = Buf("zscr")
    kk.dma("sp", zscr[:, :], erow[:].rearrange("p h w -> p (h w)"), reads=[Bt], writes=[Bz])
    for h in range(4):
        s0 = bass.AP(zscr.tensor, h * WZ + 128, [[4 * WZ - 1, 128], [1, 128]])
        s1 = bass.AP(zscr.tensor, h * WZ + 256, [[4 * WZ - 1, 128], [1, 128]])
        kk.dma("sp", T0[:, h, :], s0, reads=[Bz], writes=[Bt])
        kk.dma("sp", T1[:, h, :], s1, reads=[Bz], writes=[Bt])

    def dbg(name, ap, rd):
        if name in dbg_out:
            kk.dma("sp", dbg_out[name], ap, reads=rd)

    dbg("T0", T0[:].rearrange("p h w -> p (h w)"), [Bt])
    dbg("T1", T1[:].rearrange("p h w -> p (h w)"), [Bt])
    dbg("neglam", neg_lam[:], [Bl])
    if STOP == "p0":
        kk.barrier()
        return
    ar.reset(cmark)


    amark = ar.mark()
    hT = ar.alloc([128, 8, NTOK], BF16, "hT")
    oT = ar.alloc([128, 4, NTOK], BF16, "oT")
    poolT = ar.alloc([128, 2, NTOK], BF16, "poolT")
    omT = ar.alloc([128, 2, NTOK], BF16, "omT")
    omark = ar.mark()
    qT = ar.alloc([128, 4, NTOK], BF16, "qT")
    kT = ar.alloc([128, 4, NTOK], BF16, "kT")
    v_bf = ar.alloc([128, NT, 512], BF16, "vbf")
    qmT = ar.alloc([128, 2, NTOK], BF16, "qmT")
    B_hT = [Buf("hT%d" % t) for t in range(NT)]
    B_qT = [Buf("qT%d" % i) for i in range(len(TCH))]
    B_kT = [Buf("kT%d" % i) for i in range(len(TCH))]
    B_qmT = [Buf("qmT%d" % i) for i in range(len(TCH))]
    B_v = [Buf("v%d" % t) for t in range(NT)]
    B_oT = [Buf("oT%d" % i) for i in range(len(TCH))]
    B_poolT = [Buf("poolT%d" % i) for i in range(len(TCH))]
    B_omT = [Buf("omT%d" % i) for i in range(len(TCH))]
    wmark = ar.mark()

    def tiles_of(tc):
        o, n = TCH[tc]
        return list(range(o // 128, (o + n) // 128))

    Bc5x = []

    def norm_stats(src_rows, xin, Bx, xn, Bxn, ss, Bss, junk):
        if src_rows is not None:
            kk.dma("sp", xin[:], src_rows, writes=[Bx])
        kk.act(f_act(junk[:], xin[:], AF.Square, accum_out=ss[:, 0:1]), reads=[Bx], writes=[Bss, Bjunk])
        kk.act(f_act(ss[:, 1:2], ss[:, 0:1], AF.Sqrt, scale=1.0 / D, bias=eps_t[:, 0:1]), reads=[Bss, Bc], writes=[Bss])
        kk.dve(f_recip(ss[:, 2:3], ss[:, 1:2]), reads=[Bss], writes=[Bss])
        kk.dve(f_ts(xn[:], xin[:], ss[:, 2:3], None, ALU.mult), reads=[Bx, Bss], writes=[Bxn])

    def norm_tr(gT, dst, dst_cols, xn, Bxn, bank, Bbank, Bdst):
        pbf = bank[:].bitcast(BF16)
        for kc in range(8):
            kk.pe(f_tr(pbf[:, kc * 128:(kc + 1) * 128], xn[:, kc * 128:(kc + 1) * 128], ident_b[:]), reads=[Bxn, Bc], writes=[Bbank])
        kk.dve(f_tt(dst[:, :, dst_cols], pbf[:, 0:1024].rearrange("p (k t) -> p k t", k=8),
                    gT[:, :].unsqueeze(2).to_broadcast([128, 8, 128]), ALU.mult),
               reads=[Bbank, Bc] + Bc5x, writes=[Bdst])

    def norm_transpose(src_rows, gT, dst, dst_cols, xin, Bx, xn, Bxn, ss, Bss, bank, Bbank, Bdst, junk, i):
        norm_stats(src_rows, xin, Bx, xn, Bxn, ss, Bss, junk)
        norm_tr(gT, dst, dst_cols, xn, Bxn, bank, Bbank, Bdst)

    xins = [ar.alloc([128, D], F32, "xin%d" % i) for i in range(3)]
    Bxins = [Buf("xin%d" % i) for i in range(3)]
    xns = [ar.alloc([128, D], BF16, "xn%d" % i) for i in range(2)]
    Bxns = [Buf("xn%d" % i) for i in range(2)]
    sss = [ar.alloc([128, 4], F32, "ss%d" % i) for i in range(3)]
    Bsss = [Buf("ss%d" % i) for i in range(3)]
    junk = ar.alloc([128, D], BF16, "junk")
    Bjunk = Buf("junk")
    for t in range(NT):
        norm_transpose(I["x_all"][t * 128:(t + 1) * 128, :], g1T, hT, slice(t * 128, (t + 1) * 128),
                       xins[t % 3], Bxins[t % 3], xns[t % 2], Bxns[t % 2], sss[t % 3], Bsss[t % 3],
                       banks[t % 2], pb[t % 2], B_hT[t], junk, t)
    if "hT" in dbg_out:
        hdbg = ar.alloc([128, 8 * 128], F32, "hdbg")
        Bh = Buf("hdbg")
        kk.dve(lambda e: e.tensor_copy(out=hdbg[:].rearrange("p (k t) -> p k t", k=8), in_=hT[:, :, 2048:2176]), reads=B_hT, writes=[Bh])
        dbg("hT", hdbg[:], [Bh])
    kk.barrier()
    if STOP == "p1":
        return
    ar.reset(wmark)

    Bc5 = Buf("consts5")
    kk.dma("sp", g2T[:], I["norm2_g"].rearrange("(k p) -> p k", p=128), writes=[Bc5], allow_slow_non_contiguous=True)
    kk.dma("sp", convw[:], I["ffn_conv_w"].rearrange("j (c p) -> p j c", p=128), writes=[Bc5], allow_slow_non_contiguous=True)
    kk.dma("sp", convb[:], I["ffn_conv_b"].rearrange("(c p) -> p c", p=128), writes=[Bc5], allow_slow_non_contiguous=True)
    Bc5x.append(Bc5)
    wps = [ar.alloc([128, 8, 512], BF16, "wp%d" % i) for i in range(2)]
    Bwps = [Buf("wp%d" % i) for i in range(2)]
    stg = [ar.alloc([128, 512], F32, "stg%d" % i) for i in range(3)]
    Bstg = [Buf("stg%d" % i) for i in range(3)]
    nstg = [0]
    Eb = ar.alloc([128, 15 + 2048], F32, "Eb")
    Es = ar.alloc([128, NSEQ, 23], F32, "Es")
    W1 = ar.alloc([128, 15 + 2048], F32, "W1")
    W1s = ar.alloc([128, NSEQ, 23], F32, "W1s")
    W2 = ar.alloc([128, 15 + 2048], F32, "W2")
    W2s = ar.alloc([128, NSEQ, 23], F32, "W2s")
    dTb = ar.alloc([128, NTOK], BF16, "dTb")
    tmp16 = ar.alloc([128, 16], F32, "tmp16")
    bdw = ar.alloc([128, 128], BF16, "bdw")
    stp = ar.alloc([120, 2, 256], F32, "stp")
    BE, BW1, BW2, BdT, Bbdw, Bstp, Bt16 = Buf("E"), Buf("W1"), Buf("W2"), Buf("dT"), Buf("bdw"), Buf("stp"), Buf("t16")

    def load_wpiece(i, c0):
        kk.dma("pool", wps[i][:], I["w_in"][:, c0:c0 + 512].rearrange("(k p) c -> p k c", p=128), writes=[Bwps[i]])

    def fm_group(wp, Bwp, col0, tc, bank, Bbank):
        o, n = TCH[tc]
        for kc in range(8):
            kk.pe(mm(bank[:, 0:n], wp[:, kc, col0:col0 + 128], hT[:, kc, o:o + n], kc == 0, kc == 7),
                  reads=[Bwp] + [B_hT[t] for t in tiles_of(tc)], writes=[Bbank])

    def tm_group(wp, Bwp, t, bank, Bbank, ncols=512, c0=0):
        for kc in range(8):
            kk.pe(mm(bank[:, 0:ncols], hT[:, kc, t * 128:(t + 1) * 128], wp[:, kc, c0:c0 + ncols], kc == 0, kc == 7),
                  reads=[Bwp, B_hT[t]], writes=[Bbank])

    nb = [0]

    def next_bank():
        b = nb[0] % 4
        nb[0] += 1
        return banks[b], pb[b]

    ev = [0]

    def evac_copy(out_ap, in_ap, reads, writes):
        ev[0] += 1
        if ev[0] % 2 == 0:
            kk.dve(lambda e: e.tensor_copy(out=out_ap, in_=in_ap), reads=reads, writes=writes)
        else:
            kk.act(lambda e: e.activation(out=out_ap, in_=in_ap, func=AF.Copy), reads=reads, writes=writes)

    wps.append(ar.alloc([128, 8, 512], BF16, "wp2"))
    Bwps.append(Buf("wp2"))
    WU, WQ, WK, WV = 0, 1, 2, 1
    load_wpiece(WU, 1536)
    load_wpiece(WQ, 0)
    load_wpiece(WK, 512)

    def pool_gen():
        kk.dve(f_memset(Eb[:, 0:15], 0.0), writes=[BE])
        kk.dve(f_memset(bdw[:], 0.0), writes=[Bbdw])
        kk.dma("sp", stp[:, 0, :], I["state_pool"][0:120, :], writes=[Bstp])
        kk.dma("sp", stp[:, 1, :], I["state_pool"][120:240, :], writes=[Bstp])
        yield
        for ch in range(2):
            for tc in range(len(TCH)):
                o, n = TCH[tc]
                bk, Bb = next_bank()
                fm_group(wps[WU], Bwps[WU], ch * 128, tc, bk, Bb)
                if tc < 4:
                    evac_copy(Eb[:, 15 + o:15 + o + n], bk[:, 0:n], [Bb], [BE])
                else:
                    evac_copy(Es[:, :, 15:23], bk[:, 0:128].rearrange("p (s i) -> p s i", i=8), [Bb], [BE])
                yield
            for j in range(2):
                bk, Bb = next_bank()
                kk.pe(mm(bk[:, 0:120], stp[:, j, ch * 128:(ch + 1) * 128], ident_f[0:120, 0:120], True, True), reads=[Bstp, Bc], writes=[Bb])
                evac_copy(Es[:, j * 8:(j + 1) * 8, 0:15], bk[:, 0:120].rearrange("p (s r) -> p s r", r=15), [Bb], [BE])
                yield

            def dbl(dst, dsts, src, srcs, sh, first):
                lo = 2 * sh - 1
                kk.dve(f_tt(dst[:, lo:], src[:, lo:], src[:, lo - sh:15 + 2048 - sh], ALU.add),
                       reads=[first], writes=[BW1 if dst is W1 else BW2])
                kk.dve(f_tt(dsts[:, :, lo:], srcs[:, :, lo:], srcs[:, :, lo - sh:23 - sh], ALU.add),
                       reads=[first], writes=[BW1 if dst is W1 else BW2])
            dbl(W1, W1s, Eb, Es, 1, BE)
            yield
            dbl(W2, W2s, W1, W1s, 2, BW1)
            yield
            if ch == 1:
                dbl(W1, W1s, W2, W2s, 4, BW2)
                yield
                dbl(W2, W2s, W1, W1s, 8, BW1)
                yield
            for half, (Wb, Wbs, BWb) in enumerate(((W1, W1s, BW1), (W2, W2s, BW2))):
                ps = slice(half * 64, (half + 1) * 64)
                kk.dve(f_stt(dTb[ps, 0:2048], Wb[ps, 15:15 + 2048], poolc[ps, ch, 15:16], Eb[ps, 15:15 + 2048], ALU.mult, ALU.subtract),
                       reads=[BWb, BE, Bc], writes=[BdT])
                yield
                kk.dve(f_tt(tmp16[ps, :], Wb[ps, 15:31], poolc[ps, ch, :], ALU.mult), reads=[BWb, Bc], writes=[Bt16])
                kk.dve(f_tt(dTb[ps, 0:16], tmp16[ps, :], Eb[ps, 15:31], ALU.subtract), reads=[Bt16, BE, BdT], writes=[BdT])
                kk.dve(f_stt(dTb[ps, 2048:2176].rearrange("p (s i) -> p s i", i=8), Wbs[ps, :, 15:23], poolc[ps, ch, 15:16],
                             Es[ps, :, 15:23], ALU.mult, ALU.subtract),
                       reads=[BWb, BE, Bc, BdT], writes=[BdT])
                yield
            for half in range(2):
                ps = slice(half * 64, (half + 1) * 64)
                kk.dma("pool", bdw[ps, half * 64:(half + 1) * 64], I["w_pool_grp"][2 * ch + half, :, :], reads=[Bbdw], writes=[Bbdw])
            for tc in range(len(TCH)):
                o, n = TCH[tc]
                bk, Bb = next_bank()
                kk.pe(mm(bk[:, 0:n], bdw[:, :], dTb[:, o:o + n], True, True), reads=[Bbdw, BdT], writes=[Bb])
                kk.dve(f_ts(poolT[:, ch, o:o + n], bk[:, 0:n], pscaleT[:, ch:ch + 1], None, ALU.mult), reads=[Bb, Bc], writes=[B_poolT[tc]])
                yield

    pgen = pool_gen()

    def pstep():
        next(pgen, None)

    for hp in range(2):
        for tc in range(len(TCH)):
            o, n = TCH[tc]
            bk, Bb = next_bank()
            fm_group(wps[WU], Bwps[WU], 256 + hp * 128, tc, bk, Bb)
            evac_copy(qmT[:, hp, o:o + n], bk[:, 0:n], [Bb], [B_qmT[tc]])
    for t in (15, 16):
        bk, Bb = next_bank()
        tm_group(wps[WU], Bwps[WU], t, bk, Bb, ncols=256, c0=0)
        si = nstg[0] % 3
        nstg[0] += 1
        evac_copy(stg[si][:, 0:256], bk[:, 0:256], [Bb], [Bstg[si]])
        if t == 15:
            kk.dma("sp", O["pool_p"][:, :], stg[si][113:128, 0:256], reads=[Bstg[si]])
        else:
            for s_ in range(NSEQ):
                kk.dma("sp", O["pool_s"][s_, 7:15, :], stg[si][s_ * 8:(s_ + 1) * 8, 0:256], reads=[Bstg[si]])
    kk.dma("sp", O["pool_s"][:, 0:7, :], I["state_pool"].rearrange("(s r) c -> s r c", r=15)[:, 8:15, :])
    for h in range(4):
        for tc in range(len(TCH)):
            o, n = TCH[tc]
            bk, Bb = next_bank()
            fm_group(wps[WQ], Bwps[WQ], h * 128, tc, bk, Bb)
            evac_copy(qT[:, h, o:o + n], bk[:, 0:n], [Bb], [B_qT[tc]])
            pstep()
    load_wpiece(WV, 1024)
    for h in range(4):
        for tc in range(len(TCH)):
            o, n = TCH[tc]
            bk, Bb = next_bank()
            fm_group(wps[WK], Bwps[WK], h * 128, tc, bk, Bb)
            evac_copy(kT[:, h, o:o + n], bk[:, 0:n], [Bb], [B_kT[tc]])
            pstep()
    for t in range(NT):
        bk, Bb = next_bank()
        tm_group(wps[WK], Bwps[WK], t, bk, Bb)
        si = nstg[0] % 3
        nstg[0] += 1
        evac_copy(stg[si][:], bk[:, :], [Bb], [Bstg[si]])
        kk.dma("sp", O["newk"][t * 128:(t + 1) * 128, :], stg[si][:], reads=[Bstg[si]])
        pstep()
    for t in range(NT):
        bk, Bb = next_bank()
        tm_group(wps[WV], Bwps[WV], t, bk, Bb)
        si = nstg[0] % 3
        nstg[0] += 1
        kk.act(f_act(stg[si][:], bk[:, :], AF.Copy), reads=[Bb], writes=[Bstg[si]])
        kk.dve(f_copy(v_bf[:, t, :], stg[si][:]), reads=[Bstg[si]], writes=[B_v[t]])
        kk.dma("sp", O["newv"][t * 128:(t + 1) * 128, :], stg[si][:], reads=[Bstg[si]])
        pstep()
    for _ in pgen:
        pass
    if "poolT" in dbg_out:
        pdbg = ar.alloc([128, 2 * 256], F32, "pdbg")
        Bp = Buf("pdbg")
        kk.dve(lambda e: e.tensor_copy(out=pdbg[:, 0:128], in_=poolT[:, 0, 0:128]), reads=B_poolT, writes=[Bp])
        kk.dve(lambda e: e.tensor_copy(out=pdbg[:, 128:256], in_=poolT[:, 1, 0:128]), reads=B_poolT, writes=[Bp])
        kk.dve(lambda e: e.tensor_copy(out=pdbg[:, 256:384], in_=poolT[:, 0, 2048:2176]), reads=B_poolT, writes=[Bp])
        kk.dve(lambda e: e.tensor_copy(out=pdbg[:, 384:512], in_=poolT[:, 1, 2048:2176]), reads=B_poolT, writes=[Bp])
        dbg("poolT", pdbg[:], [Bp])
    kk.barrier()
    if STOP == "p2":
        return
    ar.reset(wmark)


    P3 = ALL or "p3" in phases
    xin0 = ar.alloc([128, D], F32, "mxin0")
    xin1 = ar.alloc([128, D], F32, "mxin1")
    mxn = ar.alloc([128, D], BF16, "mxn")
    mjunk = ar.alloc([128, D], BF16, "mjunk")
    mss = [ar.alloc([128, 4], F32, "mss%d" % i) for i in range(2)]
    mhT = ar.alloc([128, 8, 256], BF16, "mhT")
    wmkv = ar.alloc([128, 8, 512], BF16, "wmkv")
    memkT = ar.alloc([128, 2, 256], BF16, "memkT")
    memv_pad = ar.alloc([128, 2, 4, 128], BF16, "memvpad")
    onesE = ar.alloc([128, 128], BF16, "onesE")
    onesO = ar.alloc([128, 128], BF16, "onesO")
    mstg = [ar.alloc([128, 512], F32, "mstg%d" % i) for i in range(2)]
    Bmx = [Buf("mx0"), Buf("mx1")]
    Bmxn, Bmhs, Bwmkv, BmkT, Bmvp, Bones2 = Buf("mxn"), [Buf("mh0"), Buf("mh1")], Buf("wmkv"), Buf("memkT"), Buf("memvpad"), Buf("ones2")
    Bmss = [Buf("mss0"), Buf("mss1")]
    Bmstg = [Buf("mstg0"), Buf("mstg1")]
    kk.dma("pool", wmkv[:], I["w_mem_kv"].rearrange("(k p) c -> p k c", p=128), writes=[Bwmkv])
    kk.dve(f_memset(memv_pad[:], 0.0), writes=[Bmvp])
    kk.dve(f_memset(onesE[:], 0.0), writes=[Bones2])
    kk.dve(f_memset(onesO[:], 0.0), writes=[Bones2])
    kk.dve(f_memset(onesE[:, 0:64], 1.0), writes=[Bones2])
    kk.dve(f_memset(onesO[:, 64:128], 1.0), writes=[Bones2])
    for mt in range(2):
        norm_transpose(I["mem"][mt * 128:(mt + 1) * 128, :], gmT, mhT, slice(mt * 128, (mt + 1) * 128),
                       (xin0, xin1)[mt], Bmx[mt], mxn, Bmxn, mss[mt], Bmss[mt], banks[mt], pb[mt], Bmhs[mt], mjunk, mt)
    for mt in range(2):
        bk, Bb = banks[2 + mt], pb[2 + mt]
        for kc in range(8):
            kk.pe(mm(bk[:, :], mhT[:, kc, mt * 128:(mt + 1) * 128], wmkv[:, kc, :], kc == 0, kc == 7), reads=[Bmhs[mt], Bwmkv], writes=[Bb])
        kk.act(f_act(mstg[mt][:], bk[:, :], AF.Copy), reads=[Bb], writes=[Bmstg[mt]])
        for h in range(4):
            kk.dve(f_copy(memv_pad[:, mt, h, (h % 2) * 64:(h % 2) * 64 + 64], mstg[mt][:, 256 + h * 64:256 + (h + 1) * 64]), reads=[Bmstg[mt]], writes=[Bmvp])
        kk.dma("sp", O["memk"][mt * 128:(mt + 1) * 128, :], mstg[mt][:, 0:256], reads=[Bmstg[mt]])
        kk.dma("sp", O["memv"][mt * 128:(mt + 1) * 128, :], mstg[mt][:, 256:512], reads=[Bmstg[mt]])
    for hp in range(2):
        bk, Bb = banks[4 + hp], pb[4 + hp]
        for kc in range(8):
            kk.pe(mm(bk[:, 0:256], wmkv[:, kc, hp * 128:(hp + 1) * 128], mhT[:, kc, :], kc == 0, kc == 7), reads=Bmhs + [Bwmkv], writes=[Bb])
        kk.dve(f_copy(memkT[:, hp, :], bk[:, 0:256]), reads=[Bb], writes=[BmkT])

    mpT = [ar.alloc([128, 2, 512], BF16, "mpT%d" % i) for i in range(2)]
    BmpT = [Buf("mpT0"), Buf("mpT1")]
    mrs = [ar.alloc([128, 512], F32, "mrs%d" % i) for i in range(2)]
    Bmrs = [Buf("mrs0"), Buf("mrs1")]
    it = 0
    for tc in range(4):
        o, n = TCH[tc]
        for hp in range(2):
            oc, Boc = banks[4 + 2 * (it % 2)], pb[4 + 2 * (it % 2)]
            oz, Boz = banks[5 + 2 * (it % 2)], pb[5 + 2 * (it % 2)]
            for mt in range(2):
                sa, Bsa = banks[2 * mt], pb[2 * mt]
                sb, Bsb = banks[2 * mt + 1], pb[2 * mt + 1]
                p_, Bp_ = mpT[mt], BmpT[mt]
                kk.pe(mm(sa[:, :], memkT[0:64, hp, mt * 128:(mt + 1) * 128], qmT[0:64, hp, o:o + n], True, True), reads=[BmkT, B_qmT[tc]], writes=[Bsa])
                kk.pe(mm(sb[:, :], memkT[64:128, hp, mt * 128:(mt + 1) * 128], qmT[64:128, hp, o:o + n], True, True), reads=[BmkT, B_qmT[tc]], writes=[Bsb])
                kk.act(f_act(p_[:, 0, :], sa[:, :], AF.Exp, scale=0.125), reads=[Bsa], writes=[Bp_])
                kk.act(f_act(p_[:, 1, :], sb[:, :], AF.Exp, scale=0.125), reads=[Bsb], writes=[Bp_])
                kk.pe(mm(oc[:, :], memv_pad[:, mt, 2 * hp, :], p_[:, 0, :], mt == 0, False), reads=[Bmvp, Bp_], writes=[Boc])
                kk.pe(mm(oc[:, :], memv_pad[:, mt, 2 * hp + 1, :], p_[:, 1, :], False, mt == 1), reads=[Bmvp, Bp_], writes=[Boc])
                kk.pe(mm(oz[:, :], onesE[:, :], p_[:, 0, :], mt == 0, False), reads=[Bones2, Bp_], writes=[Boz])
                kk.pe(mm(oz[:, :], onesO[:, :], p_[:, 1, :], False, mt == 1), reads=[Bones2, Bp_], writes=[Boz])
            r_, Br_ = mrs[it % 2], Bmrs[it % 2]
            kk.act(f_act(r_[:], oz[:, :], AF.Ln), reads=[Boz], writes=[Br_])
            kk.act(f_act(r_[:], r_[:], AF.Exp, scale=-1.0), reads=[Br_], writes=[Br_])
            kk.dve(f_tt(omT[:, hp, o:o + n], oc[:, :], r_[:], ALU.mult), reads=[Boc, Br_], writes=[B_omT[tc]])
            it += 1
    NCM = 4
    cmk = [ar.alloc([128, 2, 256], BF16, "cmk%d" % i) for i in range(NCM)]
    Bcmk = [Buf("cmk%d" % i) for i in range(NCM)]
    cmv = [ar.alloc([128, 2, 256], BF16, "cmv%d" % i) for i in range(NCM)]
    Bcmv = [Buf("cmv%d" % i) for i in range(NCM)]
    cmvp = [ar.alloc([128, 2, 4, 128], BF16, "cmvp%d" % i) for i in range(2)]
    Bcmvp = [Buf("cmvp0"), Buf("cmvp1")]
    kTs = [ar.alloc([128, 2, 256], BF16, "kTs%d" % i) for i in range(2)]
    BkTs = [Buf("kTs0"), Buf("kTs1")]
    pTs = [ar.alloc([128, 2, 32], BF16, "pTs%d" % i) for i in range(2)]
    BpTs = [Buf("pTs0"), Buf("pTs1")]
    for i in range(2):
        kk.pool(f_memset(cmvp[i][:], 0.0), writes=[Bcmvp[i]])
    ocs, Bocs = banks[6], pb[6]
    ozs, Bozs = banks[7], pb[7]

    def load_cm(s2):
        r = s2 % NCM
        kk.dma("pool", cmk[r][:], I["cmem_k"][s2].rearrange("(t p) c -> p t c", p=128), writes=[Bcmk[r]])
        kk.dma("pool", cmv[r][:], I["cmem_v"][s2].rearrange("(t p) c -> p t c", p=128), writes=[Bcmv[r]])

    for s2 in range(NCM - 1):
        load_cm(s2)
    for s_ in range(NSEQ):
        b = s_ % 2
        r = s_ % NCM
        if s_ + NCM - 1 < NSEQ:
            load_cm(s_ + NCM - 1)
        for h in range(4):
            kk.pool(f_copy(cmvp[b][:, :, h, (h % 2) * 64:(h % 2) * 64 + 64], cmv[r][:, :, h * 64:(h + 1) * 64]), reads=[Bcmv[r]], writes=[Bcmvp[b]])
        tb, Btb = banks[b], pb[b]
        tbf = tb[:].bitcast(BF16)
        for hp in range(2):
            for mt in range(2):
                kk.pe(f_tr(tbf[:, (hp * 2 + mt) * 128:(hp * 2 + mt + 1) * 128], cmk[r][:, mt, hp * 128:(hp + 1) * 128], ident_b[:]),
                      reads=[Bcmk[r], Bc], writes=[Btb])
        kk.dve(f_copy(kTs[b][:].rearrange("p h m -> p (h m)"), tbf[:, 0:512]), reads=[Btb], writes=[BkTs[b]])
        sa, Bsa = banks[2 + 2 * b], pb[2 + 2 * b]
        sb, Bsb = banks[3 + 2 * b], pb[3 + 2 * b]
        qs = slice(2048 + 8 * s_, 2048 + 8 * s_ + 8)
        for hp in range(2):
            for mt in range(2):
                c0 = (hp * 2 + mt) * 8
                kk.pe(mm(sa[:, c0:c0 + 8], kTs[b][0:64, hp, mt * 128:(mt + 1) * 128], qmT[0:64, hp, qs], True, True), reads=[BkTs[b], B_qmT[4]], writes=[Bsa])
                kk.pe(mm(sb[:, c0:c0 + 8], kTs[b][64:128, hp, mt * 128:(mt + 1) * 128], qmT[64:128, hp, qs], True, True), reads=[BkTs[b], B_qmT[4]], writes=[Bsb])
        kk.act(f_act(pTs[b][:, 0, :], sa[:, 0:32], AF.Exp, scale=0.125), reads=[Bsa], writes=[BpTs[b]])
        kk.act(f_act(pTs[b][:, 1, :], sb[:, 0:32], AF.Exp, scale=0.125), reads=[Bsb], writes=[BpTs[b]])
        for hp in range(2):
            oc0 = (s_ * 2 + hp) * 8
            for mt in range(2):
                c0 = (hp * 2 + mt) * 8
                kk.pe(mm(ocs[:, oc0:oc0 + 8], cmvp[b][:, mt, 2 * hp, :], pTs[b][:, 0, c0:c0 + 8], mt == 0, False), reads=[Bcmvp[b], BpTs[b]], writes=[Bocs])
                kk.pe(mm(ocs[:, oc0:oc0 + 8], cmvp[b][:, mt, 2 * hp + 1, :], pTs[b][:, 1, c0:c0 + 8], False, mt == 1), reads=[Bcmvp[b], BpTs[b]], writes=[Bocs])
            for mt in range(2):
                c0 = (hp * 2 + mt) * 8
                kk.pe(mm(ozs[:, oc0:oc0 + 8], onesE[:, :], pTs[b][:, 0, c0:c0 + 8], mt == 0, False), reads=[Bones2, BpTs[b]], writes=[Bozs])
                kk.pe(mm(ozs[:, oc0:oc0 + 8], onesO[:, :], pTs[b][:, 1, c0:c0 + 8], False, mt == 1), reads=[Bones2, BpTs[b]], writes=[Bozs])
    kk.act(f_act(mrs[0][:, 0:256], ozs[:, 0:256], AF.Ln), reads=[Bozs], writes=[Bmrs[0]])
    kk.act(f_act(mrs[0][:, 0:256], mrs[0][:, 0:256], AF.Exp, scale=-1.0), reads=[Bmrs[0]], writes=[Bmrs[0]])
    kk.dve(f_tt(omT[:, :, 2048:2176].rearrange("p h (s q) -> p h s q", q=8),
                ocs[:, 0:256].rearrange("p (s h q) -> p h s q", h=2, q=8),
                mrs[0][:, 0:256].rearrange("p (s h q) -> p h s q", h=2, q=8), ALU.mult),
           reads=[Bocs, Bmrs[0]], writes=[B_omT[4]])
    if "omT" in dbg_out:
        odbg = ar.alloc([128, 512], F32, "odbg")
        Bo = Buf("odbg")
        kk.dve(f_copy(odbg[:, 0:128], omT[:, 0, 0:128]), reads=B_omT, writes=[Bo])
        kk.dve(f_copy(odbg[:, 128:256], omT[:, 1, 1920:2048]), reads=B_omT, writes=[Bo])
        kk.dve(f_copy(odbg[:, 256:384], omT[:, 0, 2048:2176]), reads=B_omT, writes=[Bo])
        kk.dve(f_copy(odbg[:, 384:512], omT[:, 1, 2048:2176]), reads=B_omT, writes=[Bo])
        dbg("omT", odbg[:], [Bo])
    kk.barrier()
    if STOP == "p3b":
        return
    ar.reset(wmark)


    apT = [ar.alloc([128, 2, 512], BF16, "apT%d" % i) for i in range(3)]
    BapT = [Buf("apT%d" % i) for i in range(3)]
    tA = ar.alloc([128, 512], F32, "tA")
    tB = ar.alloc([128, 512], F32, "tB")
    tC = ar.alloc([128, 512], F32, "tC")
    tD = ar.alloc([128, 512], F32, "tD")
    tE = ar.alloc([128, 512], F32, "tE")
    BtA, BtB, BtC, BtD, BtE = Buf("tA"), Buf("tB"), Buf("tC"), Buf("tD"), Buf("tE")
    Sset = [((banks[0], pb[0]), (banks[1], pb[1])), ((banks[6], pb[6]), (banks[7], pb[7]))]
    O0, O1, Z0, Z1 = banks[2], banks[3], banks[4], banks[5]
    BO0, BO1, BZ0, BZ1 = pb[2], pb[3], pb[4], pb[5]
    units = []
    for h in range(4):
        for c in range(4):
            for j in range(4 * c + 4):
                units.append((h, c, j))

    def u_lo(c, j):
        return max(j - 4 * c, 0) * 128

    def emit_qk(i):
        h, c, j = units[i]
        (S0, BS0), (S1, BS1) = Sset[i % 2]
        lo = u_lo(c, j)
        q0 = c * 512
        ks = slice(j * 128, (j + 1) * 128)
        kk.pe(mm(S0[:, lo:512], kT[0:64, h, ks], qT[0:64, h, q0 + lo:q0 + 512], True, True), reads=[B_kT[j // 4], B_qT[c]], writes=[BS0])
        kk.pe(mm(S1[:, lo:512], kT[64:128, h, ks], qT[64:128, h, q0 + lo:q0 + 512], True, True), reads=[B_kT[j // 4], B_qT[c]], writes=[BS1])

    def emit_softmax(i):
        h, c, j = units[i]
        (S0, BS0), (S1, BS1) = Sset[i % 2]
        lo = u_lo(c, j)
        jj = j - 4 * c
        pT_, BpT_ = apT[i % 3], BapT[i % 3]
        kk.act(f_act(pT_[:, 0, lo:512], S0[:, lo:512], AF.Exp, scale=0.125), reads=[BS0], writes=[BpT_])
        kk.act(f_act(pT_[:, 1, lo:512], S1[:, lo:512], AF.Exp, scale=0.125), reads=[BS1], writes=[BpT_])
        t0b = T0[:, h, :].unsqueeze(1).to_broadcast([128, 2, 128])
        t1b = T1[:, h, :].unsqueeze(1).to_broadcast([128, 2, 128])
        if jj >= 0:
            kk.dve(f_tt(pT_[:, :, lo:lo + 128], pT_[:, :, lo:lo + 128], t0b, ALU.mult), reads=[BpT_, Bt], writes=[BpT_])
            if jj < 3:
                kk.dve(f_tt(pT_[:, :, lo + 128:lo + 256], pT_[:, :, lo + 128:lo + 256], t1b, ALU.mult), reads=[BpT_, Bt], writes=[BpT_])
        elif jj == -1:
            kk.dve(f_tt(pT_[:, :, 0:128], pT_[:, :, 0:128], t1b, ALU.mult), reads=[BpT_, Bt], writes=[BpT_])

    def emit_pv(i):
        h, c, j = units[i]
        lo = u_lo(c, j)
        nj = 4 * c + 4
        pT_, BpT_ = apT[i % 3], BapT[i % 3]
        vv = v_bf[:, j, h * 128:(h + 1) * 128]
        for m, (Ob, BOb, Zb, BZb) in enumerate(((O0, BO0, Z0, BZ0), (O1, BO1, Z1, BZ1))):
            kk.pe(mm(Ob[:, lo:512], vv, pT_[:, m, lo:512], j == 0, j == nj - 1), reads=[B_v[j], BpT_], writes=[BOb])
            kk.pe(mm(Zb[:, lo:512], ones_b[:, :], pT_[:, m, lo:512], j == 0, j == nj - 1), reads=[Bc, BpT_], writes=[BZb])

    def emit_tail(h, c, SSb, BSS):
        q0 = c * 512
        kk.act(f_act(tA[:], Z0[:, :], AF.Ln), reads=[BZ0], writes=[BtA])
        kk.act(f_act(tB[:], Z1[:, :], AF.Ln), reads=[BZ1], writes=[BtB])
        kk.act(f_act(tA[:], tA[:], AF.Exp, scale=-1.0), reads=[BtA], writes=[BtA])
        kk.act(f_act(tB[:], tB[:], AF.Exp, scale=-1.0), reads=[BtB], writes=[BtB])
        kk.dve(f_tt(tA[:], O0[:, :], tA[:], ALU.mult), reads=[BO0, BtA], writes=[BtA])
        kk.dve(f_tt(tB[:], O1[:, :], tB[:], ALU.mult), reads=[BO1, BtB], writes=[BtB])
        yield
        kk.dve(f_stt(tC[:], tB[:], neg_lam[:, 0:1], tA[:], ALU.mult, ALU.add), reads=[BtA, BtB, Bl], writes=[BtC])
        kk.act(f_act(tD[:], tC[:], AF.Square), reads=[BtC], writes=[BtD])
        yield
        kk.pe(mm(SSb[:, :], ones_f[:, :], tD[:], True, True), reads=[Bc, BtD], writes=[BSS])
        kk.act(f_act(tE[:], SSb[:, :], AF.Ln, scale=1.0 / 128, bias=eps_t[:, 0:1]), reads=[BSS, Bc], writes=[BtE])
        kk.act(f_act(tE[:], tE[:], AF.Exp, scale=-0.5), reads=[BtE], writes=[BtE])
        yield
        kk.dve(f_stt(oT[:, h, q0:q0 + 512], tC[:], sublnT[:, 0:1], tE[:], ALU.mult, ALU.mult), reads=[BtC, BtE, Bc], writes=[B_oT[c]])

    tail_gen = [None]

    def tail_step():
        if tail_gen[0] is not None:
            if next(tail_gen[0], "done") == "done":
                tail_gen[0] = None

    emit_qk(0)
    for i, (h, c, j) in enumerate(units):
        emit_softmax(i)
        if i + 1 < len(units):
            emit_qk(i + 1)
        emit_pv(i)
        tail_step()
        if j == 4 * c + 3:
            while tail_gen[0] is not None:
                tail_step()
            (SSb, BSS), _ = Sset[i % 2]
            tail_gen[0] = emit_tail(h, c, SSb, BSS)
            tail_step()
    while tail_gen[0] is not None:
        tail_step()
    if "oTp" in dbg_out:
        odbg2 = ar.alloc([128, 512], F32, "odbg2")
        Bo2 = Buf("odbg2")
        kk.dve(f_copy(odbg2[:, 0:128], oT[:, 0, 0:128]), reads=B_oT, writes=[Bo2])
        kk.dve(f_copy(odbg2[:, 128:256], oT[:, 1, 640:768]), reads=B_oT, writes=[Bo2])
        kk.dve(f_copy(odbg2[:, 256:384], oT[:, 2, 1920:2048]), reads=B_oT, writes=[Bo2])
        kk.dve(f_copy(odbg2[:, 384:512], oT[:, 3, 1024:1152]), reads=B_oT, writes=[Bo2])
        dbg("oTp", odbg2[:], [Bo2])
    kk.barrier()
    if STOP == "p3c":
        return
    ar.reset(wmark)


    ptb = ar.alloc([128, NSEQ * NPAGE], I32, "ptb")
    idx = ar.alloc([128, NSEQ * NPAGE], I32, "idx")
    iotaf = ar.alloc([128, 1], F32, "iotaf")
    qpad = ar.alloc([128, 4, NSEQ, 16], BF16, "qpad")
    M15 = ar.alloc([128, 4, 2, 8], F32, "M15")
    MN = ar.alloc([128, NSEQ, 4, 2, 8], F32, "MN")
    gbc = ar.alloc([128, 128], F32, "gbc")
    NV = 22
    kvpg = [ar.alloc([128, 1024], BF16, "kvpg%d" % i) for i in range(NV)]
    Bkvpg = [Buf("kvpg%d" % i) for i in range(NV)]
    KTs = [ar.alloc([128, 4, 128], BF16, "KTs%d" % i) for i in range(2)]
    BKTs = [Buf("KTs0"), Buf("KTs1")]
    spT = [ar.alloc([128, 8, 64], BF16, "spT%d" % i) for i in range(2)]
    BspT = [Buf("spT0"), Buf("spT1")]
    pn = ar.alloc([128, 64], BF16, "pn")
    Bpn = Buf("pn")
    spsum = [ar.alloc([128, 64], F32, "spsum%d" % i) for i in range(2)]
    Bspsum = [Buf("spsum0"), Buf("spsum1")]
    pnf = ar.alloc([128, 64], F32, "pnf")
    Bpnf = Buf("pnf")
    rz = ar.alloc([64, 1], F32, "rz")
    onr = ar.alloc([64, 512], F32, "onr")
    c2 = ar.alloc([32, 4, 128], F32, "c2")
    sq2 = ar.alloc([32, 4, 128], F32, "sq2")
    ss2 = ar.alloc([32, 8], F32, "ss2")
    on3 = ar.alloc([32, 4, 128], BF16, "on3")
    Brz, Bonr, Bc2, Bsq2, Bss2, Bon3 = Buf("rz"), Buf("onr"), Buf("c2"), Buf("sq2"), Buf("ss2"), Buf("on3")
    Bsetup = Buf("p3dsetup")
    kk.dma("sp", ptb[:], I["page_table"][0, :].partition_broadcast(128), writes=[Bsetup])
    kk.dma("sp", iotaf[:], I["iota_f"][:, :], writes=[Bsetup])
    kk.dve(f_ts(idx[:], ptb[:], 128.0, iotaf[:, 0:1], ALU.mult, ALU.add), reads=[Bsetup], writes=[Bsetup])
    kk.dve(f_memset(qpad[:], 0.0), writes=[Bsetup])
    kk.dve(f_copy(qpad[0:64, :, :, 0:8], qT[0:64, :, 2048:2176].rearrange("p h (s q) -> p h s q", q=8)), reads=[B_qT[4], Bsetup], writes=[Bsetup])
    kk.dve(f_copy(qpad[64:128, :, :, 8:16], qT[64:128, :, 2048:2176].rearrange("p h (s q) -> p h s q", q=8)), reads=[B_qT[4], Bsetup], writes=[Bsetup])
    for c in range(2):
        kk.dve(f_copy(M15[:, :, c, :], T1[:, :, 0:8]), reads=[Bt], writes=[Bsetup])
    for h in range(4):
        for c in range(2):
            kk.dve(f_tt(MN[:, :, h, c, :], T0[:, h, :].rearrange("p (s q) -> p s q", q=8), bdiag[:].unsqueeze(2).to_broadcast([128, NSEQ, 8]), ALU.mult),
                   reads=[Bt, Bc], writes=[Bsetup])
    kk.dma("sp", gbc[:], I["subln_g"].partition_broadcast(128), writes=[Bsetup])
    kk.dve(f_ts(gbc[:], gbc[:], 1.0 - LAM_INIT, None, ALU.mult), reads=[Bsetup], writes=[Bsetup])
    OS, BOS = banks[2], pb[2]
    ZS, BZS = banks[3], pb[3]
    C2b, BC2 = banks[4], pb[4]
    TTb, BTT = banks[5], pb[5]
    steps = [(s_, j) for s_ in range(NSEQ) for j in range(NPAGE)]

    def pg_bufs(n):
        return kvpg[n % NV], Bkvpg[n % NV], banks[n % 2], pb[n % 2], KTs[n % 2], BKTs[n % 2]

    def emit_gather(n):
        s_, j = steps[n]
        kvb, Bkvb = kvpg[n % NV], Bkvpg[n % NV]
        col = s_ * NPAGE + j
        kk.op("pool", (lambda e, kvb=kvb, col=col: e.indirect_dma_start(
            out=kvb[:, :], out_offset=None, in_=I["cache_kv"][:, :],
            in_offset=bass.IndirectOffsetOnAxis(ap=idx[:, col:col + 1], axis=0))), reads=[Bsetup], writes=[Bkvb], dma=True)

    def emit_tr(n):
        kvb, Bkvb, tb, Btb, kt_, Bkt_ = pg_bufs(n)
        tbf = tb[:].bitcast(BF16)
        for h in range(4):
            kk.pe(f_tr(tbf[:, h * 128:(h + 1) * 128], kvb[:, h * 128:(h + 1) * 128], ident_b[:]), reads=[Bkvb, Bc], writes=[Btb])
        kk.dve(f_copy(kt_[:].rearrange("p h k -> p (h k)"), tbf[:, 0:512]), reads=[Btb], writes=[Bkt_])

    def emit_qk_s(n):
        s_, j = steps[n]
        kvb, Bkvb, tb, Btb, kt_, Bkt_ = pg_bufs(n)
        Sb, BSb = banks[6 + (j // 8)], pb[6 + (j // 8)]
        for h in range(4):
            c0 = (j % 8) * 64 + h * 16
            kk.pe(mm(Sb[:, c0:c0 + 16], kt_[:, h, :], qpad[:, h, s_, :], True, True), reads=[Bkt_, Bsetup], writes=[BSb])

    def sample_tail(s_):
        kk.dve(f_recip(rz[:], ZS[0:64, 0:1]), reads=[BZS], writes=[Brz])
        kk.dve(f_ts(onr[:], OS[0:64, :], rz[:, 0:1], None, ALU.mult), reads=[BOS, Brz], writes=[Bonr])
        yield
        kk.pe(mm(C2b[0:32, :], comb[:, :], onr[:, :], True, True), reads=[Bl, Bonr], writes=[BC2])
        yield
        kk.dve(f_copy(c2[:].rearrange("p h e -> p (h e)"), C2b[0:32, :]), reads=[BC2], writes=[Bc2])
        kk.act(f_act(sq2[:], c2[:], AF.Square), reads=[Bc2], writes=[Bsq2])
        yield
        kk.dve(lambda e: e.tensor_reduce(out=ss2[:, 0:4], in_=sq2[:], axis=AX.X, op=ALU.add), reads=[Bsq2], writes=[Bss2])
        kk.act(f_act(ss2[:, 4:8], ss2[:, 0:4], AF.Ln, scale=1.0 / 128, bias=eps_t[0:32, 0:1]), reads=[Bss2, Bc], writes=[Bss2])
        kk.act(f_act(ss2[:, 4:8], ss2[:, 4:8], AF.Exp, scale=-0.5), reads=[Bss2], writes=[Bss2])
        yield
        kk.dve(f_tt(c2[:], c2[:], ss2[:, 4:8].unsqueeze(2).to_broadcast([32, 4, 128]), ALU.mult), reads=[Bc2, Bss2], writes=[Bc2])
        kk.dve(f_tt(on3[:], c2[:], gbc[0:32, :].unsqueeze(1).to_broadcast([32, 4, 128]), ALU.mult), reads=[Bc2, Bsetup], writes=[Bon3])
        yield
        ttf = TTb[:].bitcast(BF16)
        for h in range(4):
            kk.pe(f_tr(ttf[:, h * 32:(h + 1) * 32], on3[:, h, :], ident_b[0:32, 0:32]), reads=[Bon3, Bc], writes=[BTT])
        yield
        for h in range(4):
            kk.dve(f_copy(oT[:, h, 2048 + 8 * s_:2048 + 8 * s_ + 8], ttf[:, h * 32 + h * 8:h * 32 + h * 8 + 8]), reads=[BTT], writes=[B_oT[4]])

    stail = [None]

    def stail_step():
        if stail[0] is not None:
            if next(stail[0], "done") == "done":
                stail[0] = None

    NPRE = NV - 8
    for n in range(min(NPRE, len(steps))):
        emit_gather(n)
    emit_tr(0)
    for n, (s_, j) in enumerate(steps):
        if n + NPRE < len(steps):
            emit_gather(n + NPRE)
        if n + 1 < len(steps):
            emit_tr(n + 1)
        emit_qk_s(n)
        stail_step()
        if j % 8 == 7:
            half = j // 8
            Sb, BSb = banks[6 + half], pb[6 + half]
            sp_, Bsp_ = spT[half], BspT[half]
            kk.act(f_act(sp_[:].rearrange("p j c -> p (j c)"), Sb[:, :], AF.Exp, scale=0.125), reads=[BSb], writes=[Bsp_])
            if half == 1:
                kk.dve(f_tt(sp_[:, 7, :], sp_[:, 7, :], M15[:].rearrange("p h c q -> p (h c q)"), ALU.mult), reads=[Bsp_, Bsetup], writes=[Bsp_])
            for jj in range(8):
                jp = half * 8 + jj
                m_ = n - 7 + jj
                kvb, Bkvb = kvpg[m_ % NV], Bkvpg[m_ % NV]
                kk.pe(mm(OS[0:64, :], sp_[:, jj, :], kvb[:, 512:1024], jp == 0, False), reads=[Bsp_, Bkvb], writes=[BOS])
                kk.pe(mm(ZS[0:64, 0:1], sp_[:, jj, :], ones_b[:, 0:1], jp == 0, False), reads=[Bsp_, Bc], writes=[BZS])
        if j != NPAGE - 1:
            continue
        Sb, BSb = banks[6], pb[6]
        for h in range(4):
            kk.pe(mm(Sb[:, h * 16:(h + 1) * 16], kT[:, h, 2048:2176], qpad[:, h, s_, :], True, True), reads=[B_kT[4], Bsetup], writes=[BSb])
        kk.act(f_act(pn[:], Sb[:, 0:64], AF.Exp, scale=0.125), reads=[BSb], writes=[Bpn])
        kk.dve(f_tt(pn[:], pn[:], MN[:, s_].rearrange("p h c q -> p (h c q)"), ALU.mult), reads=[Bpn, Bsetup], writes=[Bpn])
        kk.pe(mm(OS[0:64, :], pn[:, :], v_bf[:, 16, :], False, True), reads=[Bpn, B_v[16]], writes=[BOS])
        kk.pe(mm(ZS[0:64, 0:1], pn[:, :], ones_b[:, 0:1], False, True), reads=[Bpn, Bc], writes=[BZS])
        while stail[0] is not None:
            stail_step()
        stail[0] = sample_tail(s_)
        stail_step()
    while stail[0] is not None:
        stail_step()
    if "oTs" in dbg_out:
        odbg3 = ar.alloc([128, 512], F32, "odbg3")
        Bo3 = Buf("odbg3")
        kk.dve(f_copy(odbg3[:].rearrange("p (h t) -> p h t", h=4), oT[:, :, 2048:2176]), reads=B_oT, writes=[Bo3])
        dbg("oTs", odbg3[:], [Bo3])
    kk.barrier()
    if STOP == "p3d":
        return
    ar.reset(wmark)


    ar.reset(omark)
    mergedT = ar.alloc([128, 8, NTOK], BF16, "mergedT")
    B_mg = [Buf("mg%d" % i) for i in range(len(TCH))]
    p4mark = ar.mark()
    wg = [ar.alloc([128, 8, 3, 128], BF16, "wg%d" % i) for i in range(2)]
    Bwg = [Buf("wg0"), Buf("wg1")]
    wbr = [ar.alloc([128, 8, 128], BF16, "wbr%d" % i) for i in range(2)]
    Bwbr = [Buf("wbr0"), Buf("wbr1")]
    sg = [ar.alloc([128, 512], F32, "sg%d" % i) for i in range(6)]
    Bsg = [Buf("sg%d" % i) for i in range(6)]
    mt_ = [ar.alloc([128, 512], F32, "mtmp%d" % i) for i in range(4)]
    Bmt = [Buf("mtmp%d" % i) for i in range(4)]
    bankctr = [0]

    def rbank():
        b = bankctr[0] % 8
        bankctr[0] += 1
        return banks[b], pb[b]

    def load_p4(fc):
        b = fc % 2
        for g in range(3):
            c0 = 2048 + g * 1024 + fc * 128
            kk.dma("pool", wg[b][:, :, g, :], I["w_in"][:, c0:c0 + 128].rearrange("(k p) c -> p k c", p=128), writes=[Bwg[b]])
        kk.dma("pool", wbr[b][:, 0:4, :], I["w_br_attn"][:, fc * 128:(fc + 1) * 128].rearrange("(k p) c -> p k c", p=128), writes=[Bwbr[b]])
        kk.dma("pool", wbr[b][:, 4:6, :], I["w_br_pool"][:, fc * 128:(fc + 1) * 128].rearrange("(k p) c -> p k c", p=128), writes=[Bwbr[b]])
        kk.dma("pool", wbr[b][:, 6:8, :], I["w_br_mem"][:, fc * 128:(fc + 1) * 128].rearrange("(k p) c -> p k c", p=128), writes=[Bwbr[b]])

    load_p4(0)
    un = 0
    for fc in range(8):
        if fc + 1 < 8:
            load_p4(fc + 1)
        b = fc % 2
        for tc in range(len(TCH)):
            o, n = TCH[tc]
            hreads = [B_hT[t] for t in tiles_of(tc)]
            prods = []
            for g in range(3):
                gb, Bgb = rbank()
                for kc in range(8):
                    kk.pe(mm(gb[:, 0:n], wg[b][:, kc, g, :], hT[:, kc, o:o + n], kc == 0, kc == 7), reads=[Bwg[b]] + hreads, writes=[Bgb])
                bb, Bbb = rbank()
                if g == 0:
                    for h in range(4):
                        kk.pe(mm(bb[:, 0:n], wbr[b][:, h, :], oT[:, h, o:o + n], h == 0, h == 3), reads=[Bwbr[b], B_oT[tc]], writes=[Bbb])
                elif g == 1:
                    for ch in range(2):
                        kk.pe(mm(bb[:, 0:n], wbr[b][:, 4 + ch, :], poolT[:, ch, o:o + n], ch == 0, ch == 1), reads=[Bwbr[b], B_poolT[tc]], writes=[Bbb])
                else:
                    for hp in range(2):
                        kk.pe(mm(bb[:, 0:n], wbr[b][:, 6 + hp, :], omT[:, hp, o:o + n], hp == 0, hp == 1), reads=[Bwbr[b], B_omT[tc]], writes=[Bbb])
                si = (un * 3 + g) % 6
                kk.act(f_act(sg[si][:, 0:n], gb[:, 0:n], AF.Sigmoid), reads=[Bgb], writes=[Bsg[si]])
                kk.dve(f_tt(sg[si][:, 0:n], sg[si][:, 0:n], bb[:, 0:n], ALU.mult), reads=[Bsg[si], Bbb], writes=[Bsg[si]])
                prods.append(si)
            mi = un % 4
            kk.dve(f_tt(mt_[mi][:, 0:n], sg[prods[0]][:, 0:n], sg[prods[1]][:, 0:n], ALU.add), reads=[Bsg[prods[0]], Bsg[prods[1]]], writes=[Bmt[mi]])
            kk.dve(f_tt(mergedT[:, fc, o:o + n], mt_[mi][:, 0:n], sg[prods[2]][:, 0:n], ALU.add), reads=[Bmt[mi], Bsg[prods[2]]], writes=[B_mg[tc]])
            un += 1
    if "mergedT" in dbg_out:
        mdbg = ar.alloc([128, 512], F32, "mdbg")
        Bm_ = Buf("mdbg")
        kk.dve(f_copy(mdbg[:, 0:128], mergedT[:, 0, 0:128]), reads=B_mg, writes=[Bm_])
        kk.dve(f_copy(mdbg[:, 128:256], mergedT[:, 7, 1024:1152]), reads=B_mg, writes=[Bm_])
        kk.dve(f_copy(mdbg[:, 256:384], mergedT[:, 3, 2048:2176]), reads=B_mg, writes=[Bm_])
        kk.dve(f_copy(mdbg[:, 384:512], mergedT[:, 5, 2048:2176]), reads=B_mg, writes=[Bm_])
        dbg("mergedT", mdbg[:], [Bm_])
    kk.barrier()
    if STOP == "p4":
        return
    ar.reset(p4mark)

    arA = Arena(nc, amark, omark)
    wout = arA.alloc([128, 8, D], BF16, "wout")
    Bwout = Buf("wout")
    wd = arA.alloc([128, NFF, D], BF16, "wd")
    Bwd = [Buf("wd%d" % i) for i in range(NFF)]
    wgu = [arA.alloc([128, 8, 2, 128], BF16, "wgu%d" % i) for i in range(2)]
    Bwgu = [Buf("wgu%d" % i) for i in range(3)]
    stc = [ar.alloc([32, 512], F32, "stc%d" % i) for i in range(2)]
    stT = ar.alloc([128, NFF, NSEQ, 2], F32, "stT")
    cs = ar.alloc([128, NFF, 34], F32, "cs")
    halo = [ar.alloc([128, NFF, 2], F32, "halo%d" % i) for i in range(2)]
    Bstc, BstT, Bcs, Bhalo = [Buf("stc0"), Buf("stc1")], Buf("stT"), Buf("cs"), [Buf("halo0"), Buf("halo1")]
    for k2 in range(2):
        kk.dma("pool", wout[:, k2 * 4:(k2 + 1) * 4, :], I["w_out"][k2 * 512:(k2 + 1) * 512, :].rearrange("(k p) c -> p k c", p=128), writes=[Bwout])
    kk.dve(f_memset(halo[0][:], 0.0), writes=[Bhalo[0]])
    for q4 in range(6):
        nf = min(4, NFF - q4 * 4)
        kk.dma("sp", stc[q4 % 2][:, 0:nf * 128], I["state_conv"][:, q4 * 512:q4 * 512 + nf * 128], writes=[Bstc[q4 % 2]])
        for i in range(nf):
            fcx = q4 * 4 + i
            bk, Bb = rbank()
            kk.pe(mm(bk[:, 0:32], stc[q4 % 2][:, i * 128:(i + 1) * 128], ident_f[0:32, 0:32], True, True), reads=[Bstc[q4 % 2], Bc], writes=[Bb])
            kk.dve(f_copy(stT[:, fcx].rearrange("p s r -> p (s r)"), bk[:, 0:32]), reads=[Bb], writes=[BstT])
    x2 = ar.alloc([128, 4, D], F32, "x2")
    Bx2 = [Buf("x2_%d" % i) for i in range(4)]
    h2T = ar.alloc([128, 8, 512], BF16, "h2T")
    Bh2 = [Buf("h2_%d" % i) for i in range(4)]
    actT = ar.alloc([128, NFF, 512], BF16, "actT")
    Bact = [Buf("act%d" % i) for i in range(NFF)]
    gS = [ar.alloc([128, 2 + 512], F32, "gS%d" % i) for i in range(2)]
    BgS = [Buf("gS0"), Buf("gS1")]
    gSs = [ar.alloc([128, NSEQ, 10], F32, "gSs%d" % i) for i in range(2)]
    BgSs = [Buf("gSs0"), Buf("gSs1")]
    c1 = [ar.alloc([128, 512], F32, "c1_%d" % i) for i in range(2)]
    Bc1 = [Buf("c1_0"), Buf("c1_1")]
    ge = [ar.alloc([128, 512], F32, "ge%d" % i) for i in range(2)]
    Bge = [Buf("ge0"), Buf("ge1")]
    xin5 = [ar.alloc([128, D], F32, "xin5_%d" % i) for i in range(2)]
    Bxin5 = [Buf("xin5_0"), Buf("xin5_1")]
    xn5 = ar.alloc([128, D], BF16, "xn5")
    Bxn5 = Buf("xn5")
    junk5 = ar.alloc([128, D], BF16, "junk5")
    Bjunk5 = Buf("junk5")
    ss5 = [ar.alloc([128, 4], F32, "ss5_%d" % i) for i in range(2)]
    Bss5 = [Buf("ss5_0"), Buf("ss5_1")]
    wgu.append(ar.alloc([128, 8, 2, 128], BF16, "wgu2"))
    yt = None
    csr = [ar.alloc([34, 512], F32, "csr%d" % i) for i in range(2)]
    Bcsr = [Buf("csr0"), Buf("csr1")]
    for fcx in range(NFF):
        kk.dma("pool", wd[:, fcx, :], I["w_ffn_down"][fcx * 128:(fcx + 1) * 128, :], writes=[Bwd[fcx]])
    def emit_p5a(gi5, li):
        t0_, t1_ = GROUPS[gi5]
        t = t0_ + li
        k5 = nt5c[0]
        nt5c[0] += 1
        xb_, Bxb_ = xin5[k5 % 2], Bxin5[k5 % 2]
        kk.dma("sp", xb_[:], I["x_all"][t * 128:(t + 1) * 128, :], writes=[Bxb_])
        for half in range(2):
            bk, Bb = rbank()
            for kc in range(8):
                kk.pe(mm(bk[:, :], mergedT[:, kc, t * 128:(t + 1) * 128], wout[:, kc, half * 512:(half + 1) * 512], kc == 0, kc == 7),
                      reads=[B_mg[t // 4], Bwout], writes=[Bb])
            kk.dve(f_tt(x2[:, li, half * 512:(half + 1) * 512], bk[:, :], xb_[:, half * 512:(half + 1) * 512], ALU.add), reads=[Bb, Bxb_], writes=[Bx2[li]])
        norm_stats(None, x2[:, li, :], Bx2[li], xn5, Bxn5, ss5[k5 % 2], Bss5[k5 % 2], junk5)

    def emit_p5b(gi5, li):
        bk, Bb = rbank()
        norm_tr(g2T, h2T, slice(li * 128, (li + 1) * 128), xn5, Bxn5, bk, Bb, Bh2[li])

    def emit_p5(gi5, li):
        emit_p5a(gi5, li)
        emit_p5b(gi5, li)

    def emit_p7(gi7, li):
        t0_, t1_ = GROUPS[gi7]
        t = t0_ + li
        for half in range(2):
            bk, Bb = rbank()
            for fcx in range(NFF):
                kk.pe(mm(bk[:, :], actT[:, fcx, li * 128:(li + 1) * 128], wd[:, fcx, half * 512:(half + 1) * 512], fcx == 0, fcx == NFF - 1),
                      reads=[Bact[fcx], Bwd[fcx]], writes=[Bb])
            kk.dve(f_tt(x2[:, li, half * 512:(half + 1) * 512], bk[:, :], x2[:, li, half * 512:(half + 1) * 512], ALU.add), reads=[Bb, Bx2[li]], writes=[Bx2[li]])
        si = nt7c[0] % 2
        nt7c[0] += 1
        yb, Byb = y7[si], By7[si]
        kk.act(f_act(junk5[:], x2[:, li, :], AF.Square, accum_out=ss7[si][:, 0:1]), reads=[Bx2[li]], writes=[Bss7[si], Bjunk5])
        kk.act(f_act(ss7[si][:, 1:2], ss7[si][:, 0:1], AF.Sqrt, scale=1.0 / D, bias=eps_t[:, 0:1]), reads=[Bss7[si], Bc], writes=[Bss7[si]])
        kk.dve(f_recip(ss7[si][:, 2:3], ss7[si][:, 1:2]), reads=[Bss7[si]], writes=[Bss7[si]])
        kk.dve(f_stt(yb[:], x2[:, li, :], ss7[si][:, 2:3], gfin[:], ALU.mult, ALU.mult), reads=[Bx2[li], Bss7[si], Bc], writes=[Byb])
        kk.dma("sp", O["y_all"][t * 128:(t + 1) * 128, :], yb[:], reads=[Byb])

    nt5c = [0]
    nt7c = [0]
    ss7 = [ar.alloc([128, 4], F32, "ss7_%d" % i) for i in range(2)]
    Bss7 = [Buf("ss7_0"), Buf("ss7_1")]
    y7 = [ar.alloc([128, D], F32, "y7_0")] * 2
    By7 = [Buf("y7_0")] * 2
    for li in range(GROUPS[0][1] - GROUPS[0][0]):
        emit_p5(0, li)
    nwl = [0]
    nt5 = 0
    for gi, (t0, t1) in enumerate(GROUPS):
        ntile = t1 - t0
        smp = gi == len(GROUPS) - 1
        lastp = gi == len(GROUPS) - 2
        ntk = ntile * 128
        n = ntk
        hin, hout = halo[gi % 2], halo[(gi + 1) % 2]
        Bhin, Bhout = Bhalo[gi % 2], Bhalo[(gi + 1) % 2]
        for fcx in range(NFF):
            wi = nwl[0] % 3
            nwl[0] += 1
            wflat = wgu[wi][:].rearrange("p k g c -> p (k g c)")
            if gi == 0:
                kk.dma("pool", wgu[wi][:, :, 0, :], I["w_ffn_gate"][:, fcx * 128:(fcx + 1) * 128].rearrange("(k p) c -> p k c", p=128), writes=[Bwgu[wi]])
                kk.dma("pool", wgu[wi][:, :, 1, :], I["w_ffn_up"][:, fcx * 128:(fcx + 1) * 128].rearrange("(k p) c -> p k c", p=128), writes=[Bwgu[wi]])
                kk.dma("sp", wscr[fcx], wflat, reads=[Bwgu[wi]], writes=[Bwscr[fcx]])
            else:
                kk.dma("sp", wflat, wscr[fcx], reads=[Bwscr[fcx]], writes=[Bwgu[wi]])
            bi = fcx % 2
            g_, Bg_ = gS[bi], BgS[bi]
            gs_, Bgs_ = gSs[bi], BgSs[bi]
            c_, Bc_ = c1[bi], Bc1[bi]
            e_, Be_ = ge[bi], Bge[bi]
            hreads = [Bh2[i] for i in range(ntile)]
            gb, Bgb = rbank()
            for kc in range(8):
                kk.pe(mm(gb[:, 0:n], wgu[wi][:, kc, 0, :], h2T[:, kc, 0:n], kc == 0, kc == 7), reads=[Bwgu[wi]] + hreads, writes=[Bgb])
            ub, Bub = rbank()
            for kc in range(8):
                kk.pe(mm(ub[:, 0:n], wgu[wi][:, kc, 1, :], h2T[:, kc, 0:n], kc == 0, kc == 7), reads=[Bwgu[wi]] + hreads, writes=[Bub])
            w0, w1, w2, bb_ = convw[:, 0, fcx:fcx + 1], convw[:, 1, fcx:fcx + 1], convw[:, 2, fcx:fcx + 1], convb[:, fcx:fcx + 1]
            if not smp:
                kk.act(f_act(g_[:, 2:2 + 512], gb[:, 0:512], AF.Copy), reads=[Bgb], writes=[Bg_])
                kk.dve(f_copy(g_[:, 0:2], hin[:, fcx, :]), reads=[Bhin], writes=[Bg_])
                kk.dve(f_copy(hout[:, fcx, :], g_[:, 512:514]), reads=[Bg_], writes=[Bhout])
                kk.dve(f_ts(c_[:, 0:512], g_[:, 0:512], w0, bb_, ALU.mult, ALU.add), reads=[Bg_, Bc, Bc5], writes=[Bc_])
                kk.dve(f_stt(c_[:, 0:512], g_[:, 1:513], w1, c_[:, 0:512], ALU.mult, ALU.add), reads=[Bg_, Bc_, Bc], writes=[Bc_])
                kk.dve(f_stt(c_[:, 0:512], g_[:, 2:514], w2, c_[:, 0:512], ALU.mult, ALU.add), reads=[Bg_, Bc_, Bc], writes=[Bc_])
                if lastp:
                    kk.dve(f_copy(cs[:, fcx, 0:2], g_[:, 512:514]), reads=[Bg_], writes=[Bcs])
            else:
                kk.act(f_act(gs_[:, :, 2:10], gb[:, 0:128].rearrange("p (s i) -> p s i", i=8), AF.Copy), reads=[Bgb], writes=[Bgs_])
                kk.dve(f_copy(gs_[:, :, 0:2], stT[:, fcx]), reads=[BstT], writes=[Bgs_])
                cv = c_[:, 0:128].rearrange("p (s i) -> p s i", i=8)
                kk.dve(f_ts(cv, gs_[:, :, 0:8], w0, bb_, ALU.mult, ALU.add), reads=[Bgs_, Bc, Bc5], writes=[Bc_])
                kk.dve(f_stt(cv, gs_[:, :, 1:9], w1, cv, ALU.mult, ALU.add), reads=[Bgs_, Bc_, Bc], writes=[Bc_])
                kk.dve(f_stt(cv, gs_[:, :, 2:10], w2, cv, ALU.mult, ALU.add), reads=[Bgs_, Bc_, Bc], writes=[Bc_])
                kk.dve(f_copy(cs[:, fcx, 2:34].rearrange("p (s r) -> p s r", r=2), gs_[:, :, 8:10]), reads=[Bgs_], writes=[Bcs])
            kk.act(f_act(e_[:, 0:n], c_[:, 0:n], AF.Gelu_apprx_tanh), reads=[Bc_], writes=[Be_])
            kk.dve(f_tt(actT[:, fcx, 0:n], e_[:, 0:n], ub[:, 0:n], ALU.mult), reads=[Be_, Bub], writes=[Bact[fcx]])
        nnext = (GROUPS[gi + 1][1] - GROUPS[gi + 1][0]) if gi + 1 < len(GROUPS) else 0
        pend_b = None
        for li in range(max(ntile, nnext)):
            if li < ntile:
                emit_p7(gi, li)
            if pend_b is not None:
                emit_p5b(gi + 1, pend_b)
                pend_b = None
            if li < nnext:
                emit_p5a(gi + 1, li)
                pend_b = li
        if pend_b is not None:
            emit_p5b(gi + 1, pend_b)
    for q4 in range(6):
        bk, Bb = rbank()
        nf = min(4, NFF - q4 * 4)
        for i in range(nf):
            fcx = q4 * 4 + i
            kk.pe(mm(bk[0:34, i * 128:(i + 1) * 128], cs[:, fcx, :], ident_f[:, :], True, True), reads=[Bcs, Bc], writes=[Bb])
        kk.dve(f_copy(csr[q4 % 2][:, 0:nf * 128], bk[0:34, 0:nf * 128]), reads=[Bb], writes=[Bcsr[q4 % 2]])
        kk.dma("sp", O["conv_all"][:, q4 * 512:q4 * 512 + nf * 128], csr[q4 % 2][:, 0:nf * 128], reads=[Bcsr[q4 % 2]])

    kk.barrier()


_NC_CACHE = {}


def kernel(**inputs):
    f32 = lambda a: np.ascontiguousarray(np.asarray(a, dtype=np.float32))
    if "nc" not in _NC_CACHE:
        _NC_CACHE["nc"] = build_program()
    nc = _NC_CACHE["nc"]
    consts = host_constants()
    x_prompt = f32(inputs["x_prompt"])
    x_sample = f32(inputs["x_sample"])
    mem_prompt = f32(inputs["mem_prompt"])
    cache_kv = np.concatenate([f32(inputs["cache_k"]).reshape(-1, 512), f32(inputs["cache_v"]).reshape(-1, 512)], axis=1)
    page_table = np.ascontiguousarray(np.asarray(inputs["page_table"], dtype=np.int32))
    state_pool = f32(inputs["state_pool"])[0]
    state_conv = f32(inputs["state_ffn_conv"])[0]
    cmk = f32(inputs["cache_mem_k"])[0]
    cmv = f32(inputs["cache_mem_v"])[0]
    shared = {
        "cache_kv": cache_kv,
        "norm1_g": f32(inputs["norm1_g"])[0], "w_in": f32(inputs["w_in"])[0],
        "lam_q1": f32(inputs["lam_q1"]), "lam_k1": f32(inputs["lam_k1"]),
        "lam_q2": f32(inputs["lam_q2"]), "lam_k2": f32(inputs["lam_k2"]),
        "subln_g": f32(inputs["subln_g"])[0], "w_pool_grp": f32(inputs["w_pool_grp"])[0],
        "pool_scale": f32(inputs["pool_scale"])[0],
        "w_br_attn": f32(inputs["w_br_attn"])[0], "w_br_pool": f32(inputs["w_br_pool"])[0],
        "w_br_mem": f32(inputs["w_br_mem"])[0], "mem_norm_g": f32(inputs["mem_norm_g"])[0],
        "w_mem_kv": f32(inputs["w_mem_kv"])[0], "w_out": f32(inputs["w_out"])[0],
        "norm2_g": f32(inputs["norm2_g"])[0], "w_ffn_gate": f32(inputs["w_ffn_gate"])[0],
        "w_ffn_up": f32(inputs["w_ffn_up"])[0], "ffn_conv_w": f32(inputs["ffn_conv_w"])[0],
        "ffn_conv_b": f32(inputs["ffn_conv_b"])[0], "w_ffn_down": f32(inputs["w_ffn_down"])[0],
        "rel_bias": f32(inputs["rel_bias"]), "final_norm_g": f32(inputs["final_norm_g"]),
    }
    shared.update(consts)
    in_maps = []
    for c in range(8):
        sl = slice(NSEQ * c, NSEQ * (c + 1))
        m = dict(shared)
        m["x_all"] = np.ascontiguousarray(np.concatenate([x_prompt[c], x_sample[sl].reshape(128, D)], axis=0))
        m["mem"] = mem_prompt[c]
        m["page_table"] = np.ascontiguousarray(page_table[sl].reshape(1, NSEQ * NPAGE))
        m["state_pool"] = np.ascontiguousarray(state_pool[sl].reshape(NSEQ * 15, 256))
        m["state_conv"] = np.ascontiguousarray(state_conv[sl].reshape(NSEQ * 2, D_FF))
        m["cmem_k"] = np.ascontiguousarray(cmk[sl].reshape(NSEQ, 256, 256))
        m["cmem_v"] = np.ascontiguousarray(cmv[sl].reshape(NSEQ, 256, 256))
        in_maps.append(m)
    res = run_bass_kernel_spmd(nc, in_maps, core_ids=list(range(8)))
    R = res.results
    g = lambda k: [np.asarray(R[c][k], dtype=np.float32) for c in range(8)]
    y = g("y_all"); nk = g("newk"); nv = g("newv")
    y_prompt = np.stack([a[:2048] for a in y], 0)
    y_sample = np.concatenate([a[2048:].reshape(NSEQ, 8, D) for a in y], 0)
    nkp = np.stack([a[:2048].reshape(2048, 4, 128) for a in nk], 0)[None]
    nvp = np.stack([a[:2048].reshape(2048, 4, 128) for a in nv], 0)[None]
    nks = np.concatenate([a[2048:].reshape(NSEQ, 8, 4, 128) for a in nk], 0)[None]
    nvs = np.concatenate([a[2048:].reshape(NSEQ, 8, 4, 128) for a in nv], 0)[None]
    pp = np.stack(g("pool_p"), 0)[None]
    ps = np.concatenate(g("pool_s"), 0)[None]
    cv = g("conv_all")
    cp = np.stack([a[:2] for a in cv], 0)[None]
    cs = np.concatenate([a[2:].reshape(NSEQ, 2, D_FF) for a in cv], 0)[None]
    mk = np.stack([a.reshape(256, 4, 64) for a in g("memk")], 0)[None]
    mv = np.stack([a.reshape(256, 4, 64) for a in g("memv")], 0)[None]
    return (y_prompt, y_sample, nkp, nvp, nks, nvs, pp, ps, cp, cs, mk, mv)
```
